# Optimizing a Trainium2 kernel written in Bass

```python
import math
import jax
import jax.numpy as jnp
from jax import lax
import numpy as np


D_MODEL = 1024
BATCH = 4
SEQ = 8192
DEPTH = 2

D_INNER = 2 * D_MODEL
GROUP_W = D_INNER // 4
SHORT_CONV = 3
RMS_EPS = 1e-6

HY_W = GROUP_W
HY_ORDER = 2
HY_POS_BANDS = 8
HY_POS_DIM = 1 + 2 * HY_POS_BANDS
HY_FILT_HID = 64
HY_FILT_OUT = HY_ORDER * 2 * HY_W

MB_W = GROUP_W
MB_HEADDIM = 64
MB_H = MB_W // MB_HEADDIM
MB_GROUPS = 2
MB_STATE = 128
MB_XBC = MB_W + 2 * MB_GROUPS * MB_STATE
MB_CHUNK = 128

ML_W = GROUP_W
ML_H = 4
ML_DH = ML_W // ML_H
ML_CHUNK = 128

NA_W = GROUP_W
NA_DH = 64
NA_H = NA_W // NA_DH
NA_KR_MAX = 8
NA_KC = 16
GRID_W = 64

IN_SPLITS = (3 * HY_W, HY_W, MB_XBC, MB_W, 2 * MB_H, 2 * ML_W, ML_W, ML_W, ML_W, 4 * ML_H, 3 * NA_W, NA_W)
N_IN = sum(IN_SPLITS)

kernel_name = "hybrid_parallel_heads_encoder"


def _split_cols(h, widths):
    out, start = [], 0
    for w in widths:
        out.append(h[..., start:start + w])
        start += w
    return out


def _rmsnorm(x, w):
    xf = x.astype(jnp.float32)
    y = xf * lax.rsqrt(jnp.mean(xf * xf, axis=-1, keepdims=True) + RMS_EPS)
    return (y * w.astype(jnp.float32)).astype(x.dtype)


def _short_conv(u, w, b):
    pad = w.shape[0] // 2
    y = lax.conv_general_dilated(u, w[:, None, :], window_strides=(1,), padding=[(pad, pad)],
                                 dimension_numbers=('NWC', 'WIO', 'NWC'), feature_group_count=u.shape[-1])
    return y + b


def _hyena_positions(L):
    t = jnp.arange(L, dtype=jnp.float32)
    t_norm = t / L
    bands = jnp.arange(1, HY_POS_BANDS + 1, dtype=jnp.float32)
    ang = (2.0 * math.pi / L) * t[:, None] * bands[None, :]
    pos = jnp.concatenate([t_norm[:, None], jnp.cos(ang), jnp.sin(ang)], axis=-1)
    return pos, t_norm


def _hyena_filters(pos, t_norm, w1, b1, w2, b2, w3, freq, decay):
    hid = jnp.sin(freq * (pos @ w1 + b1))
    hid = jnp.sin(freq * (hid @ w2 + b2))
    filt = (hid @ w3) * jnp.exp(-t_norm[:, None] * decay)
    return filt.reshape(-1, HY_ORDER, 2, HY_W)


def _bidir_fftconv(u, h_fwd, h_bwd, skip):
    L, C = h_fwd.shape
    n = 2 * L
    g = jnp.concatenate([h_fwd, jnp.zeros((1, C), h_fwd.dtype), h_bwd[:0:-1]], axis=0).astype(jnp.float32)
    uf = u.astype(jnp.float32)
    y = jnp.fft.irfft(jnp.fft.rfft(uf, n=n, axis=1) * jnp.fft.rfft(g, n=n, axis=0)[None], n=n, axis=1)[:, :L]
    return (y + uf * skip.astype(jnp.float32)).astype(u.dtype)


def _hyena_branch(u3, gate, conv_w, conv_b, filt, skip):
    u3 = _short_conv(u3, conv_w, conv_b)
    v, x1, x2 = _split_cols(u3, (HY_W, HY_W, HY_W))
    z = x1 * _bidir_fftconv(v, filt[:, 0, 0], filt[:, 0, 1], skip[0])
    y = x2 * _bidir_fftconv(z, filt[:, 1, 0], filt[:, 1, 1], skip[1])
    return y * jax.nn.silu(gate)


def _segsum_exp(a):
    T = a.shape[-1]
    cs = jnp.cumsum(a, axis=-1)
    mask = jnp.tril(jnp.ones((T, T), dtype=bool))
    return jnp.exp(jnp.where(mask, cs[..., :, None] - cs[..., None, :], -jnp.inf))


def _ssd_scan(x, dt, A, Bm, Cm):
    bsz, L, H, P = x.shape
    G, N = Bm.shape[2], Bm.shape[3]
    J = H // G
    Q = MB_CHUNK
    nc = L // Q
    xc = (x * dt[..., None]).reshape(bsz, nc, Q, G, J, P)
    a = (dt * A).reshape(bsz, nc, Q, G, J).transpose(0, 3, 4, 1, 2)
    Bc = Bm.reshape(bsz, nc, Q, G, N)
    Cc = Cm.reshape(bsz, nc, Q, G, N)
    a_cs = jnp.cumsum(a, axis=-1)
    CB = jnp.einsum('bclgn,bcsgn->bcgls', Cc, Bc)
    y_diag = jnp.einsum('bcgls,bgjcls,bcsgjp->bclgjp', CB, _segsum_exp(a), xc)
    decay_states = jnp.exp(a_cs[..., -1:] - a_cs)
    states = jnp.einsum('bcsgn,bgjcs,bcsgjp->bcgjpn', Bc, decay_states, xc)
    states = jnp.concatenate([jnp.zeros_like(states[:, :1]), states], axis=1)
    decay_chunk = _segsum_exp(jnp.pad(a_cs[..., -1], ((0, 0), (0, 0), (0, 0), (1, 0))))
    prev_states = jnp.einsum('bgjzc,bcgjpn->bzgjpn', decay_chunk, states)[:, :-1]
    y_off = jnp.einsum('bclgn,bcgjpn,bgjcl->bclgjp', Cc, prev_states, jnp.exp(a_cs))
    return (y_diag + y_off).reshape(bsz, L, H, P)


def _mamba2_branch(xbc, z, dt_raw, conv_w, conv_b, dt_bias, a_log, d_skip, norm_w):
    bsz, L, _ = xbc.shape
    xbc = jax.nn.silu(_short_conv(xbc, conv_w, conv_b))
    xs, bm, cm = _split_cols(xbc, (MB_W, MB_GROUPS * MB_STATE, MB_GROUPS * MB_STATE))
    xs = xs.reshape(bsz, L, MB_H, MB_HEADDIM).astype(jnp.float32)
    bm = bm.reshape(bsz, L, MB_GROUPS, MB_STATE).astype(jnp.float32)
    cm = cm.reshape(bsz, L, MB_GROUPS, MB_STATE).astype(jnp.float32)
    dt = jax.nn.softplus(dt_raw.reshape(bsz, L, 2, MB_H).astype(jnp.float32) + dt_bias.astype(jnp.float32))
    A = -jnp.exp(a_log.astype(jnp.float32))
    y_f = _ssd_scan(xs, dt[:, :, 0], A[0], bm, cm)
    y_b = jnp.flip(_ssd_scan(jnp.flip(xs, 1), jnp.flip(dt[:, :, 1], 1), A[1], jnp.flip(bm, 1), jnp.flip(cm, 1)), 1)
    y = (y_f + y_b + xs * d_skip.astype(jnp.float32)[:, None]).reshape(bsz, L, MB_W)
    y = y.astype(z.dtype) * jax.nn.silu(z)
    y = _rmsnorm(y.reshape(bsz, L, MB_GROUPS, MB_W // MB_GROUPS), norm_w.reshape(MB_GROUPS, MB_W // MB_GROUPS))
    return y.reshape(bsz, L, MB_W)


def _mlstm_scan(q, k, v, ig, lf):
    bsz, H, L, Dh = q.shape
    Q = ML_CHUNK
    nc = L // Q

    def to_chunks(a):
        return jnp.moveaxis(a.reshape(bsz, H, nc, Q, *a.shape[3:]), 2, 0)

    causal = jnp.tril(jnp.ones((Q, Q), dtype=bool))

    def step(carry, inp):
        Cm, n, m = carry
        qq, kk, vv, ii, ff = inp
        b = jnp.cumsum(ff, axis=-1)
        dmat = jnp.where(causal, b[..., :, None] - b[..., None, :] + ii[..., None, :], -jnp.inf)
        m_inter = b + m[..., None]
        m_t = jnp.maximum(m_inter, jnp.max(dmat, axis=-1))
        w_inter = jnp.exp(m_inter - m_t)
        s = jnp.einsum('bhtd,bhsd->bhts', qq, kk) * jnp.exp(dmat - m_t[..., None])
        num = w_inter[..., None] * jnp.einsum('bhtd,bhde->bhte', qq, Cm) + jnp.einsum('bhts,bhse->bhte', s, vv)
        den = w_inter * jnp.einsum('bhtd,bhd->bht', qq, n) + jnp.sum(s, axis=-1)
        h = num / jnp.maximum(jnp.abs(den), jnp.exp(-m_t))[..., None]
        b_last = b[..., -1]
        g = b_last[..., None] - b + ii
        m_new = jnp.maximum(b_last + m, jnp.max(g, axis=-1))
        wk = jnp.exp(g - m_new[..., None])
        decay = jnp.exp(b_last + m - m_new)
        C_new = decay[..., None, None] * Cm + jnp.einsum('bhs,bhsd,bhse->bhde', wk, kk, vv)
        n_new = decay[..., None] * n + jnp.einsum('bhs,bhsd->bhd', wk, kk)
        return (C_new, n_new, m_new), h

    init = (jnp.zeros((bsz, H, Dh, Dh), jnp.float32), jnp.zeros((bsz, H, Dh), jnp.float32),
            jnp.zeros((bsz, H), jnp.float32))
    _, hs = lax.scan(step, init, (to_chunks(q), to_chunks(k), to_chunks(v), to_chunks(ig), to_chunks(lf)))
    return jnp.moveaxis(hs, 0, 2).reshape(bsz, H, L, Dh)


def _mlstm_branch(qk, v, o, z, gates, conv_w, conv_b, gate_b, norm_w):
    bsz, L, _ = v.shape
    qk = jax.nn.silu(_short_conv(qk, conv_w, conv_b))

    def heads(t):
        return t.reshape(bsz, L, ML_H, ML_DH).transpose(0, 2, 1, 3).astype(jnp.float32)

    q = heads(qk[..., :ML_W])
    k = heads(qk[..., ML_W:]) * (ML_DH ** -0.5)
    vv = heads(v)
    g = gates.reshape(bsz, L, 2, 2, ML_H).astype(jnp.float32) + gate_b.astype(jnp.float32)
    g = g.transpose(2, 3, 0, 4, 1)
    ig = g[:, 0]
    lf = jax.nn.log_sigmoid(g[:, 1])
    h_f = _mlstm_scan(q, k, vv, ig[0], lf[0])
    h_b = jnp.flip(_mlstm_scan(jnp.flip(q, 2), jnp.flip(k, 2), jnp.flip(vv, 2),
                               jnp.flip(ig[1], 2), jnp.flip(lf[1], 2)), 2)
    h = (h_f + h_b).transpose(0, 2, 1, 3).reshape(bsz, L, ML_W)
    h = jax.nn.sigmoid(o.astype(jnp.float32)) * h
    h = _rmsnorm(h.reshape(bsz, L, ML_H, ML_DH), norm_w.reshape(ML_H, ML_DH)).reshape(bsz, L, ML_W)
    return h.astype(z.dtype) * jax.nn.silu(z)


def _neighbourhood_attention(q, k, v, rpb):
    bsz, seq, nh, dh = q.shape
    rows = seq // GRID_W
    kr = min(NA_KR_MAX, rows)
    kc = NA_KC
    qg = q.reshape(bsz, rows, GRID_W, nh, dh)
    kg = k.reshape(bsz, rows, GRID_W, nh, dh)
    vg = v.reshape(bsz, rows, GRID_W, nh, dh)
    row_start = jnp.clip(jnp.arange(rows) - kr // 2, 0, rows - kr)
    cols = jnp.arange(GRID_W)
    col_idx = jnp.clip(cols - kc // 2, 0, GRID_W - kc)[:, None] + jnp.arange(kc)[None, :]
    col_off = col_idx - cols[:, None] + (kc - 1)
    rpb_cols = rpb[:, :, col_off]

    def one_row(r):
        rs = row_start[r]
        q_row = lax.dynamic_index_in_dim(qg, r, axis=1, keepdims=False)
        k_win = lax.dynamic_slice_in_dim(kg, rs, kr, axis=1)[:, :, col_idx]
        v_win = lax.dynamic_slice_in_dim(vg, rs, kr, axis=1)[:, :, col_idx]
        row_off = rs + jnp.arange(kr) - r + (NA_KR_MAX - 1)
        bias = rpb_cols[:, row_off].transpose(0, 2, 1, 3)
        s = jnp.einsum('bwhd,biwjhd->bhwij', q_row, k_win).astype(jnp.float32) + bias.astype(jnp.float32)
        p = jax.nn.softmax(s.reshape(bsz, nh, GRID_W, kr * kc), axis=-1).reshape(s.shape)
        return jnp.einsum('bhwij,biwjhd->bwhd', p.astype(v.dtype), v_win)

    out = lax.map(one_row, jnp.arange(rows))
    return out.transpose(1, 0, 2, 3, 4).reshape(bsz, seq, nh, dh)


def _na_branch(qkv, gate, qn_w, kn_w, rpb):
    bsz, L, _ = qkv.shape
    q, k, v = [t.reshape(bsz, L, NA_H, NA_DH) for t in _split_cols(qkv, (NA_W, NA_W, NA_W))]
    q = _rmsnorm(q, qn_w) * (NA_DH ** -0.5)
    k = _rmsnorm(k, kn_w)
    o = _neighbourhood_attention(q, k, v, rpb)
    return o.reshape(bsz, L, NA_W) * jax.nn.silu(gate)


def setup_inputs(seed: int = 0) -> dict:
    key = jax.random.key(seed)
    ks = iter(jax.random.split(key, 40))

    def nrm(shape, scale):
        return scale * jax.random.normal(next(ks), shape, jnp.float32)

    def near_one(shape, s=0.02):
        return 1.0 + s * jax.random.normal(next(ks), shape, jnp.float32)

    x = jax.random.normal(next(ks), (BATCH, SEQ, D_MODEL), jnp.float32)
    norm_w = near_one((DEPTH, D_MODEL))
    w_in = nrm((DEPTH, D_MODEL, N_IN), D_MODEL ** -0.5)
    w_out = nrm((DEPTH, D_INNER, D_MODEL), D_INNER ** -0.5)
    hy_conv_w = nrm((DEPTH, SHORT_CONV, 3 * HY_W), SHORT_CONV ** -0.5)
    hy_conv_b = nrm((DEPTH, 3 * HY_W), 0.01)
    hy_w1 = nrm((DEPTH, HY_POS_DIM, HY_FILT_HID), HY_POS_DIM ** -0.5)
    hy_b1 = nrm((DEPTH, HY_FILT_HID), 0.1)
    hy_w2 = nrm((DEPTH, HY_FILT_HID, HY_FILT_HID), HY_FILT_HID ** -0.5)
    hy_b2 = nrm((DEPTH, HY_FILT_HID), 0.1)
    hy_w3 = nrm((DEPTH, HY_FILT_HID, HY_FILT_OUT), 0.02 * HY_FILT_HID ** -0.5)
    hy_freq = near_one((DEPTH, HY_FILT_HID), 0.1)
    base_decay = jnp.abs(jnp.linspace(math.log(1e-2) / 1.5, math.log(1e-2) / 0.3, HY_W, dtype=jnp.float32))
    hy_decay = jnp.tile(base_decay, HY_ORDER * 2)[None, :] * near_one((DEPTH, HY_FILT_OUT), 0.05)
    hy_skip = near_one((DEPTH, HY_ORDER, HY_W), 0.1)
    mb_conv_w = nrm((DEPTH, SHORT_CONV, MB_XBC), SHORT_CONV ** -0.5)
    mb_conv_b = nrm((DEPTH, MB_XBC), 0.01)
    u = jax.random.uniform(next(ks), (DEPTH, 2, MB_H), jnp.float32)
    dt0 = jnp.exp(u * (math.log(0.1) - math.log(1e-3)) + math.log(1e-3))
    mb_dt_bias = dt0 + jnp.log(-jnp.expm1(-dt0))
    mb_a_log = jnp.log(jax.random.uniform(next(ks), (DEPTH, 2, MB_H), jnp.float32, 1.0, 16.0))
    mb_d = near_one((DEPTH, MB_H), 0.1)
    mb_norm_w = near_one((DEPTH, MB_W))
    ml_conv_w = nrm((DEPTH, SHORT_CONV, 2 * ML_W), SHORT_CONV ** -0.5)
    ml_conv_b = nrm((DEPTH, 2 * ML_W), 0.01)
    ig_b = nrm((DEPTH, 2, 1, ML_H), 0.1)
    fg_b = jnp.linspace(3.0, 6.0, ML_H, dtype=jnp.float32) + nrm((DEPTH, 2, 1, ML_H), 0.1)
    ml_gate_b = jnp.concatenate([ig_b, fg_b], axis=2)
    ml_norm_w = near_one((DEPTH, ML_W))
    na_qnorm_w = near_one((DEPTH, NA_DH))
    na_knorm_w = near_one((DEPTH, NA_DH))
    na_rpb = nrm((DEPTH, NA_H, 2 * NA_KR_MAX - 1, 2 * NA_KC - 1), 0.02)
    return {"x": x, "norm_w": norm_w, "w_in": w_in, "w_out": w_out,
            "hy_conv_w": hy_conv_w, "hy_conv_b": hy_conv_b, "hy_w1": hy_w1, "hy_b1": hy_b1,
            "hy_w2": hy_w2, "hy_b2": hy_b2, "hy_w3": hy_w3, "hy_freq": hy_freq, "hy_decay": hy_decay,
            "hy_skip": hy_skip, "mb_conv_w": mb_conv_w, "mb_conv_b": mb_conv_b, "mb_dt_bias": mb_dt_bias,
            "mb_a_log": mb_a_log, "mb_d": mb_d, "mb_norm_w": mb_norm_w, "ml_conv_w": ml_conv_w,
            "ml_conv_b": ml_conv_b, "ml_gate_b": ml_gate_b, "ml_norm_w": ml_norm_w,
            "na_qnorm_w": na_qnorm_w, "na_knorm_w": na_knorm_w, "na_rpb": na_rpb}


def reference(x, norm_w, w_in, w_out, hy_conv_w, hy_conv_b, hy_w1, hy_b1, hy_w2, hy_b2, hy_w3, hy_freq,
              hy_decay, hy_skip, mb_conv_w, mb_conv_b, mb_dt_bias, mb_a_log, mb_d, mb_norm_w, ml_conv_w,
              ml_conv_b, ml_gate_b, ml_norm_w, na_qnorm_w, na_knorm_w, na_rpb):
    L = x.shape[1]
    pos, t_norm = _hyena_positions(L)
    for l in range(DEPTH):
        h = _rmsnorm(x, norm_w[l])
        proj = h @ w_in[l]
        (hy_u, hy_g, mb_xbc, mb_z, mb_dt, ml_qk, ml_v, ml_o, ml_z, ml_gates, na_qkv, na_g) = _split_cols(proj, IN_SPLITS)
        filt = _hyena_filters(pos, t_norm, hy_w1[l], hy_b1[l], hy_w2[l], hy_b2[l], hy_w3[l], hy_freq[l], hy_decay[l])
        y_hy = _hyena_branch(hy_u, hy_g, hy_conv_w[l], hy_conv_b[l], filt, hy_skip[l])
        y_mb = _mamba2_branch(mb_xbc, mb_z, mb_dt, mb_conv_w[l], mb_conv_b[l], mb_dt_bias[l], mb_a_log[l],
                              mb_d[l], mb_norm_w[l])
        y_ml = _mlstm_branch(ml_qk, ml_v, ml_o, ml_z, ml_gates, ml_conv_w[l], ml_conv_b[l], ml_gate_b[l], ml_norm_w[l])
        y_na = _na_branch(na_qkv, na_g, na_qnorm_w[l], na_knorm_w[l], na_rpb[l])
        y = jnp.concatenate([y_hy, y_mb, y_ml, y_na], axis=-1)
        x = x + y @ w_out[l]
    return x
```

```python
import math
from contextlib import ExitStack

import numpy as np
import concourse.bass as bass
import concourse.mybir as mybir
from concourse.bass_utils import run_bass_kernel_spmd

F32 = mybir.dt.float32
BF16 = mybir.dt.bfloat16
AF = mybir.ActivationFunctionType
ALU = mybir.AluOpType

T = 8192
D = 1024
NT = 64
DEPTH = 2
EPS = 1e-6
SAME_ENGINE_SYNC = True
SAME_WAW = False
SAME_WAR = False

O_HYU, O_HYG, O_MBX, O_MBB, O_MBC, O_MBZ, O_MBDT = 0, 1536, 2048, 2560, 2816, 3072, 3584
O_MLQ, O_MLK, O_MLV, O_MLO, O_MLZ, O_MLG = 3600, 4112, 4624, 5136, 5648, 6160
O_NAQ, O_NAK, O_NAV, O_NAG = 6176, 6688, 7200, 7712

SP = 2
CW = 512 // SP
HC = CW
MBU = 8 // SP
MBG = 2 // SP
MLU = 4 // SP
NAH = 8 // SP
OW = 1024 // SP
YW = 3 * CW
NSM = 2 * MBU + 4 * MLU
NSP = 32
NB = CW // 128
FM_GROUPS = [
    ("hyu", O_HYU, 3 * NB, "conv"),
    ("hyg", O_HYG, NB, "silu"),
    ("mbx", O_MBX, NB, "convsilu"),
    ("mbB", O_MBB, MBG, "convsilu"),
    ("mbC", O_MBC, MBG, "convsilu"),
    ("mlq", O_MLQ, NB, "convsilu"),
    ("mlk", O_MLK, NB, "convsilu"),
    ("naq", O_NAQ, NB, "qknorm"),
    ("nak", O_NAK, NB, "qknorm"),
]
FM_BLOCKS = []
for _n, _o, _k, _t in FM_GROUPS:
    for _i in range(_k):
        FM_BLOCKS.append((_n, _i, _o + 128 * _i, _t))
NFM = len(FM_BLOCKS)
TM_PARTS = [("mbz", O_MBZ, "silu"), ("mlv", O_MLV, "copy"), ("mlo", O_MLO, "sigmoid"),
            ("mlz", O_MLZ, "silu"), ("nav", O_NAV, "copy"), ("nag", O_NAG, "silu")]
NTM = len(TM_PARTS) * CW // 512


import os
_SKIP = set(os.environ.get("KSKIP", "").split(","))
_UID = [0]


def _alloc(nc, kind, name, shape, dt):
    _UID[0] += 1
    nm = "%s_%d" % (name, _UID[0])
    if kind == "sbuf":
        return nc.sbuf_tensor(nm, shape, dt)
    return nc.psum_tensor(nm, shape, dt)


class Sched:
    def __init__(self, nc, stack, n_dma_sems=14):
        self.nc = nc
        self.eng = {"pe": nc.tensor, "dve": nc.vector, "act": nc.scalar, "pool": nc.gpsimd, "sp": nc.sync}
        self.sem = {e: stack.enter_context(nc.semaphore("c_" + e)) for e in self.eng}
        self.cnt = {e: 0 for e in self.eng}
        self.seen = {e: {} for e in self.eng}
        self.dq = {}
        for q in ("sp", "pool", "act"):
            self.dq[q] = [[stack.enter_context(nc.semaphore("d_%s%d" % (q, i))), 0] for i in range(n_dma_sems)]
        self.dqi = {q: 0 for q in self.dq}
        self.last_w = {}
        self.readers = {}
        self.n_wait = 0
        self.n_ins = 0
        self.ccs = []
        self.cc_toks = []

    def _wait(self, e, tok, raw=True):
        key, sem, val, prod = tok
        if prod == e and (e == "pe" or not SAME_ENGINE_SYNC or not raw):
            return
        if self.seen[e].get(key, 0) >= val:
            return
        self.eng[e].wait_ge(sem, val)
        self.seen[e][key] = val
        self.n_wait += 1

    def _deps(self, e, reads, writes):
        for k in reads:
            t = self.last_w.get(k)
            if t is not None:
                self._wait(e, t, True)
        for k in writes:
            t = self.last_w.get(k)
            if t is not None:
                self._wait(e, t, SAME_WAW)
            for t in self.readers.get(k, ()):
                self._wait(e, t, SAME_WAR)

    def _commit(self, tok, reads, writes):
        for k in writes:
            self.last_w[k] = tok
            self.readers[k] = []
        for k in reads:
            if k in writes:
                continue
            lst = self.readers.setdefault(k, [])
            lst.append(tok)
            if len(lst) > 48:
                best = {}
                for t in lst:
                    if t[0] not in best or best[t[0]][2] < t[2]:
                        best[t[0]] = t
                self.readers[k] = list(best.values())

    def op(self, e, fn, reads=(), writes=()):
        self._deps(e, reads, writes)
        ins = fn(self.eng[e])
        self.cnt[e] += 1
        ins.then_inc(self.sem[e], 1)
        tok = ("c_" + e, self.sem[e], self.cnt[e], e)
        self._commit(tok, reads, writes)
        self.n_ins += 1
        return ins

    def dma(self, q, out, in_, reads=(), writes=(), **kw):
        self._deps(q, reads, writes)
        idx = self.dqi[q]
        slot = self.dq[q][idx]
        self.dqi[q] = (idx + 1) % len(self.dq[q])
        sem, val = slot
        key = "d_%s%d" % (q, idx)
        if val > 0 and self.seen[q].get(key, 0) < val:
            self.eng[q].wait_ge(sem, val)
            self.seen[q][key] = val
        ins = self.eng[q].dma_start(out=out, in_=in_, **kw)
        ins.then_inc(sem, 16)
        slot[1] = val + 16
        tok = (key, sem, val + 16, None)
        self._commit(tok, reads, writes)
        self.n_ins += 1
        return ins

    def collective(self, src, dst, tn, stack, rpc):
        self._deps("pool", [src], [dst])
        groups = [[SP * g + r for r in range(SP)] for g in range(8 // SP)]
        rows = tn[src].shape[0]
        toks = []
        for q in range(rows // rpc):
            sem = stack.enter_context(self.nc.semaphore("cc%d" % len(self.ccs)))
            self.ccs.append(sem)
            ins = self.nc.gpsimd.collective_compute(
                "AllGather", ALU.bypass, replica_groups=groups,
                ins=[tn[src].ap()[rpc * q:rpc * q + rpc, :].opt()],
                outs=[tn[dst].ap()[SP * rpc * q:SP * rpc * q + SP * rpc, :].opt()])
            ins.then_inc(sem)
            tok = ("cc%d" % (len(self.ccs) - 1), sem, 1, None)
            self.eng["pool"].wait_ge(sem, 1)
            self.seen["pool"][tok[0]] = 1
            toks.append(tok)
            self.cc_toks.append(tok)
        self._commit(toks[-1], [src], [dst])

    def barrier(self):
        for e in self.eng:
            for t in self.cc_toks:
                self._wait(e, t)
        for e in self.eng:
            for p in self.eng:
                if p != e and self.cnt[p] > 0:
                    self._wait(e, ("c_" + p, self.sem[p], self.cnt[p], p))
                elif p == e and self.cnt[p] > 0 and e != "pe":
                    self._wait(e, ("c_" + p, self.sem[p], self.cnt[p], None))
            for q in self.dq:
                for i, (sem, val) in enumerate(self.dq[q]):
                    if val > 0:
                        self._wait(e, ("d_%s%d" % (q, i), sem, val, None))
        self.last_w = {}
        self.readers = {}


class _KeyNS:
    def __init__(self, S, prefix):
        self.S = S
        self.p = prefix

    def _k(self, keys):
        return [(self.p, k) for k in keys]

    def op(self, e, fn, reads=(), writes=()):
        return self.S.op(e, fn, self._k(reads), self._k(writes))

    def dma(self, q, out, in_, reads=(), writes=(), **kw):
        return self.S.dma(q, out, in_, self._k(reads), self._k(writes), **kw)


class Prog:
    def __init__(self, nlayers=DEPTH, debug=False, phases=("A", "HY", "NA", "DLA", "Z", "G")):
        self.nlayers = nlayers
        self.debug = debug
        self.phases = phases

    def dram_in(self, name, shape, dt=F32):
        return self.nc.dram_tensor(name, list(shape), dt, kind="ExternalInput").ap()

    def dram_scr(self, name, shape, dt=BF16, dbg=False):
        kind = "ExternalOutput" if (dbg and self.debug) else "Internal"
        if name in getattr(self, "feed", ()):
            kind = "ExternalInput"
        return self.nc.dram_tensor(name, list(shape), dt, kind=kind).ap()

    def build(self):
        nc = bass.Bass("TRN2", target_bir_lowering=False)
        self.nc = nc
        L = DEPTH
        I = self.I = {}
        I["x"] = self.dram_in("x", [T, D])
        I["norm_w"] = self.dram_in("norm_w", [L, 1, D])
        I["wfm"] = self.dram_in("wfm", [L, NFM, 128, 1024])
        I["wsm"] = self.dram_in("wsm", [L, 128, 8 * NSP])
        I["xh"] = self.dram_in("xh", [T, OW])
        I["wtm"] = self.dram_in("wtm", [L, NTM, 128, 4096])
        I["wout"] = self.dram_in("wout", [L, 128, 16 * OW])
        I["convp"] = self.dram_in("convp", [L, NFM, 128, 4])
        I["ident"] = self.dram_in("ident", [128, 128])
        I["blockones"] = self.dram_in("blockones", [128, 128])
        self.declare_mixer_inputs()
        self.out = nc.dram_tensor("out", [T, OW], F32, kind="ExternalOutput").ap()
        Sc = self.Sc = {}
        for n, rows in [("hyuT", 3 * CW), ("hygT", CW), ("mbBT", 128 * MBG), ("mbCT", 128 * MBG), ("mlqT", CW),
                        ("mlkT", CW), ("naqT", CW), ("nakT", CW)]:
            Sc[n] = self.dram_scr(n, [rows, T], dbg=True)
        for n, cols in [("mbx", CW), ("mbB", 128 * MBG), ("mlk", CW), ("mbz", CW), ("mlv", CW), ("mlo", CW),
                        ("mlz", CW), ("nav", CW), ("nag", CW)]:
            Sc[n] = self.dram_scr(n, [T, cols], dbg=True)
        Sc["smallT"] = self.dram_scr("smallT", [NSP, T], F32, dbg=True)
        self.Tn = {}
        for n, shp, dt in [("ytm", [T, YW], BF16), ("yhyT", [HC, T], BF16), ("xres", [T, OW], F32),
                           ("ytm_g", [SP * T, YW], BF16), ("yhy_g", [SP * HC, T], BF16), ("xg", [SP * T, OW], F32)]:
            if self.debug and n in ("ytm", "yhyT"):
                self.Tn[n] = nc.dram_tensor(n, shp, dt, kind="ExternalOutput")
            elif n in getattr(self, "feed", ()):
                self.Tn[n] = nc.dram_tensor(n, shp, dt, kind="ExternalInput")
            else:
                self.Tn[n] = nc.dram_tensor(n, shp, dt)
            Sc[n] = self.Tn[n].ap()
        self.declare_mixer_scratch()

        with ExitStack() as st:
            self.S = Sched(nc, st)
            S = self.S
            self.ident_f = st.enter_context(_alloc(nc, "sbuf", "ident_f", [128, 128], F32))
            self.ident_b = st.enter_context(_alloc(nc, "sbuf", "ident_b", [128, 128], BF16))
            self.bones_b = st.enter_context(_alloc(nc, "sbuf", "bones_b", [128, 128], BF16))
            tmpc = st.enter_context(_alloc(nc, "sbuf", "tmpc", [128, 128], F32))
            S.dma("sp", self.ident_f[:], I["ident"], writes=["ident_f"])
            S.dma("sp", tmpc[:], I["blockones"], writes=["tmpc"])
            S.op("dve", lambda e: e.tensor_copy(out=self.ident_b[:], in_=self.ident_f[:]), reads=["ident_f"], writes=["ident_b"])
            S.op("dve", lambda e: e.tensor_copy(out=self.bones_b[:], in_=tmpc[:]), reads=["tmpc"], writes=["bones_b"])
            S.barrier()
            for l in range(self.nlayers):
                x_dst = self.out if l == self.nlayers - 1 else Sc["xres"]
                x_res = I["xh"] if l == 0 else Sc["xres"]
                self.marks = getattr(self, "marks", [])
                mark = lambda nm: self.marks.append((nm, l, dict(S.cnt)))
                mark("start")
                if "A" in self.phases:
                    self.phaseA(l)
                    S.barrier()
                    mark("A")
                if "HY" in self.phases:
                    self.phaseHY(l)
                    S.barrier()
                    mark("HY")
                if "NA" in self.phases:
                    self.phaseNA(l)
                    S.barrier()
                    mark("NA")
                if "DLA" in self.phases:
                    self.phaseDLA(l)
                    S.barrier()
                    mark("DLA")
                if "Z" in self.phases:
                    if SP > 1 and "G" in self.phases:
                        S.collective("ytm", "ytm_g", self.Tn, st, 1024)
                        S.collective("yhyT", "yhy_g", self.Tn, st, 64)
                        S.barrier()
                    self.phaseZ(l, x_res, x_dst)
                    S.barrier()
                    if SP > 1 and l < self.nlayers - 1 and "G" in self.phases:
                        S.collective("xres", "xg", self.Tn, st, 1024)
                        S.barrier()
                    mark("Z")
            S.barrier()
        return nc

    def declare_mixer_inputs(self):
        _na_declare_inputs(self)

    def declare_mixer_scratch(self):
        _dla_declare(self)
        _hy_declare(self)

    def phaseA(self, l):
        nc, S, I, Sc = self.nc, self.S, self.I, self.Sc
        with ExitStack() as st:
            sb = lambda n, s, d=F32: st.enter_context(_alloc(nc, "sbuf", n, list(s), d))
            ps = lambda n, s, d=F32: st.enter_context(_alloc(nc, "psum", n, list(s), d))
            hT = sb("hT", [128, 8, T], BF16)
            with ExitStack() as st0:
                sb0 = lambda n, s, d=F32: st0.enter_context(_alloc(nc, "sbuf", n, list(s), d))
                nwb = sb0("nwb", [128, D])
                S.dma("sp", nwb[:], I["norm_w"][l].broadcast_to([128, D]), writes=["nwb"])
                xt = [sb0("xt%d" % i, [128, D]) for i in range(2)]
                junk = sb0("junk", [128, D], BF16)
                ss = [sb0("ss%d" % i, [128, 1]) for i in range(2)]
                xn = [sb0("xn%d" % i, [128, D], BF16) for i in range(2)]
                pt = [st0.enter_context(_alloc(nc, "psum", "pt%d" % i, [128, 512], BF16)) for i in range(2)]
                for i in range(NT):
                    b = i % 2
                    if l == 0 or SP == 1:
                        S.dma("sp", xt[b][:], I["x"][128 * i:128 * i + 128, :], writes=[("xt", b)])
                    else:
                        S.dma("sp", xt[b][:].rearrange("p (r c) -> p r c", r=SP),
                              Sc["xg"].rearrange("(q r t) c -> q t r c", q=8, r=SP)[i // 8][128 * (i % 8):128 * (i % 8) + 128, :, :], writes=[("xt", b)])
                    S.op("act", lambda e: e.activation(out=junk[:], in_=xt[b][:], func=AF.Square, accum_out=ss[b][:]),
                         reads=[("xt", b)], writes=["junk", ("ss", b)])
                    S.op("dve", lambda e: e.tensor_scalar(out=ss[b][:], in0=ss[b][:], scalar1=1.0 / D, scalar2=EPS,
                                                          op0=ALU.mult, op1=ALU.add), reads=[("ss", b)], writes=[("ss", b)])
                    S.op("act", lambda e: e.activation(out=ss[b][:], in_=ss[b][:], func=AF.Sqrt), reads=[("ss", b)], writes=[("ss", b)])
                    S.op("dve", lambda e: e.reciprocal(out=ss[b][:], in_=ss[b][:]), reads=[("ss", b)], writes=[("ss", b)])
                    S.op("dve", lambda e: e.scalar_tensor_tensor(out=xn[b][:], in0=xt[b][:], scalar=ss[b][:], in1=nwb[:],
                                                                 op0=ALU.mult, op1=ALU.mult),
                         reads=[("xt", b), ("ss", b), "nwb"], writes=[("xn", b)])
                    for h in range(2):
                        for k in range(4):
                            kc = 4 * h + k
                            S.op("pe", lambda e: e.transpose(pt[h][:, 128 * k:128 * k + 128], xn[b][:, 128 * kc:128 * kc + 128], self.ident_b[:]),
                                 reads=[("xn", b)], writes=[("pt", h)])
                        dst = hT[:, 4 * h:4 * h + 4, 128 * i:128 * i + 128]
                        src = pt[h][:].rearrange("p (k t) -> p k t", t=128)
                        if h == 0:
                            S.op("act", lambda e: e.activation(out=dst, in_=src, func=AF.Copy), reads=[("pt", h)], writes=[("hT", i)])
                        else:
                            S.op("dve", lambda e: e.tensor_copy(out=dst, in_=src), reads=[("pt", h)], writes=[("hT", i)])
            S.barrier()
            hkeys = [("hT", i) for i in range(NT)]
            with ExitStack() as st1:
                sb1 = lambda n, s, d=F32: st1.enter_context(_alloc(nc, "sbuf", n, list(s), d))
                wst = [sb1("wst%d" % i, [128, 1024]) for i in range(2)]
                wb = [sb1("wb%d" % i, [128, 8, 128], BF16) for i in range(2)]
                row = [sb1("row%d" % i, [128, T + 2], BF16) for i in range(2)]
                acc = [sb1("acc%d" % i, [128, 1024]) for i in range(2)]
                ob = [sb1("ob%d" % i, [128, 1024], BF16) for i in range(2)]
                tms = [sb1("tms%d" % i, [128, 8, 128], BF16) for i in range(2)]
                sq = [sb1("sq%d" % i, [128, 512], BF16) for i in range(2)]
                rt = [sb1("rt%d" % i, [128, 512]) for i in range(2)]
                cp = [sb1("cp%d" % i, [128, 4]) for i in range(2)]
                pm = [st1.enter_context(_alloc(nc, "psum", "pm%d" % i, [128, 512], F32)) for i in range(4)]
                ptr = [st1.enter_context(_alloc(nc, "psum", "ptr%d" % i, [128, 512], BF16)) for i in range(2)]
                pss = [st1.enter_context(_alloc(nc, "psum", "pss%d" % i, [128, 512], F32)) for i in range(2)]
                for b in range(2):
                    S.op("pool", lambda e: e.memset(row[b][:, 0:1], 0.0), writes=[("rowh", b)])
                    S.op("pool", lambda e: e.memset(row[b][:, T + 1:T + 2], 0.0), writes=[("rowh", b)])
                cnt = {"ev": 0, "tr": 0, "ob": 0, "ac": 0, "q": 0}

                def mm_block(wtile, wkey, M, dst_fn, dst_keys_fn):
                    for i in range(16):
                        p = pm[i % 4]
                        for kc in range(8):
                            S.op("pe", lambda e: e.matmul(p[0:M, :], lhsT=wtile[:, kc, 0:M], rhs=hT[:, kc, 512 * i:512 * i + 512],
                                                          start=(kc == 0), stop=(kc == 7)),
                                 reads=[wkey] + hkeys[4 * i:4 * i + 4], writes=[("pm", i % 4)])
                        dst = dst_fn(i)
                        if cnt["ev"] % 2 == 0:
                            S.op("act", lambda e: e.activation(out=dst, in_=p[0:M, :], func=AF.Copy), reads=[("pm", i % 4)], writes=dst_keys_fn(i))
                        else:
                            S.op("dve", lambda e: e.tensor_copy(out=dst, in_=p[0:M, :]), reads=[("pm", i % 4)], writes=dst_keys_fn(i))
                        cnt["ev"] += 1

                for bi, (gname, gi, col0, ptype) in enumerate(FM_BLOCKS):
                    if gname in _SKIP or "fm" in _SKIP:
                        continue
                    b = bi % 2
                    S.dma("sp", wst[b][:], I["wfm"][l, bi], writes=[("wst", b)])
                    S.op("pool", lambda e: e.tensor_copy(out=wb[b][:].rearrange("p k c -> p (k c)"), in_=wst[b][:]),
                         reads=[("wst", b)], writes=[("wtile", b)])
                    if ptype in ("conv", "convsilu"):
                        S.dma("sp", cp[b][:], I["convp"][l, bi], writes=[("cp", b)])
                    elif ptype == "qknorm":
                        S.dma("sp", cp[b][:], I["convp"][l, bi], writes=[("cp", b)])
                    rw = row[b]
                    mm_block(wb[b], ("wtile", b), 128, lambda i: rw[:, 1 + 512 * i:1 + 512 * i + 512], lambda i: [("row", b, i)])
                    rkeys = [("row", b, i) for i in range(16)] + [("rowh", b)]
                    r0 = 128 * gi
                    if ptype in ("conv", "convsilu", "silu"):
                        for j in range(8):
                            a = acc[cnt["ac"] % 2]; ak = ("acc", cnt["ac"] % 2); cnt["ac"] += 1
                            o = ob[cnt["ob"] % 2]; okey = ("ob", cnt["ob"] % 2); cnt["ob"] += 1
                            rk = [("row", b, i) for i in range(max(0, 2 * j - 1), min(16, 2 * j + 3))] + [("rowh", b)]
                            c0 = 1024 * j
                            if ptype == "silu":
                                S.op("act", lambda e: e.activation(out=o[:], in_=rw[:, 1 + c0:1 + c0 + 1024], func=AF.Silu), reads=rk, writes=[okey])
                            else:
                                S.op("act", lambda e: e.activation(out=a[:], in_=rw[:, 1 + c0:1 + c0 + 1024], func=AF.Identity,
                                                                   scale=cp[b][:, 1:2], bias=cp[b][:, 3:4]), reads=rk + [("cp", b)], writes=[ak])
                                S.op("dve", lambda e: e.scalar_tensor_tensor(out=a[:], in0=rw[:, c0:c0 + 1024], scalar=cp[b][:, 0:1], in1=a[:],
                                                                             op0=ALU.mult, op1=ALU.add), reads=rk + [("cp", b), ak], writes=[ak])
                                S.op("dve", lambda e: e.scalar_tensor_tensor(out=a[:], in0=rw[:, 2 + c0:2 + c0 + 1024], scalar=cp[b][:, 2:3], in1=a[:],
                                                                             op0=ALU.mult, op1=ALU.add), reads=rk + [("cp", b), ak], writes=[ak])
                                if ptype == "convsilu":
                                    S.op("act", lambda e: e.activation(out=o[:], in_=a[:], func=AF.Silu), reads=[ak], writes=[okey])
                                else:
                                    S.op("pool", lambda e: e.tensor_copy(out=o[:], in_=a[:]), reads=[ak], writes=[okey])
                            fm_dst = {"hyu": "hyuT", "hyg": "hygT", "mbB": "mbBT", "mbC": "mbCT", "mlq": "mlqT", "mlk": "mlkT"}.get(gname)
                            if fm_dst is not None:
                                S.dma("pool", Sc[fm_dst][r0:r0 + 128, c0:c0 + 1024], o[:], reads=[okey], writes=[(fm_dst, gi, j)])
                            tm_dst = {"mbx": "mbx", "mbB": "mbB", "mlk": "mlk"}.get(gname)
                            if tm_dst is not None:
                                tmb = cnt["tr"] % 2; cnt["tr"] += 1
                                for h in range(2):
                                    for k in range(4):
                                        s = 4 * h + k
                                        S.op("pe", lambda e: e.transpose(ptr[h][:, 128 * k:128 * k + 128], o[:, 128 * s:128 * s + 128], self.ident_b[:]),
                                             reads=[okey], writes=[("ptr", h)])
                                    src = ptr[h][:].rearrange("p (k c) -> p k c", c=128)
                                    if h == 0:
                                        S.op("act", lambda e: e.activation(out=tms[tmb][:, 0:4, :], in_=src, func=AF.Copy), reads=[("ptr", h)], writes=[("tms", tmb, 0)])
                                    else:
                                        S.op("dve", lambda e: e.tensor_copy(out=tms[tmb][:, 4:8, :], in_=src), reads=[("ptr", h)], writes=[("tms", tmb, 1)])
                                d = Sc[tm_dst][c0:c0 + 1024, r0:r0 + 128].rearrange("(s p) c -> p s c", p=128)
                                S.dma("pool", d, tms[tmb][:], reads=[("tms", tmb, 0), ("tms", tmb, 1)], writes=[(tm_dst, gi, j)])
                    elif ptype == "qknorm":
                        fm_dst = {"naq": "naqT", "nak": "nakT"}[gname]
                        for j in range(8):
                            o = ob[cnt["ob"] % 2]; okey = ("ob", cnt["ob"] % 2); cnt["ob"] += 1
                            for hh in range(2):
                                q = cnt["q"] % 2; cnt["q"] += 1
                                c0 = 1024 * j + 512 * hh
                                rk = [("row", b, 2 * j + hh)]
                                S.op("act", lambda e: e.activation(out=sq[q][:], in_=rw[:, 1 + c0:1 + c0 + 512], func=AF.Square), reads=rk, writes=[("sq", q)])
                                S.op("pe", lambda e: e.matmul(pss[q][:], lhsT=self.bones_b[:], rhs=sq[q][:], start=True, stop=True),
                                     reads=[("sq", q)], writes=[("pss", q)])
                                S.op("dve", lambda e: e.tensor_scalar(out=rt[q][:], in0=pss[q][:], scalar1=1.0 / 64, scalar2=EPS, op0=ALU.mult, op1=ALU.add),
                                     reads=[("pss", q)], writes=[("rt", q)])
                                S.op("act", lambda e: e.activation(out=rt[q][:], in_=rt[q][:], func=AF.Sqrt), reads=[("rt", q)], writes=[("rt", q)])
                                S.op("dve", lambda e: e.reciprocal(out=rt[q][:], in_=rt[q][:]), reads=[("rt", q)], writes=[("rt", q)])
                                S.op("dve", lambda e: e.scalar_tensor_tensor(out=o[:, 512 * hh:512 * hh + 512], in0=rw[:, 1 + c0:1 + c0 + 512], scalar=cp[b][:, 0:1],
                                                                             in1=rt[q][:], op0=ALU.mult, op1=ALU.mult),
                                     reads=rk + [("rt", q), ("cp", b)], writes=[okey])
                            S.dma("pool", Sc[fm_dst][r0:r0 + 128, 1024 * j:1024 * j + 1024], o[:], reads=[okey], writes=[(fm_dst, gi, j)])
            S.barrier()
            with ExitStack() as st3:
                sb3 = lambda n, s, d=F32: st3.enter_context(_alloc(nc, "sbuf", n, list(s), d))
                smallsb = sb3("smallsb", [NSP, T])
                wsst = sb3("wsst", [128, 8 * NSP])
                wsb = sb3("wsb", [128, 8, NSP], BF16)
                pm3 = [st3.enter_context(_alloc(nc, "psum", "pm3%d" % i, [128, 512], F32)) for i in range(4)]
                S.dma("sp", wsst[:], I["wsm"][l], writes=["wsst"])
                S.op("pool", lambda e: e.tensor_copy(out=wsb[:].rearrange("p k c -> p (k c)"), in_=wsst[:]), reads=["wsst"], writes=["wsb"])
                for i in range(16 if "small" not in _SKIP else 0):
                    p = pm3[i % 4]
                    for kc in range(8):
                        S.op("pe", lambda e: e.matmul(p[0:NSP, :], lhsT=wsb[:, kc, :], rhs=hT[:, kc, 512 * i:512 * i + 512], start=(kc == 0), stop=(kc == 7)),
                             reads=["wsb"] + hkeys[4 * i:4 * i + 4], writes=[("pm3", i % 4)])
                    S.op("act", lambda e: e.activation(out=smallsb[:, 512 * i:512 * i + 512], in_=p[0:NSP, :], func=AF.Copy), reads=[("pm3", i % 4)], writes=[("smallsb", i)])
                S.dma("pool", Sc["smallT"], smallsb[:], reads=[("smallsb", i) for i in range(16)], writes=["smallT"])
            S.barrier()
            with ExitStack() as st2:
                sb2 = lambda n, s, d=F32: st2.enter_context(_alloc(nc, "sbuf", n, list(s), d))
                wst2 = sb2("wst2", [128, 4096])
                wtb = [sb2("wtb%d" % i, [128, 8, 512], BF16) for i in range(2)]
                ot = [sb2("ot%d" % i, [128, 512], BF16) for i in range(4)]
                pm2 = [st2.enter_context(_alloc(nc, "psum", "pm2%d" % i, [128, 512], F32)) for i in range(4)]
                ppg = 512 // CW
                for g in range(NTM if "tm" not in _SKIP else 0):
                    b = g % 2
                    parts = TM_PARTS[ppg * g:ppg * g + ppg]
                    S.dma("sp", wst2[:], I["wtm"][l, g], writes=["wst2"])
                    S.op("pool", lambda e: e.tensor_copy(out=wtb[b][:].rearrange("p k c -> p (k c)"), in_=wst2[:]), reads=["wst2"], writes=[("wtb", b)])
                    for i in range(NT):
                        p = pm2[i % 4]
                        for kc in range(8):
                            S.op("pe", lambda e: e.matmul(p[:], lhsT=hT[:, kc, 128 * i:128 * i + 128], rhs=wtb[b][:, kc, :], start=(kc == 0), stop=(kc == 7)),
                                 reads=[("wtb", b), ("hT", i)], writes=[("pm2", i % 4)])
                        o = ot[i % 4]
                        for pi, (gname, col0, act) in enumerate(parts):
                            func = {"silu": AF.Silu, "copy": AF.Copy, "sigmoid": AF.Sigmoid}[act]
                            sl = slice(CW * pi, CW * pi + CW)
                            S.op("act", lambda e: e.activation(out=o[:, sl], in_=p[:, sl], func=func), reads=[("pm2", i % 4)], writes=[("ot", i % 4, pi)])
                            S.dma("pool" if (i + pi) % 2 else "sp", Sc[gname][128 * i:128 * i + 128, :], o[:, sl], reads=[("ot", i % 4, pi)], writes=[(gname, i)])

    def phaseZ(self, l, x_res, x_dst):
        nc, S, I, Sc = self.nc, self.S, self.I, self.Sc
        gathered = SP > 1 and "G" in self.phases
        with ExitStack() as st:
            sb = lambda n, s, d=F32: st.enter_context(_alloc(nc, "sbuf", n, list(s), d))
            wo = sb("wo", [128, 16, OW], BF16)
            wos = [sb("wos%d" % i, [128, 2048]) for i in range(2)]
            nck = 16 * OW // 2048
            kpc = 2048 // OW
            for c in range(nck):
                S.dma("sp", wos[c % 2][:], I["wout"][l, :, 2048 * c:2048 * c + 2048], writes=[("wos", c % 2)])
                S.op("pool", lambda e: e.tensor_copy(out=wo[:, kpc * c:kpc * c + kpc, :].rearrange("p k c -> p (k c)"), in_=wos[c % 2][:]),
                     reads=[("wos", c % 2)], writes=["wo"])
            yt = [sb("yt%d" % i, [128, SP, YW], BF16) for i in range(2)]
            yT = [sb("yT%d" % i, [128, 16, 128], BF16) for i in range(2)]
            xt = [sb("xz%d" % i, [128, OW]) for i in range(2)]
            oz = [sb("oz%d" % i, [128, OW]) for i in range(2)]
            ptz = [st.enter_context(_alloc(nc, "psum", "ptz%d" % i, [128, 512], BF16)) for i in range(3)]
            pz = [st.enter_context(_alloc(nc, "psum", "pz%d" % i, [128, 512], F32)) for i in range(4)]
            ysrc = Sc["ytm_g"] if gathered else Sc["ytm"]
            hsrc = Sc["yhy_g"] if gathered else Sc["yhyT"]
            nr = SP if gathered else 1
            cpr = YW // 128
            for i in range(NT):
                b = i % 2
                S.dma("sp", yt[b][:, 0:nr, :], ysrc.rearrange("(q r t) c -> q t r c", q=8, r=nr)[i // 8][128 * (i % 8):128 * (i % 8) + 128, :, :], writes=[("yt", b)])
                S.dma("sp", xt[b][:], x_res[128 * i:128 * i + 128, :], writes=[("xz", b)])
                S.dma("pool", yT[b][:, 0:4 * nr // SP, :], hsrc[:, 128 * i:128 * i + 128].rearrange("(k p) t -> p k t", p=128), writes=[("yTh", b)])
                for h in range(3):
                    for k in range(4):
                        q = 4 * h + k
                        r, cc = q // cpr, q % cpr
                        S.op("pe", lambda e: e.transpose(ptz[h][:, 128 * k:128 * k + 128], yt[b][:, r, 128 * cc:128 * cc + 128], self.ident_b[:]),
                             reads=[("yt", b)], writes=[("ptz", h)])
                    src = ptz[h][:].rearrange("p (k c) -> p k c", c=128)
                    dst = yT[b][:, 4 + 4 * h:8 + 4 * h, :]
                    if h == 1:
                        S.op("dve", lambda e: e.tensor_copy(out=dst, in_=src), reads=[("ptz", h)], writes=[("yTt", b, h)])
                    else:
                        S.op("act", lambda e: e.activation(out=dst, in_=src, func=AF.Copy), reads=[("ptz", h)], writes=[("yTt", b, h)])
                for half in range(OW // 512):
                    p = pz[(2 * i + half) % 4]
                    pk = ("pz", (2 * i + half) % 4)
                    for kc in range(16):
                        S.op("pe", lambda e: e.matmul(p[:], lhsT=yT[b][:, kc, :], rhs=wo[:, kc, 512 * half:512 * half + 512], start=(kc == 0), stop=(kc == 15)),
                             reads=["wo", ("yTh", b)] + [("yTt", b, h) for h in range(3)], writes=[pk])
                    S.op("dve", lambda e: e.tensor_tensor(out=oz[b][:, 512 * half:512 * half + 512], in0=p[:], in1=xt[b][:, 512 * half:512 * half + 512], op=ALU.add),
                         reads=[pk, ("xz", b)], writes=[("oz", b, half)])
                S.dma("pool", x_dst[128 * i:128 * i + 128, :], oz[b][:], reads=[("oz", b, h2) for h2 in range(OW // 512)], writes=[("xdst", i)])


def _fm_col0(gname, gi, j):
    if gname == "hyu":
        part, sub = gi // NB, gi % NB
        return O_HYU + 512 * part + CW * j + 128 * sub
    base = {"hyg": O_HYG, "mbx": O_MBX, "mlq": O_MLQ, "mlk": O_MLK, "naq": O_NAQ, "nak": O_NAK}.get(gname)
    if base is not None:
        return base + CW * j + 128 * gi
    return {"mbB": O_MBB, "mbC": O_MBC}[gname] + 128 * (MBG * j + gi)


def _small_cols(j):
    cols = []
    for d in range(2):
        cols += [O_MBDT + 8 * d + MBU * j + u for u in range(MBU)]
    for d in range(2):
        for g in range(2):
            cols += [O_MLG + 8 * d + 4 * g + MLU * j + u for u in range(MLU)]
    return cols


def _prep_weights(inp, j):
    L = DEPTH
    w_in = np.asarray(inp["w_in"], np.float32)
    w_out = np.asarray(inp["w_out"], np.float32)
    W = {}
    wfm = np.empty((L, NFM, 128, 1024), np.float32)
    convp = np.zeros((L, NFM, 128, 4), np.float32)
    for bi, (gname, gi, _c, ptype) in enumerate(FM_BLOCKS):
        col0 = _fm_col0(gname, gi, j)
        blk = w_in[:, :, col0:col0 + 128]
        wfm[:, bi] = blk.reshape(L, 8, 128, 128).transpose(0, 2, 1, 3).reshape(L, 128, 1024)
        if ptype in ("conv", "convsilu"):
            if gname == "hyu":
                cw, cb, c0 = inp["hy_conv_w"], inp["hy_conv_b"], col0 - O_HYU
            elif gname in ("mbx", "mbB", "mbC"):
                cw, cb, c0 = inp["mb_conv_w"], inp["mb_conv_b"], col0 - O_MBX
            else:
                cw, cb, c0 = inp["ml_conv_w"], inp["ml_conv_b"], col0 - O_MLQ
            convp[:, bi, :, 0:3] = np.asarray(cw)[:, :, c0:c0 + 128].transpose(0, 2, 1)
            convp[:, bi, :, 3] = np.asarray(cb)[:, c0:c0 + 128]
        elif ptype == "qknorm":
            nw = np.asarray(inp["na_qnorm_w"] if gname == "naq" else inp["na_knorm_w"])
            convp[:, bi, :, 0] = np.tile(nw, (1, 2))
    W["wfm"] = wfm
    W["convp"] = convp
    scols = _small_cols(j)
    wsm = np.zeros((L, 8, 128, NSP), np.float32)
    wsm[:, :, :, :NSM] = w_in[:, :, scols].reshape(L, 8, 128, NSM)
    W["wsm"] = np.ascontiguousarray(wsm.transpose(0, 2, 1, 3).reshape(L, 128, 8 * NSP))
    wtm = np.empty((L, NTM, 128, 4096), np.float32)
    ppg = 512 // CW
    for g in range(NTM):
        cols = []
        for (gname, col0, act) in TM_PARTS[ppg * g:ppg * g + ppg]:
            cols += list(range(col0 + CW * j, col0 + CW * j + CW))
        wtm[:, g] = w_in[:, :, cols].reshape(L, 8, 128, 512).transpose(0, 2, 1, 3).reshape(L, 128, 4096)
    W["wtm"] = wtm
    rows = []
    for q in range(HC // 64):
        for r in range(SP):
            rows += list(range(HC * r + 64 * q, HC * r + 64 * q + 64))
    for r in range(SP):
        for base in (512, 1024, 1536):
            rows += list(range(base + CW * r, base + CW * r + CW))
    wo = w_out[:, rows, OW * j:OW * j + OW]
    W["wout"] = np.ascontiguousarray(wo.reshape(L, 16, 128, OW).transpose(0, 2, 1, 3).reshape(L, 128, 16 * OW))
    W["norm_w"] = np.asarray(inp["norm_w"], np.float32).reshape(L, 1, D)
    W["ident"] = np.eye(128, dtype=np.float32)
    bo = np.zeros((128, 128), np.float32)
    bo[:64, :64] = 1.0
    bo[64:, 64:] = 1.0
    W["blockones"] = bo
    return W


def _prep_na(inp, j):
    jc = j
    L = DEPTH
    rpb = np.asarray(inp["na_rpb"], np.float32)
    kk = np.arange(128)
    il, kc = kk // 64, kk % 64
    w = np.arange(64)
    cs = np.clip(w - 8, 0, 48)
    valid = (kc[:, None] >= cs[None, :]) & (kc[:, None] < cs[None, :] + 16)
    coff = np.clip(kc[:, None] - w[None, :] + 15, 0, 30)
    bias = np.zeros((L, 8, 128, 8, 4, 64), np.float32)
    for v in range(8):
        for j in range(4):
            i = 2 * j + il
            roff = v + i
            bias[:, :, :, v, j, :] = rpb[:, :, roff[:, None], coff]
    mask = np.broadcast_to(valid[:, None, :], (128, 4, 64)).astype(np.float32).reshape(128, 256)
    return {"na_bias": np.ascontiguousarray(bias.reshape(L, 8, 128, 2048)[:, NAH * jc:NAH * jc + NAH]), "na_mask": np.ascontiguousarray(mask)}


def _na_declare_inputs(self):
    self.I["na_bias"] = self.dram_in("na_bias", [DEPTH, NAH, 128, 2048])
    self.I["na_mask"] = self.dram_in("na_mask", [128, 256])


def _phaseNA(self, l):
    nc, S, I, Sc = self.nc, self.S, self.I, self.Sc
    NH = NAH
    with ExitStack() as st:
        sb = lambda n, s, d=F32: st.enter_context(_alloc(nc, "sbuf", n, list(s), d))
        KT = [sb("naKT%d" % i, [64, T], BF16) for i in range(2)]
        QT = [sb("naQT%d" % i, [64, T], BF16) for i in range(2)]
        Ve = [sb("naVe%d" % i, [128, 64, 65], BF16) for i in range(2)]
        Vo = [sb("naVo%d" % i, [128, 63, 65], BF16) for i in range(2)]
        EBr = sb("naEBr", [128, 2048])
        EBM = [sb("naEBM%d" % i, [128, 8, 256]) for i in range(2)]
        msk = sb("namask", [128, 256])
        G = [sb("naG%d" % i, [64, 128, 64], BF16) for i in range(2)]
        O = sb("naO", [64, 128, 64])
        Ob = sb("naOb", [64, 128, 64], BF16)
        E = [sb("naE%d" % i, [128, 256]) for i in range(2)]
        Pb = [sb("naP%d" % i, [128, 256], BF16) for i in range(2)]
        rec = [sb("narec%d" % i, [64, 1]) for i in range(2)]
        pS = [st.enter_context(_alloc(nc, "psum", "napS%d" % i, [128, 512], F32)) for i in range(2)]
        pO = [st.enter_context(_alloc(nc, "psum", "napO%d" % i, [128, 512], F32)) for i in range(2)]
        S.dma("sp", msk[:], I["na_mask"], writes=["namask"])
        for b in range(2):
            S.op("pool", lambda e: e.memset(Ve[b][:, :, 64:65], 1.0), writes=[("Veo", b)])
            S.op("pool", lambda e: e.memset(Vo[b][:, :, 64:65], 1.0), writes=[("Voo", b)])
        for h in range(NH):
            b = h % 2
            S.dma("sp", KT[b][:], Sc["nakT"][64 * h:64 * h + 64, :], writes=[("KT", b)])
            S.dma("sp", QT[b][:], Sc["naqT"][64 * h:64 * h + 64, :], writes=[("QT", b)])
            S.dma("pool", Ve[b][:, :, 0:64], Sc["nav"][:, 64 * h:64 * h + 64].rearrange("(i p) d -> p i d", p=128), writes=[("Ve", b)])
            S.dma("pool", Vo[b][:, :, 0:64], Sc["nav"][64:T - 64, 64 * h:64 * h + 64].rearrange("(i p) d -> p i d", p=128), writes=[("Vo", b)])
            S.dma("sp", G[b][:], Sc["nag"][:, 64 * h:64 * h + 64].rearrange("(r w) d -> w r d", w=64), writes=[("G", b)])
            S.dma("sp", EBr[:], I["na_bias"][l, h], writes=["EBr"])
            S.op("act", lambda e: e.activation(out=EBr[:], in_=EBr[:], func=AF.Exp), reads=["EBr"], writes=["EBr"])
            S.op("dve", lambda e: e.tensor_tensor(out=EBM[b][:], in0=EBr[:].rearrange("p (v c) -> p v c", c=256),
                                                  in1=msk[:].unsqueeze(1).broadcast_to([128, 8, 256]), op=ALU.mult),
                 reads=["EBr", "namask"], writes=[("EBM", b)])
            for r in range(128):
                rs = min(max(r - 4, 0), 120)
                v = rs - r + 7
                rb = r % 2
                for j in range(4):
                    S.op("pe", lambda e: e.matmul(pS[rb][:, 64 * j:64 * j + 64], lhsT=KT[b][:, 64 * rs + 128 * j:64 * rs + 128 * j + 128],
                                                  rhs=QT[b][:, 64 * r:64 * r + 64], start=True, stop=True),
                         reads=[("KT", b), ("QT", b)], writes=[("pS", rb)])
                S.op("act", lambda e: e.activation(out=E[rb][:], in_=pS[rb][:, 0:256], func=AF.Exp, scale=0.125), reads=[("pS", rb)], writes=[("E", rb)])
                S.op("dve", lambda e: e.tensor_tensor(out=Pb[rb][:], in0=E[rb][:], in1=EBM[b][:, v, :], op=ALU.mult),
                     reads=[("E", rb), ("EBM", b)], writes=[("P", rb)])
                for j in range(4):
                    if rs % 2 == 0:
                        vt = Ve[b][:, rs // 2 + j, :]
                    else:
                        vt = Vo[b][:, (rs - 1) // 2 + j, :]
                    S.op("pe", lambda e: e.matmul(pO[rb][0:64, 0:65], lhsT=Pb[rb][:, 64 * j:64 * j + 64], rhs=vt, start=(j == 0), stop=(j == 3)),
                         reads=[("P", rb), ("Ve", b), ("Vo", b), ("Veo", b), ("Voo", b)], writes=[("pO", rb)])
                S.op("dve", lambda e: e.reciprocal(out=rec[rb][:], in_=pO[rb][0:64, 64:65]), reads=[("pO", rb)], writes=[("rec", rb)])
                S.op("act", lambda e: e.activation(out=O[:, r, :], in_=pO[rb][0:64, 0:64], func=AF.Copy, scale=rec[rb][:]),
                     reads=[("pO", rb), ("rec", rb)], writes=["O"])
            S.op("dve", lambda e: e.tensor_tensor(out=Ob[:].rearrange("p r d -> p (r d)"), in0=O[:].rearrange("p r d -> p (r d)"),
                                                  in1=G[b][:].rearrange("p r d -> p (r d)"), op=ALU.mult), reads=["O", ("G", b)], writes=["Ob"])
            S.dma("pool", Sc["ytm"][:, 2 * CW + 64 * h:2 * CW + 64 * h + 64].rearrange("(r w) d -> w r d", w=64), Ob[:], reads=["Ob"], writes=[("ytm_na", h)])


Prog.phaseNA = _phaseNA


def _prep_mixer(inp, j):
    W = {}
    W.update(_prep_na(inp, j))
    W.update(_prep_dla(inp, j))
    W.update(_prep_hy(inp, j))
    return W


def _prep_dla(inp, j):
    L = DEPTH
    gpar = np.zeros((L, 4, 64, 2), np.float32)
    dtb = np.asarray(inp["mb_dt_bias"], np.float32)[:, :, MBU * j:MBU * j + MBU]
    alog = np.asarray(inp["mb_a_log"], np.float32)[:, :, MBU * j:MBU * j + MBU]
    gb = np.asarray(inp["ml_gate_b"], np.float32)[:, :, :, MLU * j:MLU * j + MLU]
    for d in range(2):
        gpar[:, d, :8 * MBU, 0] = np.repeat(dtb[:, d, :], 8, axis=1)
        gpar[:, d, :8 * MBU, 1] = np.repeat(alog[:, d, :], 8, axis=1)
        gpar[:, 2 + d, :8 * MLU, 0] = np.repeat(gb[:, d, 0, :], 8, axis=1)
        gpar[:, 2 + d, :8 * MLU, 1] = np.repeat(gb[:, d, 1, :], 8, axis=1)
    rmask = np.ones((64, 1024), np.float32)
    rmask[:, ::128] = 0.0
    s = np.arange(128)[:, None]
    ll = np.arange(128)[None, :]
    negmask = np.stack([np.where(s <= ll, 0.0, -30000.0), np.where(s >= ll, 0.0, -30000.0)]).astype(np.float32)
    return {"gpar": gpar, "rmask": rmask, "negmask": negmask,
            "dsk": np.ascontiguousarray(np.asarray(inp["mb_d"], np.float32)[:, MBU * j:MBU * j + MBU]).reshape(L, 1, MBU),
            "mbnw": np.ascontiguousarray(np.asarray(inp["mb_norm_w"], np.float32)[:, CW * j:CW * j + CW]).reshape(L, 1, CW),
            "mlnw": np.ascontiguousarray(np.asarray(inp["ml_norm_w"], np.float32)[:, CW * j:CW * j + CW]).reshape(L, 1, CW)}


def _dla_declare(self):
    I, Sc = self.I, self.Sc
    I["gpar"] = self.dram_in("gpar", [DEPTH, 4, 64, 2])
    I["rmask"] = self.dram_in("rmask", [64, 1024])
    I["negmask"] = self.dram_in("negmask", [2, 128, 128])
    I["dsk"] = self.dram_in("dsk", [DEPTH, 1, MBU])
    I["mbnw"] = self.dram_in("mbnw", [DEPTH, 1, CW])
    I["mlnw"] = self.dram_in("mlnw", [DEPTH, 1, CW])
    Sc["gq"] = self.dram_scr("gq", [4, 4, 8, T], F32, dbg=True)
    Sc["gcs"] = self.dram_scr("gcs", [4, 8, T], F32, dbg=True)
    Sc["gtot"] = self.dram_scr("gtot", [4, 8, 64], F32, dbg=True)
    Sc["yf_mb"] = self.dram_scr("yf_mb", [T, CW], F32)
    Sc["hf_ml"] = self.dram_scr("hf_ml", [T, CW], F32)
    Sc["yb_mb"] = self.dram_scr("yb_mb", [T, CW], F32)
    Sc["hb_ml"] = self.dram_scr("hb_ml", [T, CW], F32)


def _dla_streams(self, l, mixer):
    nc, S, I, Sc = self.nc, self.S, self.I, self.Sc
    U = MBU if mixer == "mb" else MLU
    P = 8 * U
    with ExitStack() as st:
        sb = lambda n, s, d=F32: st.enter_context(_alloc(nc, "sbuf", n, list(s), d))
        rm = sb("rm", [64, 1024])
        S.dma("sp", rm[:], I["rmask"], writes=["rm"])
        for d in range(2):
            md = (0 if mixer == "mb" else 2) + d
            k = lambda n: (n, d)
            gp = sb("gp%d" % d, [64, 2])
            S.dma("sp", gp[:], I["gpar"][l, md], writes=[k("gp")])
            sc = sb("sc%d" % d, [64, 1024]); a = sb("a%d" % d, [64, 1024]); cs = sb("cs%d" % d, [64, 1024])
            t1 = sb("t1%d" % d, [64, 1024]); t2 = sb("t2%d" % d, [64, 1024]); pp = sb("pp%d" % d, [64, 2])
            if mixer == "mb":
                S.dma("sp", t1[0:P, :], Sc["smallT"][MBU * d:MBU * d + MBU, :].rearrange("u (s n) -> (u s) n", n=1024), writes=[k("t1")])
                S.op("act", lambda e: e.activation(out=t1[0:P, :], in_=t1[0:P, :], func=AF.Exp, bias=gp[0:P, 0:1]), reads=[k("t1"), k("gp")], writes=[k("t1")])
                S.op("act", lambda e: e.activation(out=sc[0:P, :], in_=t1[0:P, :], func=AF.Ln, bias=1.0), reads=[k("t1")], writes=[k("sc")])
                S.op("act", lambda e: e.activation(out=pp[0:P, 0:1], in_=gp[0:P, 1:2], func=AF.Exp), reads=[k("gp")], writes=[k("pp")])
                S.op("dve", lambda e: e.tensor_scalar(out=pp[0:P, 0:1], in0=pp[0:P, 0:1], scalar1=-1.0, scalar2=None, op0=ALU.mult), reads=[k("pp")], writes=[k("pp")])
                S.op("dve", lambda e: e.tensor_scalar(out=a[0:P, :], in0=sc[0:P, :], scalar1=pp[0:P, 0:1], scalar2=None, op0=ALU.mult),
                     reads=[k("sc"), k("pp")], writes=[k("a")])
            else:
                r0 = 2 * MBU + 2 * MLU * d
                S.dma("sp", t1[0:P, :], Sc["smallT"][r0:r0 + MLU, :].rearrange("u (s n) -> (u s) n", n=1024), writes=[k("t1")])
                S.dma("sp", t2[0:P, :], Sc["smallT"][r0 + MLU:r0 + 2 * MLU, :].rearrange("u (s n) -> (u s) n", n=1024), writes=[k("t2")])
                S.op("act", lambda e: e.activation(out=sc[0:P, :], in_=t1[0:P, :], func=AF.Exp, bias=gp[0:P, 0:1]), reads=[k("t1"), k("gp")], writes=[k("sc")])
                S.op("dve", lambda e: e.tensor_scalar(out=sc[0:P, :], in0=sc[0:P, :], scalar1=float(128.0 ** -0.5), scalar2=None, op0=ALU.mult), reads=[k("sc")], writes=[k("sc")])
                S.op("dve", lambda e: e.tensor_scalar(out=pp[0:P, 0:1], in0=gp[0:P, 1:2], scalar1=-1.0, scalar2=None, op0=ALU.mult), reads=[k("gp")], writes=[k("pp")])
                S.op("act", lambda e: e.activation(out=t2[0:P, :], in_=t2[0:P, :], func=AF.Exp, scale=-1.0, bias=pp[0:P, 0:1]), reads=[k("t2"), k("pp")], writes=[k("t2")])
                S.op("act", lambda e: e.activation(out=t2[0:P, :], in_=t2[0:P, :], func=AF.Ln, bias=1.0), reads=[k("t2")], writes=[k("t2")])
                S.op("dve", lambda e: e.tensor_scalar(out=a[0:P, :], in0=t2[0:P, :], scalar1=-1.0, scalar2=None, op0=ALU.mult), reads=[k("t2")], writes=[k("a")])
            S.op("dve", lambda e: e.tensor_tensor_scan(out=cs[0:P, :], data0=rm[0:P, :], data1=a[0:P, :], initial=0.0, op0=ALU.mult, op1=ALU.add),
                 reads=["rm", k("a")], writes=[k("cs")])
            cs3 = cs[0:P, :].rearrange("p (c n) -> p c n", n=128)
            totb = cs3[:, :, 127:128].broadcast_to([P, 8, 128])
            S.dma("sp", Sc["gtot"][md, 0:U, :].rearrange("u (s c) -> (u s) c", c=8), cs3[:, :, 127], reads=[k("cs")], writes=[("gtot", md)], allow_slow_non_contiguous=True)
            t13 = t1[0:P, :].rearrange("p (c n) -> p c n", n=128)
            S.op("dve", lambda e: e.tensor_tensor(out=t13, in0=totb, in1=cs3, op=ALU.subtract), reads=[k("cs")], writes=[k("t1")])
            if d == 1:
                S.op("dve", lambda e: e.tensor_tensor(out=t2[0:P, :], in0=cs[0:P, :], in1=a[0:P, :], op=ALU.subtract), reads=[k("cs"), k("a")], writes=[k("t2")])
                S.op("dve", lambda e: e.tensor_tensor(out=cs[0:P, :], in0=t1[0:P, :], in1=a[0:P, :], op=ALU.add), reads=[k("t1"), k("a")], writes=[k("cs")])
                wexp = t2
                wk = k("t2")
            else:
                wexp = t1
                wk = k("t1")
            unf = lambda ap: ap.rearrange("u (s n) -> (u s) n", n=1024)
            S.dma("sp", unf(Sc["gcs"][md, 0:U, :]), cs[0:P, :], reads=[k("cs")], writes=[("gcs", md)])
            S.op("act", lambda e: e.activation(out=wexp[0:P, :], in_=wexp[0:P, :], func=AF.Exp), reads=[wk], writes=[wk])
            S.op("dve", lambda e: e.tensor_tensor(out=wexp[0:P, :], in0=wexp[0:P, :], in1=sc[0:P, :], op=ALU.mult), reads=[wk, k("sc")], writes=[wk])
            S.dma("sp", unf(Sc["gq"][md, 2, 0:U, :]), wexp[0:P, :], reads=[wk], writes=[("gq", md, 2)])
            S.dma("sp", unf(Sc["gq"][md, 3, 0:U, :]), sc[0:P, :], reads=[k("sc")], writes=[("gq", md, 3)])
            S.op("act", lambda e: e.activation(out=a[0:P, :], in_=cs[0:P, :], func=AF.Exp), reads=[k("cs")], writes=[k("a")])
            S.dma("sp", unf(Sc["gq"][md, 1, 0:U, :]), a[0:P, :], reads=[k("a")], writes=[("gq", md, 1)])
            S.op("dve", lambda e: e.tensor_scalar(out=cs[0:P, :], in0=cs[0:P, :], scalar1=-1.0, scalar2=None, op0=ALU.mult), reads=[k("cs")], writes=[k("cs")])
            S.dma("sp", unf(Sc["gq"][md, 0, 0:U, :]), cs[0:P, :], reads=[k("cs")], writes=[("gq", md, 0)])


def _dla_run(self, l, mixer, d):
    nc, I, Sc = self.nc, self.I, self.Sc
    S = _KeyNS(self.S, (mixer, d))
    mb = mixer == "mb"
    U = MBU if mb else MLU
    PW = 64 if mb else 129
    PS = 64 if mb else 256
    md = (0 if mb else 2) + d
    with ExitStack() as st:
        sb = lambda n, s, dt=F32: st.enter_context(_alloc(nc, "sbuf", n, list(s), dt))
        pst = lambda n, dt=F32: st.enter_context(_alloc(nc, "psum", n, [128, 512], dt))
        Q4 = self.Q4s[d]
        q4t = sb("q4t", [128, 64, 32])
        etot = sb("etot", [128, U, 64])
        negm = sb("negm", [128, 128])
        H = sb("H", [128, U, PW]); Hb = sb("Hb", [128, U, PW], BF16)
        csb = [sb("csb%d" % i, [128, U, 128]) for i in range(2)]
        LT = [sb("LT%d" % i, [128, U, 128]) for i in range(2)]
        MT = [sb("MT%d" % i, [128, U, 128], BF16) for i in range(2)]
        Xv = [sb("Xv%d" % i, [128, U, PW], BF16) for i in range(2)]
        Xw = [sb("Xw%d" % i, [128, U, PW], BF16) for i in range(2)]
        Kt = [sb("Kt%d" % i, [128, 128 * MBG if mb else CW], BF16) for i in range(2)]
        NG = MBG if mb else MLU
        KTf = [sb("KTf%d" % i, [128, NG, 128], BF16) for i in range(2)]
        QTf = [sb("QTf%d" % i, [128, NG, 128], BF16) for i in range(2)]
        y2s = [sb("y2s%d" % i, [128, U, PW]) for i in range(2)]
        yo = [sb("yo%d" % i, [128, U, PW]) for i in range(2)]
        fin = [sb("fin%d" % i, [128, CW]) for i in range(2)]
        pG = pst("pG")
        NPT = 1 if mb else (MLU + 1) // 2
        py1 = [pst("py1%d" % i) for i in range(NPT)]
        py2 = [pst("py2%d" % i) for i in range(NPT)]
        pS_ = [pst("pS%d" % i) for i in range(NPT)]
        pQ = py1[0]

        def pview(tiles, u, w):
            if mb:
                return tiles[0][:, 64 * u:64 * u + w]
            return tiles[u // 2][:, 256 * (u % 2):256 * (u % 2) + w]

        def pall(tiles, h):
            if mb:
                return tiles[0][:, 0:64 * U].rearrange("p (u w) -> p u w", w=64)
            return tiles[h][:].rearrange("p (u w) -> p u w", w=256)[:, :, 0:129]

        S.op("pool", lambda e: e.memset(Q4[:], 0.0), writes=["Q4"])
        S.dma("sp", Q4[:], Sc["gq"][md].rearrange("q u t -> (q u) t"), writes=["Q4"])
        for g4 in range(4):
            for k in range(16):
                c = 16 * g4 + k
                S.op("pe", lambda e: e.transpose(pQ[:, 32 * k:32 * k + 32], Q4[:, 128 * c:128 * c + 128], self.ident_f[0:32, 0:32]), reads=["Q4"], writes=["py1"])
            S.op("dve", lambda e: e.tensor_copy(out=q4t[:, 16 * g4:16 * g4 + 16, :], in_=pQ[:].rearrange("p (k q) -> p k q", q=32)), reads=["py1"], writes=["q4t"])
        S.dma("sp", etot[:].rearrange("p u c -> p (u c)"), Sc["gtot"][md:md + 1, 0:U, :].rearrange("o u c -> o (u c)").broadcast_to([128, U * 64]), writes=["etot"])
        S.op("act", lambda e: e.activation(out=etot[:], in_=etot[:], func=AF.Exp), reads=["etot"], writes=["etot"])
        S.dma("sp", negm[:], I["negmask"][d], writes=["negm"])
        S.op("pool", lambda e: e.memset(H[:], 0.0), writes=["H"])
        S.op("pool", lambda e: e.memset(Hb[:], 0.0), writes=["Hb"])
        if not mb:
            for i in range(2):
                S.op("pool", lambda e: e.memset(Xv[i][:, :, 128:129], 1.0), writes=[("Xvo", i)])
        rden = [sb("rden%d" % i, [128, 4]) for i in range(2)]

        order = list(range(64)) if d == 0 else list(range(63, -1, -1))
        yield "setup"
        for step, c in enumerate(order):
            k = step % 2
            r0 = 128 * c
            S.dma("sp", csb[k][:], Sc["gcs"][md, 0:U, r0:r0 + 128].partition_broadcast(128), writes=[("csb", k)])
            if mb:
                S.dma("sp", Xv[k][:].rearrange("p u w -> p (u w)"), Sc["mbx"][r0:r0 + 128, :], writes=[("Xv", k)])
                S.dma("sp", Kt[k][:], Sc["mbB"][r0:r0 + 128, :], writes=[("Kt", k)])
                S.dma("sp", KTf[k][:], Sc["mbBT"][:, r0:r0 + 128].rearrange("(g n) s -> n g s", n=128), writes=[("KTf", k)])
                S.dma("sp", QTf[k][:], Sc["mbCT"][:, r0:r0 + 128].rearrange("(g n) s -> n g s", n=128), writes=[("QTf", k)])
            else:
                S.dma("sp", Xv[k][:, :, 0:128], Sc["mlv"][r0:r0 + 128, :].rearrange("p (u w) -> p u w", w=128), writes=[("Xv", k)])
                S.dma("sp", Kt[k][:], Sc["mlk"][r0:r0 + 128, :], writes=[("Kt", k)])
                S.dma("sp", KTf[k][:], Sc["mlkT"][:, r0:r0 + 128].rearrange("(g n) s -> n g s", n=128), writes=[("KTf", k)])
                S.dma("sp", QTf[k][:], Sc["mlqT"][:, r0:r0 + 128].rearrange("(g n) s -> n g s", n=128), writes=[("QTf", k)])
            xvk = [("Xv", k)] + ([] if mb else [("Xvo", k)])
            yield "s"
            for g in range(NG):
                S.op("pe", lambda e: e.matmul(pG[:, 128 * g:128 * g + 128], lhsT=KTf[k][:, g, :], rhs=QTf[k][:, g, :], start=True, stop=True),
                     reads=[("KTf", k), ("QTf", k)], writes=[("pG", g)])
            S.op("dve", lambda e: e.tensor_tensor(out=csb[k][:], in0=csb[k][:], in1=negm[:].unsqueeze(1).broadcast_to([128, U, 128]), op=ALU.add),
                 reads=[("csb", k), "negm"], writes=[("csb", k)])
            yield "s"
            for u in range(U):
                S.op("act", lambda e: e.activation(out=LT[k][:, u, :], in_=csb[k][:, u, :], func=AF.Exp, bias=q4t[:, c, u:u + 1]),
                     reads=[("csb", k), "q4t"], writes=[("LT", k, u)])
            yield "s"
            for u in range(U):
                g = (u // 4) if mb else u
                S.op("dve", lambda e: e.scalar_tensor_tensor(out=MT[k][:, u, :], in0=pG[:, 128 * g:128 * g + 128], scalar=q4t[:, c, 24 + u:25 + u],
                                                             in1=LT[k][:, u, :], op0=ALU.mult, op1=ALU.mult),
                     reads=[("pG", g), ("LT", k, u), "q4t"], writes=[("MT", k, u)])
            yield "s"
            for u in range(U):
                S.op("pe", lambda e: e.matmul(pview(py1, u, PW), lhsT=MT[k][:, u, :], rhs=Xv[k][:, u, :], start=True, stop=True),
                     reads=[("MT", k, u)] + xvk, writes=["py1"])
            if mb:
                for g in range(MBG):
                    S.op("pe", lambda e: e.matmul(py2[0][:, 256 * g:256 * g + 256], lhsT=QTf[k][:, g, :], rhs=Hb[:, 4 * g:4 * g + 4, :],
                                                  start=True, stop=True), reads=[("QTf", k), "Hb"], writes=["py2"])
            else:
                for u in range(U):
                    S.op("pe", lambda e: e.matmul(pview(py2, u, PW), lhsT=QTf[k][:, u, :], rhs=Hb[:, u, :], start=True, stop=True),
                         reads=[("QTf", k), "Hb"], writes=["py2"])
            yield "s"
            ecs_b = lambda u0, n: q4t[:, c, 8 + u0:8 + u0 + n].unsqueeze(2).broadcast_to([128, n, PW])
            w_b = q4t[:, c, 16:16 + U].unsqueeze(2).broadcast_to([128, U, PW])
            if mb:
                S.op("dve", lambda e: e.tensor_tensor(out=y2s[k][:], in0=pall(py2, 0), in1=ecs_b(0, U), op=ALU.mult), reads=["py2", "q4t"], writes=[("y2s", k)])
                S.op("dve", lambda e: e.tensor_tensor(out=yo[k][:], in0=pall(py1, 0), in1=y2s[k][:], op=ALU.add), reads=["py1", ("y2s", k)], writes=[("yo", k)])
            else:
                for h in range(NPT):
                    S.op("dve", lambda e: e.tensor_tensor(out=y2s[k][:, 2 * h:2 * h + 2, :], in0=pall(py2, h), in1=ecs_b(2 * h, 2), op=ALU.mult),
                         reads=["py2", "q4t"], writes=[("y2s", k, h)])
                    S.op("dve", lambda e: e.tensor_tensor(out=yo[k][:, 2 * h:2 * h + 2, :], in0=pall(py1, h), in1=y2s[k][:, 2 * h:2 * h + 2, :], op=ALU.add),
                         reads=["py1", ("y2s", k, h)], writes=[("yo", k, h)])
            yok = [("yo", k)] if mb else [("yo", k, h) for h in range(NPT)]
            yield "s"
            S.op("pool", lambda e: e.tensor_tensor(out=Xw[k][:], in0=Xv[k][:], in1=w_b, op=ALU.mult), reads=xvk + ["q4t"], writes=[("Xw", k)])
            if mb:
                for g in range(MBG):
                    S.op("pe", lambda e: e.matmul(pS_[0][:, 256 * g:256 * g + 256], lhsT=Kt[k][:, 128 * g:128 * g + 128], rhs=Xw[k][:, 4 * g:4 * g + 4, :],
                                                  start=True, stop=True), reads=[("Kt", k), ("Xw", k)], writes=["pS"])
            else:
                for u in range(U):
                    S.op("pe", lambda e: e.matmul(pview(pS_, u, PW), lhsT=Kt[k][:, 128 * u:128 * u + 128], rhs=Xw[k][:, u, :], start=True, stop=True),
                         reads=[("Kt", k), ("Xw", k)], writes=["pS"])
            yield "s"
            S.op("pool", lambda e: e.tensor_tensor(out=H[:], in0=H[:], in1=etot[:, :, c:c + 1].broadcast_to([128, U, PW]), op=ALU.mult),
                 reads=["H", "etot"], writes=["H"])
            if mb:
                S.op("dve", lambda e: e.tensor_tensor(out=H[:], in0=pall(pS_, 0), in1=H[:], op=ALU.add), reads=["H", "pS"], writes=["H"])
            else:
                for h in range(NPT):
                    S.op("dve", lambda e: e.tensor_tensor(out=H[:, 2 * h:2 * h + 2, :], in0=pall(pS_, h), in1=H[:, 2 * h:2 * h + 2, :], op=ALU.add),
                         reads=["H", "pS"], writes=["H"])
            S.op("act", lambda e: e.activation(out=Hb[:], in_=H[:], func=AF.Copy), reads=["H"], writes=["Hb"])
            yield "s"
            f = fin[k]
            if mb:
                ysrc = yo[k][:].rearrange("p u w -> p (u w)")
                fk = yok
            else:
                S.op("act", lambda e: e.activation(out=rden[k][:, 0:MLU], in_=yo[k][:, :, 128], func=AF.Abs), reads=yok, writes=[("rden", k)])
                S.op("dve", lambda e: e.tensor_scalar(out=rden[k][:], in0=rden[k][:], scalar1=1.0, scalar2=None, op0=ALU.max), reads=[("rden", k)], writes=[("rden", k)])
                S.op("dve", lambda e: e.reciprocal(out=rden[k][:], in_=rden[k][:]), reads=[("rden", k)], writes=[("rden", k)])
                S.op("pool", lambda e: e.tensor_tensor(out=f[:].rearrange("p (u w) -> p u w", w=128), in0=yo[k][:, :, 0:128],
                                                       in1=rden[k][:, 0:MLU].unsqueeze(2).broadcast_to([128, MLU, 128]), op=ALU.mult),
                     reads=yok + [("rden", k)], writes=[("fin", k)])
                ysrc = f[:]
                fk = [("fin", k)]
            dst_f = (Sc["yf_mb"] if mb else Sc["hf_ml"]) if d == 0 else (Sc["yb_mb"] if mb else Sc["hb_ml"])
            S.dma("sp", dst_f[r0:r0 + 128, :], ysrc, reads=fk, writes=[("ydir", c)])
            yield "chunk"
        yield "done"


def _dla_final(self, l, mixer):
    nc, S, I, Sc = self.nc, self.S, self.I, self.Sc
    mb = mixer == "mb"
    with ExitStack() as st:
        sb = lambda n, s, dt=F32: st.enter_context(_alloc(nc, "sbuf", n, list(s), dt))
        nwb = sb("nwb2", [128, CW])
        S.dma("sp", nwb[:], I["mbnw" if mb else "mlnw"][l].broadcast_to([128, CW]), writes=["nwb2"])
        if mb:
            dskb = sb("dskb", [128, MBU])
            S.dma("sp", dskb[:], I["dsk"][l].broadcast_to([128, MBU]), writes=["dskb"])
        NB_ = 3
        prev = [sb("prev%d" % i, [128, CW]) for i in range(NB_)]
        cur = [sb("cur%d" % i, [128, CW]) for i in range(NB_)]
        Zt = [sb("Zt%d" % i, [128, CW], BF16) for i in range(NB_)]
        Ot = [sb("Ot%d" % i, [128, CW], BF16) for i in range(NB_)]
        ssq = [sb("ssq%d" % i, [128, 4]) for i in range(NB_)]
        junk = sb("junkd", [128, CW], BF16)
        outb = [sb("outb%d" % i, [128, CW], BF16) for i in range(NB_)]
        for c in range(64):
            k = c % NB_
            r0 = 128 * c
            S.dma("sp", prev[k][:], (Sc["yf_mb"] if mb else Sc["hf_ml"])[r0:r0 + 128, :], writes=[("prev", k)])
            S.dma("pool", cur[k][:], (Sc["yb_mb"] if mb else Sc["hb_ml"])[r0:r0 + 128, :], writes=[("cur", k)])
            S.dma("sp", Zt[k][:], (Sc["mbz"] if mb else Sc["mlz"])[r0:r0 + 128, :], writes=[("Zt", k)])
            S.dma("pool", Ot[k][:], (Sc["mbx"] if mb else Sc["mlo"])[r0:r0 + 128, :], writes=[("Ot", k)])
            S.op("pool", lambda e: e.tensor_tensor(out=prev[k][:], in0=prev[k][:], in1=cur[k][:], op=ALU.add), reads=[("prev", k), ("cur", k)], writes=[("prev", k)])
            if mb:
                S.op("pool", lambda e: e.tensor_tensor(out=cur[k][:].rearrange("p (u w) -> p u w", w=64), in0=Ot[k][:].rearrange("p (u w) -> p u w", w=64),
                                                       in1=dskb[:].unsqueeze(2).broadcast_to([128, MBU, 64]), op=ALU.mult),
                     reads=[("Ot", k), ("cur", k), "dskb"], writes=[("cur", k)])
                S.op("dve", lambda e: e.tensor_tensor(out=prev[k][:], in0=prev[k][:], in1=cur[k][:], op=ALU.add), reads=[("prev", k), ("cur", k)], writes=[("prev", k)])
                S.op("dve", lambda e: e.tensor_tensor(out=prev[k][:], in0=prev[k][:], in1=Zt[k][:], op=ALU.mult), reads=[("prev", k), ("Zt", k)], writes=[("prev", k)])
                ngr, gw = MBG, 256
            else:
                S.op("dve", lambda e: e.tensor_tensor(out=prev[k][:], in0=prev[k][:], in1=Ot[k][:], op=ALU.mult), reads=[("prev", k), ("Ot", k)], writes=[("prev", k)])
                ngr, gw = MLU, 128
            for g in range(ngr):
                S.op("act", lambda e: e.activation(out=junk[:, 0:gw], in_=prev[k][:, gw * g:gw * g + gw], func=AF.Square, accum_out=ssq[k][:, g:g + 1]),
                     reads=[("prev", k)], writes=["junkd", ("ssq", k)])
            S.op("dve", lambda e: e.tensor_scalar(out=ssq[k][:, 0:ngr], in0=ssq[k][:, 0:ngr], scalar1=1.0 / gw, scalar2=EPS, op0=ALU.mult, op1=ALU.add),
                 reads=[("ssq", k)], writes=[("ssq", k)])
            S.op("act", lambda e: e.activation(out=ssq[k][:, 0:ngr], in_=ssq[k][:, 0:ngr], func=AF.Sqrt), reads=[("ssq", k)], writes=[("ssq", k)])
            S.op("dve", lambda e: e.reciprocal(out=ssq[k][:, 0:ngr], in_=ssq[k][:, 0:ngr]), reads=[("ssq", k)], writes=[("ssq", k)])
            for g in range(ngr):
                S.op("dve", lambda e: e.scalar_tensor_tensor(out=(outb[k] if mb else prev[k])[:, gw * g:gw * g + gw], in0=prev[k][:, gw * g:gw * g + gw],
                                                             scalar=ssq[k][:, g:g + 1], in1=nwb[:, gw * g:gw * g + gw], op0=ALU.mult, op1=ALU.mult),
                     reads=[("prev", k), ("ssq", k), "nwb2"], writes=[("outb", k) if mb else ("prev", k)])
            if not mb:
                S.op("pool", lambda e: e.tensor_tensor(out=outb[k][:], in0=prev[k][:], in1=Zt[k][:], op=ALU.mult), reads=[("prev", k), ("Zt", k)], writes=[("outb", k)])
            col0 = 0 if mb else CW
            S.dma("pool", Sc["ytm"][r0:r0 + 128, col0:col0 + CW], outb[k][:], reads=[("outb", k)], writes=[("ytm_dla", mixer, c)])


def _phaseDLA(self, l):
    S, nc = self.S, self.nc
    for mixer in ("mb", "ml"):
        _dla_streams(self, l, mixer)
        S.barrier()
        with ExitStack() as st:
            self.Q4s = [st.enter_context(_alloc(nc, "sbuf", "Q4_%d" % d, [32, T], F32)) for d in range(2)]
            gens = [_dla_run(self, l, mixer, d) for d in (0, 1)]
            for g in gens:
                next(g)
            while True:
                rs = [next(g) for g in gens]
                if all(r == "done" for r in rs):
                    break
            S.barrier()
            for g in reversed(gens):
                try:
                    next(g)
                except StopIteration:
                    pass
        _dla_final(self, l, mixer)
        S.barrier()


Prog.phaseDLA = _phaseDLA


N2L = 2 * T
CG = 32


def _prep_hy(inp, j):
    L = DEPTH
    n = np.arange(128)
    ang = 2.0 * np.pi * np.outer(n, n) / 128.0
    Fre, Fim = np.cos(ang), -np.sin(ang)
    dft = np.stack([Fre, Fim, Fre, -Fim], axis=1).astype(np.float32)
    angt = 2.0 * np.pi * np.outer(n, n) / float(N2L)
    twd = np.stack([np.cos(angt), -np.sin(angt)], axis=1).astype(np.float32)
    t = np.arange(T, dtype=np.float32)
    t_norm = t / np.float32(T)
    bands = np.arange(1, 9, dtype=np.float32)
    a = (np.float32(2.0 * math.pi / T)) * t[:, None] * bands[None, :]
    pos = np.concatenate([t_norm[:, None], np.cos(a), np.sin(a)], axis=-1).astype(np.float32)
    hyp = np.zeros((L, 64, 4), np.float32)
    hyp[:, :, 0] = np.asarray(inp["hy_b1"]); hyp[:, :, 1] = np.asarray(inp["hy_freq"]); hyp[:, :, 2] = np.asarray(inp["hy_b2"])
    dec = np.asarray(inp["hy_decay"], np.float32).reshape(L, 4, 512)[:, :, HC * j:HC * j + HC].reshape(L, 4 * NB, 128).transpose(0, 2, 1)
    w3 = np.asarray(inp["hy_w3"], np.float32).reshape(L, 64, 4, 512)[:, :, :, HC * j:HC * j + HC].reshape(L, 64, 4 * HC)
    skip = np.asarray(inp["hy_skip"], np.float32)[:, :, HC * j:HC * j + HC].reshape(L, 2, 1, HC)
    return {"dft": np.ascontiguousarray(dft.reshape(128, 512)), "twd": np.ascontiguousarray(twd.reshape(128, 256)),
            "posT": np.ascontiguousarray(pos.T), "tneg": (-t_norm).reshape(1, T).astype(np.float32),
            "hyp": hyp, "hydec": np.ascontiguousarray(dec),
            "hyw1": np.asarray(inp["hy_w1"], np.float32), "hyw2": np.asarray(inp["hy_w2"], np.float32),
            "hyw3": np.ascontiguousarray(w3), "hyskip": np.ascontiguousarray(skip)}


def _hy_declare(self):
    I, Sc = self.I, self.Sc
    I["dft"] = self.dram_in("dft", [128, 512]); I["twd"] = self.dram_in("twd", [128, 256])
    I["posT"] = self.dram_in("posT", [17, T]); I["tneg"] = self.dram_in("tneg", [1, T])
    I["hyp"] = self.dram_in("hyp", [DEPTH, 64, 4]); I["hydec"] = self.dram_in("hydec", [DEPTH, 128, 4 * NB])
    I["hyw1"] = self.dram_in("hyw1", [DEPTH, 17, 64]); I["hyw2"] = self.dram_in("hyw2", [DEPTH, 64, 64])
    I["hyw3"] = self.dram_in("hyw3", [DEPTH, 64, 4 * HC]); I["hyskip"] = self.dram_in("hyskip", [DEPTH, 2, 1, HC])
    Sc["gflt"] = self.dram_scr("gflt", [2, HC, N2L], BF16, dbg=True)


def _phaseHY(self, l):
    nc, S, I, Sc = self.nc, self.S, self.I, self.Sc
    PI = float(np.pi)
    with ExitStack() as st:
        sb = lambda n, s, d=F32: st.enter_context(_alloc(nc, "sbuf", n, list(s), d))
        pst = lambda n, d=F32: st.enter_context(_alloc(nc, "psum", n, [128, 512], d))
        w1 = sb("hw1", [17, 64]); w2 = sb("hw2", [64, 64]); w3f = sb("hw3f", [64, 4 * HC]); w3b = sb("hw3b", [64, 4 * HC], BF16)
        hp = sb("hhp", [64, 4]); fb = sb("hfb", [64, 2])
        hid = sb("hhid", [64, T], BF16)
        tn = sb("htn", [128, T]); dec = sb("hdec", [128, 4 * NB])
        S.dma("sp", w1[:], I["hyw1"][l], writes=["w1"]); S.dma("sp", w2[:], I["hyw2"][l], writes=["w2"])
        S.dma("sp", w3f[:], I["hyw3"][l], writes=["w3f"]); S.dma("sp", hp[:], I["hyp"][l], writes=["hp"])
        S.dma("sp", tn[:], I["tneg"].broadcast_to([128, T]), writes=["tn"]); S.dma("sp", dec[:], I["hydec"][l], writes=["dec"])
        S.op("pool", lambda e: e.tensor_copy(out=w3b[:], in_=w3f[:]), reads=["w3f"], writes=["w3b"])
        S.op("dve", lambda e: e.tensor_tensor(out=fb[:, 0:1], in0=hp[:, 0:1], in1=hp[:, 1:2], op=ALU.mult), reads=["hp"], writes=["fb"])
        S.op("dve", lambda e: e.tensor_tensor(out=fb[:, 1:2], in0=hp[:, 2:3], in1=hp[:, 1:2], op=ALU.mult), reads=["hp", "fb"], writes=["fb"])
        pt_ = [sb("hpt%d" % i, [17, 512]) for i in range(2)]
        arg = [sb("harg%d" % i, [64, 512]) for i in range(2)]
        ta = [sb("hta%d" % i, [64, 512]) for i in range(2)]
        tb = [sb("htb%d" % i, [64, 512]) for i in range(2)]
        h1 = [sb("hh1%d" % i, [64, 512]) for i in range(2)]
        pz = [pst("hpz%d" % i) for i in range(2)]

        def sin_layer(src_ps, pk, col, out_ap, okey, k):
            a = arg[k]; ak = ("arg", k)
            S.op("dve", lambda e: e.tensor_scalar(out=a[:], in0=src_ps[0:64, :], scalar1=hp[:, 1:2], scalar2=fb[:, col:col + 1], op0=ALU.mult, op1=ALU.add),
                 reads=[pk, "hp", "fb"], writes=[ak])
            S.op("dve", lambda e: e.tensor_scalar(out=ta[k][:], in0=a[:], scalar1=PI, scalar2=-2 * PI, op0=ALU.is_gt, op1=ALU.mult), reads=[ak], writes=[("ta", k)])
            S.op("dve", lambda e: e.tensor_scalar(out=tb[k][:], in0=a[:], scalar1=-PI, scalar2=2 * PI, op0=ALU.is_lt, op1=ALU.mult), reads=[ak], writes=[("tb", k)])
            S.op("pool", lambda e: e.tensor_tensor(out=ta[k][:], in0=ta[k][:], in1=tb[k][:], op=ALU.add), reads=[("ta", k), ("tb", k)], writes=[("ta", k)])
            S.op("pool", lambda e: e.tensor_tensor(out=a[:], in0=a[:], in1=ta[k][:], op=ALU.add), reads=[ak, ("ta", k)], writes=[ak])
            S.op("act", lambda e: e.activation(out=out_ap, in_=a[:], func=AF.Sin), reads=[ak], writes=[okey])

        for c in range(16):
            k = c % 2
            S.dma("sp", pt_[k][:], I["posT"][:, 512 * c:512 * c + 512], writes=[("pt", k)])
            S.op("pe", lambda e: e.matmul(pz[0][0:64, :], lhsT=w1[:], rhs=pt_[k][:], start=True, stop=True), reads=["w1", ("pt", k)], writes=["pz0"])
            sin_layer(pz[0], "pz0", 0, h1[k][:], ("h1", k), k)
            S.op("pe", lambda e: e.matmul(pz[1][0:64, :], lhsT=w2[:], rhs=h1[k][:], start=True, stop=True), reads=["w2", ("h1", k)], writes=["pz1"])
            sin_layer(pz[1], "pz1", 1, hid[:, 512 * c:512 * c + 512], ("hid", c), k)
        hidk = [("hid", c) for c in range(16)]
        gt = [sb("hgt%d" % i, [128, N2L], BF16) for i in range(2)]
        win = [sb("hwin%d" % i, [128, 512]) for i in range(2)]
        pf = [pst("hpf%d" % i) for i in range(2)]
        for i in range(2):
            S.op("pool", lambda e: e.memset(gt[i][:, T:T + 1], 0.0), writes=[("gtz", i)])
        it = 0
        for o in range(2):
            for cb in range(NB):
                g = gt[(o * NB + cb) % 2]; gk = ("gt", (o * NB + cb) % 2)
                gparts = []
                for dr in range(2):
                    col0 = (o * 2 + dr) * HC + 128 * cb
                    di = (o * 2 + dr) * NB + cb
                    for c in range(16):
                        k = it % 2; it += 1
                        S.op("pe", lambda e: e.matmul(pf[k][:], lhsT=w3b[:, col0:col0 + 128], rhs=hid[:, 512 * c:512 * c + 512], start=True, stop=True),
                             reads=["w3b", ("hid", c)], writes=[("pf", k)])
                        S.op("act", lambda e: e.activation(out=win[k][:], in_=tn[:, 512 * c:512 * c + 512], func=AF.Exp, scale=dec[:, di:di + 1]),
                             reads=["tn", "dec"], writes=[("win", k)])
                        pk = (gk, dr, c)
                        gparts.append(pk)
                        if dr == 0:
                            S.op("dve", lambda e: e.tensor_tensor(out=g[:, 512 * c:512 * c + 512], in0=pf[k][:], in1=win[k][:], op=ALU.mult),
                                 reads=[("pf", k), ("win", k)], writes=[pk])
                        else:
                            j0 = 1 if c == 0 else 0
                            lo = N2L - 512 * c - 511
                            hi = N2L - 512 * c - j0 + 1
                            S.op("dve", lambda e: e.tensor_tensor(out=g[:, lo:hi][:, ::-1], in0=pf[k][:, j0:512], in1=win[k][:, j0:512], op=ALU.mult),
                                 reads=[("pf", k), ("win", k)], writes=[pk])
                S.dma("pool", Sc["gflt"][o, 128 * cb:128 * cb + 128, :], g[:], reads=gparts + [("gtz", (o * NB + cb) % 2)], writes=[("gflt", o, cb)])
    S.barrier()
    with ExitStack() as st:
        sb = lambda n, s, d=F32: st.enter_context(_alloc(nc, "sbuf", n, list(s), d))
        dftf = sb("dftf", [128, 512]); dft = sb("dftb", [128, 4, 128], BF16); twd = sb("twd", [128, 2, 128])
        S.dma("sp", dftf[:], I["dft"], writes=["dftf"]); S.dma("sp", twd[:].rearrange("p a k -> p (a k)"), I["twd"], writes=["twd"])
        S.op("dve", lambda e: e.tensor_copy(out=dft[:].rearrange("p a k -> p (a k)"), in_=dftf[:]), reads=["dftf"], writes=["dft"])
        Fre, Fim, nFim = dft[:, 0, :], dft[:, 1, :], dft[:, 3, :]
        Fcat = dft[:, 0:2, :].rearrange("p a k -> p (a k)")
        Fci2 = dft[:, 1:3, :].rearrange("p a k -> p (a k)")
        Fci1 = dft[:, 2:4, :].rearrange("p a k -> p (a k)")
        G = sb("hyG", [128, 2, CG, 2, 128], BF16)
        gblk = [sb("gblk%d" % i, [128, CG, 128], BF16) for i in range(2)]
        sig = {n: sb("sig_" + n, [64, CG, 128], BF16) for n in ("v", "x1", "x2", "g")}
        zblk = sb("zblk", [64, CG, 128], BF16); oblk = sb("oblk", [64, CG, 128], BF16)
        skb = sb("skb", [64, 2, CG])
        Ap = [sb("Ap%d" % i, [128, 8, 2, 128], BF16) for i in range(2)]
        Yp = [sb("Yp%d" % i, [128, 8, 2, 128], BF16) for i in range(2)]
        Bp = [sb("Bp%d" % i, [128, 8, 2, 128], BF16) for i in range(2)]
        tt = [[sb("tt%d_%d" % (i, j), [128, 8, 128]) for j in range(4)] for i in range(2)]
        ep = [[sb("ep%d_%d" % (i, j), [64, 8, 128]) for j in range(2)] for i in range(2)]
        pA = st.enter_context(_alloc(nc, "psum", "hpA", [128, 2048], F32))
        pB = st.enter_context(_alloc(nc, "psum", "hpB", [128, 2048], F32))
        cn = {"c": 0}

        def cmul(out_t, okey, are, aim, akeys, bre, bim, bkeys, conj):
            i = cn["c"] % 2; cn["c"] += 1
            t1, t2, t3, t4 = tt[i]
            ks = [("tt", i, j) for j in range(4)]
            shp = lambda a: a
            S.op("dve", lambda e: e.tensor_tensor(out=t1[:], in0=are, in1=bre, op=ALU.mult), reads=akeys + bkeys, writes=[ks[0]])
            S.op("dve", lambda e: e.tensor_tensor(out=t2[:], in0=aim, in1=bim, op=ALU.mult), reads=akeys + bkeys, writes=[ks[1]])
            S.op("dve", lambda e: e.tensor_tensor(out=t3[:], in0=are, in1=bim, op=ALU.mult), reads=akeys + bkeys, writes=[ks[2]])
            S.op("dve", lambda e: e.tensor_tensor(out=t4[:], in0=aim, in1=bre, op=ALU.mult), reads=akeys + bkeys, writes=[ks[3]])
            if not conj:
                S.op("pool", lambda e: e.tensor_tensor(out=out_t[:, :, 0, :], in0=t1[:], in1=t2[:], op=ALU.subtract), reads=ks[0:2], writes=[okey + ("re",)])
                S.op("pool", lambda e: e.tensor_tensor(out=out_t[:, :, 1, :], in0=t3[:], in1=t4[:], op=ALU.add), reads=ks[2:4], writes=[okey + ("im",)])
            else:
                S.op("pool", lambda e: e.tensor_tensor(out=out_t[:, :, 0, :], in0=t1[:], in1=t2[:], op=ALU.add), reads=ks[0:2], writes=[okey + ("re",)])
                S.op("pool", lambda e: e.tensor_tensor(out=out_t[:, :, 1, :], in0=t4[:], in1=t3[:], op=ALU.subtract), reads=ks[2:4], writes=[okey + ("im",)])

        pA3 = pA[:].rearrange("p (c k) -> p c k", k=256)
        Tre = twd[:, 0, :].unsqueeze(1).broadcast_to([128, 8, 128])
        Tim = twd[:, 1, :].unsqueeze(1).broadcast_to([128, 8, 128])
        pBq = pB[:].rearrange("p (q r c k) -> p q r c k", q=2, r=2, c=4)

        def fwd_octet(src_fn, K, skeys, i):
            for ch in range(8):
                S.op("pe", lambda e: e.matmul(pA3[:, ch, :], lhsT=src_fn(ch), rhs=Fcat[0:K, :], start=True, stop=True),
                     reads=skeys + ["dft"], writes=["pA"])
            cmul(Ap[i], ("Ap", i), pA3[:, :, 0:128], pA3[:, :, 128:256], ["pA"], Tre, Tim, ["twd"], False)
            for q in range(2):
                rre = Ap[i][:, 4 * q:4 * q + 4, 0, :]
                rim = Ap[i][:, 4 * q:4 * q + 4, 1, :]
                kk = [("Ap", i, "re"), ("Ap", i, "im"), "dft"]
                S.op("pe", lambda e: e.matmul(pB[:, 1024 * q:1024 * q + 512], lhsT=Fre, rhs=rre, start=True, stop=False), reads=kk, writes=["pB"])
                S.op("pe", lambda e: e.matmul(pB[:, 1024 * q:1024 * q + 512], lhsT=nFim, rhs=rim, start=False, stop=True), reads=kk, writes=["pB"])
                S.op("pe", lambda e: e.matmul(pB[:, 1024 * q + 512:1024 * q + 1024], lhsT=Fim, rhs=rre, start=True, stop=False), reads=kk, writes=["pB"])
                S.op("pe", lambda e: e.matmul(pB[:, 1024 * q + 512:1024 * q + 1024], lhsT=Fre, rhs=rim, start=False, stop=True), reads=kk, writes=["pB"])

        oc = {"n": 0}
        for cg in range(HC // CG):
            c0 = CG * cg
            for o in range(2):
                S.dma("sp", gblk[o][:], Sc["gflt"][o, c0:c0 + CG, :].rearrange("c (a b) -> a c b", b=128), writes=[("gblk", o)])
                for oc8 in range(CG // 8):
                    i = oc["n"] % 2; oc["n"] += 1
                    fwd_octet(lambda ch: gblk[o][:, 8 * oc8 + ch, :], 128, [("gblk", o)], i)
                    gv = G[:, o, 8 * oc8:8 * oc8 + 8, :, :]
                    S.op("act", lambda e: e.activation(out=gv[:, :, 0, :].rearrange("p (q c) k -> p q c k", q=2), in_=pBq[:, :, 0, :, :], func=AF.Copy),
                         reads=["pB"], writes=[("G", o, oc8, 0)])
                    S.op("act", lambda e: e.activation(out=gv[:, :, 1, :].rearrange("p (q c) k -> p q c k", q=2), in_=pBq[:, :, 1, :, :], func=AF.Copy),
                         reads=["pB"], writes=[("G", o, oc8, 1)])
            for n_, src, r0 in (("v", "hyuT", 0), ("x1", "hyuT", HC), ("x2", "hyuT", 2 * HC), ("g", "hygT", 0)):
                S.dma("pool", sig[n_][:], Sc[src][r0 + c0:r0 + c0 + CG, :].rearrange("c (a b) -> a c b", b=128), writes=[("sig", n_)])
            S.dma("sp", skb[:].rearrange("p o c -> p (o c)") if False else skb[:], I["hyskip"][l, :, :, c0:c0 + CG].rearrange("o x c -> x o c").broadcast_to([64, 2, CG]),
                  writes=["skb"])
            for o in range(2):
                src_t = sig["v"] if o == 0 else zblk
                src_k = [("sig", "v")] if o == 0 else [("zblk", q) for q in range(CG // 8)]
                for oc8 in range(CG // 8):
                    i = oc["n"] % 2; oc["n"] += 1
                    ch0 = 8 * oc8
                    skeys = [("sig", "v")] if o == 0 else [("zblk", oc8)]
                    fwd_octet(lambda ch: src_t[:, ch0 + ch, :], 64, skeys, i)
                    gv = G[:, o, ch0:ch0 + 8, :, :]
                    gre = gv[:, :, 0, :].rearrange("p (q c) k -> p q c k", q=2)
                    gim = gv[:, :, 1, :].rearrange("p (q c) k -> p q c k", q=2)
                    ii = cn["c"] % 2; cn["c"] += 1
                    t1, t2, t3, t4 = [t[:].rearrange("p (q c) k -> p q c k", q=2) for t in tt[ii]]
                    ks = [("tt", ii, j) for j in range(4)]
                    gk = [("G", o, oc8, 0), ("G", o, oc8, 1)]
                    S.op("dve", lambda e: e.tensor_tensor(out=t1, in0=pBq[:, :, 0, :, :], in1=gre, op=ALU.mult), reads=["pB"] + gk, writes=[ks[0]])
                    S.op("dve", lambda e: e.tensor_tensor(out=t2, in0=pBq[:, :, 1, :, :], in1=gim, op=ALU.mult), reads=["pB"] + gk, writes=[ks[1]])
                    S.op("dve", lambda e: e.tensor_tensor(out=t3, in0=pBq[:, :, 0, :, :], in1=gim, op=ALU.mult), reads=["pB"] + gk, writes=[ks[2]])
                    S.op("dve", lambda e: e.tensor_tensor(out=t4, in0=pBq[:, :, 1, :, :], in1=gre, op=ALU.mult), reads=["pB"] + gk, writes=[ks[3]])
                    S.op("pool", lambda e: e.tensor_tensor(out=Yp[i][:, :, 0, :], in0=tt[ii][0][:], in1=tt[ii][1][:], op=ALU.subtract), reads=ks[0:2], writes=[("Yp", i, "re")])
                    S.op("pool", lambda e: e.tensor_tensor(out=Yp[i][:, :, 1, :], in0=tt[ii][2][:], in1=tt[ii][3][:], op=ALU.add), reads=ks[2:4], writes=[("Yp", i, "im")])
                    for ch in range(8):
                        S.op("pe", lambda e: e.matmul(pA3[:, ch, :], lhsT=Yp[i][:, ch, 0, :], rhs=Fci1, start=True, stop=False),
                             reads=[("Yp", i, "re"), ("Yp", i, "im"), "dft"], writes=["pA"])
                        S.op("pe", lambda e: e.matmul(pA3[:, ch, :], lhsT=Yp[i][:, ch, 1, :], rhs=Fci2, start=False, stop=True),
                             reads=[("Yp", i, "re"), ("Yp", i, "im"), "dft"], writes=["pA"])
                    cmul(Bp[i], ("Bp", i), pA3[:, :, 0:128], pA3[:, :, 128:256], ["pA"], Tre, Tim, ["twd"], True)
                    for q in range(2):
                        kk = [("Bp", i, "re"), ("Bp", i, "im"), "dft"]
                        S.op("pe", lambda e: e.matmul(pB[0:64, 512 * q:512 * q + 512], lhsT=Fre[:, 0:64], rhs=Bp[i][:, 4 * q:4 * q + 4, 0, :], start=True, stop=False),
                             reads=kk, writes=["pB"])
                        S.op("pe", lambda e: e.matmul(pB[0:64, 512 * q:512 * q + 512], lhsT=Fim[:, 0:64], rhs=Bp[i][:, 4 * q:4 * q + 4, 1, :], start=False, stop=True),
                             reads=kk, writes=["pB"])
                    e1, e2 = ep[i]
                    yv = pB[0:64, 0:1024].rearrange("p (c k) -> p c k", k=128)
                    skv = skb[:, o, ch0:ch0 + 8].unsqueeze(2).broadcast_to([64, 8, 128])
                    uu = src_t[:, ch0:ch0 + 8, :]
                    S.op("pool", lambda e: e.tensor_tensor(out=e1[:], in0=uu, in1=skv, op=ALU.mult), reads=skeys + ["skb"], writes=[("e1", i)])
                    S.op("dve", lambda e: e.scalar_tensor_tensor(out=e2[:], in0=yv, scalar=1.0 / N2L, in1=e1[:], op0=ALU.mult, op1=ALU.add),
                         reads=["pB", ("e1", i)], writes=[("e2", i)])
                    if o == 0:
                        S.op("pool", lambda e: e.tensor_tensor(out=zblk[:, ch0:ch0 + 8, :], in0=e2[:], in1=sig["x1"][:, ch0:ch0 + 8, :], op=ALU.mult),
                             reads=[("e2", i), ("sig", "x1")], writes=[("zblk", oc8)])
                    else:
                        S.op("pool", lambda e: e.tensor_tensor(out=e1[:], in0=e2[:], in1=sig["x2"][:, ch0:ch0 + 8, :], op=ALU.mult),
                             reads=[("e2", i), ("sig", "x2")], writes=[("e1", i)])
                        S.op("pool", lambda e: e.tensor_tensor(out=oblk[:, ch0:ch0 + 8, :], in0=e1[:], in1=sig["g"][:, ch0:ch0 + 8, :], op=ALU.mult),
                             reads=[("e1", i), ("sig", "g")], writes=[("oblk", oc8)])
            S.dma("pool", Sc["yhyT"][c0:c0 + CG, :].rearrange("c (a b) -> a c b", b=128), oblk[:], reads=[("oblk", q) for q in range(CG // 8)], writes=[("yhyT", cg)])


Prog.phaseHY = _phaseHY


NCORES = 8


def kernel(**inputs):
    P = Prog(nlayers=DEPTH, debug=False)
    nc = P.build()
    names = set(P.I.keys())
    x = np.asarray(inputs["x"], np.float32)
    Wj = []
    for j in range(SP):
        W = _prep_weights(inputs, j)
        W.update(_prep_mixer(inputs, j))
        Wj.append({k: v for k, v in W.items() if k in names})
    in_maps = []
    for c in range(NCORES):
        b, j = c // SP, c % SP
        m = dict(Wj[j])
        m["x"] = np.ascontiguousarray(x[b])
        m["xh"] = np.ascontiguousarray(x[b][:, OW * j:OW * j + OW])
        in_maps.append(m)
    res = run_bass_kernel_spmd(nc, in_maps, core_ids=list(range(NCORES)))
    out = np.empty((NCORES // SP, T, D), np.float32)
    for c in range(NCORES):
        b, j = c // SP, c % SP
        out[b][:, OW * j:OW * j + OW] = np.asarray(res.results[c]["out"], np.float32)
    return out
```

```python
import math
from contextlib import ExitStack

import numpy as np
import concourse.bass as bass
import concourse.mybir as mybir
from concourse.bass_utils import run_bass_kernel_spmd

F32 = mybir.dt.float32
BF16 = mybir.dt.bfloat16
AF = mybir.ActivationFunctionType
ALU = mybir.AluOpType

T = 8192
D = 1024
NT = 64
DEPTH = 2
EPS = 1e-6
SAME_ENGINE_SYNC = True
SAME_WAW = False
SAME_WAR = False

O_HYU, O_HYG, O_MBX, O_MBB, O_MBC, O_MBZ, O_MBDT = 0, 1536, 2048, 2560, 2816, 3072, 3584
O_MLQ, O_MLK, O_MLV, O_MLO, O_MLZ, O_MLG = 3600, 4112, 4624, 5136, 5648, 6160
O_NAQ, O_NAK, O_NAV, O_NAG = 6176, 6688, 7200, 7712

SP = 2
CW = 512 // SP
HC = CW
MBU = 8 // SP
MBG = 2 // SP
MLU = 4 // SP
NAH = 8 // SP
OW = 1024 // SP
YW = 3 * CW
NSM = 2 * MBU + 4 * MLU
NSP = 32
NB = CW // 128
FM_GROUPS = [
    ("hyu", O_HYU, 3 * NB, "conv"),
    ("hyg", O_HYG, NB, "silu"),
    ("mbx", O_MBX, NB, "convsilu"),
    ("mbB", O_MBB, MBG, "convsilu"),
    ("mbC", O_MBC, MBG, "convsilu"),
    ("mlq", O_MLQ, NB, "convsilu"),
    ("mlk", O_MLK, NB, "convsilu"),
    ("naq", O_NAQ, NB, "qknorm"),
    ("nak", O_NAK, NB, "qknorm"),
]
FM_BLOCKS = []
for _n, _o, _k, _t in FM_GROUPS:
    for _i in range(_k):
        FM_BLOCKS.append((_n, _i, _o + 128 * _i, _t))
NFM = len(FM_BLOCKS)
TM_PARTS = [("mbz", O_MBZ, "silu"), ("mlv", O_MLV, "copy"), ("mlo", O_MLO, "sigmoid"),
            ("mlz", O_MLZ, "silu"), ("nav", O_NAV, "copy"), ("nag", O_NAG, "silu")]
NTM = len(TM_PARTS) * CW // 512


import os
_SKIP = set(os.environ.get("KSKIP", "").split(","))
_UID = [0]


def _alloc(nc, kind, name, shape, dt):
    _UID[0] += 1
    nm = "%s_%d" % (name, _UID[0])
    if kind == "sbuf":
        return nc.sbuf_tensor(nm, shape, dt)
    return nc.psum_tensor(nm, shape, dt)


class Sched:
    def __init__(self, nc, stack, n_dma_sems=14):
        self.nc = nc
        self.eng = {"pe": nc.tensor, "dve": nc.vector, "act": nc.scalar, "pool": nc.gpsimd, "sp": nc.sync}
        self.sem = {e: stack.enter_context(nc.semaphore("c_" + e)) for e in self.eng}
        self.cnt = {e: 0 for e in self.eng}
        self.seen = {e: {} for e in self.eng}
        self.dq = {}
        for q in ("sp", "pool", "act"):
            self.dq[q] = [[stack.enter_context(nc.semaphore("d_%s%d" % (q, i))), 0] for i in range(n_dma_sems)]
        self.dqi = {q: 0 for q in self.dq}
        self.last_w = {}
        self.readers = {}
        self.n_wait = 0
        self.n_ins = 0
        self.ccs = []
        self.cc_toks = []

    def _wait(self, e, tok, raw=True):
        key, sem, val, prod = tok
        if prod == e and (e == "pe" or not SAME_ENGINE_SYNC or not raw):
            return
        if self.seen[e].get(key, 0) >= val:
            return
        self.eng[e].wait_ge(sem, val)
        self.seen[e][key] = val
        self.n_wait += 1

    def _deps(self, e, reads, writes):
        for k in reads:
            t = self.last_w.get(k)
            if t is not None:
                self._wait(e, t, True)
        for k in writes:
            t = self.last_w.get(k)
            if t is not None:
                self._wait(e, t, SAME_WAW)
            for t in self.readers.get(k, ()):
                self._wait(e, t, SAME_WAR)

    def _commit(self, tok, reads, writes):
        for k in writes:
            self.last_w[k] = tok
            self.readers[k] = []
        for k in reads:
            if k in writes:
                continue
            lst = self.readers.setdefault(k, [])
            lst.append(tok)
            if len(lst) > 48:
                best = {}
                for t in lst:
                    if t[0] not in best or best[t[0]][2] < t[2]:
                        best[t[0]] = t
                self.readers[k] = list(best.values())

    def op(self, e, fn, reads=(), writes=()):
        self._deps(e, reads, writes)
        ins = fn(self.eng[e])
        self.cnt[e] += 1
        ins.then_inc(self.sem[e], 1)
        tok = ("c_" + e, self.sem[e], self.cnt[e], e)
        self._commit(tok, reads, writes)
        self.n_ins += 1
        return ins

    def dma(self, q, out, in_, reads=(), writes=(), **kw):
        self._deps(q, reads, writes)
        idx = self.dqi[q]
        slot = self.dq[q][idx]
        self.dqi[q] = (idx + 1) % len(self.dq[q])
        sem, val = slot
        key = "d_%s%d" % (q, idx)
        if val > 0 and self.seen[q].get(key, 0) < val:
            self.eng[q].wait_ge(sem, val)
            self.seen[q][key] = val
        ins = self.eng[q].dma_start(out=out, in_=in_, **kw)
        ins.then_inc(sem, 16)
        slot[1] = val + 16
        tok = (key, sem, val + 16, None)
        self._commit(tok, reads, writes)
        self.n_ins += 1
        return ins

    def collective(self, src, dst, tn, stack, rpc):
        self._deps("pool", [src], [dst])
        groups = [[SP * g + r for r in range(SP)] for g in range(8 // SP)]
        rows = tn[src].shape[0]
        toks = []
        for q in range(rows // rpc):
            sem = stack.enter_context(self.nc.semaphore("cc%d" % len(self.ccs)))
            self.ccs.append(sem)
            ins = self.nc.gpsimd.collective_compute(
                "AllGather", ALU.bypass, replica_groups=groups,
                ins=[tn[src].ap()[rpc * q:rpc * q + rpc, :].opt()],
                outs=[tn[dst].ap()[SP * rpc * q:SP * rpc * q + SP * rpc, :].opt()])
            ins.then_inc(sem)
            tok = ("cc%d" % (len(self.ccs) - 1), sem, 1, None)
            self.eng["pool"].wait_ge(sem, 1)
            self.seen["pool"][tok[0]] = 1
            toks.append(tok)
            self.cc_toks.append(tok)
        self._commit(toks[-1], [src], [dst])

    def barrier(self):
        for e in self.eng:
            for t in self.cc_toks:
                self._wait(e, t)
        for e in self.eng:
            for p in self.eng:
                if p != e and self.cnt[p] > 0:
                    self._wait(e, ("c_" + p, self.sem[p], self.cnt[p], p))
                elif p == e and self.cnt[p] > 0 and e != "pe":
                    self._wait(e, ("c_" + p, self.sem[p], self.cnt[p], None))
            for q in self.dq:
                for i, (sem, val) in enumerate(self.dq[q]):
                    if val > 0:
                        self._wait(e, ("d_%s%d" % (q, i), sem, val, None))
        self.last_w = {}
        self.readers = {}


class _KeyNS:
    def __init__(self, S, prefix):
        self.S = S
        self.p = prefix

    def _k(self, keys):
        return [(self.p, k) for k in keys]

    def op(self, e, fn, reads=(), writes=()):
        return self.S.op(e, fn, self._k(reads), self._k(writes))

    def dma(self, q, out, in_, reads=(), writes=(), **kw):
        return self.S.dma(q, out, in_, self._k(reads), self._k(writes), **kw)


class Prog:
    def __init__(self, nlayers=DEPTH, debug=False, phases=("A", "HY", "NA", "DLA", "Z", "G")):
        self.nlayers = nlayers
        self.debug = debug
        self.phases = phases

    def dram_in(self, name, shape, dt=F32):
        return self.nc.dram_tensor(name, list(shape), dt, kind="ExternalInput").ap()

    def dram_scr(self, name, shape, dt=BF16, dbg=False):
        kind = "ExternalOutput" if (dbg and self.debug) else "Internal"
        if name in getattr(self, "feed", ()):
            kind = "ExternalInput"
        return self.nc.dram_tensor(name, list(shape), dt, kind=kind).ap()

    def build(self):
        nc = bass.Bass("TRN2", target_bir_lowering=False)
        self.nc = nc
        L = DEPTH
        I = self.I = {}
        I["x"] = self.dram_in("x", [T, D])
        I["norm_w"] = self.dram_in("norm_w", [L, 1, D])
        I["wfm"] = self.dram_in("wfm", [L, NFM, 128, 1024])
        I["wsm"] = self.dram_in("wsm", [L, 128, 8 * NSP])
        I["xh"] = self.dram_in("xh", [T, OW])
        I["wtm"] = self.dram_in("wtm", [L, NTM, 128, 4096])
        I["wout"] = self.dram_in("wout", [L, 128, 16 * OW])
        I["convp"] = self.dram_in("convp", [L, NFM, 128, 4])
        I["ident"] = self.dram_in("ident", [128, 128])
        I["blockones"] = self.dram_in("blockones", [128, 128])
        self.declare_mixer_inputs()
        self.out = nc.dram_tensor("out", [T, OW], F32, kind="ExternalOutput").ap()
        Sc = self.Sc = {}
        for n, rows in [("hyuT", 3 * CW), ("hygT", CW), ("mbBT", 128 * MBG), ("mbCT", 128 * MBG), ("mlqT", CW),
                        ("mlkT", CW), ("naqT", CW), ("nakT", CW)]:
            Sc[n] = self.dram_scr(n, [rows, T], dbg=True)
        for n, cols in [("mbx", CW), ("mbB", 128 * MBG), ("mlk", CW), ("mbz", CW), ("mlv", CW), ("mlo", CW),
                        ("mlz", CW), ("nav", CW), ("nag", CW)]:
            Sc[n] = self.dram_scr(n, [T, cols], dbg=True)
        Sc["smallT"] = self.dram_scr("smallT", [NSP, T], F32, dbg=True)
        self.Tn = {}
        for n, shp, dt in [("ytm", [T, YW], BF16), ("yhyT", [HC, T], BF16), ("xres", [T, OW], F32),
                           ("ytm_g", [SP * T, YW], BF16), ("yhy_g", [SP * HC, T], BF16), ("xg", [SP * T, OW], F32)]:
            if self.debug and n in ("ytm", "yhyT"):
                self.Tn[n] = nc.dram_tensor(n, shp, dt, kind="ExternalOutput")
            elif n in getattr(self, "feed", ()):
                self.Tn[n] = nc.dram_tensor(n, shp, dt, kind="ExternalInput")
            else:
                self.Tn[n] = nc.dram_tensor(n, shp, dt)
            Sc[n] = self.Tn[n].ap()
        self.declare_mixer_scratch()

        with ExitStack() as st:
            self.S = Sched(nc, st)
            S = self.S
            self.ident_f = st.enter_context(_alloc(nc, "sbuf", "ident_f", [128, 128], F32))
            self.ident_b = st.enter_context(_alloc(nc, "sbuf", "ident_b", [128, 128], BF16))
            self.bones_b = st.enter_context(_alloc(nc, "sbuf", "bones_b", [128, 128], BF16))
            tmpc = st.enter_context(_alloc(nc, "sbuf", "tmpc", [128, 128], F32))
            S.dma("sp", self.ident_f[:], I["ident"], writes=["ident_f"])
            S.dma("sp", tmpc[:], I["blockones"], writes=["tmpc"])
            S.op("dve", lambda e: e.tensor_copy(out=self.ident_b[:], in_=self.ident_f[:]), reads=["ident_f"], writes=["ident_b"])
            S.op("dve", lambda e: e.tensor_copy(out=self.bones_b[:], in_=tmpc[:]), reads=["tmpc"], writes=["bones_b"])
            S.barrier()
            for l in range(self.nlayers):
                x_dst = self.out if l == self.nlayers - 1 else Sc["xres"]
                x_res = I["xh"] if l == 0 else Sc["xres"]
                self.marks = getattr(self, "marks", [])
                mark = lambda nm: self.marks.append((nm, l, dict(S.cnt)))
                mark("start")
                if "A" in self.phases:
                    self.phaseA(l)
                    S.barrier()
                    mark("A")
                if "HY" in self.phases:
                    self.phaseHY(l)
                    S.barrier()
                    mark("HY")
                if "NA" in self.phases:
                    self.phaseNA(l)
                    S.barrier()
                    mark("NA")
                if "DLA" in self.phases:
                    self.phaseDLA(l)
                    S.barrier()
                    mark("DLA")
                if "Z" in self.phases:
                    if SP > 1 and "G" in self.phases:
                        S.collective("ytm", "ytm_g", self.Tn, st, 1024)
                        S.collective("yhyT", "yhy_g", self.Tn, st, 64)
                        S.barrier()
                    self.phaseZ(l, x_res, x_dst)
                    S.barrier()
                    if SP > 1 and l < self.nlayers - 1 and "G" in self.phases:
                        S.collective("xres", "xg", self.Tn, st, 1024)
                        S.barrier()
                    mark("Z")
            S.barrier()
        return nc

    def declare_mixer_inputs(self):
        _na_declare_inputs(self)

    def declare_mixer_scratch(self):
        _dla_declare(self)
        _hy_declare(self)

    def phaseA(self, l):
        nc, S, I, Sc = self.nc, self.S, self.I, self.Sc
        with ExitStack() as st:
            sb = lambda n, s, d=F32: st.enter_context(_alloc(nc, "sbuf", n, list(s), d))
            ps = lambda n, s, d=F32: st.enter_context(_alloc(nc, "psum", n, list(s), d))
            hT = sb("hT", [128, 8, T], BF16)
            with ExitStack() as st0:
                sb0 = lambda n, s, d=F32: st0.enter_context(_alloc(nc, "sbuf", n, list(s), d))
                nwb = sb0("nwb", [128, D])
                S.dma("sp", nwb[:], I["norm_w"][l].broadcast_to([128, D]), writes=["nwb"])
                xt = [sb0("xt%d" % i, [128, D]) for i in range(2)]
                junk = sb0("junk", [128, D], BF16)
                ss = [sb0("ss%d" % i, [128, 1]) for i in range(2)]
                xn = [sb0("xn%d" % i, [128, D], BF16) for i in range(2)]
                pt = [st0.enter_context(_alloc(nc, "psum", "pt%d" % i, [128, 512], BF16)) for i in range(2)]
                for i in range(NT):
                    b = i % 2
                    if l == 0 or SP == 1:
                        S.dma("sp", xt[b][:], I["x"][128 * i:128 * i + 128, :], writes=[("xt", b)])
                    else:
                        S.dma("sp", xt[b][:].rearrange("p (r c) -> p r c", r=SP),
                              Sc["xg"].rearrange("(q r t) c -> q t r c", q=8, r=SP)[i // 8][128 * (i % 8):128 * (i % 8) + 128, :, :], writes=[("xt", b)])
                    S.op("act", lambda e: e.activation(out=junk[:], in_=xt[b][:], func=AF.Square, accum_out=ss[b][:]),
                         reads=[("xt", b)], writes=["junk", ("ss", b)])
                    S.op("dve", lambda e: e.tensor_scalar(out=ss[b][:], in0=ss[b][:], scalar1=1.0 / D, scalar2=EPS,
                                                          op0=ALU.mult, op1=ALU.add), reads=[("ss", b)], writes=[("ss", b)])
                    S.op("act", lambda e: e.activation(out=ss[b][:], in_=ss[b][:], func=AF.Sqrt), reads=[("ss", b)], writes=[("ss", b)])
                    S.op("dve", lambda e: e.reciprocal(out=ss[b][:], in_=ss[b][:]), reads=[("ss", b)], writes=[("ss", b)])
                    S.op("dve", lambda e: e.scalar_tensor_tensor(out=xn[b][:], in0=xt[b][:], scalar=ss[b][:], in1=nwb[:],
                                                                 op0=ALU.mult, op1=ALU.mult),
                         reads=[("xt", b), ("ss", b), "nwb"], writes=[("xn", b)])
                    for h in range(2):
                        for k in range(4):
                            kc = 4 * h + k
                            S.op("pe", lambda e: e.transpose(pt[h][:, 128 * k:128 * k + 128], xn[b][:, 128 * kc:128 * kc + 128], self.ident_b[:]),
                                 reads=[("xn", b)], writes=[("pt", h)])
                        dst = hT[:, 4 * h:4 * h + 4, 128 * i:128 * i + 128]
                        src = pt[h][:].rearrange("p (k t) -> p k t", t=128)
                        if h == 0:
                            S.op("act", lambda e: e.activation(out=dst, in_=src, func=AF.Copy), reads=[("pt", h)], writes=[("hT", i)])
                        else:
                            S.op("dve", lambda e: e.tensor_copy(out=dst, in_=src), reads=[("pt", h)], writes=[("hT", i)])
            S.barrier()
            hkeys = [("hT", i) for i in range(NT)]
            with ExitStack() as st1:
                sb1 = lambda n, s, d=F32: st1.enter_context(_alloc(nc, "sbuf", n, list(s), d))
                wst = [sb1("wst%d" % i, [128, 1024]) for i in range(2)]
                wb = [sb1("wb%d" % i, [128, 8, 128], BF16) for i in range(2)]
                row = [sb1("row%d" % i, [128, T + 2], BF16) for i in range(2)]
                acc = [sb1("acc%d" % i, [128, 1024]) for i in range(2)]
                ob = [sb1("ob%d" % i, [128, 1024], BF16) for i in range(2)]
                tms = [sb1("tms%d" % i, [128, 8, 128], BF16) for i in range(2)]
                sq = [sb1("sq%d" % i, [128, 512], BF16) for i in range(2)]
                rt = [sb1("rt%d" % i, [128, 512]) for i in range(2)]
                cp = [sb1("cp%d" % i, [128, 4]) for i in range(2)]
                pm = [st1.enter_context(_alloc(nc, "psum", "pm%d" % i, [128, 512], F32)) for i in range(4)]
                ptr = [st1.enter_context(_alloc(nc, "psum", "ptr%d" % i, [128, 512], BF16)) for i in range(2)]
                pss = [st1.enter_context(_alloc(nc, "psum", "pss%d" % i, [128, 512], F32)) for i in range(2)]
                for b in range(2):
                    S.op("pool", lambda e: e.memset(row[b][:, 0:1], 0.0), writes=[("rowh", b)])
                    S.op("pool", lambda e: e.memset(row[b][:, T + 1:T + 2], 0.0), writes=[("rowh", b)])
                cnt = {"ev": 0, "tr": 0, "ob": 0, "ac": 0, "q": 0}

                def mm_block(wtile, wkey, M, dst_fn, dst_keys_fn):
                    for i in range(16):
                        p = pm[i % 4]
                        for kc in range(8):
                            S.op("pe", lambda e: e.matmul(p[0:M, :], lhsT=wtile[:, kc, 0:M], rhs=hT[:, kc, 512 * i:512 * i + 512],
                                                          start=(kc == 0), stop=(kc == 7)),
                                 reads=[wkey] + hkeys[4 * i:4 * i + 4], writes=[("pm", i % 4)])
                        dst = dst_fn(i)
                        if cnt["ev"] % 2 == 0:
                            S.op("act", lambda e: e.activation(out=dst, in_=p[0:M, :], func=AF.Copy), reads=[("pm", i % 4)], writes=dst_keys_fn(i))
                        else:
                            S.op("dve", lambda e: e.tensor_copy(out=dst, in_=p[0:M, :]), reads=[("pm", i % 4)], writes=dst_keys_fn(i))
                        cnt["ev"] += 1

                for bi, (gname, gi, col0, ptype) in enumerate(FM_BLOCKS):
                    if gname in _SKIP or "fm" in _SKIP:
                        continue
                    b = bi % 2
                    S.dma("sp", wst[b][:], I["wfm"][l, bi], writes=[("wst", b)])
                    S.op("pool", lambda e: e.tensor_copy(out=wb[b][:].rearrange("p k c -> p (k c)"), in_=wst[b][:]),
                         reads=[("wst", b)], writes=[("wtile", b)])
                    if ptype in ("conv", "convsilu"):
                        S.dma("sp", cp[b][:], I["convp"][l, bi], writes=[("cp", b)])
                    elif ptype == "qknorm":
                        S.dma("sp", cp[b][:], I["convp"][l, bi], writes=[("cp", b)])
                    rw = row[b]
                    mm_block(wb[b], ("wtile", b), 128, lambda i: rw[:, 1 + 512 * i:1 + 512 * i + 512], lambda i: [("row", b, i)])
                    rkeys = [("row", b, i) for i in range(16)] + [("rowh", b)]
                    r0 = 128 * gi
                    if ptype in ("conv", "convsilu", "silu"):
                        for j in range(8):
                            a = acc[cnt["ac"] % 2]; ak = ("acc", cnt["ac"] % 2); cnt["ac"] += 1
                            o = ob[cnt["ob"] % 2]; okey = ("ob", cnt["ob"] % 2); cnt["ob"] += 1
                            rk = [("row", b, i) for i in range(max(0, 2 * j - 1), min(16, 2 * j + 3))] + [("rowh", b)]
                            c0 = 1024 * j
                            if ptype == "silu":
                                S.op("act", lambda e: e.activation(out=o[:], in_=rw[:, 1 + c0:1 + c0 + 1024], func=AF.Silu), reads=rk, writes=[okey])
                            else:
                                S.op("act", lambda e: e.activation(out=a[:], in_=rw[:, 1 + c0:1 + c0 + 1024], func=AF.Identity,
                                                                   scale=cp[b][:, 1:2], bias=cp[b][:, 3:4]), reads=rk + [("cp", b)], writes=[ak])
                                S.op("dve", lambda e: e.scalar_tensor_tensor(out=a[:], in0=rw[:, c0:c0 + 1024], scalar=cp[b][:, 0:1], in1=a[:],
                                                                             op0=ALU.mult, op1=ALU.add), reads=rk + [("cp", b), ak], writes=[ak])
                                S.op("dve", lambda e: e.scalar_tensor_tensor(out=a[:], in0=rw[:, 2 + c0:2 + c0 + 1024], scalar=cp[b][:, 2:3], in1=a[:],
                                                                             op0=ALU.mult, op1=ALU.add), reads=rk + [("cp", b), ak], writes=[ak])
                                if ptype == "convsilu":
                                    S.op("act", lambda e: e.activation(out=o[:], in_=a[:], func=AF.Silu), reads=[ak], writes=[okey])
                                else:
                                    S.op("pool", lambda e: e.tensor_copy(out=o[:], in_=a[:]), reads=[ak], writes=[okey])
                            fm_dst = {"hyu": "hyuT", "hyg": "hygT", "mbB": "mbBT", "mbC": "mbCT", "mlq": "mlqT", "mlk": "mlkT"}.get(gname)
                            if fm_dst is not None:
                                S.dma("pool", Sc[fm_dst][r0:r0 + 128, c0:c0 + 1024], o[:], reads=[okey], writes=[(fm_dst, gi, j)])
                            tm_dst = {"mbx": "mbx", "mbB": "mbB", "mlk": "mlk"}.get(gname)
                            if tm_dst is not None:
                                tmb = cnt["tr"] % 2; cnt["tr"] += 1
                                for h in range(2):
                                    for k in range(4):
                                        s = 4 * h + k
                                        S.op("pe", lambda e: e.transpose(ptr[h][:, 128 * k:128 * k + 128], o[:, 128 * s:128 * s + 128], self.ident_b[:]),
                                             reads=[okey], writes=[("ptr", h)])
                                    src = ptr[h][:].rearrange("p (k c) -> p k c", c=128)
                                    if h == 0:
                                        S.op("act", lambda e: e.activation(out=tms[tmb][:, 0:4, :], in_=src, func=AF.Copy), reads=[("ptr", h)], writes=[("tms", tmb, 0)])
                                    else:
                                        S.op("dve", lambda e: e.tensor_copy(out=tms[tmb][:, 4:8, :], in_=src), reads=[("ptr", h)], writes=[("tms", tmb, 1)])
                                d = Sc[tm_dst][c0:c0 + 1024, r0:r0 + 128].rearrange("(s p) c -> p s c", p=128)
                                S.dma("pool", d, tms[tmb][:], reads=[("tms", tmb, 0), ("tms", tmb, 1)], writes=[(tm_dst, gi, j)])
                    elif ptype == "qknorm":
                        fm_dst = {"naq": "naqT", "nak": "nakT"}[gname]
                        for j in range(8):
                            o = ob[cnt["ob"] % 2]; okey = ("ob", cnt["ob"] % 2); cnt["ob"] += 1
                            for hh in range(2):
                                q = cnt["q"] % 2; cnt["q"] += 1
                                c0 = 1024 * j + 512 * hh
                                rk = [("row", b, 2 * j + hh)]
                                S.op("act", lambda e: e.activation(out=sq[q][:], in_=rw[:, 1 + c0:1 + c0 + 512], func=AF.Square), reads=rk, writes=[("sq", q)])
                                S.op("pe", lambda e: e.matmul(pss[q][:], lhsT=self.bones_b[:], rhs=sq[q][:], start=True, stop=True),
                                     reads=[("sq", q)], writes=[("pss", q)])
                                S.op("dve", lambda e: e.tensor_scalar(out=rt[q][:], in0=pss[q][:], scalar1=1.0 / 64, scalar2=EPS, op0=ALU.mult, op1=ALU.add),
                                     reads=[("pss", q)], writes=[("rt", q)])
                                S.op("act", lambda e: e.activation(out=rt[q][:], in_=rt[q][:], func=AF.Sqrt), reads=[("rt", q)], writes=[("rt", q)])
                                S.op("dve", lambda e: e.reciprocal(out=rt[q][:], in_=rt[q][:]), reads=[("rt", q)], writes=[("rt", q)])
                                S.op("dve", lambda e: e.scalar_tensor_tensor(out=o[:, 512 * hh:512 * hh + 512], in0=rw[:, 1 + c0:1 + c0 + 512], scalar=cp[b][:, 0:1],
                                                                             in1=rt[q][:], op0=ALU.mult, op1=ALU.mult),
                                     reads=rk + [("rt", q), ("cp", b)], writes=[okey])
                            S.dma("pool", Sc[fm_dst][r0:r0 + 128, 1024 * j:1024 * j + 1024], o[:], reads=[okey], writes=[(fm_dst, gi, j)])
            S.barrier()
            with ExitStack() as st3:
                sb3 = lambda n, s, d=F32: st3.enter_context(_alloc(nc, "sbuf", n, list(s), d))
                smallsb = sb3("smallsb", [NSP, T])
                wsst = sb3("wsst", [128, 8 * NSP])
                wsb = sb3("wsb", [128, 8, NSP], BF16)
                pm3 = [st3.enter_context(_alloc(nc, "psum", "pm3%d" % i, [128, 512], F32)) for i in range(4)]
                S.dma("sp", wsst[:], I["wsm"][l], writes=["wsst"])
                S.op("pool", lambda e: e.tensor_copy(out=wsb[:].rearrange("p k c -> p (k c)"), in_=wsst[:]), reads=["wsst"], writes=["wsb"])
                for i in range(16 if "small" not in _SKIP else 0):
                    p = pm3[i % 4]
                    for kc in range(8):
                        S.op("pe", lambda e: e.matmul(p[0:NSP, :], lhsT=wsb[:, kc, :], rhs=hT[:, kc, 512 * i:512 * i + 512], start=(kc == 0), stop=(kc == 7)),
                             reads=["wsb"] + hkeys[4 * i:4 * i + 4], writes=[("pm3", i % 4)])
                    S.op("act", lambda e: e.activation(out=smallsb[:, 512 * i:512 * i + 512], in_=p[0:NSP, :], func=AF.Copy), reads=[("pm3", i % 4)], writes=[("smallsb", i)])
                S.dma("pool", Sc["smallT"], smallsb[:], reads=[("smallsb", i) for i in range(16)], writes=["smallT"])
            S.barrier()
            with ExitStack() as st2:
                sb2 = lambda n, s, d=F32: st2.enter_context(_alloc(nc, "sbuf", n, list(s), d))
                wst2 = sb2("wst2", [128, 4096])
                wtb = [sb2("wtb%d" % i, [128, 8, 512], BF16) for i in range(2)]
                ot = [sb2("ot%d" % i, [128, 512], BF16) for i in range(4)]
                pm2 = [st2.enter_context(_alloc(nc, "psum", "pm2%d" % i, [128, 512], F32)) for i in range(4)]
                ppg = 512 // CW
                for g in range(NTM if "tm" not in _SKIP else 0):
                    b = g % 2
                    parts = TM_PARTS[ppg * g:ppg * g + ppg]
                    S.dma("sp", wst2[:], I["wtm"][l, g], writes=["wst2"])
                    S.op("pool", lambda e: e.tensor_copy(out=wtb[b][:].rearrange("p k c -> p (k c)"), in_=wst2[:]), reads=["wst2"], writes=[("wtb", b)])
                    for i in range(NT):
                        p = pm2[i % 4]
                        for kc in range(8):
                            S.op("pe", lambda e: e.matmul(p[:], lhsT=hT[:, kc, 128 * i:128 * i + 128], rhs=wtb[b][:, kc, :], start=(kc == 0), stop=(kc == 7)),
                                 reads=[("wtb", b), ("hT", i)], writes=[("pm2", i % 4)])
                        o = ot[i % 4]
                        for pi, (gname, col0, act) in enumerate(parts):
                            func = {"silu": AF.Silu, "copy": AF.Copy, "sigmoid": AF.Sigmoid}[act]
                            sl = slice(CW * pi, CW * pi + CW)
                            S.op("act", lambda e: e.activation(out=o[:, sl], in_=p[:, sl], func=func), reads=[("pm2", i % 4)], writes=[("ot", i % 4, pi)])
                            S.dma("pool" if (i + pi) % 2 else "sp", Sc[gname][128 * i:128 * i + 128, :], o[:, sl], reads=[("ot", i % 4, pi)], writes=[(gname, i)])

    def phaseZ(self, l, x_res, x_dst):
        nc, S, I, Sc = self.nc, self.S, self.I, self.Sc
        gathered = SP > 1 and "G" in self.phases
        with ExitStack() as st:
            sb = lambda n, s, d=F32: st.enter_context(_alloc(nc, "sbuf", n, list(s), d))
            wo = sb("wo", [128, 16, OW], BF16)
            wos = [sb("wos%d" % i, [128, 2048]) for i in range(2)]
            nck = 16 * OW // 2048
            kpc = 2048 // OW
            for c in range(nck):
                S.dma("sp", wos[c % 2][:], I["wout"][l, :, 2048 * c:2048 * c + 2048], writes=[("wos", c % 2)])
                S.op("pool", lambda e: e.tensor_copy(out=wo[:, kpc * c:kpc * c + kpc, :].rearrange("p k c -> p (k c)"), in_=wos[c % 2][:]),
                     reads=[("wos", c % 2)], writes=["wo"])
            yt = [sb("yt%d" % i, [128, SP, YW], BF16) for i in range(2)]
            yT = [sb("yT%d" % i, [128, 16, 128], BF16) for i in range(2)]
            xt = [sb("xz%d" % i, [128, OW]) for i in range(2)]
            oz = [sb("oz%d" % i, [128, OW]) for i in range(2)]
            ptz = [st.enter_context(_alloc(nc, "psum", "ptz%d" % i, [128, 512], BF16)) for i in range(3)]
            pz = [st.enter_context(_alloc(nc, "psum", "pz%d" % i, [128, 512], F32)) for i in range(4)]
            ysrc = Sc["ytm_g"] if gathered else Sc["ytm"]
            hsrc = Sc["yhy_g"] if gathered else Sc["yhyT"]
            nr = SP if gathered else 1
            cpr = YW // 128
            for i in range(NT):
                b = i % 2
                S.dma("sp", yt[b][:, 0:nr, :], ysrc.rearrange("(q r t) c -> q t r c", q=8, r=nr)[i // 8][128 * (i % 8):128 * (i % 8) + 128, :, :], writes=[("yt", b)])
                S.dma("sp", xt[b][:], x_res[128 * i:128 * i + 128, :], writes=[("xz", b)])
                S.dma("pool", yT[b][:, 0:4 * nr // SP, :], hsrc[:, 128 * i:128 * i + 128].rearrange("(k p) t -> p k t", p=128), writes=[("yTh", b)])
                for h in range(3):
                    for k in range(4):
                        q = 4 * h + k
                        r, cc = q // cpr, q % cpr
                        S.op("pe", lambda e: e.transpose(ptz[h][:, 128 * k:128 * k + 128], yt[b][:, r, 128 * cc:128 * cc + 128], self.ident_b[:]),
                             reads=[("yt", b)], writes=[("ptz", h)])
                    src = ptz[h][:].rearrange("p (k c) -> p k c", c=128)
                    dst = yT[b][:, 4 + 4 * h:8 + 4 * h, :]
                    if h == 1:
                        S.op("dve", lambda e: e.tensor_copy(out=dst, in_=src), reads=[("ptz", h)], writes=[("yTt", b, h)])
                    else:
                        S.op("act", lambda e: e.activation(out=dst, in_=src, func=AF.Copy), reads=[("ptz", h)], writes=[("yTt", b, h)])
                for half in range(OW // 512):
                    p = pz[(2 * i + half) % 4]
                    pk = ("pz", (2 * i + half) % 4)
                    for kc in range(16):
                        S.op("pe", lambda e: e.matmul(p[:], lhsT=yT[b][:, kc, :], rhs=wo[:, kc, 512 * half:512 * half + 512], start=(kc == 0), stop=(kc == 15)),
                             reads=["wo", ("yTh", b)] + [("yTt", b, h) for h in range(3)], writes=[pk])
                    S.op("dve", lambda e: e.tensor_tensor(out=oz[b][:, 512 * half:512 * half + 512], in0=p[:], in1=xt[b][:, 512 * half:512 * half + 512], op=ALU.add),
                         reads=[pk, ("xz", b)], writes=[("oz", b, half)])
                S.dma("pool", x_dst[128 * i:128 * i + 128, :], oz[b][:], reads=[("oz", b, h2) for h2 in range(OW // 512)], writes=[("xdst", i)])


def _fm_col0(gname, gi, j):
    if gname == "hyu":
        part, sub = gi // NB, gi % NB
        return O_HYU + 512 * part + CW * j + 128 * sub
    base = {"hyg": O_HYG, "mbx": O_MBX, "mlq": O_MLQ, "mlk": O_MLK, "naq": O_NAQ, "nak": O_NAK}.get(gname)
    if base is not None:
        return base + CW * j + 128 * gi
    return {"mbB": O_MBB, "mbC": O_MBC}[gname] + 128 * (MBG * j + gi)


def _small_cols(j):
    cols = []
    for d in range(2):
        cols += [O_MBDT + 8 * d + MBU * j + u for u in range(MBU)]
    for d in range(2):
        for g in range(2):
            cols += [O_MLG + 8 * d + 4 * g + MLU * j + u for u in range(MLU)]
    return cols


def _prep_weights(inp, j):
    L = DEPTH
    w_in = np.asarray(inp["w_in"], np.float32)
    w_out = np.asarray(inp["w_out"], np.float32)
    W = {}
    wfm = np.empty((L, NFM, 128, 1024), np.float32)
    convp = np.zeros((L, NFM, 128, 4), np.float32)
    for bi, (gname, gi, _c, ptype) in enumerate(FM_BLOCKS):
        col0 = _fm_col0(gname, gi, j)
        blk = w_in[:, :, col0:col0 + 128]
        wfm[:, bi] = blk.reshape(L, 8, 128, 128).transpose(0, 2, 1, 3).reshape(L, 128, 1024)
        if ptype in ("conv", "convsilu"):
            if gname == "hyu":
                cw, cb, c0 = inp["hy_conv_w"], inp["hy_conv_b"], col0 - O_HYU
            elif gname in ("mbx", "mbB", "mbC"):
                cw, cb, c0 = inp["mb_conv_w"], inp["mb_conv_b"], col0 - O_MBX
            else:
                cw, cb, c0 = inp["ml_conv_w"], inp["ml_conv_b"], col0 - O_MLQ
            convp[:, bi, :, 0:3] = np.asarray(cw)[:, :, c0:c0 + 128].transpose(0, 2, 1)
            convp[:, bi, :, 3] = np.asarray(cb)[:, c0:c0 + 128]
        elif ptype == "qknorm":
            nw = np.asarray(inp["na_qnorm_w"] if gname == "naq" else inp["na_knorm_w"])
            convp[:, bi, :, 0] = np.tile(nw, (1, 2))
    W["wfm"] = wfm
    W["convp"] = convp
    scols = _small_cols(j)
    wsm = np.zeros((L, 8, 128, NSP), np.float32)
    wsm[:, :, :, :NSM] = w_in[:, :, scols].reshape(L, 8, 128, NSM)
    W["wsm"] = np.ascontiguousarray(wsm.transpose(0, 2, 1, 3).reshape(L, 128, 8 * NSP))
    wtm = np.empty((L, NTM, 128, 4096), np.float32)
    ppg = 512 // CW
    for g in range(NTM):
        cols = []
        for (gname, col0, act) in TM_PARTS[ppg * g:ppg * g + ppg]:
            cols += list(range(col0 + CW * j, col0 + CW * j + CW))
        wtm[:, g] = w_in[:, :, cols].reshape(L, 8, 128, 512).transpose(0, 2, 1, 3).reshape(L, 128, 4096)
    W["wtm"] = wtm
    rows = []
    for q in range(HC // 64):
        for r in range(SP):
            rows += list(range(HC * r + 64 * q, HC * r + 64 * q + 64))
    for r in range(SP):
        for base in (512, 1024, 1536):
            rows += list(range(base + CW * r, base + CW * r + CW))
    wo = w_out[:, rows, OW * j:OW * j + OW]
    W["wout"] = np.ascontiguousarray(wo.reshape(L, 16, 128, OW).transpose(0, 2, 1, 3).reshape(L, 128, 16 * OW))
    W["norm_w"] = np.asarray(inp["norm_w"], np.float32).reshape(L, 1, D)
    W["ident"] = np.eye(128, dtype=np.float32)
    bo = np.zeros((128, 128), np.float32)
    bo[:64, :64] = 1.0
    bo[64:, 64:] = 1.0
    W["blockones"] = bo
    return W


def _prep_na(inp, j):
    jc = j
    L = DEPTH
    rpb = np.asarray(inp["na_rpb"], np.float32)
    kk = np.arange(128)
    il, kc = kk // 64, kk % 64
    w = np.arange(64)
    cs = np.clip(w - 8, 0, 48)
    valid = (kc[:, None] >= cs[None, :]) & (kc[:, None] < cs[None, :] + 16)
    coff = np.clip(kc[:, None] - w[None, :] + 15, 0, 30)
    bias = np.zeros((L, 8, 128, 8, 4, 64), np.float32)
    for v in range(8):
        for j in range(4):
            i = 2 * j + il
            roff = v + i
            bias[:, :, :, v, j, :] = rpb[:, :, roff[:, None], coff]
    mask = np.broadcast_to(valid[:, None, :], (128, 4, 64)).astype(np.float32).reshape(128, 256)
    return {"na_bias": np.ascontiguousarray(bias.reshape(L, 8, 128, 2048)[:, NAH * jc:NAH * jc + NAH]), "na_mask": np.ascontiguousarray(mask)}


def _na_declare_inputs(self):
    self.I["na_bias"] = self.dram_in("na_bias", [DEPTH, NAH, 128, 2048])
    self.I["na_mask"] = self.dram_in("na_mask", [128, 256])


def _phaseNA(self, l):
    nc, S, I, Sc = self.nc, self.S, self.I, self.Sc
    NH = NAH
    with ExitStack() as st:
        sb = lambda n, s, d=F32: st.enter_context(_alloc(nc, "sbuf", n, list(s), d))
        KT = [sb("naKT%d" % i, [64, T], BF16) for i in range(2)]
        QT = [sb("naQT%d" % i, [64, T], BF16) for i in range(2)]
        Ve = [sb("naVe%d" % i, [128, 64, 65], BF16) for i in range(2)]
        Vo = [sb("naVo%d" % i, [128, 63, 65], BF16) for i in range(2)]
        EBr = sb("naEBr", [128, 2048])
        EBM = [sb("naEBM%d" % i, [128, 8, 256]) for i in range(2)]
        msk = sb("namask", [128, 256])
        G = [sb("naG%d" % i, [64, 128, 64], BF16) for i in range(2)]
        O = sb("naO", [64, 128, 64])
        Ob = sb("naOb", [64, 128, 64], BF16)
        E = [sb("naE%d" % i, [128, 256]) for i in range(2)]
        Pb = [sb("naP%d" % i, [128, 256], BF16) for i in range(2)]
        rec = [sb("narec%d" % i, [64, 1]) for i in range(2)]
        pS = [st.enter_context(_alloc(nc, "psum", "napS%d" % i, [128, 512], F32)) for i in range(2)]
        pO = [st.enter_context(_alloc(nc, "psum", "napO%d" % i, [128, 512], F32)) for i in range(2)]
        S.dma("sp", msk[:], I["na_mask"], writes=["namask"])
        for b in range(2):
            S.op("pool", lambda e: e.memset(Ve[b][:, :, 64:65], 1.0), writes=[("Veo", b)])
            S.op("pool", lambda e: e.memset(Vo[b][:, :, 64:65], 1.0), writes=[("Voo", b)])
        for h in range(NH):
            b = h % 2
            S.dma("sp", KT[b][:], Sc["nakT"][64 * h:64 * h + 64, :], writes=[("KT", b)])
            S.dma("sp", QT[b][:], Sc["naqT"][64 * h:64 * h + 64, :], writes=[("QT", b)])
            S.dma("pool", Ve[b][:, :, 0:64], Sc["nav"][:, 64 * h:64 * h + 64].rearrange("(i p) d -> p i d", p=128), writes=[("Ve", b)])
            S.dma("pool", Vo[b][:, :, 0:64], Sc["nav"][64:T - 64, 64 * h:64 * h + 64].rearrange("(i p) d -> p i d", p=128), writes=[("Vo", b)])
            S.dma("sp", G[b][:], Sc["nag"][:, 64 * h:64 * h + 64].rearrange("(r w) d -> w r d", w=64), writes=[("G", b)])
            S.dma("sp", EBr[:], I["na_bias"][l, h], writes=["EBr"])
            S.op("act", lambda e: e.activation(out=EBr[:], in_=EBr[:], func=AF.Exp), reads=["EBr"], writes=["EBr"])
            S.op("dve", lambda e: e.tensor_tensor(out=EBM[b][:], in0=EBr[:].rearrange("p (v c) -> p v c", c=256),
                                                  in1=msk[:].unsqueeze(1).broadcast_to([128, 8, 256]), op=ALU.mult),
                 reads=["EBr", "namask"], writes=[("EBM", b)])
            for r in range(128):
                rs = min(max(r - 4, 0), 120)
                v = rs - r + 7
                rb = r % 2
                for j in range(4):
                    S.op("pe", lambda e: e.matmul(pS[rb][:, 64 * j:64 * j + 64], lhsT=KT[b][:, 64 * rs + 128 * j:64 * rs + 128 * j + 128],
                                                  rhs=QT[b][:, 64 * r:64 * r + 64], start=True, stop=True),
                         reads=[("KT", b), ("QT", b)], writes=[("pS", rb)])
                S.op("act", lambda e: e.activation(out=E[rb][:], in_=pS[rb][:, 0:256], func=AF.Exp, scale=0.125), reads=[("pS", rb)], writes=[("E", rb)])
                S.op("dve", lambda e: e.tensor_tensor(out=Pb[rb][:], in0=E[rb][:], in1=EBM[b][:, v, :], op=ALU.mult),
                     reads=[("E", rb), ("EBM", b)], writes=[("P", rb)])
                for j in range(4):
                    if rs % 2 == 0:
                        vt = Ve[b][:, rs // 2 + j, :]
                    else:
                        vt = Vo[b][:, (rs - 1) // 2 + j, :]
                    S.op("pe", lambda e: e.matmul(pO[rb][0:64, 0:65], lhsT=Pb[rb][:, 64 * j:64 * j + 64], rhs=vt, start=(j == 0), stop=(j == 3)),
                         reads=[("P", rb), ("Ve", b), ("Vo", b), ("Veo", b), ("Voo", b)], writes=[("pO", rb)])
                S.op("dve", lambda e: e.reciprocal(out=rec[rb][:], in_=pO[rb][0:64, 64:65]), reads=[("pO", rb)], writes=[("rec", rb)])
                S.op("act", lambda e: e.activation(out=O[:, r, :], in_=pO[rb][0:64, 0:64], func=AF.Copy, scale=rec[rb][:]),
                     reads=[("pO", rb), ("rec", rb)], writes=["O"])
            S.op("dve", lambda e: e.tensor_tensor(out=Ob[:].rearrange("p r d -> p (r d)"), in0=O[:].rearrange("p r d -> p (r d)"),
                                                  in1=G[b][:].rearrange("p r d -> p (r d)"), op=ALU.mult), reads=["O", ("G", b)], writes=["Ob"])
            S.dma("pool", Sc["ytm"][:, 2 * CW + 64 * h:2 * CW + 64 * h + 64].rearrange("(r w) d -> w r d", w=64), Ob[:], reads=["Ob"], writes=[("ytm_na", h)])


Prog.phaseNA = _phaseNA


def _prep_mixer(inp, j):
    W = {}
    W.update(_prep_na(inp, j))
    W.update(_prep_dla(inp, j))
    W.update(_prep_hy(inp, j))
    return W


def _prep_dla(inp, j):
    L = DEPTH
    gpar = np.zeros((L, 4, 64, 2), np.float32)
    dtb = np.asarray(inp["mb_dt_bias"], np.float32)[:, :, MBU * j:MBU * j + MBU]
    alog = np.asarray(inp["mb_a_log"], np.float32)[:, :, MBU * j:MBU * j + MBU]
    gb = np.asarray(inp["ml_gate_b"], np.float32)[:, :, :, MLU * j:MLU * j + MLU]
    for d in range(2):
        gpar[:, d, :8 * MBU, 0] = np.repeat(dtb[:, d, :], 8, axis=1)
        gpar[:, d, :8 * MBU, 1] = np.repeat(alog[:, d, :], 8, axis=1)
        gpar[:, 2 + d, :8 * MLU, 0] = np.repeat(gb[:, d, 0, :], 8, axis=1)
        gpar[:, 2 + d, :8 * MLU, 1] = np.repeat(gb[:, d, 1, :], 8, axis=1)
    rmask = np.ones((64, 1024), np.float32)
    rmask[:, ::128] = 0.0
    s = np.arange(128)[:, None]
    ll = np.arange(128)[None, :]
    negmask = np.stack([np.where(s <= ll, 0.0, -30000.0), np.where(s >= ll, 0.0, -30000.0)]).astype(np.float32)
    return {"gpar": gpar, "rmask": rmask, "negmask": negmask,
            "dsk": np.ascontiguousarray(np.asarray(inp["mb_d"], np.float32)[:, MBU * j:MBU * j + MBU]).reshape(L, 1, MBU),
            "mbnw": np.ascontiguousarray(np.asarray(inp["mb_norm_w"], np.float32)[:, CW * j:CW * j + CW]).reshape(L, 1, CW),
            "mlnw": np.ascontiguousarray(np.asarray(inp["ml_norm_w"], np.float32)[:, CW * j:CW * j + CW]).reshape(L, 1, CW)}


def _dla_declare(self):
    I, Sc = self.I, self.Sc
    I["gpar"] = self.dram_in("gpar", [DEPTH, 4, 64, 2])
    I["rmask"] = self.dram_in("rmask", [64, 1024])
    I["negmask"] = self.dram_in("negmask", [2, 128, 128])
    I["dsk"] = self.dram_in("dsk", [DEPTH, 1, MBU])
    I["mbnw"] = self.dram_in("mbnw", [DEPTH, 1, CW])
    I["mlnw"] = self.dram_in("mlnw", [DEPTH, 1, CW])
    Sc["gq"] = self.dram_scr("gq", [4, 4, 8, T], F32, dbg=True)
    Sc["gcs"] = self.dram_scr("gcs", [4, 8, T], F32, dbg=True)
    Sc["gtot"] = self.dram_scr("gtot", [4, 8, 64], F32, dbg=True)
    Sc["yf_mb"] = self.dram_scr("yf_mb", [T, CW], F32)
    Sc["hf_ml"] = self.dram_scr("hf_ml", [T, CW], F32)
    Sc["yb_mb"] = self.dram_scr("yb_mb", [T, CW], F32)
    Sc["hb_ml"] = self.dram_scr("hb_ml", [T, CW], F32)


def _dla_streams(self, l, mixer):
    nc, S, I, Sc = self.nc, self.S, self.I, self.Sc
    U = MBU if mixer == "mb" else MLU
    P = 8 * U
    with ExitStack() as st:
        sb = lambda n, s, d=F32: st.enter_context(_alloc(nc, "sbuf", n, list(s), d))
        rm = sb("rm", [64, 1024])
        S.dma("sp", rm[:], I["rmask"], writes=["rm"])
        for d in range(2):
            md = (0 if mixer == "mb" else 2) + d
            k = lambda n: (n, d)
            gp = sb("gp%d" % d, [64, 2])
            S.dma("sp", gp[:], I["gpar"][l, md], writes=[k("gp")])
            sc = sb("sc%d" % d, [64, 1024]); a = sb("a%d" % d, [64, 1024]); cs = sb("cs%d" % d, [64, 1024])
            t1 = sb("t1%d" % d, [64, 1024]); t2 = sb("t2%d" % d, [64, 1024]); pp = sb("pp%d" % d, [64, 2])
            if mixer == "mb":
                S.dma("sp", t1[0:P, :], Sc["smallT"][MBU * d:MBU * d + MBU, :].rearrange("u (s n) -> (u s) n", n=1024), writes=[k("t1")])
                S.op("act", lambda e: e.activation(out=t1[0:P, :], in_=t1[0:P, :], func=AF.Exp, bias=gp[0:P, 0:1]), reads=[k("t1"), k("gp")], writes=[k("t1")])
                S.op("act", lambda e: e.activation(out=sc[0:P, :], in_=t1[0:P, :], func=AF.Ln, bias=1.0), reads=[k("t1")], writes=[k("sc")])
                S.op("act", lambda e: e.activation(out=pp[0:P, 0:1], in_=gp[0:P, 1:2], func=AF.Exp), reads=[k("gp")], writes=[k("pp")])
                S.op("dve", lambda e: e.tensor_scalar(out=pp[0:P, 0:1], in0=pp[0:P, 0:1], scalar1=-1.0, scalar2=None, op0=ALU.mult), reads=[k("pp")], writes=[k("pp")])
                S.op("dve", lambda e: e.tensor_scalar(out=a[0:P, :], in0=sc[0:P, :], scalar1=pp[0:P, 0:1], scalar2=None, op0=ALU.mult),
                     reads=[k("sc"), k("pp")], writes=[k("a")])
            else:
                r0 = 2 * MBU + 2 * MLU * d
                S.dma("sp", t1[0:P, :], Sc["smallT"][r0:r0 + MLU, :].rearrange("u (s n) -> (u s) n", n=1024), writes=[k("t1")])
                S.dma("sp", t2[0:P, :], Sc["smallT"][r0 + MLU:r0 + 2 * MLU, :].rearrange("u (s n) -> (u s) n", n=1024), writes=[k("t2")])
                S.op("act", lambda e: e.activation(out=sc[0:P, :], in_=t1[0:P, :], func=AF.Exp, bias=gp[0:P, 0:1]), reads=[k("t1"), k("gp")], writes=[k("sc")])
                S.op("dve", lambda e: e.tensor_scalar(out=sc[0:P, :], in0=sc[0:P, :], scalar1=float(128.0 ** -0.5), scalar2=None, op0=ALU.mult), reads=[k("sc")], writes=[k("sc")])
                S.op("dve", lambda e: e.tensor_scalar(out=pp[0:P, 0:1], in0=gp[0:P, 1:2], scalar1=-1.0, scalar2=None, op0=ALU.mult), reads=[k("gp")], writes=[k("pp")])
                S.op("act", lambda e: e.activation(out=t2[0:P, :], in_=t2[0:P, :], func=AF.Exp, scale=-1.0, bias=pp[0:P, 0:1]), reads=[k("t2"), k("pp")], writes=[k("t2")])
                S.op("act", lambda e: e.activation(out=t2[0:P, :], in_=t2[0:P, :], func=AF.Ln, bias=1.0), reads=[k("t2")], writes=[k("t2")])
                S.op("dve", lambda e: e.tensor_scalar(out=a[0:P, :], in0=t2[0:P, :], scalar1=-1.0, scalar2=None, op0=ALU.mult), reads=[k("t2")], writes=[k("a")])
            S.op("dve", lambda e: e.tensor_tensor_scan(out=cs[0:P, :], data0=rm[0:P, :], data1=a[0:P, :], initial=0.0, op0=ALU.mult, op1=ALU.add),
                 reads=["rm", k("a")], writes=[k("cs")])
            cs3 = cs[0:P, :].rearrange("p (c n) -> p c n", n=128)
            totb = cs3[:, :, 127:128].broadcast_to([P, 8, 128])
            S.dma("sp", Sc["gtot"][md, 0:U, :].rearrange("u (s c) -> (u s) c", c=8), cs3[:, :, 127], reads=[k("cs")], writes=[("gtot", md)], allow_slow_non_contiguous=True)
            t13 = t1[0:P, :].rearrange("p (c n) -> p c n", n=128)
            S.op("dve", lambda e: e.tensor_tensor(out=t13, in0=totb, in1=cs3, op=ALU.subtract), reads=[k("cs")], writes=[k("t1")])
            if d == 1:
                S.op("dve", lambda e: e.tensor_tensor(out=t2[0:P, :], in0=cs[0:P, :], in1=a[0:P, :], op=ALU.subtract), reads=[k("cs"), k("a")], writes=[k("t2")])
                S.op("dve", lambda e: e.tensor_tensor(out=cs[0:P, :], in0=t1[0:P, :], in1=a[0:P, :], op=ALU.add), reads=[k("t1"), k("a")], writes=[k("cs")])
                wexp = t2
                wk = k("t2")
            else:
                wexp = t1
                wk = k("t1")
            unf = lambda ap: ap.rearrange("u (s n) -> (u s) n", n=1024)
            S.dma("sp", unf(Sc["gcs"][md, 0:U, :]), cs[0:P, :], reads=[k("cs")], writes=[("gcs", md)])
            S.op("act", lambda e: e.activation(out=wexp[0:P, :], in_=wexp[0:P, :], func=AF.Exp), reads=[wk], writes=[wk])
            S.op("dve", lambda e: e.tensor_tensor(out=wexp[0:P, :], in0=wexp[0:P, :], in1=sc[0:P, :], op=ALU.mult), reads=[wk, k("sc")], writes=[wk])
            S.dma("sp", unf(Sc["gq"][md, 2, 0:U, :]), wexp[0:P, :], reads=[wk], writes=[("gq", md, 2)])
            S.dma("sp", unf(Sc["gq"][md, 3, 0:U, :]), sc[0:P, :], reads=[k("sc")], writes=[("gq", md, 3)])
            S.op("act", lambda e: e.activation(out=a[0:P, :], in_=cs[0:P, :], func=AF.Exp), reads=[k("cs")], writes=[k("a")])
            S.dma("sp", unf(Sc["gq"][md, 1, 0:U, :]), a[0:P, :], reads=[k("a")], writes=[("gq", md, 1)])
            S.op("dve", lambda e: e.tensor_scalar(out=cs[0:P, :], in0=cs[0:P, :], scalar1=-1.0, scalar2=None, op0=ALU.mult), reads=[k("cs")], writes=[k("cs")])
            S.dma("sp", unf(Sc["gq"][md, 0, 0:U, :]), cs[0:P, :], reads=[k("cs")], writes=[("gq", md, 0)])


def _dla_run(self, l, mixer, d):
    nc, I, Sc = self.nc, self.I, self.Sc
    S = _KeyNS(self.S, (mixer, d))
    mb = mixer == "mb"
    U = MBU if mb else MLU
    PW = 64 if mb else 129
    PS = 64 if mb else 256
    md = (0 if mb else 2) + d
    with ExitStack() as st:
        sb = lambda n, s, dt=F32: st.enter_context(_alloc(nc, "sbuf", n, list(s), dt))
        pst = lambda n, dt=F32: st.enter_context(_alloc(nc, "psum", n, [128, 512], dt))
        Q4 = self.Q4s[d]
        q4t = sb("q4t", [128, 64, 32])
        etot = sb("etot", [128, U, 64])
        negm = sb("negm", [128, 128])
        H = sb("H", [128, U, PW]); Hb = sb("Hb", [128, U, PW], BF16)
        csb = [sb("csb%d" % i, [128, U, 128]) for i in range(4)]
        LT = [sb("LT%d" % i, [128, U, 128]) for i in range(2)]
        MT = [sb("MT%d" % i, [128, U, 128], BF16) for i in range(2)]
        Xv = [sb("Xv%d" % i, [128, U, PW], BF16) for i in range(4)]
        Xw = [sb("Xw%d" % i, [128, U, PW], BF16) for i in range(2)]
        Kt = [sb("Kt%d" % i, [128, 128 * MBG if mb else CW], BF16) for i in range(4)]
        NG = MBG if mb else MLU
        KTf = [sb("KTf%d" % i, [128, NG, 128], BF16) for i in range(4)]
        QTf = [sb("QTf%d" % i, [128, NG, 128], BF16) for i in range(4)]
        y2s = [sb("y2s%d" % i, [128, U, PW]) for i in range(2)]
        yo = [sb("yo%d" % i, [128, U, PW]) for i in range(2)]
        fin = [sb("fin%d" % i, [128, CW]) for i in range(2)]
        pG = pst("pG")
        NPT = 1 if mb else (MLU + 1) // 2
        py1 = [pst("py1%d" % i) for i in range(NPT)]
        py2 = [pst("py2%d" % i) for i in range(NPT)]
        pS_ = [pst("pS%d" % i) for i in range(NPT)]
        pQ = py1[0]

        def pview(tiles, u, w):
            if mb:
                return tiles[0][:, 64 * u:64 * u + w]
            return tiles[u // 2][:, 256 * (u % 2):256 * (u % 2) + w]

        def pall(tiles, h):
            if mb:
                return tiles[0][:, 0:64 * U].rearrange("p (u w) -> p u w", w=64)
            return tiles[h][:].rearrange("p (u w) -> p u w", w=256)[:, :, 0:129]

        S.op("pool", lambda e: e.memset(Q4[:], 0.0), writes=["Q4"])
        S.dma("sp", Q4[:], Sc["gq"][md].rearrange("q u t -> (q u) t"), writes=["Q4"])
        for g4 in range(4):
            for k in range(16):
                c = 16 * g4 + k
                S.op("pe", lambda e: e.transpose(pQ[:, 32 * k:32 * k + 32], Q4[:, 128 * c:128 * c + 128], self.ident_f[0:32, 0:32]), reads=["Q4"], writes=["py1"])
            S.op("dve", lambda e: e.tensor_copy(out=q4t[:, 16 * g4:16 * g4 + 16, :], in_=pQ[:].rearrange("p (k q) -> p k q", q=32)), reads=["py1"], writes=["q4t"])
        S.dma("sp", etot[:].rearrange("p u c -> p (u c)"), Sc["gtot"][md:md + 1, 0:U, :].rearrange("o u c -> o (u c)").broadcast_to([128, U * 64]), writes=["etot"])
        S.op("act", lambda e: e.activation(out=etot[:], in_=etot[:], func=AF.Exp), reads=["etot"], writes=["etot"])
        S.dma("sp", negm[:], I["negmask"][d], writes=["negm"])
        S.op("pool", lambda e: e.memset(H[:], 0.0), writes=["H"])
        S.op("pool", lambda e: e.memset(Hb[:], 0.0), writes=["Hb"])
        if not mb:
            for i in range(4):
                S.op("pool", lambda e: e.memset(Xv[i][:, :, 128:129], 1.0), writes=[("Xvo", i)])
        rden = [sb("rden%d" % i, [128, 4]) for i in range(2)]

        order = list(range(64)) if d == 0 else list(range(63, -1, -1))
        yield "setup"
        for step, c in enumerate(order):
            k = step % 2
            k4 = step % 4
            r0 = 128 * c
            S.dma("sp", csb[k4][:], Sc["gcs"][md, 0:U, r0:r0 + 128].partition_broadcast(128), writes=[("csb", k4)])
            if mb:
                S.dma("act", Xv[k4][:].rearrange("p u w -> p (u w)"), Sc["mbx"][r0:r0 + 128, :], writes=[("Xv", k4)])
                S.dma("act", Kt[k4][:], Sc["mbB"][r0:r0 + 128, :], writes=[("Kt", k4)])
                S.dma("act", KTf[k4][:], Sc["mbBT"][:, r0:r0 + 128].rearrange("(g n) s -> n g s", n=128), writes=[("KTf", k4)])
                S.dma("act", QTf[k4][:], Sc["mbCT"][:, r0:r0 + 128].rearrange("(g n) s -> n g s", n=128), writes=[("QTf", k4)])
            else:
                S.dma("act", Xv[k4][:, :, 0:128], Sc["mlv"][r0:r0 + 128, :].rearrange("p (u w) -> p u w", w=128), writes=[("Xv", k4)])
                S.dma("act", Kt[k4][:], Sc["mlk"][r0:r0 + 128, :], writes=[("Kt", k4)])
                S.dma("act", KTf[k4][:], Sc["mlkT"][:, r0:r0 + 128].rearrange("(g n) s -> n g s", n=128), writes=[("KTf", k4)])
                S.dma("act", QTf[k4][:], Sc["mlqT"][:, r0:r0 + 128].rearrange("(g n) s -> n g s", n=128), writes=[("QTf", k4)])
            xvk = [("Xv", k4)] + ([] if mb else [("Xvo", k4)])
            yield "s"
            for g in range(NG):
                S.op("pe", lambda e: e.matmul(pG[:, 128 * g:128 * g + 128], lhsT=KTf[k4][:, g, :], rhs=QTf[k4][:, g, :], start=True, stop=True),
                     reads=[("KTf", k4), ("QTf", k4)], writes=[("pG", g)])
            S.op("dve", lambda e: e.tensor_tensor(out=csb[k4][:], in0=csb[k4][:], in1=negm[:].unsqueeze(1).broadcast_to([128, U, 128]), op=ALU.add),
                 reads=[("csb", k4), "negm"], writes=[("csb", k4)])
            yield "s"
            for u in range(U):
                S.op("act", lambda e: e.activation(out=LT[k][:, u, :], in_=csb[k4][:, u, :], func=AF.Exp, bias=q4t[:, c, u:u + 1]),
                     reads=[("csb", k4), "q4t"], writes=[("LT", k, u)])
            yield "s"
            for u in range(U):
                g = (u // 4) if mb else u
                S.op("dve", lambda e: e.scalar_tensor_tensor(out=MT[k][:, u, :], in0=pG[:, 128 * g:128 * g + 128], scalar=q4t[:, c, 24 + u:25 + u],
                                                             in1=LT[k][:, u, :], op0=ALU.mult, op1=ALU.mult),
                     reads=[("pG", g), ("LT", k, u), "q4t"], writes=[("MT", k, u)])
            yield "s"
            for u in range(U):
                S.op("pe", lambda e: e.matmul(pview(py1, u, PW), lhsT=MT[k][:, u, :], rhs=Xv[k4][:, u, :], start=True, stop=True),
                     reads=[("MT", k, u)] + xvk, writes=["py1"])
            if mb:
                for g in range(MBG):
                    S.op("pe", lambda e: e.matmul(py2[0][:, 256 * g:256 * g + 256], lhsT=QTf[k4][:, g, :], rhs=Hb[:, 4 * g:4 * g + 4, :],
                                                  start=True, stop=True), reads=[("QTf", k4), "Hb"], writes=["py2"])
            else:
                for u in range(U):
                    S.op("pe", lambda e: e.matmul(pview(py2, u, PW), lhsT=QTf[k4][:, u, :], rhs=Hb[:, u, :], start=True, stop=True),
                         reads=[("QTf", k4), "Hb"], writes=["py2"])
            yield "s"
            ecs_b = lambda u0, n: q4t[:, c, 8 + u0:8 + u0 + n].unsqueeze(2).broadcast_to([128, n, PW])
            w_b = q4t[:, c, 16:16 + U].unsqueeze(2).broadcast_to([128, U, PW])
            if mb:
                S.op("dve", lambda e: e.tensor_tensor(out=y2s[k][:], in0=pall(py2, 0), in1=ecs_b(0, U), op=ALU.mult), reads=["py2", "q4t"], writes=[("y2s", k)])
                S.op("dve", lambda e: e.tensor_tensor(out=yo[k][:], in0=pall(py1, 0), in1=y2s[k][:], op=ALU.add), reads=["py1", ("y2s", k)], writes=[("yo", k)])
            else:
                for h in range(NPT):
                    S.op("dve", lambda e: e.tensor_tensor(out=y2s[k][:, 2 * h:2 * h + 2, :], in0=pall(py2, h), in1=ecs_b(2 * h, 2), op=ALU.mult),
                         reads=["py2", "q4t"], writes=[("y2s", k, h)])
                    S.op("dve", lambda e: e.tensor_tensor(out=yo[k][:, 2 * h:2 * h + 2, :], in0=pall(py1, h), in1=y2s[k][:, 2 * h:2 * h + 2, :], op=ALU.add),
                         reads=["py1", ("y2s", k, h)], writes=[("yo", k, h)])
            yok = [("yo", k)] if mb else [("yo", k, h) for h in range(NPT)]
            yield "s"
            S.op("pool", lambda e: e.tensor_tensor(out=Xw[k][:], in0=Xv[k4][:], in1=w_b, op=ALU.mult), reads=xvk + ["q4t"], writes=[("Xw", k)])
            if mb:
                for g in range(MBG):
                    S.op("pe", lambda e: e.matmul(pS_[0][:, 256 * g:256 * g + 256], lhsT=Kt[k4][:, 128 * g:128 * g + 128], rhs=Xw[k][:, 4 * g:4 * g + 4, :],
                                                  start=True, stop=True), reads=[("Kt", k4), ("Xw", k)], writes=["pS"])
            else:
                for u in range(U):
                    S.op("pe", lambda e: e.matmul(pview(pS_, u, PW), lhsT=Kt[k4][:, 128 * u:128 * u + 128], rhs=Xw[k][:, u, :], start=True, stop=True),
                         reads=[("Kt", k4), ("Xw", k)], writes=["pS"])
            yield "s"
            S.op("pool", lambda e: e.tensor_tensor(out=H[:], in0=H[:], in1=etot[:, :, c:c + 1].broadcast_to([128, U, PW]), op=ALU.mult),
                 reads=["H", "etot"], writes=["H"])
            if mb:
                S.op("dve", lambda e: e.tensor_tensor(out=H[:], in0=pall(pS_, 0), in1=H[:], op=ALU.add), reads=["H", "pS"], writes=["H"])
            else:
                for h in range(NPT):
                    S.op("dve", lambda e: e.tensor_tensor(out=H[:, 2 * h:2 * h + 2, :], in0=pall(pS_, h), in1=H[:, 2 * h:2 * h + 2, :], op=ALU.add),
                         reads=["H", "pS"], writes=["H"])
            S.op("act", lambda e: e.activation(out=Hb[:], in_=H[:], func=AF.Copy), reads=["H"], writes=["Hb"])
            yield "s"
            f = fin[k]
            if mb:
                ysrc = yo[k][:].rearrange("p u w -> p (u w)")
                fk = yok
            else:
                S.op("act", lambda e: e.activation(out=rden[k][:, 0:MLU], in_=yo[k][:, :, 128], func=AF.Abs), reads=yok, writes=[("rden", k)])
                S.op("dve", lambda e: e.tensor_scalar(out=rden[k][:], in0=rden[k][:], scalar1=1.0, scalar2=None, op0=ALU.max), reads=[("rden", k)], writes=[("rden", k)])
                S.op("dve", lambda e: e.reciprocal(out=rden[k][:], in_=rden[k][:]), reads=[("rden", k)], writes=[("rden", k)])
                S.op("pool", lambda e: e.tensor_tensor(out=f[:].rearrange("p (u w) -> p u w", w=128), in0=yo[k][:, :, 0:128],
                                                       in1=rden[k][:, 0:MLU].unsqueeze(2).broadcast_to([128, MLU, 128]), op=ALU.mult),
                     reads=yok + [("rden", k)], writes=[("fin", k)])
                ysrc = f[:]
                fk = [("fin", k)]
            dst_f = (Sc["yf_mb"] if mb else Sc["hf_ml"]) if d == 0 else (Sc["yb_mb"] if mb else Sc["hb_ml"])
            S.dma("pool", dst_f[r0:r0 + 128, :], ysrc, reads=fk, writes=[("ydir", c)])
            yield "chunk"
        yield "done"


def _dla_final(self, l, mixer):
    nc, S, I, Sc = self.nc, self.S, self.I, self.Sc
    mb = mixer == "mb"
    with ExitStack() as st:
        sb = lambda n, s, dt=F32: st.enter_context(_alloc(nc, "sbuf", n, list(s), dt))
        nwb = sb("nwb2", [128, CW])
        S.dma("sp", nwb[:], I["mbnw" if mb else "mlnw"][l].broadcast_to([128, CW]), writes=["nwb2"])
        if mb:
            dskb = sb("dskb", [128, MBU])
            S.dma("sp", dskb[:], I["dsk"][l].broadcast_to([128, MBU]), writes=["dskb"])
        NB_ = 3
        prev = [sb("prev%d" % i, [128, CW]) for i in range(NB_)]
        cur = [sb("cur%d" % i, [128, CW]) for i in range(NB_)]
        Zt = [sb("Zt%d" % i, [128, CW], BF16) for i in range(NB_)]
        Ot = [sb("Ot%d" % i, [128, CW], BF16) for i in range(NB_)]
        ssq = [sb("ssq%d" % i, [128, 4]) for i in range(NB_)]
        junk = sb("junkd", [128, CW], BF16)
        outb = [sb("outb%d" % i, [128, CW], BF16) for i in range(NB_)]
        for c in range(64):
            k = c % NB_
            r0 = 128 * c
            S.dma("sp", prev[k][:], (Sc["yf_mb"] if mb else Sc["hf_ml"])[r0:r0 + 128, :], writes=[("prev", k)])
            S.dma("pool", cur[k][:], (Sc["yb_mb"] if mb else Sc["hb_ml"])[r0:r0 + 128, :], writes=[("cur", k)])
            S.dma("sp", Zt[k][:], (Sc["mbz"] if mb else Sc["mlz"])[r0:r0 + 128, :], writes=[("Zt", k)])
            S.dma("pool", Ot[k][:], (Sc["mbx"] if mb else Sc["mlo"])[r0:r0 + 128, :], writes=[("Ot", k)])
            S.op("pool", lambda e: e.tensor_tensor(out=prev[k][:], in0=prev[k][:], in1=cur[k][:], op=ALU.add), reads=[("prev", k), ("cur", k)], writes=[("prev", k)])
            if mb:
                S.op("pool", lambda e: e.tensor_tensor(out=cur[k][:].rearrange("p (u w) -> p u w", w=64), in0=Ot[k][:].rearrange("p (u w) -> p u w", w=64),
                                                       in1=dskb[:].unsqueeze(2).broadcast_to([128, MBU, 64]), op=ALU.mult),
                     reads=[("Ot", k), ("cur", k), "dskb"], writes=[("cur", k)])
                S.op("dve", lambda e: e.tensor_tensor(out=prev[k][:], in0=prev[k][:], in1=cur[k][:], op=ALU.add), reads=[("prev", k), ("cur", k)], writes=[("prev", k)])
                S.op("dve", lambda e: e.tensor_tensor(out=prev[k][:], in0=prev[k][:], in1=Zt[k][:], op=ALU.mult), reads=[("prev", k), ("Zt", k)], writes=[("prev", k)])
                ngr, gw = MBG, 256
            else:
                S.op("dve", lambda e: e.tensor_tensor(out=prev[k][:], in0=prev[k][:], in1=Ot[k][:], op=ALU.mult), reads=[("prev", k), ("Ot", k)], writes=[("prev", k)])
                ngr, gw = MLU, 128
            for g in range(ngr):
                S.op("act", lambda e: e.activation(out=junk[:, 0:gw], in_=prev[k][:, gw * g:gw * g + gw], func=AF.Square, accum_out=ssq[k][:, g:g + 1]),
                     reads=[("prev", k)], writes=["junkd", ("ssq", k)])
            S.op("dve", lambda e: e.tensor_scalar(out=ssq[k][:, 0:ngr], in0=ssq[k][:, 0:ngr], scalar1=1.0 / gw, scalar2=EPS, op0=ALU.mult, op1=ALU.add),
                 reads=[("ssq", k)], writes=[("ssq", k)])
            S.op("act", lambda e: e.activation(out=ssq[k][:, 0:ngr], in_=ssq[k][:, 0:ngr], func=AF.Sqrt), reads=[("ssq", k)], writes=[("ssq", k)])
            S.op("dve", lambda e: e.reciprocal(out=ssq[k][:, 0:ngr], in_=ssq[k][:, 0:ngr]), reads=[("ssq", k)], writes=[("ssq", k)])
            for g in range(ngr):
                S.op("dve", lambda e: e.scalar_tensor_tensor(out=(outb[k] if mb else prev[k])[:, gw * g:gw * g + gw], in0=prev[k][:, gw * g:gw * g + gw],
                                                             scalar=ssq[k][:, g:g + 1], in1=nwb[:, gw * g:gw * g + gw], op0=ALU.mult, op1=ALU.mult),
                     reads=[("prev", k), ("ssq", k), "nwb2"], writes=[("outb", k) if mb else ("prev", k)])
            if not mb:
                S.op("pool", lambda e: e.tensor_tensor(out=outb[k][:], in0=prev[k][:], in1=Zt[k][:], op=ALU.mult), reads=[("prev", k), ("Zt", k)], writes=[("outb", k)])
            col0 = 0 if mb else CW
            S.dma("pool", Sc["ytm"][r0:r0 + 128, col0:col0 + CW], outb[k][:], reads=[("outb", k)], writes=[("ytm_dla", mixer, c)])


def _phaseDLA(self, l):
    S, nc = self.S, self.nc
    for mixer in ("mb", "ml"):
        _dla_streams(self, l, mixer)
        S.barrier()
        with ExitStack() as st:
            self.Q4s = [st.enter_context(_alloc(nc, "sbuf", "Q4_%d" % d, [32, T], F32)) for d in range(2)]
            gens = [_dla_run(self, l, mixer, d) for d in (0, 1)]
            for g in gens:
                next(g)
            while True:
                rs = [next(g) for g in gens]
                if all(r == "done" for r in rs):
                    break
            S.barrier()
            for g in reversed(gens):
                try:
                    next(g)
                except StopIteration:
                    pass
        _dla_final(self, l, mixer)
        S.barrier()


Prog.phaseDLA = _phaseDLA


N2L = 2 * T
CG = 32


def _prep_hy(inp, j):
    L = DEPTH
    n = np.arange(128)
    ang = 2.0 * np.pi * np.outer(n, n) / 128.0
    Fre, Fim = np.cos(ang), -np.sin(ang)
    dft = np.stack([Fre, Fim, Fre, -Fim], axis=1).astype(np.float32)
    angt = 2.0 * np.pi * np.outer(n, n) / float(N2L)
    twd = np.stack([np.cos(angt), -np.sin(angt)], axis=1).astype(np.float32)
    t = np.arange(T, dtype=np.float32)
    t_norm = t / np.float32(T)
    bands = np.arange(1, 9, dtype=np.float32)
    a = (np.float32(2.0 * math.pi / T)) * t[:, None] * bands[None, :]
    pos = np.concatenate([t_norm[:, None], np.cos(a), np.sin(a)], axis=-1).astype(np.float32)
    hyp = np.zeros((L, 64, 4), np.float32)
    hyp[:, :, 0] = np.asarray(inp["hy_b1"]); hyp[:, :, 1] = np.asarray(inp["hy_freq"]); hyp[:, :, 2] = np.asarray(inp["hy_b2"])
    dec = np.asarray(inp["hy_decay"], np.float32).reshape(L, 4, 512)[:, :, HC * j:HC * j + HC].reshape(L, 4 * NB, 128).transpose(0, 2, 1)
    w3 = np.asarray(inp["hy_w3"], np.float32).reshape(L, 64, 4, 512)[:, :, :, HC * j:HC * j + HC].reshape(L, 64, 4 * HC)
    skip = np.asarray(inp["hy_skip"], np.float32)[:, :, HC * j:HC * j + HC].reshape(L, 2, 1, HC)
    return {"dft": np.ascontiguousarray(dft.reshape(128, 512)), "twd": np.ascontiguousarray(twd.reshape(128, 256)),
            "posT": np.ascontiguousarray(pos.T), "tneg": (-t_norm).reshape(1, T).astype(np.float32),
            "hyp": hyp, "hydec": np.ascontiguousarray(dec),
            "hyw1": np.asarray(inp["hy_w1"], np.float32), "hyw2": np.asarray(inp["hy_w2"], np.float32),
            "hyw3": np.ascontiguousarray(w3), "hyskip": np.ascontiguousarray(skip)}


def _hy_declare(self):
    I, Sc = self.I, self.Sc
    I["dft"] = self.dram_in("dft", [128, 512]); I["twd"] = self.dram_in("twd", [128, 256])
    I["posT"] = self.dram_in("posT", [17, T]); I["tneg"] = self.dram_in("tneg", [1, T])
    I["hyp"] = self.dram_in("hyp", [DEPTH, 64, 4]); I["hydec"] = self.dram_in("hydec", [DEPTH, 128, 4 * NB])
    I["hyw1"] = self.dram_in("hyw1", [DEPTH, 17, 64]); I["hyw2"] = self.dram_in("hyw2", [DEPTH, 64, 64])
    I["hyw3"] = self.dram_in("hyw3", [DEPTH, 64, 4 * HC]); I["hyskip"] = self.dram_in("hyskip", [DEPTH, 2, 1, HC])
    Sc["gflt"] = self.dram_scr("gflt", [2, HC, N2L], BF16, dbg=True)


def _phaseHY(self, l):
    nc, S, I, Sc = self.nc, self.S, self.I, self.Sc
    PI = float(np.pi)
    with ExitStack() as st:
        sb = lambda n, s, d=F32: st.enter_context(_alloc(nc, "sbuf", n, list(s), d))
        pst = lambda n, d=F32: st.enter_context(_alloc(nc, "psum", n, [128, 512], d))
        w1 = sb("hw1", [17, 64]); w2 = sb("hw2", [64, 64]); w3f = sb("hw3f", [64, 4 * HC]); w3b = sb("hw3b", [64, 4 * HC], BF16)
        hp = sb("hhp", [64, 4]); fb = sb("hfb", [64, 2])
        hid = sb("hhid", [64, T], BF16)
        tn = sb("htn", [128, T]); dec = sb("hdec", [128, 4 * NB])
        S.dma("sp", w1[:], I["hyw1"][l], writes=["w1"]); S.dma("sp", w2[:], I["hyw2"][l], writes=["w2"])
        S.dma("sp", w3f[:], I["hyw3"][l], writes=["w3f"]); S.dma("sp", hp[:], I["hyp"][l], writes=["hp"])
        S.dma("sp", tn[:], I["tneg"].broadcast_to([128, T]), writes=["tn"]); S.dma("sp", dec[:], I["hydec"][l], writes=["dec"])
        S.op("pool", lambda e: e.tensor_copy(out=w3b[:], in_=w3f[:]), reads=["w3f"], writes=["w3b"])
        S.op("dve", lambda e: e.tensor_tensor(out=fb[:, 0:1], in0=hp[:, 0:1], in1=hp[:, 1:2], op=ALU.mult), reads=["hp"], writes=["fb"])
        S.op("dve", lambda e: e.tensor_tensor(out=fb[:, 1:2], in0=hp[:, 2:3], in1=hp[:, 1:2], op=ALU.mult), reads=["hp", "fb"], writes=["fb"])
        pt_ = [sb("hpt%d" % i, [17, 512]) for i in range(2)]
        arg = [sb("harg%d" % i, [64, 512]) for i in range(2)]
        ta = [sb("hta%d" % i, [64, 512]) for i in range(2)]
        tb = [sb("htb%d" % i, [64, 512]) for i in range(2)]
        h1 = [sb("hh1%d" % i, [64, 512]) for i in range(2)]
        pz = [pst("hpz%d" % i) for i in range(2)]

        def sin_layer(src_ps, pk, col, out_ap, okey, k):
            a = arg[k]; ak = ("arg", k)
            S.op("dve", lambda e: e.tensor_scalar(out=a[:], in0=src_ps[0:64, :], scalar1=hp[:, 1:2], scalar2=fb[:, col:col + 1], op0=ALU.mult, op1=ALU.add),
                 reads=[pk, "hp", "fb"], writes=[ak])
            S.op("dve", lambda e: e.tensor_scalar(out=ta[k][:], in0=a[:], scalar1=PI, scalar2=-2 * PI, op0=ALU.is_gt, op1=ALU.mult), reads=[ak], writes=[("ta", k)])
            S.op("dve", lambda e: e.tensor_scalar(out=tb[k][:], in0=a[:], scalar1=-PI, scalar2=2 * PI, op0=ALU.is_lt, op1=ALU.mult), reads=[ak], writes=[("tb", k)])
            S.op("pool", lambda e: e.tensor_tensor(out=ta[k][:], in0=ta[k][:], in1=tb[k][:], op=ALU.add), reads=[("ta", k), ("tb", k)], writes=[("ta", k)])
            S.op("pool", lambda e: e.tensor_tensor(out=a[:], in0=a[:], in1=ta[k][:], op=ALU.add), reads=[ak, ("ta", k)], writes=[ak])
            S.op("act", lambda e: e.activation(out=out_ap, in_=a[:], func=AF.Sin), reads=[ak], writes=[okey])

        for c in range(16):
            k = c % 2
            S.dma("sp", pt_[k][:], I["posT"][:, 512 * c:512 * c + 512], writes=[("pt", k)])
            S.op("pe", lambda e: e.matmul(pz[0][0:64, :], lhsT=w1[:], rhs=pt_[k][:], start=True, stop=True), reads=["w1", ("pt", k)], writes=["pz0"])
            sin_layer(pz[0], "pz0", 0, h1[k][:], ("h1", k), k)
            S.op("pe", lambda e: e.matmul(pz[1][0:64, :], lhsT=w2[:], rhs=h1[k][:], start=True, stop=True), reads=["w2", ("h1", k)], writes=["pz1"])
            sin_layer(pz[1], "pz1", 1, hid[:, 512 * c:512 * c + 512], ("hid", c), k)
        hidk = [("hid", c) for c in range(16)]
        gt = [sb("hgt%d" % i, [128, N2L], BF16) for i in range(2)]
        win = [sb("hwin%d" % i, [128, 512]) for i in range(2)]
        pf = [pst("hpf%d" % i) for i in range(2)]
        for i in range(2):
            S.op("pool", lambda e: e.memset(gt[i][:, T:T + 1], 0.0), writes=[("gtz", i)])
        it = 0
        for o in range(2):
            for cb in range(NB):
                g = gt[(o * NB + cb) % 2]; gk = ("gt", (o * NB + cb) % 2)
                gparts = []
                for dr in range(2):
                    col0 = (o * 2 + dr) * HC + 128 * cb
                    di = (o * 2 + dr) * NB + cb
                    for c in range(16):
                        k = it % 2; it += 1
                        S.op("pe", lambda e: e.matmul(pf[k][:], lhsT=w3b[:, col0:col0 + 128], rhs=hid[:, 512 * c:512 * c + 512], start=True, stop=True),
                             reads=["w3b", ("hid", c)], writes=[("pf", k)])
                        S.op("act", lambda e: e.activation(out=win[k][:], in_=tn[:, 512 * c:512 * c + 512], func=AF.Exp, scale=dec[:, di:di + 1]),
                             reads=["tn", "dec"], writes=[("win", k)])
                        pk = (gk, dr, c)
                        gparts.append(pk)
                        if dr == 0:
                            S.op("dve", lambda e: e.tensor_tensor(out=g[:, 512 * c:512 * c + 512], in0=pf[k][:], in1=win[k][:], op=ALU.mult),
                                 reads=[("pf", k), ("win", k)], writes=[pk])
                        else:
                            j0 = 1 if c == 0 else 0
                            lo = N2L - 512 * c - 511
                            hi = N2L - 512 * c - j0 + 1
                            S.op("dve", lambda e: e.tensor_tensor(out=g[:, lo:hi][:, ::-1], in0=pf[k][:, j0:512], in1=win[k][:, j0:512], op=ALU.mult),
                                 reads=[("pf", k), ("win", k)], writes=[pk])
                S.dma("pool", Sc["gflt"][o, 128 * cb:128 * cb + 128, :], g[:], reads=gparts + [("gtz", (o * NB + cb) % 2)], writes=[("gflt", o, cb)])
    S.barrier()
    with ExitStack() as st:
        sb = lambda n, s, d=F32: st.enter_context(_alloc(nc, "sbuf", n, list(s), d))
        dftf = sb("dftf", [128, 512]); dft = sb("dftb", [128, 4, 128], BF16); twd = sb("twd", [128, 2, 128])
        S.dma("sp", dftf[:], I["dft"], writes=["dftf"]); S.dma("sp", twd[:].rearrange("p a k -> p (a k)"), I["twd"], writes=["twd"])
        S.op("dve", lambda e: e.tensor_copy(out=dft[:].rearrange("p a k -> p (a k)"), in_=dftf[:]), reads=["dftf"], writes=["dft"])
        Fre, Fim, nFim = dft[:, 0, :], dft[:, 1, :], dft[:, 3, :]
        Fcat = dft[:, 0:2, :].rearrange("p a k -> p (a k)")
        Fci2 = dft[:, 1:3, :].rearrange("p a k -> p (a k)")
        Fci1 = dft[:, 2:4, :].rearrange("p a k -> p (a k)")
        G = sb("hyG", [128, 2, CG, 2, 128], BF16)
        gblk = [sb("gblk%d" % i, [128, CG, 128], BF16) for i in range(2)]
        sig = {n: sb("sig_" + n, [64, CG, 128], BF16) for n in ("v", "x1", "x2", "g")}
        zblk = sb("zblk", [64, CG, 128], BF16); oblk = sb("oblk", [64, CG, 128], BF16)
        skb = sb("skb", [64, 2, CG])
        Ap = [sb("Ap%d" % i, [128, 8, 2, 128], BF16) for i in range(2)]
        Yp = [sb("Yp%d" % i, [128, 8, 2, 128], BF16) for i in range(2)]
        Bp = [sb("Bp%d" % i, [128, 8, 2, 128], BF16) for i in range(2)]
        tt = [[sb("tt%d_%d" % (i, j), [128, 8, 128]) for j in range(4)] for i in range(2)]
        ep = [[sb("ep%d_%d" % (i, j), [64, 8, 128]) for j in range(2)] for i in range(2)]
        pA = st.enter_context(_alloc(nc, "psum", "hpA", [128, 2048], F32))
        pB = st.enter_context(_alloc(nc, "psum", "hpB", [128, 2048], F32))
        cn = {"c": 0}

        def cmul(out_t, okey, are, aim, akeys, bre, bim, bkeys, conj):
            i = cn["c"] % 2; cn["c"] += 1
            t1, t2, t3, t4 = tt[i]
            ks = [("tt", i, j) for j in range(4)]
            shp = lambda a: a
            S.op("dve", lambda e: e.tensor_tensor(out=t1[:], in0=are, in1=bre, op=ALU.mult), reads=akeys + bkeys, writes=[ks[0]])
            S.op("dve", lambda e: e.tensor_tensor(out=t2[:], in0=aim, in1=bim, op=ALU.mult), reads=akeys + bkeys, writes=[ks[1]])
            S.op("dve", lambda e: e.tensor_tensor(out=t3[:], in0=are, in1=bim, op=ALU.mult), reads=akeys + bkeys, writes=[ks[2]])
            S.op("dve", lambda e: e.tensor_tensor(out=t4[:], in0=aim, in1=bre, op=ALU.mult), reads=akeys + bkeys, writes=[ks[3]])
            if not conj:
                S.op("pool", lambda e: e.tensor_tensor(out=out_t[:, :, 0, :], in0=t1[:], in1=t2[:], op=ALU.subtract), reads=ks[0:2], writes=[okey + ("re",)])
                S.op("pool", lambda e: e.tensor_tensor(out=out_t[:, :, 1, :], in0=t3[:], in1=t4[:], op=ALU.add), reads=ks[2:4], writes=[okey + ("im",)])
            else:
                S.op("pool", lambda e: e.tensor_tensor(out=out_t[:, :, 0, :], in0=t1[:], in1=t2[:], op=ALU.add), reads=ks[0:2], writes=[okey + ("re",)])
                S.op("pool", lambda e: e.tensor_tensor(out=out_t[:, :, 1, :], in0=t4[:], in1=t3[:], op=ALU.subtract), reads=ks[2:4], writes=[okey + ("im",)])

        pA3 = pA[:].rearrange("p (c k) -> p c k", k=256)
        Tre = twd[:, 0, :].unsqueeze(1).broadcast_to([128, 8, 128])
        Tim = twd[:, 1, :].unsqueeze(1).broadcast_to([128, 8, 128])
        pBq = pB[:].rearrange("p (q r c k) -> p q r c k", q=2, r=2, c=4)

        def fwd_octet(src_fn, K, skeys, i):
            for ch in range(8):
                S.op("pe", lambda e: e.matmul(pA3[:, ch, :], lhsT=src_fn(ch), rhs=Fcat[0:K, :], start=True, stop=True),
                     reads=skeys + ["dft"], writes=["pA"])
            cmul(Ap[i], ("Ap", i), pA3[:, :, 0:128], pA3[:, :, 128:256], ["pA"], Tre, Tim, ["twd"], False)
            for q in range(2):
                rre = Ap[i][:, 4 * q:4 * q + 4, 0, :]
                rim = Ap[i][:, 4 * q:4 * q + 4, 1, :]
                kk = [("Ap", i, "re"), ("Ap", i, "im"), "dft"]
                S.op("pe", lambda e: e.matmul(pB[:, 1024 * q:1024 * q + 512], lhsT=Fre, rhs=rre, start=True, stop=False), reads=kk, writes=["pB"])
                S.op("pe", lambda e: e.matmul(pB[:, 1024 * q:1024 * q + 512], lhsT=nFim, rhs=rim, start=False, stop=True), reads=kk, writes=["pB"])
                S.op("pe", lambda e: e.matmul(pB[:, 1024 * q + 512:1024 * q + 1024], lhsT=Fim, rhs=rre, start=True, stop=False), reads=kk, writes=["pB"])
                S.op("pe", lambda e: e.matmul(pB[:, 1024 * q + 512:1024 * q + 1024], lhsT=Fre, rhs=rim, start=False, stop=True), reads=kk, writes=["pB"])

        oc = {"n": 0}
        for cg in range(HC // CG):
            c0 = CG * cg
            for o in range(2):
                S.dma("sp", gblk[o][:], Sc["gflt"][o, c0:c0 + CG, :].rearrange("c (a b) -> a c b", b=128), writes=[("gblk", o)])
                for oc8 in range(CG // 8):
                    i = oc["n"] % 2; oc["n"] += 1
                    fwd_octet(lambda ch: gblk[o][:, 8 * oc8 + ch, :], 128, [("gblk", o)], i)
                    gv = G[:, o, 8 * oc8:8 * oc8 + 8, :, :]
                    S.op("act", lambda e: e.activation(out=gv[:, :, 0, :].rearrange("p (q c) k -> p q c k", q=2), in_=pBq[:, :, 0, :, :], func=AF.Copy),
                         reads=["pB"], writes=[("G", o, oc8, 0)])
                    S.op("act", lambda e: e.activation(out=gv[:, :, 1, :].rearrange("p (q c) k -> p q c k", q=2), in_=pBq[:, :, 1, :, :], func=AF.Copy),
                         reads=["pB"], writes=[("G", o, oc8, 1)])
            for n_, src, r0 in (("v", "hyuT", 0), ("x1", "hyuT", HC), ("x2", "hyuT", 2 * HC), ("g", "hygT", 0)):
                S.dma("pool", sig[n_][:], Sc[src][r0 + c0:r0 + c0 + CG, :].rearrange("c (a b) -> a c b", b=128), writes=[("sig", n_)])
            S.dma("sp", skb[:].rearrange("p o c -> p (o c)") if False else skb[:], I["hyskip"][l, :, :, c0:c0 + CG].rearrange("o x c -> x o c").broadcast_to([64, 2, CG]),
                  writes=["skb"])
            for o in range(2):
                src_t = sig["v"] if o == 0 else zblk
                src_k = [("sig", "v")] if o == 0 else [("zblk", q) for q in range(CG // 8)]
                for oc8 in range(CG // 8):
                    i = oc["n"] % 2; oc["n"] += 1
                    ch0 = 8 * oc8
                    skeys = [("sig", "v")] if o == 0 else [("zblk", oc8)]
                    fwd_octet(lambda ch: src_t[:, ch0 + ch, :], 64, skeys, i)
                    gv = G[:, o, ch0:ch0 + 8, :, :]
                    gre = gv[:, :, 0, :].rearrange("p (q c) k -> p q c k", q=2)
                    gim = gv[:, :, 1, :].rearrange("p (q c) k -> p q c k", q=2)
                    ii = cn["c"] % 2; cn["c"] += 1
                    t1, t2, t3, t4 = [t[:].rearrange("p (q c) k -> p q c k", q=2) for t in tt[ii]]
                    ks = [("tt", ii, j) for j in range(4)]
                    gk = [("G", o, oc8, 0), ("G", o, oc8, 1)]
                    S.op("dve", lambda e: e.tensor_tensor(out=t1, in0=pBq[:, :, 0, :, :], in1=gre, op=ALU.mult), reads=["pB"] + gk, writes=[ks[0]])
                    S.op("dve", lambda e: e.tensor_tensor(out=t2, in0=pBq[:, :, 1, :, :], in1=gim, op=ALU.mult), reads=["pB"] + gk, writes=[ks[1]])
                    S.op("dve", lambda e: e.tensor_tensor(out=t3, in0=pBq[:, :, 0, :, :], in1=gim, op=ALU.mult), reads=["pB"] + gk, writes=[ks[2]])
                    S.op("dve", lambda e: e.tensor_tensor(out=t4, in0=pBq[:, :, 1, :, :], in1=gre, op=ALU.mult), reads=["pB"] + gk, writes=[ks[3]])
                    S.op("pool", lambda e: e.tensor_tensor(out=Yp[i][:, :, 0, :], in0=tt[ii][0][:], in1=tt[ii][1][:], op=ALU.subtract), reads=ks[0:2], writes=[("Yp", i, "re")])
                    S.op("pool", lambda e: e.tensor_tensor(out=Yp[i][:, :, 1, :], in0=tt[ii][2][:], in1=tt[ii][3][:], op=ALU.add), reads=ks[2:4], writes=[("Yp", i, "im")])
                    for ch in range(8):
                        S.op("pe", lambda e: e.matmul(pA3[:, ch, :], lhsT=Yp[i][:, ch, 0, :], rhs=Fci1, start=True, stop=False),
                             reads=[("Yp", i, "re"), ("Yp", i, "im"), "dft"], writes=["pA"])
                        S.op("pe", lambda e: e.matmul(pA3[:, ch, :], lhsT=Yp[i][:, ch, 1, :], rhs=Fci2, start=False, stop=True),
                             reads=[("Yp", i, "re"), ("Yp", i, "im"), "dft"], writes=["pA"])
                    cmul(Bp[i], ("Bp", i), pA3[:, :, 0:128], pA3[:, :, 128:256], ["pA"], Tre, Tim, ["twd"], True)
                    for q in range(2):
                        kk = [("Bp", i, "re"), ("Bp", i, "im"), "dft"]
                        S.op("pe", lambda e: e.matmul(pB[0:64, 512 * q:512 * q + 512], lhsT=Fre[:, 0:64], rhs=Bp[i][:, 4 * q:4 * q + 4, 0, :], start=True, stop=False),
                             reads=kk, writes=["pB"])
                        S.op("pe", lambda e: e.matmul(pB[0:64, 512 * q:512 * q + 512], lhsT=Fim[:, 0:64], rhs=Bp[i][:, 4 * q:4 * q + 4, 1, :], start=False, stop=True),
                             reads=kk, writes=["pB"])
                    e1, e2 = ep[i]
                    yv = pB[0:64, 0:1024].rearrange("p (c k) -> p c k", k=128)
                    skv = skb[:, o, ch0:ch0 + 8].unsqueeze(2).broadcast_to([64, 8, 128])
                    uu = src_t[:, ch0:ch0 + 8, :]
                    S.op("pool", lambda e: e.tensor_tensor(out=e1[:], in0=uu, in1=skv, op=ALU.mult), reads=skeys + ["skb"], writes=[("e1", i)])
                    S.op("dve", lambda e: e.scalar_tensor_tensor(out=e2[:], in0=yv, scalar=1.0 / N2L, in1=e1[:], op0=ALU.mult, op1=ALU.add),
                         reads=["pB", ("e1", i)], writes=[("e2", i)])
                    if o == 0:
                        S.op("pool", lambda e: e.tensor_tensor(out=zblk[:, ch0:ch0 + 8, :], in0=e2[:], in1=sig["x1"][:, ch0:ch0 + 8, :], op=ALU.mult),
                             reads=[("e2", i), ("sig", "x1")], writes=[("zblk", oc8)])
                    else:
                        S.op("pool", lambda e: e.tensor_tensor(out=e1[:], in0=e2[:], in1=sig["x2"][:, ch0:ch0 + 8, :], op=ALU.mult),
                             reads=[("e2", i), ("sig", "x2")], writes=[("e1", i)])
                        S.op("pool", lambda e: e.tensor_tensor(out=oblk[:, ch0:ch0 + 8, :], in0=e1[:], in1=sig["g"][:, ch0:ch0 + 8, :], op=ALU.mult),
                             reads=[("e1", i), ("sig", "g")], writes=[("oblk", oc8)])
            S.dma("pool", Sc["yhyT"][c0:c0 + CG, :].rearrange("c (a b) -> a c b", b=128), oblk[:], reads=[("oblk", q) for q in range(CG // 8)], writes=[("yhyT", cg)])


Prog.phaseHY = _phaseHY


NCORES = 8


def kernel(**inputs):
    P = Prog(nlayers=DEPTH, debug=False)
    nc = P.build()
    names = set(P.I.keys())
    x = np.asarray(inputs["x"], np.float32)
    Wj = []
    for j in range(SP):
        W = _prep_weights(inputs, j)
        W.update(_prep_mixer(inputs, j))
        Wj.append({k: v for k, v in W.items() if k in names})
    in_maps = []
    for c in range(NCORES):
        b, j = c // SP, c % SP
        m = dict(Wj[j])
        m["x"] = np.ascontiguousarray(x[b])
        m["xh"] = np.ascontiguousarray(x[b][:, OW * j:OW * j + OW])
        in_maps.append(m)
    res = run_bass_kernel_spmd(nc, in_maps, core_ids=list(range(NCORES)))
    out = np.empty((NCORES // SP, T, D), np.float32)
    for c in range(NCORES):
        b, j = c // SP, c % SP
        out[b][:, OW * j:OW * j + OW] = np.asarray(res.results[c]["out"], np.float32)
    return out
```

```python
import math
from contextlib import ExitStack

import numpy as np
import concourse.bass as bass
import concourse.mybir as mybir
from concourse.bass_utils import run_bass_kernel_spmd

F32 = mybir.dt.float32
BF16 = mybir.dt.bfloat16
AF = mybir.ActivationFunctionType
ALU = mybir.AluOpType

T = 8192
D = 1024
NT = 64
DEPTH = 2
EPS = 1e-6
SAME_ENGINE_SYNC = True
SAME_WAW = False
SAME_WAR = False

O_HYU, O_HYG, O_MBX, O_MBB, O_MBC, O_MBZ, O_MBDT = 0, 1536, 2048, 2560, 2816, 3072, 3584
O_MLQ, O_MLK, O_MLV, O_MLO, O_MLZ, O_MLG = 3600, 4112, 4624, 5136, 5648, 6160
O_NAQ, O_NAK, O_NAV, O_NAG = 6176, 6688, 7200, 7712

SP = 2
CW = 512 // SP
HC = CW
MBU = 8 // SP
MBG = 2 // SP
MLU = 4 // SP
NAH = 8 // SP
OW = 1024 // SP
YW = 3 * CW
NSM = 2 * MBU + 4 * MLU
NSP = 32
NB = CW // 128
FM_GROUPS = [
    ("hyu", O_HYU, 3 * NB, "conv"),
    ("hyg", O_HYG, NB, "silu"),
    ("mbx", O_MBX, NB, "convsilu"),
    ("mbB", O_MBB, MBG, "convsilu"),
    ("mbC", O_MBC, MBG, "convsilu"),
    ("mlq", O_MLQ, NB, "convsilu"),
    ("mlk", O_MLK, NB, "convsilu"),
    ("naq", O_NAQ, NB, "qknorm"),
    ("nak", O_NAK, NB, "qknorm"),
]
FM_BLOCKS = []
for _n, _o, _k, _t in FM_GROUPS:
    for _i in range(_k):
        FM_BLOCKS.append((_n, _i, _o + 128 * _i, _t))
NFM = len(FM_BLOCKS)
TM_PARTS = [("mbz", O_MBZ, "silu"), ("mlv", O_MLV, "copy"), ("mlo", O_MLO, "sigmoid"),
            ("mlz", O_MLZ, "silu"), ("nav", O_NAV, "copy"), ("nag", O_NAG, "silu")]
NTM = len(TM_PARTS) * CW // 512


import os
_SKIP = set(os.environ.get("KSKIP", "").split(","))
_UID = [0]


def _alloc(nc, kind, name, shape, dt):
    _UID[0] += 1
    nm = "%s_%d" % (name, _UID[0])
    if kind == "sbuf":
        return nc.sbuf_tensor(nm, shape, dt)
    return nc.psum_tensor(nm, shape, dt)


class Sched:
    def __init__(self, nc, stack, n_dma_sems=14):
        self.nc = nc
        self.eng = {"pe": nc.tensor, "dve": nc.vector, "act": nc.scalar, "pool": nc.gpsimd, "sp": nc.sync}
        self.sem = {e: stack.enter_context(nc.semaphore("c_" + e)) for e in self.eng}
        self.cnt = {e: 0 for e in self.eng}
        self.seen = {e: {} for e in self.eng}
        self.dq = {}
        for q in ("sp", "pool", "act"):
            self.dq[q] = [[stack.enter_context(nc.semaphore("d_%s%d" % (q, i))), 0] for i in range(n_dma_sems)]
        self.dqi = {q: 0 for q in self.dq}
        self.last_w = {}
        self.readers = {}
        self.n_wait = 0
        self.n_ins = 0
        self.ccs = []
        self.cc_toks = []

    def _wait(self, e, tok, raw=True):
        key, sem, val, prod = tok
        if prod == e and (e == "pe" or not SAME_ENGINE_SYNC or not raw):
            return
        if self.seen[e].get(key, 0) >= val:
            return
        self.eng[e].wait_ge(sem, val)
        self.seen[e][key] = val
        self.n_wait += 1

    def _deps(self, e, reads, writes):
        for k in reads:
            t = self.last_w.get(k)
            if t is not None:
                self._wait(e, t, True)
        for k in writes:
            t = self.last_w.get(k)
            if t is not None:
                self._wait(e, t, SAME_WAW)
            for t in self.readers.get(k, ()):
                self._wait(e, t, SAME_WAR)

    def _commit(self, tok, reads, writes):
        for k in writes:
            self.last_w[k] = tok
            self.readers[k] = []
        for k in reads:
            if k in writes:
                continue
            lst = self.readers.setdefault(k, [])
            lst.append(tok)
            if len(lst) > 48:
                best = {}
                for t in lst:
                    if t[0] not in best or best[t[0]][2] < t[2]:
                        best[t[0]] = t
                self.readers[k] = list(best.values())

    def op(self, e, fn, reads=(), writes=()):
        self._deps(e, reads, writes)
        ins = fn(self.eng[e])
        self.cnt[e] += 1
        ins.then_inc(self.sem[e], 1)
        tok = ("c_" + e, self.sem[e], self.cnt[e], e)
        self._commit(tok, reads, writes)
        self.n_ins += 1
        return ins

    def dma(self, q, out, in_, reads=(), writes=(), **kw):
        self._deps(q, reads, writes)
        idx = self.dqi[q]
        slot = self.dq[q][idx]
        self.dqi[q] = (idx + 1) % len(self.dq[q])
        sem, val = slot
        key = "d_%s%d" % (q, idx)
        if val > 0 and self.seen[q].get(key, 0) < val:
            self.eng[q].wait_ge(sem, val)
            self.seen[q][key] = val
        ins = self.eng[q].dma_start(out=out, in_=in_, **kw)
        ins.then_inc(sem, 16)
        slot[1] = val + 16
        tok = (key, sem, val + 16, None)
        self._commit(tok, reads, writes)
        self.n_ins += 1
        return ins

    def collective(self, src, dst, tn, stack, rpc):
        self._deps("pool", [src], [dst])
        groups = [[SP * g + r for r in range(SP)] for g in range(8 // SP)]
        rows = tn[src].shape[0]
        toks = []
        for q in range(rows // rpc):
            sem = stack.enter_context(self.nc.semaphore("cc%d" % len(self.ccs)))
            self.ccs.append(sem)
            ins = self.nc.gpsimd.collective_compute(
                "AllGather", ALU.bypass, replica_groups=groups,
                ins=[tn[src].ap()[rpc * q:rpc * q + rpc, :].opt()],
                outs=[tn[dst].ap()[SP * rpc * q:SP * rpc * q + SP * rpc, :].opt()])
            ins.then_inc(sem)
            tok = ("cc%d" % (len(self.ccs) - 1), sem, 1, None)
            self.eng["pool"].wait_ge(sem, 1)
            self.seen["pool"][tok[0]] = 1
            toks.append(tok)
            self.cc_toks.append(tok)
        self._commit(toks[-1], [src], [dst])

    def barrier(self):
        for e in self.eng:
            for t in self.cc_toks:
                self._wait(e, t)
        for e in self.eng:
            for p in self.eng:
                if p != e and self.cnt[p] > 0:
                    self._wait(e, ("c_" + p, self.sem[p], self.cnt[p], p))
                elif p == e and self.cnt[p] > 0 and e != "pe":
                    self._wait(e, ("c_" + p, self.sem[p], self.cnt[p], None))
            for q in self.dq:
                for i, (sem, val) in enumerate(self.dq[q]):
                    if val > 0:
                        self._wait(e, ("d_%s%d" % (q, i), sem, val, None))
        self.last_w = {}
        self.readers = {}


class _KeyNS:
    def __init__(self, S, prefix):
        self.S = S
        self.p = prefix

    def _k(self, keys):
        return [(self.p, k) for k in keys]

    def op(self, e, fn, reads=(), writes=()):
        return self.S.op(e, fn, self._k(reads), self._k(writes))

    def dma(self, q, out, in_, reads=(), writes=(), **kw):
        return self.S.dma(q, out, in_, self._k(reads), self._k(writes), **kw)


class Prog:
    def __init__(self, nlayers=DEPTH, debug=False, phases=("A", "HY", "NA", "DLA", "Z", "G")):
        self.nlayers = nlayers
        self.debug = debug
        self.phases = phases

    def dram_in(self, name, shape, dt=F32):
        return self.nc.dram_tensor(name, list(shape), dt, kind="ExternalInput").ap()

    def dram_scr(self, name, shape, dt=BF16, dbg=False):
        kind = "ExternalOutput" if (dbg and self.debug) else "Internal"
        if name in getattr(self, "feed", ()):
            kind = "ExternalInput"
        return self.nc.dram_tensor(name, list(shape), dt, kind=kind).ap()

    def build(self):
        nc = bass.Bass("TRN2", target_bir_lowering=False)
        self.nc = nc
        L = DEPTH
        I = self.I = {}
        I["x"] = self.dram_in("x", [T, D])
        I["norm_w"] = self.dram_in("norm_w", [L, 1, D])
        I["wfm"] = self.dram_in("wfm", [L, NFM, 128, 1024])
        I["wsm"] = self.dram_in("wsm", [L, 128, 8 * NSP])
        I["xh"] = self.dram_in("xh", [T, OW])
        I["wtm"] = self.dram_in("wtm", [L, NTM, 128, 4096])
        I["wout"] = self.dram_in("wout", [L, 128, 16 * OW])
        I["convp"] = self.dram_in("convp", [L, NFM, 128, 4])
        I["ident"] = self.dram_in("ident", [128, 128])
        I["blockones"] = self.dram_in("blockones", [128, 128])
        self.declare_mixer_inputs()
        self.out = nc.dram_tensor("out", [T, OW], F32, kind="ExternalOutput").ap()
        Sc = self.Sc = {}
        for n, rows in [("hyuT", 3 * CW), ("hygT", CW), ("mbBT", 128 * MBG), ("mbCT", 128 * MBG), ("mlqT", CW),
                        ("mlkT", CW), ("naqT", CW), ("nakT", CW)]:
            Sc[n] = self.dram_scr(n, [rows, T], dbg=True)
        for n, cols in [("mbx", CW), ("mbB", 128 * MBG), ("mlk", CW), ("mbz", CW), ("mlv", CW), ("mlo", CW),
                        ("mlz", CW), ("nav", CW), ("nag", CW)]:
            Sc[n] = self.dram_scr(n, [T, cols], dbg=True)
        Sc["smallT"] = self.dram_scr("smallT", [NSP, T], F32, dbg=True)
        mbxB = self.dram_scr("mbxB", [T, CW + 128 * MBG])
        Sc["mbx"], Sc["mbB"], Sc["mbxB"] = mbxB[:, 0:CW], mbxB[:, CW:CW + 128 * MBG], mbxB
        mbBCT = self.dram_scr("mbBCT", [2 * 128 * MBG, T])
        Sc["mbBT"], Sc["mbCT"], Sc["mbBCT"] = mbBCT[0:128 * MBG, :], mbBCT[128 * MBG:2 * 128 * MBG, :], mbBCT
        mlkqT = self.dram_scr("mlkqT", [2 * CW, T])
        Sc["mlkT"], Sc["mlqT"], Sc["mlkqT"] = mlkqT[0:CW, :], mlkqT[CW:2 * CW, :], mlkqT
        self.Tn = {}
        for n, shp, dt in [("ytm", [T, YW], BF16), ("yhyT", [HC, T], BF16), ("xres", [T, OW], F32),
                           ("ytm_g", [SP * T, YW], BF16), ("yhy_g", [SP * HC, T], BF16), ("xg", [SP * T, OW], F32)]:
            if self.debug and n in ("ytm", "yhyT"):
                self.Tn[n] = nc.dram_tensor(n, shp, dt, kind="ExternalOutput")
            elif n in getattr(self, "feed", ()):
                self.Tn[n] = nc.dram_tensor(n, shp, dt, kind="ExternalInput")
            else:
                self.Tn[n] = nc.dram_tensor(n, shp, dt)
            Sc[n] = self.Tn[n].ap()
        self.declare_mixer_scratch()

        with ExitStack() as st:
            self.S = Sched(nc, st)
            S = self.S
            self.ident_f = st.enter_context(_alloc(nc, "sbuf", "ident_f", [128, 128], F32))
            self.ident_b = st.enter_context(_alloc(nc, "sbuf", "ident_b", [128, 128], BF16))
            self.bones_b = st.enter_context(_alloc(nc, "sbuf", "bones_b", [128, 128], BF16))
            tmpc = st.enter_context(_alloc(nc, "sbuf", "tmpc", [128, 128], F32))
            S.dma("sp", self.ident_f[:], I["ident"], writes=["ident_f"])
            S.dma("sp", tmpc[:], I["blockones"], writes=["tmpc"])
            S.op("dve", lambda e: e.tensor_copy(out=self.ident_b[:], in_=self.ident_f[:]), reads=["ident_f"], writes=["ident_b"])
            S.op("dve", lambda e: e.tensor_copy(out=self.bones_b[:], in_=tmpc[:]), reads=["tmpc"], writes=["bones_b"])
            S.barrier()
            for l in range(self.nlayers):
                x_dst = self.out if l == self.nlayers - 1 else Sc["xres"]
                x_res = I["xh"] if l == 0 else Sc["xres"]
                self.marks = getattr(self, "marks", [])
                mark = lambda nm: self.marks.append((nm, l, dict(S.cnt)))
                mark("start")
                if "A" in self.phases:
                    self.phaseA(l)
                    S.barrier()
                    mark("A")
                if "HY" in self.phases:
                    self.phaseHY(l)
                    S.barrier()
                    mark("HY")
                if "NA" in self.phases:
                    self.phaseNA(l)
                    S.barrier()
                    mark("NA")
                if "DLA" in self.phases:
                    self.phaseDLA(l)
                    S.barrier()
                    mark("DLA")
                if "Z" in self.phases:
                    if SP > 1 and "G" in self.phases:
                        S.collective("ytm", "ytm_g", self.Tn, st, 1024)
                        S.collective("yhyT", "yhy_g", self.Tn, st, 64)
                        S.barrier()
                    self.phaseZ(l, x_res, x_dst)
                    S.barrier()
                    if SP > 1 and l < self.nlayers - 1 and "G" in self.phases:
                        S.collective("xres", "xg", self.Tn, st, 1024)
                        S.barrier()
                    mark("Z")
            S.barrier()
        return nc

    def declare_mixer_inputs(self):
        _na_declare_inputs(self)

    def declare_mixer_scratch(self):
        _dla_declare(self)
        _hy_declare(self)

    def phaseA(self, l):
        nc, S, I, Sc = self.nc, self.S, self.I, self.Sc
        with ExitStack() as st:
            sb = lambda n, s, d=F32: st.enter_context(_alloc(nc, "sbuf", n, list(s), d))
            ps = lambda n, s, d=F32: st.enter_context(_alloc(nc, "psum", n, list(s), d))
            hT = sb("hT", [128, 8, T], BF16)
            with ExitStack() as st0:
                sb0 = lambda n, s, d=F32: st0.enter_context(_alloc(nc, "sbuf", n, list(s), d))
                nwb = sb0("nwb", [128, D])
                S.dma("sp", nwb[:], I["norm_w"][l].broadcast_to([128, D]), writes=["nwb"])
                xt = [sb0("xt%d" % i, [128, D]) for i in range(2)]
                junk = sb0("junk", [128, D], BF16)
                ss = [sb0("ss%d" % i, [128, 1]) for i in range(2)]
                xn = [sb0("xn%d" % i, [128, D], BF16) for i in range(2)]
                pt = [st0.enter_context(_alloc(nc, "psum", "pt%d" % i, [128, 512], BF16)) for i in range(2)]
                for i in range(NT):
                    b = i % 2
                    if l == 0 or SP == 1:
                        S.dma("sp", xt[b][:], I["x"][128 * i:128 * i + 128, :], writes=[("xt", b)])
                    else:
                        S.dma("sp", xt[b][:].rearrange("p (r c) -> p r c", r=SP),
                              Sc["xg"].rearrange("(q r t) c -> q t r c", q=8, r=SP)[i // 8][128 * (i % 8):128 * (i % 8) + 128, :, :], writes=[("xt", b)])
                    S.op("act", lambda e: e.activation(out=junk[:], in_=xt[b][:], func=AF.Square, accum_out=ss[b][:]),
                         reads=[("xt", b)], writes=["junk", ("ss", b)])
                    S.op("dve", lambda e: e.tensor_scalar(out=ss[b][:], in0=ss[b][:], scalar1=1.0 / D, scalar2=EPS,
                                                          op0=ALU.mult, op1=ALU.add), reads=[("ss", b)], writes=[("ss", b)])
                    S.op("act", lambda e: e.activation(out=ss[b][:], in_=ss[b][:], func=AF.Sqrt), reads=[("ss", b)], writes=[("ss", b)])
                    S.op("dve", lambda e: e.reciprocal(out=ss[b][:], in_=ss[b][:]), reads=[("ss", b)], writes=[("ss", b)])
                    S.op("dve", lambda e: e.scalar_tensor_tensor(out=xn[b][:], in0=xt[b][:], scalar=ss[b][:], in1=nwb[:],
                                                                 op0=ALU.mult, op1=ALU.mult),
                         reads=[("xt", b), ("ss", b), "nwb"], writes=[("xn", b)])
                    for h in range(2):
                        for k in range(4):
                            kc = 4 * h + k
                            S.op("pe", lambda e: e.transpose(pt[h][:, 128 * k:128 * k + 128], xn[b][:, 128 * kc:128 * kc + 128], self.ident_b[:]),
                                 reads=[("xn", b)], writes=[("pt", h)])
                        dst = hT[:, 4 * h:4 * h + 4, 128 * i:128 * i + 128]
                        src = pt[h][:].rearrange("p (k t) -> p k t", t=128)
                        if h == 0:
                            S.op("act", lambda e: e.activation(out=dst, in_=src, func=AF.Copy), reads=[("pt", h)], writes=[("hT", i)])
                        else:
                            S.op("dve", lambda e: e.tensor_copy(out=dst, in_=src), reads=[("pt", h)], writes=[("hT", i)])
            S.barrier()
            hkeys = [("hT", i) for i in range(NT)]
            with ExitStack() as st1:
                sb1 = lambda n, s, d=F32: st1.enter_context(_alloc(nc, "sbuf", n, list(s), d))
                wst = [sb1("wst%d" % i, [128, 1024]) for i in range(2)]
                wb = [sb1("wb%d" % i, [128, 8, 128], BF16) for i in range(2)]
                row = [sb1("row%d" % i, [128, T + 2], BF16) for i in range(2)]
                acc = [sb1("acc%d" % i, [128, 1024]) for i in range(2)]
                ob = [sb1("ob%d" % i, [128, 1024], BF16) for i in range(2)]
                tms = [sb1("tms%d" % i, [128, 8, 128], BF16) for i in range(2)]
                sq = [sb1("sq%d" % i, [128, 512], BF16) for i in range(2)]
                rt = [sb1("rt%d" % i, [128, 512]) for i in range(2)]
                cp = [sb1("cp%d" % i, [128, 4]) for i in range(2)]
                pm = [st1.enter_context(_alloc(nc, "psum", "pm%d" % i, [128, 512], F32)) for i in range(4)]
                ptr = [st1.enter_context(_alloc(nc, "psum", "ptr%d" % i, [128, 512], BF16)) for i in range(2)]
                pss = [st1.enter_context(_alloc(nc, "psum", "pss%d" % i, [128, 512], F32)) for i in range(2)]
                for b in range(2):
                    S.op("pool", lambda e: e.memset(row[b][:, 0:1], 0.0), writes=[("rowh", b)])
                    S.op("pool", lambda e: e.memset(row[b][:, T + 1:T + 2], 0.0), writes=[("rowh", b)])
                cnt = {"ev": 0, "tr": 0, "ob": 0, "ac": 0, "q": 0}

                def mm_block(wtile, wkey, M, dst_fn, dst_keys_fn):
                    for i in range(16):
                        p = pm[i % 4]
                        for kc in range(8):
                            S.op("pe", lambda e: e.matmul(p[0:M, :], lhsT=wtile[:, kc, 0:M], rhs=hT[:, kc, 512 * i:512 * i + 512],
                                                          start=(kc == 0), stop=(kc == 7)),
                                 reads=[wkey] + hkeys[4 * i:4 * i + 4], writes=[("pm", i % 4)])
                        dst = dst_fn(i)
                        if cnt["ev"] % 2 == 0:
                            S.op("act", lambda e: e.activation(out=dst, in_=p[0:M, :], func=AF.Copy), reads=[("pm", i % 4)], writes=dst_keys_fn(i))
                        else:
                            S.op("dve", lambda e: e.tensor_copy(out=dst, in_=p[0:M, :]), reads=[("pm", i % 4)], writes=dst_keys_fn(i))
                        cnt["ev"] += 1

                for bi, (gname, gi, col0, ptype) in enumerate(FM_BLOCKS):
                    if gname in _SKIP or "fm" in _SKIP:
                        continue
                    b = bi % 2
                    S.dma("sp", wst[b][:], I["wfm"][l, bi], writes=[("wst", b)])
                    S.op("pool", lambda e: e.tensor_copy(out=wb[b][:].rearrange("p k c -> p (k c)"), in_=wst[b][:]),
                         reads=[("wst", b)], writes=[("wtile", b)])
                    if ptype in ("conv", "convsilu"):
                        S.dma("sp", cp[b][:], I["convp"][l, bi], writes=[("cp", b)])
                    elif ptype == "qknorm":
                        S.dma("sp", cp[b][:], I["convp"][l, bi], writes=[("cp", b)])
                    rw = row[b]
                    mm_block(wb[b], ("wtile", b), 128, lambda i: rw[:, 1 + 512 * i:1 + 512 * i + 512], lambda i: [("row", b, i)])
                    rkeys = [("row", b, i) for i in range(16)] + [("rowh", b)]
                    r0 = 128 * gi
                    if ptype in ("conv", "convsilu", "silu"):
                        for j in range(8):
                            a = acc[cnt["ac"] % 2]; ak = ("acc", cnt["ac"] % 2); cnt["ac"] += 1
                            o = ob[cnt["ob"] % 2]; okey = ("ob", cnt["ob"] % 2); cnt["ob"] += 1
                            rk = [("row", b, i) for i in range(max(0, 2 * j - 1), min(16, 2 * j + 3))] + [("rowh", b)]
                            c0 = 1024 * j
                            if ptype == "silu":
                                S.op("act", lambda e: e.activation(out=o[:], in_=rw[:, 1 + c0:1 + c0 + 1024], func=AF.Silu), reads=rk, writes=[okey])
                            else:
                                S.op("act", lambda e: e.activation(out=a[:], in_=rw[:, 1 + c0:1 + c0 + 1024], func=AF.Identity,
                                                                   scale=cp[b][:, 1:2], bias=cp[b][:, 3:4]), reads=rk + [("cp", b)], writes=[ak])
                                S.op("dve", lambda e: e.scalar_tensor_tensor(out=a[:], in0=rw[:, c0:c0 + 1024], scalar=cp[b][:, 0:1], in1=a[:],
                                                                             op0=ALU.mult, op1=ALU.add), reads=rk + [("cp", b), ak], writes=[ak])
                                S.op("dve", lambda e: e.scalar_tensor_tensor(out=a[:], in0=rw[:, 2 + c0:2 + c0 + 1024], scalar=cp[b][:, 2:3], in1=a[:],
                                                                             op0=ALU.mult, op1=ALU.add), reads=rk + [("cp", b), ak], writes=[ak])
                                if ptype == "convsilu":
                                    S.op("act", lambda e: e.activation(out=o[:], in_=a[:], func=AF.Silu), reads=[ak], writes=[okey])
                                else:
                                    S.op("pool", lambda e: e.tensor_copy(out=o[:], in_=a[:]), reads=[ak], writes=[okey])
                            fm_dst = {"hyu": "hyuT", "hyg": "hygT", "mbB": "mbBT", "mbC": "mbCT", "mlq": "mlqT", "mlk": "mlkT"}.get(gname)
                            if fm_dst is not None:
                                S.dma("pool", Sc[fm_dst][r0:r0 + 128, c0:c0 + 1024], o[:], reads=[okey], writes=[(fm_dst, gi, j)])
                            tm_dst = {"mbx": "mbx", "mbB": "mbB", "mlk": "mlk"}.get(gname)
                            if tm_dst is not None:
                                tmb = cnt["tr"] % 2; cnt["tr"] += 1
                                for h in range(2):
                                    for k in range(4):
                                        s = 4 * h + k
                                        S.op("pe", lambda e: e.transpose(ptr[h][:, 128 * k:128 * k + 128], o[:, 128 * s:128 * s + 128], self.ident_b[:]),
                                             reads=[okey], writes=[("ptr", h)])
                                    src = ptr[h][:].rearrange("p (k c) -> p k c", c=128)
                                    if h == 0:
                                        S.op("act", lambda e: e.activation(out=tms[tmb][:, 0:4, :], in_=src, func=AF.Copy), reads=[("ptr", h)], writes=[("tms", tmb, 0)])
                                    else:
                                        S.op("dve", lambda e: e.tensor_copy(out=tms[tmb][:, 4:8, :], in_=src), reads=[("ptr", h)], writes=[("tms", tmb, 1)])
                                d = Sc[tm_dst][c0:c0 + 1024, r0:r0 + 128].rearrange("(s p) c -> p s c", p=128)
                                S.dma("pool", d, tms[tmb][:], reads=[("tms", tmb, 0), ("tms", tmb, 1)], writes=[(tm_dst, gi, j)])
                    elif ptype == "qknorm":
                        fm_dst = {"naq": "naqT", "nak": "nakT"}[gname]
                        for j in range(8):
                            o = ob[cnt["ob"] % 2]; okey = ("ob", cnt["ob"] % 2); cnt["ob"] += 1
                            for hh in range(2):
                                q = cnt["q"] % 2; cnt["q"] += 1
                                c0 = 1024 * j + 512 * hh
                                rk = [("row", b, 2 * j + hh)]
                                S.op("act", lambda e: e.activation(out=sq[q][:], in_=rw[:, 1 + c0:1 + c0 + 512], func=AF.Square), reads=rk, writes=[("sq", q)])
                                S.op("pe", lambda e: e.matmul(pss[q][:], lhsT=self.bones_b[:], rhs=sq[q][:], start=True, stop=True),
                                     reads=[("sq", q)], writes=[("pss", q)])
                                S.op("dve", lambda e: e.tensor_scalar(out=rt[q][:], in0=pss[q][:], scalar1=1.0 / 64, scalar2=EPS, op0=ALU.mult, op1=ALU.add),
                                     reads=[("pss", q)], writes=[("rt", q)])
                                S.op("act", lambda e: e.activation(out=rt[q][:], in_=rt[q][:], func=AF.Sqrt), reads=[("rt", q)], writes=[("rt", q)])
                                S.op("dve", lambda e: e.reciprocal(out=rt[q][:], in_=rt[q][:]), reads=[("rt", q)], writes=[("rt", q)])
                                S.op("dve", lambda e: e.scalar_tensor_tensor(out=o[:, 512 * hh:512 * hh + 512], in0=rw[:, 1 + c0:1 + c0 + 512], scalar=cp[b][:, 0:1],
                                                                             in1=rt[q][:], op0=ALU.mult, op1=ALU.mult),
                                     reads=rk + [("rt", q), ("cp", b)], writes=[okey])
                            S.dma("pool", Sc[fm_dst][r0:r0 + 128, 1024 * j:1024 * j + 1024], o[:], reads=[okey], writes=[(fm_dst, gi, j)])
            S.barrier()
            with ExitStack() as st3:
                sb3 = lambda n, s, d=F32: st3.enter_context(_alloc(nc, "sbuf", n, list(s), d))
                smallsb = sb3("smallsb", [NSP, T])
                wsst = sb3("wsst", [128, 8 * NSP])
                wsb = sb3("wsb", [128, 8, NSP], BF16)
                pm3 = [st3.enter_context(_alloc(nc, "psum", "pm3%d" % i, [128, 512], F32)) for i in range(4)]
                S.dma("sp", wsst[:], I["wsm"][l], writes=["wsst"])
                S.op("pool", lambda e: e.tensor_copy(out=wsb[:].rearrange("p k c -> p (k c)"), in_=wsst[:]), reads=["wsst"], writes=["wsb"])
                for i in range(16 if "small" not in _SKIP else 0):
                    p = pm3[i % 4]
                    for kc in range(8):
                        S.op("pe", lambda e: e.matmul(p[0:NSP, :], lhsT=wsb[:, kc, :], rhs=hT[:, kc, 512 * i:512 * i + 512], start=(kc == 0), stop=(kc == 7)),
                             reads=["wsb"] + hkeys[4 * i:4 * i + 4], writes=[("pm3", i % 4)])
                    S.op("act", lambda e: e.activation(out=smallsb[:, 512 * i:512 * i + 512], in_=p[0:NSP, :], func=AF.Copy), reads=[("pm3", i % 4)], writes=[("smallsb", i)])
                S.dma("pool", Sc["smallT"], smallsb[:], reads=[("smallsb", i) for i in range(16)], writes=["smallT"])
            S.barrier()
            with ExitStack() as st2:
                sb2 = lambda n, s, d=F32: st2.enter_context(_alloc(nc, "sbuf", n, list(s), d))
                wst2 = sb2("wst2", [128, 4096])
                wtb = [sb2("wtb%d" % i, [128, 8, 512], BF16) for i in range(2)]
                ot = [sb2("ot%d" % i, [128, 512], BF16) for i in range(4)]
                pm2 = [st2.enter_context(_alloc(nc, "psum", "pm2%d" % i, [128, 512], F32)) for i in range(4)]
                ppg = 512 // CW
                for g in range(NTM if "tm" not in _SKIP else 0):
                    b = g % 2
                    parts = TM_PARTS[ppg * g:ppg * g + ppg]
                    S.dma("sp", wst2[:], I["wtm"][l, g], writes=["wst2"])
                    S.op("pool", lambda e: e.tensor_copy(out=wtb[b][:].rearrange("p k c -> p (k c)"), in_=wst2[:]), reads=["wst2"], writes=[("wtb", b)])
                    for i in range(NT):
                        p = pm2[i % 4]
                        for kc in range(8):
                            S.op("pe", lambda e: e.matmul(p[:], lhsT=hT[:, kc, 128 * i:128 * i + 128], rhs=wtb[b][:, kc, :], start=(kc == 0), stop=(kc == 7)),
                                 reads=[("wtb", b), ("hT", i)], writes=[("pm2", i % 4)])
                        o = ot[i % 4]
                        for pi, (gname, col0, act) in enumerate(parts):
                            func = {"silu": AF.Silu, "copy": AF.Copy, "sigmoid": AF.Sigmoid}[act]
                            sl = slice(CW * pi, CW * pi + CW)
                            S.op("act", lambda e: e.activation(out=o[:, sl], in_=p[:, sl], func=func), reads=[("pm2", i % 4)], writes=[("ot", i % 4, pi)])
                            S.dma("pool" if (i + pi) % 2 else "sp", Sc[gname][128 * i:128 * i + 128, :], o[:, sl], reads=[("ot", i % 4, pi)], writes=[(gname, i)])

    def phaseZ(self, l, x_res, x_dst):
        nc, S, I, Sc = self.nc, self.S, self.I, self.Sc
        gathered = SP > 1 and "G" in self.phases
        with ExitStack() as st:
            sb = lambda n, s, d=F32: st.enter_context(_alloc(nc, "sbuf", n, list(s), d))
            wo = sb("wo", [128, 16, OW], BF16)
            wos = [sb("wos%d" % i, [128, 2048]) for i in range(2)]
            nck = 16 * OW // 2048
            kpc = 2048 // OW
            for c in range(nck):
                S.dma("sp", wos[c % 2][:], I["wout"][l, :, 2048 * c:2048 * c + 2048], writes=[("wos", c % 2)])
                S.op("pool", lambda e: e.tensor_copy(out=wo[:, kpc * c:kpc * c + kpc, :].rearrange("p k c -> p (k c)"), in_=wos[c % 2][:]),
                     reads=[("wos", c % 2)], writes=["wo"])
            yt = [sb("yt%d" % i, [128, SP, YW], BF16) for i in range(2)]
            yT = [sb("yT%d" % i, [128, 16, 128], BF16) for i in range(2)]
            xt = [sb("xz%d" % i, [128, OW]) for i in range(2)]
            oz = [sb("oz%d" % i, [128, OW]) for i in range(2)]
            ptz = [st.enter_context(_alloc(nc, "psum", "ptz%d" % i, [128, 512], BF16)) for i in range(3)]
            pz = [st.enter_context(_alloc(nc, "psum", "pz%d" % i, [128, 512], F32)) for i in range(4)]
            ysrc = Sc["ytm_g"] if gathered else Sc["ytm"]
            hsrc = Sc["yhy_g"] if gathered else Sc["yhyT"]
            nr = SP if gathered else 1
            cpr = YW // 128
            for i in range(NT):
                b = i % 2
                S.dma("sp", yt[b][:, 0:nr, :], ysrc.rearrange("(q r t) c -> q t r c", q=8, r=nr)[i // 8][128 * (i % 8):128 * (i % 8) + 128, :, :], writes=[("yt", b)])
                S.dma("sp", xt[b][:], x_res[128 * i:128 * i + 128, :], writes=[("xz", b)])
                S.dma("pool", yT[b][:, 0:4 * nr // SP, :], hsrc[:, 128 * i:128 * i + 128].rearrange("(k p) t -> p k t", p=128), writes=[("yTh", b)])
                for h in range(3):
                    for k in range(4):
                        q = 4 * h + k
                        r, cc = q // cpr, q % cpr
                        S.op("pe", lambda e: e.transpose(ptz[h][:, 128 * k:128 * k + 128], yt[b][:, r, 128 * cc:128 * cc + 128], self.ident_b[:]),
                             reads=[("yt", b)], writes=[("ptz", h)])
                    src = ptz[h][:].rearrange("p (k c) -> p k c", c=128)
                    dst = yT[b][:, 4 + 4 * h:8 + 4 * h, :]
                    if h == 1:
                        S.op("dve", lambda e: e.tensor_copy(out=dst, in_=src), reads=[("ptz", h)], writes=[("yTt", b, h)])
                    else:
                        S.op("act", lambda e: e.activation(out=dst, in_=src, func=AF.Copy), reads=[("ptz", h)], writes=[("yTt", b, h)])
                for half in range(OW // 512):
                    p = pz[(2 * i + half) % 4]
                    pk = ("pz", (2 * i + half) % 4)
                    for kc in range(16):
                        S.op("pe", lambda e: e.matmul(p[:], lhsT=yT[b][:, kc, :], rhs=wo[:, kc, 512 * half:512 * half + 512], start=(kc == 0), stop=(kc == 15)),
                             reads=["wo", ("yTh", b)] + [("yTt", b, h) for h in range(3)], writes=[pk])
                    S.op("dve", lambda e: e.tensor_tensor(out=oz[b][:, 512 * half:512 * half + 512], in0=p[:], in1=xt[b][:, 512 * half:512 * half + 512], op=ALU.add),
                         reads=[pk, ("xz", b)], writes=[("oz", b, half)])
                S.dma("pool", x_dst[128 * i:128 * i + 128, :], oz[b][:], reads=[("oz", b, h2) for h2 in range(OW // 512)], writes=[("xdst", i)])


def _fm_col0(gname, gi, j):
    if gname == "hyu":
        part, sub = gi // NB, gi % NB
        return O_HYU + 512 * part + CW * j + 128 * sub
    base = {"hyg": O_HYG, "mbx": O_MBX, "mlq": O_MLQ, "mlk": O_MLK, "naq": O_NAQ, "nak": O_NAK}.get(gname)
    if base is not None:
        return base + CW * j + 128 * gi
    return {"mbB": O_MBB, "mbC": O_MBC}[gname] + 128 * (MBG * j + gi)


def _small_cols(j):
    cols = []
    for d in range(2):
        cols += [O_MBDT + 8 * d + MBU * j + u for u in range(MBU)]
    for d in range(2):
        for g in range(2):
            cols += [O_MLG + 8 * d + 4 * g + MLU * j + u for u in range(MLU)]
    return cols


def _prep_weights(inp, j):
    L = DEPTH
    w_in = np.asarray(inp["w_in"], np.float32)
    w_out = np.asarray(inp["w_out"], np.float32)
    W = {}
    wfm = np.empty((L, NFM, 128, 1024), np.float32)
    convp = np.zeros((L, NFM, 128, 4), np.float32)
    for bi, (gname, gi, _c, ptype) in enumerate(FM_BLOCKS):
        col0 = _fm_col0(gname, gi, j)
        blk = w_in[:, :, col0:col0 + 128]
        wfm[:, bi] = blk.reshape(L, 8, 128, 128).transpose(0, 2, 1, 3).reshape(L, 128, 1024)
        if ptype in ("conv", "convsilu"):
            if gname == "hyu":
                cw, cb, c0 = inp["hy_conv_w"], inp["hy_conv_b"], col0 - O_HYU
            elif gname in ("mbx", "mbB", "mbC"):
                cw, cb, c0 = inp["mb_conv_w"], inp["mb_conv_b"], col0 - O_MBX
            else:
                cw, cb, c0 = inp["ml_conv_w"], inp["ml_conv_b"], col0 - O_MLQ
            convp[:, bi, :, 0:3] = np.asarray(cw)[:, :, c0:c0 + 128].transpose(0, 2, 1)
            convp[:, bi, :, 3] = np.asarray(cb)[:, c0:c0 + 128]
        elif ptype == "qknorm":
            nw = np.asarray(inp["na_qnorm_w"] if gname == "naq" else inp["na_knorm_w"])
            convp[:, bi, :, 0] = np.tile(nw, (1, 2))
    W["wfm"] = wfm
    W["convp"] = convp
    scols = _small_cols(j)
    wsm = np.zeros((L, 8, 128, NSP), np.float32)
    wsm[:, :, :, :NSM] = w_in[:, :, scols].reshape(L, 8, 128, NSM)
    W["wsm"] = np.ascontiguousarray(wsm.transpose(0, 2, 1, 3).reshape(L, 128, 8 * NSP))
    wtm = np.empty((L, NTM, 128, 4096), np.float32)
    ppg = 512 // CW
    for g in range(NTM):
        cols = []
        for (gname, col0, act) in TM_PARTS[ppg * g:ppg * g + ppg]:
            cols += list(range(col0 + CW * j, col0 + CW * j + CW))
        wtm[:, g] = w_in[:, :, cols].reshape(L, 8, 128, 512).transpose(0, 2, 1, 3).reshape(L, 128, 4096)
    W["wtm"] = wtm
    rows = []
    for q in range(HC // 64):
        for r in range(SP):
            rows += list(range(HC * r + 64 * q, HC * r + 64 * q + 64))
    for r in range(SP):
        for base in (512, 1024, 1536):
            rows += list(range(base + CW * r, base + CW * r + CW))
    wo = w_out[:, rows, OW * j:OW * j + OW]
    W["wout"] = np.ascontiguousarray(wo.reshape(L, 16, 128, OW).transpose(0, 2, 1, 3).reshape(L, 128, 16 * OW))
    W["norm_w"] = np.asarray(inp["norm_w"], np.float32).reshape(L, 1, D)
    W["ident"] = np.eye(128, dtype=np.float32)
    bo = np.zeros((128, 128), np.float32)
    bo[:64, :64] = 1.0
    bo[64:, 64:] = 1.0
    W["blockones"] = bo
    return W


def _prep_na(inp, j):
    jc = j
    L = DEPTH
    rpb = np.asarray(inp["na_rpb"], np.float32)
    kk = np.arange(128)
    il, kc = kk // 64, kk % 64
    w = np.arange(64)
    cs = np.clip(w - 8, 0, 48)
    valid = (kc[:, None] >= cs[None, :]) & (kc[:, None] < cs[None, :] + 16)
    coff = np.clip(kc[:, None] - w[None, :] + 15, 0, 30)
    bias = np.zeros((L, 8, 128, 8, 4, 64), np.float32)
    for v in range(8):
        for j in range(4):
            i = 2 * j + il
            roff = v + i
            bias[:, :, :, v, j, :] = rpb[:, :, roff[:, None], coff]
    mask = np.broadcast_to(valid[:, None, :], (128, 4, 64)).astype(np.float32).reshape(128, 256)
    return {"na_bias": np.ascontiguousarray(bias.reshape(L, 8, 128, 2048)[:, NAH * jc:NAH * jc + NAH]), "na_mask": np.ascontiguousarray(mask)}


def _na_declare_inputs(self):
    self.I["na_bias"] = self.dram_in("na_bias", [DEPTH, NAH, 128, 2048])
    self.I["na_mask"] = self.dram_in("na_mask", [128, 256])


def _phaseNA(self, l):
    nc, S, I, Sc = self.nc, self.S, self.I, self.Sc
    NH = NAH
    with ExitStack() as st:
        sb = lambda n, s, d=F32: st.enter_context(_alloc(nc, "sbuf", n, list(s), d))
        KT = [sb("naKT%d" % i, [64, T], BF16) for i in range(2)]
        QT = [sb("naQT%d" % i, [64, T], BF16) for i in range(2)]
        Ve = [sb("naVe%d" % i, [128, 64, 65], BF16) for i in range(2)]
        Vo = [sb("naVo%d" % i, [128, 63, 65], BF16) for i in range(2)]
        EBr = sb("naEBr", [128, 2048])
        EBM = [sb("naEBM%d" % i, [128, 8, 256]) for i in range(2)]
        msk = sb("namask", [128, 256])
        G = [sb("naG%d" % i, [64, 128, 64], BF16) for i in range(2)]
        O = sb("naO", [64, 128, 64])
        Ob = sb("naOb", [64, 128, 64], BF16)
        E = [sb("naE%d" % i, [128, 256]) for i in range(2)]
        Pb = [sb("naP%d" % i, [128, 256], BF16) for i in range(2)]
        rec = [sb("narec%d" % i, [64, 1]) for i in range(2)]
        pS = [st.enter_context(_alloc(nc, "psum", "napS%d" % i, [128, 512], F32)) for i in range(2)]
        pO = [st.enter_context(_alloc(nc, "psum", "napO%d" % i, [128, 512], F32)) for i in range(2)]
        S.dma("sp", msk[:], I["na_mask"], writes=["namask"])
        for b in range(2):
            S.op("pool", lambda e: e.memset(Ve[b][:, :, 64:65], 1.0), writes=[("Veo", b)])
            S.op("pool", lambda e: e.memset(Vo[b][:, :, 64:65], 1.0), writes=[("Voo", b)])
        for h in range(NH):
            b = h % 2
            S.dma("sp", KT[b][:], Sc["nakT"][64 * h:64 * h + 64, :], writes=[("KT", b)])
            S.dma("sp", QT[b][:], Sc["naqT"][64 * h:64 * h + 64, :], writes=[("QT", b)])
            S.dma("pool", Ve[b][:, :, 0:64], Sc["nav"][:, 64 * h:64 * h + 64].rearrange("(i p) d -> p i d", p=128), writes=[("Ve", b)])
            S.dma("pool", Vo[b][:, :, 0:64], Sc["nav"][64:T - 64, 64 * h:64 * h + 64].rearrange("(i p) d -> p i d", p=128), writes=[("Vo", b)])
            S.dma("sp", G[b][:], Sc["nag"][:, 64 * h:64 * h + 64].rearrange("(r w) d -> w r d", w=64), writes=[("G", b)])
            S.dma("sp", EBr[:], I["na_bias"][l, h], writes=["EBr"])
            S.op("act", lambda e: e.activation(out=EBr[:], in_=EBr[:], func=AF.Exp), reads=["EBr"], writes=["EBr"])
            S.op("dve", lambda e: e.tensor_tensor(out=EBM[b][:], in0=EBr[:].rearrange("p (v c) -> p v c", c=256),
                                                  in1=msk[:].unsqueeze(1).broadcast_to([128, 8, 256]), op=ALU.mult),
                 reads=["EBr", "namask"], writes=[("EBM", b)])
            for r in range(128):
                rs = min(max(r - 4, 0), 120)
                v = rs - r + 7
                rb = r % 2
                for j in range(4):
                    S.op("pe", lambda e: e.matmul(pS[rb][:, 64 * j:64 * j + 64], lhsT=KT[b][:, 64 * rs + 128 * j:64 * rs + 128 * j + 128],
                                                  rhs=QT[b][:, 64 * r:64 * r + 64], start=True, stop=True),
                         reads=[("KT", b), ("QT", b)], writes=[("pS", rb)])
                S.op("act", lambda e: e.activation(out=E[rb][:], in_=pS[rb][:, 0:256], func=AF.Exp, scale=0.125), reads=[("pS", rb)], writes=[("E", rb)])
                S.op("dve", lambda e: e.tensor_tensor(out=Pb[rb][:], in0=E[rb][:], in1=EBM[b][:, v, :], op=ALU.mult),
                     reads=[("E", rb), ("EBM", b)], writes=[("P", rb)])
                for j in range(4):
                    if rs % 2 == 0:
                        vt = Ve[b][:, rs // 2 + j, :]
                    else:
                        vt = Vo[b][:, (rs - 1) // 2 + j, :]
                    S.op("pe", lambda e: e.matmul(pO[rb][0:64, 0:65], lhsT=Pb[rb][:, 64 * j:64 * j + 64], rhs=vt, start=(j == 0), stop=(j == 3)),
                         reads=[("P", rb), ("Ve", b), ("Vo", b), ("Veo", b), ("Voo", b)], writes=[("pO", rb)])
                S.op("dve", lambda e: e.reciprocal(out=rec[rb][:], in_=pO[rb][0:64, 64:65]), reads=[("pO", rb)], writes=[("rec", rb)])
                S.op("act", lambda e: e.activation(out=O[:, r, :], in_=pO[rb][0:64, 0:64], func=AF.Copy, scale=rec[rb][:]),
                     reads=[("pO", rb), ("rec", rb)], writes=["O"])
            S.op("dve", lambda e: e.tensor_tensor(out=Ob[:].rearrange("p r d -> p (r d)"), in0=O[:].rearrange("p r d -> p (r d)"),
                                                  in1=G[b][:].rearrange("p r d -> p (r d)"), op=ALU.mult), reads=["O", ("G", b)], writes=["Ob"])
            S.dma("pool", Sc["ytm"][:, 2 * CW + 64 * h:2 * CW + 64 * h + 64].rearrange("(r w) d -> w r d", w=64), Ob[:], reads=["Ob"], writes=[("ytm_na", h)])


Prog.phaseNA = _phaseNA


def _prep_mixer(inp, j):
    W = {}
    W.update(_prep_na(inp, j))
    W.update(_prep_dla(inp, j))
    W.update(_prep_hy(inp, j))
    return W


def _prep_dla(inp, j):
    L = DEPTH
    gpar = np.zeros((L, 4, 64, 2), np.float32)
    dtb = np.asarray(inp["mb_dt_bias"], np.float32)[:, :, MBU * j:MBU * j + MBU]
    alog = np.asarray(inp["mb_a_log"], np.float32)[:, :, MBU * j:MBU * j + MBU]
    gb = np.asarray(inp["ml_gate_b"], np.float32)[:, :, :, MLU * j:MLU * j + MLU]
    for d in range(2):
        gpar[:, d, :8 * MBU, 0] = np.repeat(dtb[:, d, :], 8, axis=1)
        gpar[:, d, :8 * MBU, 1] = np.repeat(alog[:, d, :], 8, axis=1)
        gpar[:, 2 + d, :8 * MLU, 0] = np.repeat(gb[:, d, 0, :], 8, axis=1)
        gpar[:, 2 + d, :8 * MLU, 1] = np.repeat(gb[:, d, 1, :], 8, axis=1)
    rmask = np.ones((64, 1024), np.float32)
    rmask[:, ::128] = 0.0
    s = np.arange(128)[:, None]
    ll = np.arange(128)[None, :]
    negmask = np.stack([np.where(s <= ll, 0.0, -30000.0), np.where(s >= ll, 0.0, -30000.0)]).astype(np.float32)
    return {"gpar": gpar, "rmask": rmask, "negmask": negmask,
            "dsk": np.ascontiguousarray(np.asarray(inp["mb_d"], np.float32)[:, MBU * j:MBU * j + MBU]).reshape(L, 1, MBU),
            "mbnw": np.ascontiguousarray(np.asarray(inp["mb_norm_w"], np.float32)[:, CW * j:CW * j + CW]).reshape(L, 1, CW),
            "mlnw": np.ascontiguousarray(np.asarray(inp["ml_norm_w"], np.float32)[:, CW * j:CW * j + CW]).reshape(L, 1, CW)}


def _dla_declare(self):
    I, Sc = self.I, self.Sc
    I["gpar"] = self.dram_in("gpar", [DEPTH, 4, 64, 2])
    I["rmask"] = self.dram_in("rmask", [64, 1024])
    I["negmask"] = self.dram_in("negmask", [2, 128, 128])
    I["dsk"] = self.dram_in("dsk", [DEPTH, 1, MBU])
    I["mbnw"] = self.dram_in("mbnw", [DEPTH, 1, CW])
    I["mlnw"] = self.dram_in("mlnw", [DEPTH, 1, CW])
    Sc["gq"] = self.dram_scr("gq", [4, 4, 8, T], F32, dbg=True)
    Sc["gcs"] = self.dram_scr("gcs", [4, 8, T], F32, dbg=True)
    Sc["gtot"] = self.dram_scr("gtot", [4, 8, 64], F32, dbg=True)
    Sc["yf_mb"] = self.dram_scr("yf_mb", [T, CW], F32)
    Sc["hf_ml"] = self.dram_scr("hf_ml", [T, CW], F32)
    Sc["yb_mb"] = self.dram_scr("yb_mb", [T, CW], F32)
    Sc["hb_ml"] = self.dram_scr("hb_ml", [T, CW], F32)


def _dla_streams(self, l, mixer):
    nc, S, I, Sc = self.nc, self.S, self.I, self.Sc
    U = MBU if mixer == "mb" else MLU
    P = 8 * U
    with ExitStack() as st:
        sb = lambda n, s, d=F32: st.enter_context(_alloc(nc, "sbuf", n, list(s), d))
        rm = sb("rm", [64, 1024])
        S.dma("sp", rm[:], I["rmask"], writes=["rm"])
        for d in range(2):
            md = (0 if mixer == "mb" else 2) + d
            k = lambda n: (n, d)
            gp = sb("gp%d" % d, [64, 2])
            S.dma("sp", gp[:], I["gpar"][l, md], writes=[k("gp")])
            sc = sb("sc%d" % d, [64, 1024]); a = sb("a%d" % d, [64, 1024]); cs = sb("cs%d" % d, [64, 1024])
            t1 = sb("t1%d" % d, [64, 1024]); t2 = sb("t2%d" % d, [64, 1024]); pp = sb("pp%d" % d, [64, 2])
            if mixer == "mb":
                S.dma("sp", t1[0:P, :], Sc["smallT"][MBU * d:MBU * d + MBU, :].rearrange("u (s n) -> (u s) n", n=1024), writes=[k("t1")])
                S.op("act", lambda e: e.activation(out=t1[0:P, :], in_=t1[0:P, :], func=AF.Exp, bias=gp[0:P, 0:1]), reads=[k("t1"), k("gp")], writes=[k("t1")])
                S.op("act", lambda e: e.activation(out=sc[0:P, :], in_=t1[0:P, :], func=AF.Ln, bias=1.0), reads=[k("t1")], writes=[k("sc")])
                S.op("act", lambda e: e.activation(out=pp[0:P, 0:1], in_=gp[0:P, 1:2], func=AF.Exp), reads=[k("gp")], writes=[k("pp")])
                S.op("dve", lambda e: e.tensor_scalar(out=pp[0:P, 0:1], in0=pp[0:P, 0:1], scalar1=-1.0, scalar2=None, op0=ALU.mult), reads=[k("pp")], writes=[k("pp")])
                S.op("dve", lambda e: e.tensor_scalar(out=a[0:P, :], in0=sc[0:P, :], scalar1=pp[0:P, 0:1], scalar2=None, op0=ALU.mult),
                     reads=[k("sc"), k("pp")], writes=[k("a")])
            else:
                r0 = 2 * MBU + 2 * MLU * d
                S.dma("sp", t1[0:P, :], Sc["smallT"][r0:r0 + MLU, :].rearrange("u (s n) -> (u s) n", n=1024), writes=[k("t1")])
                S.dma("sp", t2[0:P, :], Sc["smallT"][r0 + MLU:r0 + 2 * MLU, :].rearrange("u (s n) -> (u s) n", n=1024), writes=[k("t2")])
                S.op("act", lambda e: e.activation(out=sc[0:P, :], in_=t1[0:P, :], func=AF.Exp, bias=gp[0:P, 0:1]), reads=[k("t1"), k("gp")], writes=[k("sc")])
                S.op("dve", lambda e: e.tensor_scalar(out=sc[0:P, :], in0=sc[0:P, :], scalar1=float(128.0 ** -0.5), scalar2=None, op0=ALU.mult), reads=[k("sc")], writes=[k("sc")])
                S.op("dve", lambda e: e.tensor_scalar(out=pp[0:P, 0:1], in0=gp[0:P, 1:2], scalar1=-1.0, scalar2=None, op0=ALU.mult), reads=[k("gp")], writes=[k("pp")])
                S.op("act", lambda e: e.activation(out=t2[0:P, :], in_=t2[0:P, :], func=AF.Exp, scale=-1.0, bias=pp[0:P, 0:1]), reads=[k("t2"), k("pp")], writes=[k("t2")])
                S.op("act", lambda e: e.activation(out=t2[0:P, :], in_=t2[0:P, :], func=AF.Ln, bias=1.0), reads=[k("t2")], writes=[k("t2")])
                S.op("dve", lambda e: e.tensor_scalar(out=a[0:P, :], in0=t2[0:P, :], scalar1=-1.0, scalar2=None, op0=ALU.mult), reads=[k("t2")], writes=[k("a")])
            S.op("dve", lambda e: e.tensor_tensor_scan(out=cs[0:P, :], data0=rm[0:P, :], data1=a[0:P, :], initial=0.0, op0=ALU.mult, op1=ALU.add),
                 reads=["rm", k("a")], writes=[k("cs")])
            cs3 = cs[0:P, :].rearrange("p (c n) -> p c n", n=128)
            totb = cs3[:, :, 127:128].broadcast_to([P, 8, 128])
            S.dma("sp", Sc["gtot"][md, 0:U, :].rearrange("u (s c) -> (u s) c", c=8), cs3[:, :, 127], reads=[k("cs")], writes=[("gtot", md)], allow_slow_non_contiguous=True)
            t13 = t1[0:P, :].rearrange("p (c n) -> p c n", n=128)
            S.op("dve", lambda e: e.tensor_tensor(out=t13, in0=totb, in1=cs3, op=ALU.subtract), reads=[k("cs")], writes=[k("t1")])
            if d == 1:
                S.op("dve", lambda e: e.tensor_tensor(out=t2[0:P, :], in0=cs[0:P, :], in1=a[0:P, :], op=ALU.subtract), reads=[k("cs"), k("a")], writes=[k("t2")])
                S.op("dve", lambda e: e.tensor_tensor(out=cs[0:P, :], in0=t1[0:P, :], in1=a[0:P, :], op=ALU.add), reads=[k("t1"), k("a")], writes=[k("cs")])
                wexp = t2
                wk = k("t2")
            else:
                wexp = t1
                wk = k("t1")
            unf = lambda ap: ap.rearrange("u (s n) -> (u s) n", n=1024)
            S.dma("sp", unf(Sc["gcs"][md, 0:U, :]), cs[0:P, :], reads=[k("cs")], writes=[("gcs", md)])
            S.op("act", lambda e: e.activation(out=wexp[0:P, :], in_=wexp[0:P, :], func=AF.Exp), reads=[wk], writes=[wk])
            S.op("dve", lambda e: e.tensor_tensor(out=wexp[0:P, :], in0=wexp[0:P, :], in1=sc[0:P, :], op=ALU.mult), reads=[wk, k("sc")], writes=[wk])
            S.dma("sp", unf(Sc["gq"][md, 2, 0:U, :]), wexp[0:P, :], reads=[wk], writes=[("gq", md, 2)])
            S.dma("sp", unf(Sc["gq"][md, 3, 0:U, :]), sc[0:P, :], reads=[k("sc")], writes=[("gq", md, 3)])
            S.op("act", lambda e: e.activation(out=a[0:P, :], in_=cs[0:P, :], func=AF.Exp), reads=[k("cs")], writes=[k("a")])
            S.dma("sp", unf(Sc["gq"][md, 1, 0:U, :]), a[0:P, :], reads=[k("a")], writes=[("gq", md, 1)])
            S.op("dve", lambda e: e.tensor_scalar(out=cs[0:P, :], in0=cs[0:P, :], scalar1=-1.0, scalar2=None, op0=ALU.mult), reads=[k("cs")], writes=[k("cs")])
            S.dma("sp", unf(Sc["gq"][md, 0, 0:U, :]), cs[0:P, :], reads=[k("cs")], writes=[("gq", md, 0)])


def _dla_run(self, l, mixer, d):
    nc, I, Sc = self.nc, self.I, self.Sc
    S = _KeyNS(self.S, (mixer, d))
    mb = mixer == "mb"
    U = MBU if mb else MLU
    PW = 64 if mb else 129
    PS = 64 if mb else 256
    md = (0 if mb else 2) + d
    with ExitStack() as st:
        sb = lambda n, s, dt=F32: st.enter_context(_alloc(nc, "sbuf", n, list(s), dt))
        pst = lambda n, dt=F32: st.enter_context(_alloc(nc, "psum", n, [128, 512], dt))
        Q4 = self.Q4s[d]
        q4t = sb("q4t", [128, 64, 32])
        etot = sb("etot", [128, U, 64])
        negm = sb("negm", [128, 128])
        H = sb("H", [128, U, PW]); Hb = sb("Hb", [128, U, PW], BF16)
        csb = [sb("csb%d" % i, [128, U, 128]) for i in range(4)]
        LT = [sb("LT%d" % i, [128, U, 128]) for i in range(2)]
        MT = [sb("MT%d" % i, [128, U, 128], BF16) for i in range(2)]
        if mb:
            XK = [sb("XK%d" % i, [128, U * 64 + 128 * MBG], BF16) for i in range(4)]
            Xv = [t[:, 0:U * 64].rearrange("p (u w) -> p u w", w=64) for t in XK]
        else:
            Xv = [sb("Xv%d" % i, [128, U, PW], BF16)[:] for i in range(4)]
        Xw = [sb("Xw%d" % i, [128, U, PW], BF16) for i in range(2)]
        if mb:
            Kt = [t[:, U * 64:U * 64 + 128 * MBG] for t in XK]
        else:
            Kt = [sb("Kt%d" % i, [128, CW], BF16)[:] for i in range(4)]
        NG = MBG if mb else MLU
        KQ = [sb("KQ%d" % i, [128, 2, NG, 128], BF16) for i in range(4)]
        KTf = [t[:, 0, :, :] for t in KQ]
        QTf = [t[:, 1, :, :] for t in KQ]
        y2s = [sb("y2s%d" % i, [128, U, PW]) for i in range(2)]
        yo = [sb("yo%d" % i, [128, U, PW]) for i in range(2)]
        fin = [sb("fin%d" % i, [128, CW]) for i in range(2)]
        pG = pst("pG")
        NPT = 1 if mb else (MLU + 1) // 2
        py1 = [pst("py1%d" % i) for i in range(NPT)]
        py2 = [pst("py2%d" % i) for i in range(NPT)]
        pS_ = [pst("pS%d" % i) for i in range(NPT)]
        pQ = py1[0]

        def pview(tiles, u, w):
            if mb:
                return tiles[0][:, 64 * u:64 * u + w]
            return tiles[u // 2][:, 256 * (u % 2):256 * (u % 2) + w]

        def pall(tiles, h):
            if mb:
                return tiles[0][:, 0:64 * U].rearrange("p (u w) -> p u w", w=64)
            return tiles[h][:].rearrange("p (u w) -> p u w", w=256)[:, :, 0:129]

        S.op("pool", lambda e: e.memset(Q4[:], 0.0), writes=["Q4"])
        S.dma("sp", Q4[:], Sc["gq"][md].rearrange("q u t -> (q u) t"), writes=["Q4"])
        for g4 in range(4):
            for k in range(16):
                c = 16 * g4 + k
                S.op("pe", lambda e: e.transpose(pQ[:, 32 * k:32 * k + 32], Q4[:, 128 * c:128 * c + 128], self.ident_f[0:32, 0:32]), reads=["Q4"], writes=["py1"])
            S.op("dve", lambda e: e.tensor_copy(out=q4t[:, 16 * g4:16 * g4 + 16, :], in_=pQ[:].rearrange("p (k q) -> p k q", q=32)), reads=["py1"], writes=["q4t"])
        S.dma("sp", etot[:].rearrange("p u c -> p (u c)"), Sc["gtot"][md:md + 1, 0:U, :].rearrange("o u c -> o (u c)").broadcast_to([128, U * 64]), writes=["etot"])
        S.op("act", lambda e: e.activation(out=etot[:], in_=etot[:], func=AF.Exp), reads=["etot"], writes=["etot"])
        S.dma("sp", negm[:], I["negmask"][d], writes=["negm"])
        S.op("pool", lambda e: e.memset(H[:], 0.0), writes=["H"])
        S.op("pool", lambda e: e.memset(Hb[:], 0.0), writes=["Hb"])
        if not mb:
            for i in range(4):
                S.op("pool", lambda e: e.memset(Xv[i][:, :, 128:129], 1.0), writes=[("Xvo", i)])
        rden = [sb("rden%d" % i, [128, 4]) for i in range(2)]

        order = list(range(64)) if d == 0 else list(range(63, -1, -1))
        srcKQ = Sc["mbBCT"] if mb else Sc["mlkqT"]

        def issue_loads(step_):
            c_, b4 = order[step_], step_ % 4
            q0 = 128 * c_
            S.dma("sp", csb[b4][:], Sc["gcs"][md, 0:U, q0:q0 + 128].partition_broadcast(128), writes=[("csb", b4)])
            if mb:
                S.dma("act", XK[b4][:], Sc["mbxB"][q0:q0 + 128, :], writes=[("Xv", b4), ("Kt", b4)])
            else:
                S.dma("act", Xv[b4][:, :, 0:128], Sc["mlv"][q0:q0 + 128, :].rearrange("p (u w) -> p u w", w=128), writes=[("Xv", b4)])
                S.dma("act", Kt[b4][:], Sc["mlk"][q0:q0 + 128, :], writes=[("Kt", b4)])
            S.dma("act", KQ[b4][:], srcKQ[:, q0:q0 + 128].rearrange("(a g n) s -> n a g s", a=2, n=128), writes=[("KTf", b4), ("QTf", b4)])

        for s0 in range(3):
            issue_loads(s0)
        yield "setup"
        for step, c in enumerate(order):
            k = step % 2
            k4 = step % 4
            r0 = 128 * c
            xvk = [("Xv", k4)] + ([] if mb else [("Xvo", k4)])
            yield "s"
            for g in range(NG):
                S.op("pe", lambda e: e.matmul(pG[:, 128 * g:128 * g + 128], lhsT=KTf[k4][:, g, :], rhs=QTf[k4][:, g, :], start=True, stop=True),
                     reads=[("KTf", k4), ("QTf", k4)], writes=[("pG", g)])
            S.op("dve", lambda e: e.tensor_tensor(out=csb[k4][:], in0=csb[k4][:], in1=negm[:].unsqueeze(1).broadcast_to([128, U, 128]), op=ALU.add),
                 reads=[("csb", k4), "negm"], writes=[("csb", k4)])
            yield "s"
            for u in range(U):
                S.op("act", lambda e: e.activation(out=LT[k][:, u, :], in_=csb[k4][:, u, :], func=AF.Exp, bias=q4t[:, c, u:u + 1]),
                     reads=[("csb", k4), "q4t"], writes=[("LT", k, u)])
            if step + 3 < 64:
                issue_loads(step + 3)
            yield "s"
            for u in range(U):
                g = (u // 4) if mb else u
                S.op("dve", lambda e: e.scalar_tensor_tensor(out=MT[k][:, u, :], in0=pG[:, 128 * g:128 * g + 128], scalar=q4t[:, c, 24 + u:25 + u],
                                                             in1=LT[k][:, u, :], op0=ALU.mult, op1=ALU.mult),
                     reads=[("pG", g), ("LT", k, u), "q4t"], writes=[("MT", k, u)])
            yield "s"
            for u in range(U):
                S.op("pe", lambda e: e.matmul(pview(py1, u, PW), lhsT=MT[k][:, u, :], rhs=Xv[k4][:, u, :], start=True, stop=True),
                     reads=[("MT", k, u)] + xvk, writes=["py1"])
            if mb:
                for g in range(MBG):
                    S.op("pe", lambda e: e.matmul(py2[0][:, 256 * g:256 * g + 256], lhsT=QTf[k4][:, g, :], rhs=Hb[:, 4 * g:4 * g + 4, :],
                                                  start=True, stop=True), reads=[("QTf", k4), "Hb"], writes=["py2"])
            else:
                for u in range(U):
                    S.op("pe", lambda e: e.matmul(pview(py2, u, PW), lhsT=QTf[k4][:, u, :], rhs=Hb[:, u, :], start=True, stop=True),
                         reads=[("QTf", k4), "Hb"], writes=["py2"])
            yield "s"
            ecs_b = lambda u0, n: q4t[:, c, 8 + u0:8 + u0 + n].unsqueeze(2).broadcast_to([128, n, PW])
            w_b = q4t[:, c, 16:16 + U].unsqueeze(2).broadcast_to([128, U, PW])
            if mb:
                S.op("dve", lambda e: e.tensor_tensor(out=y2s[k][:], in0=pall(py2, 0), in1=ecs_b(0, U), op=ALU.mult), reads=["py2", "q4t"], writes=[("y2s", k)])
                S.op("dve", lambda e: e.tensor_tensor(out=yo[k][:], in0=pall(py1, 0), in1=y2s[k][:], op=ALU.add), reads=["py1", ("y2s", k)], writes=[("yo", k)])
            else:
                for h in range(NPT):
                    S.op("dve", lambda e: e.tensor_tensor(out=y2s[k][:, 2 * h:2 * h + 2, :], in0=pall(py2, h), in1=ecs_b(2 * h, 2), op=ALU.mult),
                         reads=["py2", "q4t"], writes=[("y2s", k, h)])
                    S.op("dve", lambda e: e.tensor_tensor(out=yo[k][:, 2 * h:2 * h + 2, :], in0=pall(py1, h), in1=y2s[k][:, 2 * h:2 * h + 2, :], op=ALU.add),
                         reads=["py1", ("y2s", k, h)], writes=[("yo", k, h)])
            yok = [("yo", k)] if mb else [("yo", k, h) for h in range(NPT)]
            yield "s"
            S.op("pool", lambda e: e.tensor_tensor(out=Xw[k][:], in0=Xv[k4], in1=w_b, op=ALU.mult), reads=xvk + ["q4t"], writes=[("Xw", k)])
            if mb:
                for g in range(MBG):
                    S.op("pe", lambda e: e.matmul(pS_[0][:, 256 * g:256 * g + 256], lhsT=Kt[k4][:, 128 * g:128 * g + 128], rhs=Xw[k][:, 4 * g:4 * g + 4, :],
                                                  start=True, stop=True), reads=[("Kt", k4), ("Xw", k)], writes=["pS"])
            else:
                for u in range(U):
                    S.op("pe", lambda e: e.matmul(pview(pS_, u, PW), lhsT=Kt[k4][:, 128 * u:128 * u + 128], rhs=Xw[k][:, u, :], start=True, stop=True),
                         reads=[("Kt", k4), ("Xw", k)], writes=["pS"])
            yield "s"
            S.op("pool", lambda e: e.tensor_tensor(out=H[:], in0=H[:], in1=etot[:, :, c:c + 1].broadcast_to([128, U, PW]), op=ALU.mult),
                 reads=["H", "etot"], writes=["H"])
            if mb:
                S.op("dve", lambda e: e.tensor_tensor(out=H[:], in0=pall(pS_, 0), in1=H[:], op=ALU.add), reads=["H", "pS"], writes=["H"])
            else:
                for h in range(NPT):
                    S.op("dve", lambda e: e.tensor_tensor(out=H[:, 2 * h:2 * h + 2, :], in0=pall(pS_, h), in1=H[:, 2 * h:2 * h + 2, :], op=ALU.add),
                         reads=["H", "pS"], writes=["H"])
            S.op("act", lambda e: e.activation(out=Hb[:], in_=H[:], func=AF.Copy), reads=["H"], writes=["Hb"])
            yield "s"
            f = fin[k]
            if mb:
                ysrc = yo[k][:].rearrange("p u w -> p (u w)")
                fk = yok
            else:
                S.op("act", lambda e: e.activation(out=rden[k][:, 0:MLU], in_=yo[k][:, :, 128], func=AF.Abs), reads=yok, writes=[("rden", k)])
                S.op("dve", lambda e: e.tensor_scalar(out=rden[k][:], in0=rden[k][:], scalar1=1.0, scalar2=None, op0=ALU.max), reads=[("rden", k)], writes=[("rden", k)])
                S.op("dve", lambda e: e.reciprocal(out=rden[k][:], in_=rden[k][:]), reads=[("rden", k)], writes=[("rden", k)])
                S.op("pool", lambda e: e.tensor_tensor(out=f[:].rearrange("p (u w) -> p u w", w=128), in0=yo[k][:, :, 0:128],
                                                       in1=rden[k][:, 0:MLU].unsqueeze(2).broadcast_to([128, MLU, 128]), op=ALU.mult),
                     reads=yok + [("rden", k)], writes=[("fin", k)])
                ysrc = f[:]
                fk = [("fin", k)]
            dst_f = (Sc["yf_mb"] if mb else Sc["hf_ml"]) if d == 0 else (Sc["yb_mb"] if mb else Sc["hb_ml"])
            S.dma("pool", dst_f[r0:r0 + 128, :], ysrc, reads=fk, writes=[("ydir", c)])
            yield "chunk"
        yield "done"


def _dla_final(self, l, mixer):
    nc, S, I, Sc = self.nc, self.S, self.I, self.Sc
    mb = mixer == "mb"
    with ExitStack() as st:
        sb = lambda n, s, dt=F32: st.enter_context(_alloc(nc, "sbuf", n, list(s), dt))
        nwb = sb("nwb2", [128, CW])
        S.dma("sp", nwb[:], I["mbnw" if mb else "mlnw"][l].broadcast_to([128, CW]), writes=["nwb2"])
        if mb:
            dskb = sb("dskb", [128, MBU])
            S.dma("sp", dskb[:], I["dsk"][l].broadcast_to([128, MBU]), writes=["dskb"])
        NB_ = 3
        prev = [sb("prev%d" % i, [128, CW]) for i in range(NB_)]
        cur = [sb("cur%d" % i, [128, CW]) for i in range(NB_)]
        Zt = [sb("Zt%d" % i, [128, CW], BF16) for i in range(NB_)]
        Ot = [sb("Ot%d" % i, [128, CW], BF16) for i in range(NB_)]
        ssq = [sb("ssq%d" % i, [128, 4]) for i in range(NB_)]
        junk = sb("junkd", [128, CW], BF16)
        outb = [sb("outb%d" % i, [128, CW], BF16) for i in range(NB_)]
        for c in range(64):
            k = c % NB_
            r0 = 128 * c
            S.dma("sp", prev[k][:], (Sc["yf_mb"] if mb else Sc["hf_ml"])[r0:r0 + 128, :], writes=[("prev", k)])
            S.dma("pool", cur[k][:], (Sc["yb_mb"] if mb else Sc["hb_ml"])[r0:r0 + 128, :], writes=[("cur", k)])
            S.dma("sp", Zt[k][:], (Sc["mbz"] if mb else Sc["mlz"])[r0:r0 + 128, :], writes=[("Zt", k)])
            S.dma("pool", Ot[k][:], (Sc["mbx"] if mb else Sc["mlo"])[r0:r0 + 128, :], writes=[("Ot", k)])
            S.op("pool", lambda e: e.tensor_tensor(out=prev[k][:], in0=prev[k][:], in1=cur[k][:], op=ALU.add), reads=[("prev", k), ("cur", k)], writes=[("prev", k)])
            if mb:
                S.op("pool", lambda e: e.tensor_tensor(out=cur[k][:].rearrange("p (u w) -> p u w", w=64), in0=Ot[k][:].rearrange("p (u w) -> p u w", w=64),
                                                       in1=dskb[:].unsqueeze(2).broadcast_to([128, MBU, 64]), op=ALU.mult),
                     reads=[("Ot", k), ("cur", k), "dskb"], writes=[("cur", k)])
                S.op("dve", lambda e: e.tensor_tensor(out=prev[k][:], in0=prev[k][:], in1=cur[k][:], op=ALU.add), reads=[("prev", k), ("cur", k)], writes=[("prev", k)])
                S.op("dve", lambda e: e.tensor_tensor(out=prev[k][:], in0=prev[k][:], in1=Zt[k][:], op=ALU.mult), reads=[("prev", k), ("Zt", k)], writes=[("prev", k)])
                ngr, gw = MBG, 256
            else:
                S.op("dve", lambda e: e.tensor_tensor(out=prev[k][:], in0=prev[k][:], in1=Ot[k][:], op=ALU.mult), reads=[("prev", k), ("Ot", k)], writes=[("prev", k)])
                ngr, gw = MLU, 128
            for g in range(ngr):
                S.op("act", lambda e: e.activation(out=junk[:, 0:gw], in_=prev[k][:, gw * g:gw * g + gw], func=AF.Square, accum_out=ssq[k][:, g:g + 1]),
                     reads=[("prev", k)], writes=["junkd", ("ssq", k)])
            S.op("dve", lambda e: e.tensor_scalar(out=ssq[k][:, 0:ngr], in0=ssq[k][:, 0:ngr], scalar1=1.0 / gw, scalar2=EPS, op0=ALU.mult, op1=ALU.add),
                 reads=[("ssq", k)], writes=[("ssq", k)])
            S.op("act", lambda e: e.activation(out=ssq[k][:, 0:ngr], in_=ssq[k][:, 0:ngr], func=AF.Sqrt), reads=[("ssq", k)], writes=[("ssq", k)])
            S.op("dve", lambda e: e.reciprocal(out=ssq[k][:, 0:ngr], in_=ssq[k][:, 0:ngr]), reads=[("ssq", k)], writes=[("ssq", k)])
            for g in range(ngr):
                S.op("dve", lambda e: e.scalar_tensor_tensor(out=(outb[k] if mb else prev[k])[:, gw * g:gw * g + gw], in0=prev[k][:, gw * g:gw * g + gw],
                                                             scalar=ssq[k][:, g:g + 1], in1=nwb[:, gw * g:gw * g + gw], op0=ALU.mult, op1=ALU.mult),
                     reads=[("prev", k), ("ssq", k), "nwb2"], writes=[("outb", k) if mb else ("prev", k)])
            if not mb:
                S.op("pool", lambda e: e.tensor_tensor(out=outb[k][:], in0=prev[k][:], in1=Zt[k][:], op=ALU.mult), reads=[("prev", k), ("Zt", k)], writes=[("outb", k)])
            col0 = 0 if mb else CW
            S.dma("pool", Sc["ytm"][r0:r0 + 128, col0:col0 + CW], outb[k][:], reads=[("outb", k)], writes=[("ytm_dla", mixer, c)])


def _phaseDLA(self, l):
    S, nc = self.S, self.nc
    for mixer in ("mb", "ml"):
        _dla_streams(self, l, mixer)
        S.barrier()
        with ExitStack() as st:
            self.Q4s = [st.enter_context(_alloc(nc, "sbuf", "Q4_%d" % d, [32, T], F32)) for d in range(2)]
            gens = [_dla_run(self, l, mixer, d) for d in (0, 1)]
            for g in gens:
                next(g)
            while True:
                rs = [next(g) for g in gens]
                if all(r == "done" for r in rs):
                    break
            S.barrier()
            for g in reversed(gens):
                try:
                    next(g)
                except StopIteration:
                    pass
        _dla_final(self, l, mixer)
        S.barrier()


Prog.phaseDLA = _phaseDLA


N2L = 2 * T
CG = 32


def _prep_hy(inp, j):
    L = DEPTH
    n = np.arange(128)
    ang = 2.0 * np.pi * np.outer(n, n) / 128.0
    Fre, Fim = np.cos(ang), -np.sin(ang)
    dft = np.stack([Fre, Fim, Fre, -Fim], axis=1).astype(np.float32)
    angt = 2.0 * np.pi * np.outer(n, n) / float(N2L)
    twd = np.stack([np.cos(angt), -np.sin(angt)], axis=1).astype(np.float32)
    t = np.arange(T, dtype=np.float32)
    t_norm = t / np.float32(T)
    bands = np.arange(1, 9, dtype=np.float32)
    a = (np.float32(2.0 * math.pi / T)) * t[:, None] * bands[None, :]
    pos = np.concatenate([t_norm[:, None], np.cos(a), np.sin(a)], axis=-1).astype(np.float32)
    hyp = np.zeros((L, 64, 4), np.float32)
    hyp[:, :, 0] = np.asarray(inp["hy_b1"]); hyp[:, :, 1] = np.asarray(inp["hy_freq"]); hyp[:, :, 2] = np.asarray(inp["hy_b2"])
    dec = np.asarray(inp["hy_decay"], np.float32).reshape(L, 4, 512)[:, :, HC * j:HC * j + HC].reshape(L, 4 * NB, 128).transpose(0, 2, 1)
    w3 = np.asarray(inp["hy_w3"], np.float32).reshape(L, 64, 4, 512)[:, :, :, HC * j:HC * j + HC].reshape(L, 64, 4 * HC)
    skip = np.asarray(inp["hy_skip"], np.float32)[:, :, HC * j:HC * j + HC].reshape(L, 2, 1, HC)
    return {"dft": np.ascontiguousarray(dft.reshape(128, 512)), "twd": np.ascontiguousarray(twd.reshape(128, 256)),
            "posT": np.ascontiguousarray(pos.T), "tneg": (-t_norm).reshape(1, T).astype(np.float32),
            "hyp": hyp, "hydec": np.ascontiguousarray(dec),
            "hyw1": np.asarray(inp["hy_w1"], np.float32), "hyw2": np.asarray(inp["hy_w2"], np.float32),
            "hyw3": np.ascontiguousarray(w3), "hyskip": np.ascontiguousarray(skip)}


def _hy_declare(self):
    I, Sc = self.I, self.Sc
    I["dft"] = self.dram_in("dft", [128, 512]); I["twd"] = self.dram_in("twd", [128, 256])
    I["posT"] = self.dram_in("posT", [17, T]); I["tneg"] = self.dram_in("tneg", [1, T])
    I["hyp"] = self.dram_in("hyp", [DEPTH, 64, 4]); I["hydec"] = self.dram_in("hydec", [DEPTH, 128, 4 * NB])
    I["hyw1"] = self.dram_in("hyw1", [DEPTH, 17, 64]); I["hyw2"] = self.dram_in("hyw2", [DEPTH, 64, 64])
    I["hyw3"] = self.dram_in("hyw3", [DEPTH, 64, 4 * HC]); I["hyskip"] = self.dram_in("hyskip", [DEPTH, 2, 1, HC])
    Sc["gflt"] = self.dram_scr("gflt", [2, HC, N2L], BF16, dbg=True)


def _phaseHY(self, l):
    nc, S, I, Sc = self.nc, self.S, self.I, self.Sc
    PI = float(np.pi)
    with ExitStack() as st:
        sb = lambda n, s, d=F32: st.enter_context(_alloc(nc, "sbuf", n, list(s), d))
        pst = lambda n, d=F32: st.enter_context(_alloc(nc, "psum", n, [128, 512], d))
        w1 = sb("hw1", [17, 64]); w2 = sb("hw2", [64, 64]); w3f = sb("hw3f", [64, 4 * HC]); w3b = sb("hw3b", [64, 4 * HC], BF16)
        hp = sb("hhp", [64, 4]); fb = sb("hfb", [64, 2])
        hid = sb("hhid", [64, T], BF16)
        tn = sb("htn", [128, T]); dec = sb("hdec", [128, 4 * NB])
        S.dma("sp", w1[:], I["hyw1"][l], writes=["w1"]); S.dma("sp", w2[:], I["hyw2"][l], writes=["w2"])
        S.dma("sp", w3f[:], I["hyw3"][l], writes=["w3f"]); S.dma("sp", hp[:], I["hyp"][l], writes=["hp"])
        S.dma("sp", tn[:], I["tneg"].broadcast_to([128, T]), writes=["tn"]); S.dma("sp", dec[:], I["hydec"][l], writes=["dec"])
        S.op("pool", lambda e: e.tensor_copy(out=w3b[:], in_=w3f[:]), reads=["w3f"], writes=["w3b"])
        S.op("dve", lambda e: e.tensor_tensor(out=fb[:, 0:1], in0=hp[:, 0:1], in1=hp[:, 1:2], op=ALU.mult), reads=["hp"], writes=["fb"])
        S.op("dve", lambda e: e.tensor_tensor(out=fb[:, 1:2], in0=hp[:, 2:3], in1=hp[:, 1:2], op=ALU.mult), reads=["hp", "fb"], writes=["fb"])
        pt_ = [sb("hpt%d" % i, [17, 512]) for i in range(2)]
        arg = [sb("harg%d" % i, [64, 512]) for i in range(2)]
        ta = [sb("hta%d" % i, [64, 512]) for i in range(2)]
        tb = [sb("htb%d" % i, [64, 512]) for i in range(2)]
        h1 = [sb("hh1%d" % i, [64, 512]) for i in range(2)]
        pz = [pst("hpz%d" % i) for i in range(2)]

        def sin_layer(src_ps, pk, col, out_ap, okey, k):
            a = arg[k]; ak = ("arg", k)
            S.op("dve", lambda e: e.tensor_scalar(out=a[:], in0=src_ps[0:64, :], scalar1=hp[:, 1:2], scalar2=fb[:, col:col + 1], op0=ALU.mult, op1=ALU.add),
                 reads=[pk, "hp", "fb"], writes=[ak])
            S.op("dve", lambda e: e.tensor_scalar(out=ta[k][:], in0=a[:], scalar1=PI, scalar2=-2 * PI, op0=ALU.is_gt, op1=ALU.mult), reads=[ak], writes=[("ta", k)])
            S.op("dve", lambda e: e.tensor_scalar(out=tb[k][:], in0=a[:], scalar1=-PI, scalar2=2 * PI, op0=ALU.is_lt, op1=ALU.mult), reads=[ak], writes=[("tb", k)])
            S.op("pool", lambda e: e.tensor_tensor(out=ta[k][:], in0=ta[k][:], in1=tb[k][:], op=ALU.add), reads=[("ta", k), ("tb", k)], writes=[("ta", k)])
            S.op("pool", lambda e: e.tensor_tensor(out=a[:], in0=a[:], in1=ta[k][:], op=ALU.add), reads=[ak, ("ta", k)], writes=[ak])
            S.op("act", lambda e: e.activation(out=out_ap, in_=a[:], func=AF.Sin), reads=[ak], writes=[okey])

        for c in range(16):
            k = c % 2
            S.dma("sp", pt_[k][:], I["posT"][:, 512 * c:512 * c + 512], writes=[("pt", k)])
            S.op("pe", lambda e: e.matmul(pz[0][0:64, :], lhsT=w1[:], rhs=pt_[k][:], start=True, stop=True), reads=["w1", ("pt", k)], writes=["pz0"])
            sin_layer(pz[0], "pz0", 0, h1[k][:], ("h1", k), k)
            S.op("pe", lambda e: e.matmul(pz[1][0:64, :], lhsT=w2[:], rhs=h1[k][:], start=True, stop=True), reads=["w2", ("h1", k)], writes=["pz1"])
            sin_layer(pz[1], "pz1", 1, hid[:, 512 * c:512 * c + 512], ("hid", c), k)
        hidk = [("hid", c) for c in range(16)]
        gt = [sb("hgt%d" % i, [128, N2L], BF16) for i in range(2)]
        win = [sb("hwin%d" % i, [128, 512]) for i in range(2)]
        pf = [pst("hpf%d" % i) for i in range(2)]
        for i in range(2):
            S.op("pool", lambda e: e.memset(gt[i][:, T:T + 1], 0.0), writes=[("gtz", i)])
        it = 0
        for o in range(2):
            for cb in range(NB):
                g = gt[(o * NB + cb) % 2]; gk = ("gt", (o * NB + cb) % 2)
                gparts = []
                for dr in range(2):
                    col0 = (o * 2 + dr) * HC + 128 * cb
                    di = (o * 2 + dr) * NB + cb
                    for c in range(16):
                        k = it % 2; it += 1
                        S.op("pe", lambda e: e.matmul(pf[k][:], lhsT=w3b[:, col0:col0 + 128], rhs=hid[:, 512 * c:512 * c + 512], start=True, stop=True),
                             reads=["w3b", ("hid", c)], writes=[("pf", k)])
                        S.op("act", lambda e: e.activation(out=win[k][:], in_=tn[:, 512 * c:512 * c + 512], func=AF.Exp, scale=dec[:, di:di + 1]),
                             reads=["tn", "dec"], writes=[("win", k)])
                        pk = (gk, dr, c)
                        gparts.append(pk)
                        if dr == 0:
                            S.op("dve", lambda e: e.tensor_tensor(out=g[:, 512 * c:512 * c + 512], in0=pf[k][:], in1=win[k][:], op=ALU.mult),
                                 reads=[("pf", k), ("win", k)], writes=[pk])
                        else:
                            j0 = 1 if c == 0 else 0
                            lo = N2L - 512 * c - 511
                            hi = N2L - 512 * c - j0 + 1
                            S.op("dve", lambda e: e.tensor_tensor(out=g[:, lo:hi][:, ::-1], in0=pf[k][:, j0:512], in1=win[k][:, j0:512], op=ALU.mult),
                                 reads=[("pf", k), ("win", k)], writes=[pk])
                S.dma("pool", Sc["gflt"][o, 128 * cb:128 * cb + 128, :], g[:], reads=gparts + [("gtz", (o * NB + cb) % 2)], writes=[("gflt", o, cb)])
    S.barrier()
    with ExitStack() as st:
        sb = lambda n, s, d=F32: st.enter_context(_alloc(nc, "sbuf", n, list(s), d))
        dftf = sb("dftf", [128, 512]); dft = sb("dftb", [128, 4, 128], BF16); twd = sb("twd", [128, 2, 128])
        S.dma("sp", dftf[:], I["dft"], writes=["dftf"]); S.dma("sp", twd[:].rearrange("p a k -> p (a k)"), I["twd"], writes=["twd"])
        S.op("dve", lambda e: e.tensor_copy(out=dft[:].rearrange("p a k -> p (a k)"), in_=dftf[:]), reads=["dftf"], writes=["dft"])
        Fre, Fim, nFim = dft[:, 0, :], dft[:, 1, :], dft[:, 3, :]
        Fcat = dft[:, 0:2, :].rearrange("p a k -> p (a k)")
        Fci2 = dft[:, 1:3, :].rearrange("p a k -> p (a k)")
        Fci1 = dft[:, 2:4, :].rearrange("p a k -> p (a k)")
        G = sb("hyG", [128, 2, CG, 2, 128], BF16)
        gblk = [sb("gblk%d" % i, [128, CG, 128], BF16) for i in range(2)]
        sig = {n: sb("sig_" + n, [64, CG, 128], BF16) for n in ("v", "x1", "x2", "g")}
        zblk = sb("zblk", [64, CG, 128], BF16); oblk = sb("oblk", [64, CG, 128], BF16)
        skb = sb("skb", [64, 2, CG])
        Ap = [sb("Ap%d" % i, [128, 8, 2, 128], BF16) for i in range(2)]
        Yp = [sb("Yp%d" % i, [128, 8, 2, 128], BF16) for i in range(2)]
        Bp = [sb("Bp%d" % i, [128, 8, 2, 128], BF16) for i in range(2)]
        tt = [[sb("tt%d_%d" % (i, j), [128, 8, 128]) for j in range(4)] for i in range(2)]
        ep = [[sb("ep%d_%d" % (i, j), [64, 8, 128]) for j in range(2)] for i in range(2)]
        pA = st.enter_context(_alloc(nc, "psum", "hpA", [128, 2048], F32))
        pB = st.enter_context(_alloc(nc, "psum", "hpB", [128, 2048], F32))
        cn = {"c": 0}

        def cmul(out_t, okey, are, aim, akeys, bre, bim, bkeys, conj):
            i = cn["c"] % 2; cn["c"] += 1
            t1, t2, t3, t4 = tt[i]
            ks = [("tt", i, j) for j in range(4)]
            shp = lambda a: a
            S.op("dve", lambda e: e.tensor_tensor(out=t1[:], in0=are, in1=bre, op=ALU.mult), reads=akeys + bkeys, writes=[ks[0]])
            S.op("dve", lambda e: e.tensor_tensor(out=t2[:], in0=aim, in1=bim, op=ALU.mult), reads=akeys + bkeys, writes=[ks[1]])
            S.op("dve", lambda e: e.tensor_tensor(out=t3[:], in0=are, in1=bim, op=ALU.mult), reads=akeys + bkeys, writes=[ks[2]])
            S.op("dve", lambda e: e.tensor_tensor(out=t4[:], in0=aim, in1=bre, op=ALU.mult), reads=akeys + bkeys, writes=[ks[3]])
            if not conj:
                S.op("pool", lambda e: e.tensor_tensor(out=out_t[:, :, 0, :], in0=t1[:], in1=t2[:], op=ALU.subtract), reads=ks[0:2], writes=[okey + ("re",)])
                S.op("pool", lambda e: e.tensor_tensor(out=out_t[:, :, 1, :], in0=t3[:], in1=t4[:], op=ALU.add), reads=ks[2:4], writes=[okey + ("im",)])
            else:
                S.op("pool", lambda e: e.tensor_tensor(out=out_t[:, :, 0, :], in0=t1[:], in1=t2[:], op=ALU.add), reads=ks[0:2], writes=[okey + ("re",)])
                S.op("pool", lambda e: e.tensor_tensor(out=out_t[:, :, 1, :], in0=t4[:], in1=t3[:], op=ALU.subtract), reads=ks[2:4], writes=[okey + ("im",)])

        pA3 = pA[:].rearrange("p (c k) -> p c k", k=256)
        Tre = twd[:, 0, :].unsqueeze(1).broadcast_to([128, 8, 128])
        Tim = twd[:, 1, :].unsqueeze(1).broadcast_to([128, 8, 128])
        pBq = pB[:].rearrange("p (q r c k) -> p q r c k", q=2, r=2, c=4)

        def fwd_octet(src_fn, K, skeys, i):
            for ch in range(8):
                S.op("pe", lambda e: e.matmul(pA3[:, ch, :], lhsT=src_fn(ch), rhs=Fcat[0:K, :], start=True, stop=True),
                     reads=skeys + ["dft"], writes=["pA"])
            cmul(Ap[i], ("Ap", i), pA3[:, :, 0:128], pA3[:, :, 128:256], ["pA"], Tre, Tim, ["twd"], False)
            for q in range(2):
                rre = Ap[i][:, 4 * q:4 * q + 4, 0, :]
                rim = Ap[i][:, 4 * q:4 * q + 4, 1, :]
                kk = [("Ap", i, "re"), ("Ap", i, "im"), "dft"]
                S.op("pe", lambda e: e.matmul(pB[:, 1024 * q:1024 * q + 512], lhsT=Fre, rhs=rre, start=True, stop=False), reads=kk, writes=["pB"])
                S.op("pe", lambda e: e.matmul(pB[:, 1024 * q:1024 * q + 512], lhsT=nFim, rhs=rim, start=False, stop=True), reads=kk, writes=["pB"])
                S.op("pe", lambda e: e.matmul(pB[:, 1024 * q + 512:1024 * q + 1024], lhsT=Fim, rhs=rre, start=True, stop=False), reads=kk, writes=["pB"])
                S.op("pe", lambda e: e.matmul(pB[:, 1024 * q + 512:1024 * q + 1024], lhsT=Fre, rhs=rim, start=False, stop=True), reads=kk, writes=["pB"])

        oc = {"n": 0}
        for cg in range(HC // CG):
            c0 = CG * cg
            for o in range(2):
                S.dma("sp", gblk[o][:], Sc["gflt"][o, c0:c0 + CG, :].rearrange("c (a b) -> a c b", b=128), writes=[("gblk", o)])
                for oc8 in range(CG // 8):
                    i = oc["n"] % 2; oc["n"] += 1
                    fwd_octet(lambda ch: gblk[o][:, 8 * oc8 + ch, :], 128, [("gblk", o)], i)
                    gv = G[:, o, 8 * oc8:8 * oc8 + 8, :, :]
                    S.op("act", lambda e: e.activation(out=gv[:, :, 0, :].rearrange("p (q c) k -> p q c k", q=2), in_=pBq[:, :, 0, :, :], func=AF.Copy),
                         reads=["pB"], writes=[("G", o, oc8, 0)])
                    S.op("act", lambda e: e.activation(out=gv[:, :, 1, :].rearrange("p (q c) k -> p q c k", q=2), in_=pBq[:, :, 1, :, :], func=AF.Copy),
                         reads=["pB"], writes=[("G", o, oc8, 1)])
            for n_, src, r0 in (("v", "hyuT", 0), ("x1", "hyuT", HC), ("x2", "hyuT", 2 * HC), ("g", "hygT", 0)):
                S.dma("pool", sig[n_][:], Sc[src][r0 + c0:r0 + c0 + CG, :].rearrange("c (a b) -> a c b", b=128), writes=[("sig", n_)])
            S.dma("sp", skb[:].rearrange("p o c -> p (o c)") if False else skb[:], I["hyskip"][l, :, :, c0:c0 + CG].rearrange("o x c -> x o c").broadcast_to([64, 2, CG]),
                  writes=["skb"])
            for o in range(2):
                src_t = sig["v"] if o == 0 else zblk
                src_k = [("sig", "v")] if o == 0 else [("zblk", q) for q in range(CG // 8)]
                for oc8 in range(CG // 8):
                    i = oc["n"] % 2; oc["n"] += 1
                    ch0 = 8 * oc8
                    skeys = [("sig", "v")] if o == 0 else [("zblk", oc8)]
                    fwd_octet(lambda ch: src_t[:, ch0 + ch, :], 64, skeys, i)
                    gv = G[:, o, ch0:ch0 + 8, :, :]
                    gre = gv[:, :, 0, :].rearrange("p (q c) k -> p q c k", q=2)
                    gim = gv[:, :, 1, :].rearrange("p (q c) k -> p q c k", q=2)
                    ii = cn["c"] % 2; cn["c"] += 1
                    t1, t2, t3, t4 = [t[:].rearrange("p (q c) k -> p q c k", q=2) for t in tt[ii]]
                    ks = [("tt", ii, j) for j in range(4)]
                    gk = [("G", o, oc8, 0), ("G", o, oc8, 1)]
                    S.op("dve", lambda e: e.tensor_tensor(out=t1, in0=pBq[:, :, 0, :, :], in1=gre, op=ALU.mult), reads=["pB"] + gk, writes=[ks[0]])
                    S.op("dve", lambda e: e.tensor_tensor(out=t2, in0=pBq[:, :, 1, :, :], in1=gim, op=ALU.mult), reads=["pB"] + gk, writes=[ks[1]])
                    S.op("dve", lambda e: e.tensor_tensor(out=t3, in0=pBq[:, :, 0, :, :], in1=gim, op=ALU.mult), reads=["pB"] + gk, writes=[ks[2]])
                    S.op("dve", lambda e: e.tensor_tensor(out=t4, in0=pBq[:, :, 1, :, :], in1=gre, op=ALU.mult), reads=["pB"] + gk, writes=[ks[3]])
                    S.op("pool", lambda e: e.tensor_tensor(out=Yp[i][:, :, 0, :], in0=tt[ii][0][:], in1=tt[ii][1][:], op=ALU.subtract), reads=ks[0:2], writes=[("Yp", i, "re")])
                    S.op("pool", lambda e: e.tensor_tensor(out=Yp[i][:, :, 1, :], in0=tt[ii][2][:], in1=tt[ii][3][:], op=ALU.add), reads=ks[2:4], writes=[("Yp", i, "im")])
                    for ch in range(8):
                        S.op("pe", lambda e: e.matmul(pA3[:, ch, :], lhsT=Yp[i][:, ch, 0, :], rhs=Fci1, start=True, stop=False),
                             reads=[("Yp", i, "re"), ("Yp", i, "im"), "dft"], writes=["pA"])
                        S.op("pe", lambda e: e.matmul(pA3[:, ch, :], lhsT=Yp[i][:, ch, 1, :], rhs=Fci2, start=False, stop=True),
                             reads=[("Yp", i, "re"), ("Yp", i, "im"), "dft"], writes=["pA"])
                    cmul(Bp[i], ("Bp", i), pA3[:, :, 0:128], pA3[:, :, 128:256], ["pA"], Tre, Tim, ["twd"], True)
                    for q in range(2):
                        kk = [("Bp", i, "re"), ("Bp", i, "im"), "dft"]
                        S.op("pe", lambda e: e.matmul(pB[0:64, 512 * q:512 * q + 512], lhsT=Fre[:, 0:64], rhs=Bp[i][:, 4 * q:4 * q + 4, 0, :], start=True, stop=False),
                             reads=kk, writes=["pB"])
                        S.op("pe", lambda e: e.matmul(pB[0:64, 512 * q:512 * q + 512], lhsT=Fim[:, 0:64], rhs=Bp[i][:, 4 * q:4 * q + 4, 1, :], start=False, stop=True),
                             reads=kk, writes=["pB"])
                    e1, e2 = ep[i]
                    yv = pB[0:64, 0:1024].rearrange("p (c k) -> p c k", k=128)
                    skv = skb[:, o, ch0:ch0 + 8].unsqueeze(2).broadcast_to([64, 8, 128])
                    uu = src_t[:, ch0:ch0 + 8, :]
                    S.op("pool", lambda e: e.tensor_tensor(out=e1[:], in0=uu, in1=skv, op=ALU.mult), reads=skeys + ["skb"], writes=[("e1", i)])
                    S.op("dve", lambda e: e.scalar_tensor_tensor(out=e2[:], in0=yv, scalar=1.0 / N2L, in1=e1[:], op0=ALU.mult, op1=ALU.add),
                         reads=["pB", ("e1", i)], writes=[("e2", i)])
                    if o == 0:
                        S.op("pool", lambda e: e.tensor_tensor(out=zblk[:, ch0:ch0 + 8, :], in0=e2[:], in1=sig["x1"][:, ch0:ch0 + 8, :], op=ALU.mult),
                             reads=[("e2", i), ("sig", "x1")], writes=[("zblk", oc8)])
                    else:
                        S.op("pool", lambda e: e.tensor_tensor(out=e1[:], in0=e2[:], in1=sig["x2"][:, ch0:ch0 + 8, :], op=ALU.mult),
                             reads=[("e2", i), ("sig", "x2")], writes=[("e1", i)])
                        S.op("pool", lambda e: e.tensor_tensor(out=oblk[:, ch0:ch0 + 8, :], in0=e1[:], in1=sig["g"][:, ch0:ch0 + 8, :], op=ALU.mult),
                             reads=[("e1", i), ("sig", "g")], writes=[("oblk", oc8)])
            S.dma("pool", Sc["yhyT"][c0:c0 + CG, :].rearrange("c (a b) -> a c b", b=128), oblk[:], reads=[("oblk", q) for q in range(CG // 8)], writes=[("yhyT", cg)])


Prog.phaseHY = _phaseHY


NCORES = 8


def kernel(**inputs):
    P = Prog(nlayers=DEPTH, debug=False)
    nc = P.build()
    names = set(P.I.keys())
    x = np.asarray(inputs["x"], np.float32)
    Wj = []
    for j in range(SP):
        W = _prep_weights(inputs, j)
        W.update(_prep_mixer(inputs, j))
        Wj.append({k: v for k, v in W.items() if k in names})
    in_maps = []
    for c in range(NCORES):
        b, j = c // SP, c % SP
        m = dict(Wj[j])
        m["x"] = np.ascontiguousarray(x[b])
        m["xh"] = np.ascontiguousarray(x[b][:, OW * j:OW * j + OW])
        in_maps.append(m)
    res = run_bass_kernel_spmd(nc, in_maps, core_ids=list(range(NCORES)))
    out = np.empty((NCORES // SP, T, D), np.float32)
    for c in range(NCORES):
        b, j = c // SP, c % SP
        out[b][:, OW * j:OW * j + OW] = np.asarray(res.results[c]["out"], np.float32)
    return out
```

```python
import math
from contextlib import ExitStack

import numpy as np
import concourse.bass as bass
import concourse.mybir as mybir
from concourse.bass_utils import run_bass_kernel_spmd

F32 = mybir.dt.float32
BF16 = mybir.dt.bfloat16
AF = mybir.ActivationFunctionType
ALU = mybir.AluOpType

T = 8192
D = 1024
NT = 64
DEPTH = 2
EPS = 1e-6
SAME_ENGINE_SYNC = True
SAME_WAW = False
SAME_WAR = False

O_HYU, O_HYG, O_MBX, O_MBB, O_MBC, O_MBZ, O_MBDT = 0, 1536, 2048, 2560, 2816, 3072, 3584
O_MLQ, O_MLK, O_MLV, O_MLO, O_MLZ, O_MLG = 3600, 4112, 4624, 5136, 5648, 6160
O_NAQ, O_NAK, O_NAV, O_NAG = 6176, 6688, 7200, 7712

SP = 2
CW = 512 // SP
HC = CW
MBU = 8 // SP
MBG = 2 // SP
MLU = 4 // SP
NAH = 8 // SP
OW = 1024 // SP
YW = 3 * CW
NSM = 2 * MBU + 4 * MLU
NSP = 32
NB = CW // 128
FM_GROUPS = [
    ("hyu", O_HYU, 3 * NB, "conv"),
    ("hyg", O_HYG, NB, "silu"),
    ("mbx", O_MBX, NB, "convsilu"),
    ("mbB", O_MBB, MBG, "convsilu"),
    ("mbC", O_MBC, MBG, "convsilu"),
    ("mlq", O_MLQ, NB, "convsilu"),
    ("mlk", O_MLK, NB, "convsilu"),
    ("naq", O_NAQ, NB, "qknorm"),
    ("nak", O_NAK, NB, "qknorm"),
]
FM_BLOCKS = []
for _n, _o, _k, _t in FM_GROUPS:
    for _i in range(_k):
        FM_BLOCKS.append((_n, _i, _o + 128 * _i, _t))
NFM = len(FM_BLOCKS)
TM_PARTS = [("mbz", O_MBZ, "silu"), ("mlv", O_MLV, "copy"), ("mlo", O_MLO, "sigmoid"),
            ("mlz", O_MLZ, "silu"), ("nav", O_NAV, "copy"), ("nag", O_NAG, "silu")]
NTM = len(TM_PARTS) * CW // 512


import os
_SKIP = set(os.environ.get("KSKIP", "").split(","))
_UID = [0]


def _alloc(nc, kind, name, shape, dt):
    _UID[0] += 1
    nm = "%s_%d" % (name, _UID[0])
    if kind == "sbuf":
        return nc.sbuf_tensor(nm, shape, dt)
    return nc.psum_tensor(nm, shape, dt)


class Sched:
    def __init__(self, nc, stack, n_dma_sems=14):
        self.nc = nc
        self.eng = {"pe": nc.tensor, "dve": nc.vector, "act": nc.scalar, "pool": nc.gpsimd, "sp": nc.sync}
        self.sem = {e: stack.enter_context(nc.semaphore("c_" + e)) for e in self.eng}
        self.cnt = {e: 0 for e in self.eng}
        self.seen = {e: {} for e in self.eng}
        self.dq = {}
        for q in ("sp", "pool", "act"):
            self.dq[q] = [[stack.enter_context(nc.semaphore("d_%s%d" % (q, i))), 0] for i in range(n_dma_sems)]
        self.dqi = {q: 0 for q in self.dq}
        self.last_w = {}
        self.readers = {}
        self.n_wait = 0
        self.n_ins = 0
        self.ccs = []
        self.cc_toks = []

    def _wait(self, e, tok, raw=True):
        key, sem, val, prod = tok
        if prod == e and (e == "pe" or not SAME_ENGINE_SYNC or not raw):
            return
        if self.seen[e].get(key, 0) >= val:
            return
        self.eng[e].wait_ge(sem, val)
        self.seen[e][key] = val
        self.n_wait += 1

    def _deps(self, e, reads, writes):
        for k in reads:
            t = self.last_w.get(k)
            if t is not None:
                self._wait(e, t, True)
        for k in writes:
            t = self.last_w.get(k)
            if t is not None:
                self._wait(e, t, SAME_WAW)
            for t in self.readers.get(k, ()):
                self._wait(e, t, SAME_WAR)

    def _commit(self, tok, reads, writes):
        for k in writes:
            self.last_w[k] = tok
            self.readers[k] = []
        for k in reads:
            if k in writes:
                continue
            lst = self.readers.setdefault(k, [])
            lst.append(tok)
            if len(lst) > 48:
                best = {}
                for t in lst:
                    if t[0] not in best or best[t[0]][2] < t[2]:
                        best[t[0]] = t
                self.readers[k] = list(best.values())

    def op(self, e, fn, reads=(), writes=()):
        self._deps(e, reads, writes)
        ins = fn(self.eng[e])
        self.cnt[e] += 1
        ins.then_inc(self.sem[e], 1)
        tok = ("c_" + e, self.sem[e], self.cnt[e], e)
        self._commit(tok, reads, writes)
        self.n_ins += 1
        return ins

    def dma(self, q, out, in_, reads=(), writes=(), **kw):
        self._deps(q, reads, writes)
        idx = self.dqi[q]
        slot = self.dq[q][idx]
        self.dqi[q] = (idx + 1) % len(self.dq[q])
        sem, val = slot
        key = "d_%s%d" % (q, idx)
        if val > 0 and self.seen[q].get(key, 0) < val:
            self.eng[q].wait_ge(sem, val)
            self.seen[q][key] = val
        ins = self.eng[q].dma_start(out=out, in_=in_, **kw)
        ins.then_inc(sem, 16)
        slot[1] = val + 16
        tok = (key, sem, val + 16, None)
        self._commit(tok, reads, writes)
        self.n_ins += 1
        return ins

    def collective(self, src, dst, tn, stack, rpc):
        self._deps("pool", [src], [dst])
        groups = [[SP * g + r for r in range(SP)] for g in range(8 // SP)]
        rows = tn[src].shape[0]
        toks = []
        for q in range(rows // rpc):
            sem = stack.enter_context(self.nc.semaphore("cc%d" % len(self.ccs)))
            self.ccs.append(sem)
            ins = self.nc.gpsimd.collective_compute(
                "AllGather", ALU.bypass, replica_groups=groups,
                ins=[tn[src].ap()[rpc * q:rpc * q + rpc, :].opt()],
                outs=[tn[dst].ap()[SP * rpc * q:SP * rpc * q + SP * rpc, :].opt()])
            ins.then_inc(sem)
            tok = ("cc%d" % (len(self.ccs) - 1), sem, 1, None)
            self.eng["pool"].wait_ge(sem, 1)
            self.seen["pool"][tok[0]] = 1
            toks.append(tok)
            self.cc_toks.append(tok)
        self._commit(toks[-1], [src], [dst])

    def barrier(self):
        for e in self.eng:
            for t in self.cc_toks:
                self._wait(e, t)
        for e in self.eng:
            for p in self.eng:
                if p != e and self.cnt[p] > 0:
                    self._wait(e, ("c_" + p, self.sem[p], self.cnt[p], p))
                elif p == e and self.cnt[p] > 0 and e != "pe":
                    self._wait(e, ("c_" + p, self.sem[p], self.cnt[p], None))
            for q in self.dq:
                for i, (sem, val) in enumerate(self.dq[q]):
                    if val > 0:
                        self._wait(e, ("d_%s%d" % (q, i), sem, val, None))
        self.last_w = {}
        self.readers = {}


class _KeyNS:
    def __init__(self, S, prefix):
        self.S = S
        self.p = prefix

    def _k(self, keys):
        return [(self.p, k) for k in keys]

    def op(self, e, fn, reads=(), writes=()):
        return self.S.op(e, fn, self._k(reads), self._k(writes))

    def dma(self, q, out, in_, reads=(), writes=(), **kw):
        return self.S.dma(q, out, in_, self._k(reads), self._k(writes), **kw)


class Prog:
    def __init__(self, nlayers=DEPTH, debug=False, phases=("A", "HY", "NA", "DLA", "Z", "G")):
        self.nlayers = nlayers
        self.debug = debug
        self.phases = phases

    def dram_in(self, name, shape, dt=F32):
        return self.nc.dram_tensor(name, list(shape), dt, kind="ExternalInput").ap()

    def dram_scr(self, name, shape, dt=BF16, dbg=False):
        kind = "ExternalOutput" if (dbg and self.debug) else "Internal"
        if name in getattr(self, "feed", ()):
            kind = "ExternalInput"
        return self.nc.dram_tensor(name, list(shape), dt, kind=kind).ap()

    def build(self):
        nc = bass.Bass("TRN2", target_bir_lowering=False)
        self.nc = nc
        L = DEPTH
        I = self.I = {}
        I["x"] = self.dram_in("x", [T, D])
        I["norm_w"] = self.dram_in("norm_w", [L, 1, D])
        I["wfm"] = self.dram_in("wfm", [L, NFM, 128, 1024])
        I["wsm"] = self.dram_in("wsm", [L, 128, 8 * NSP])
        I["xh"] = self.dram_in("xh", [T, OW])
        I["wtm"] = self.dram_in("wtm", [L, NTM, 128, 4096])
        I["wout"] = self.dram_in("wout", [L, 128, 16 * OW])
        I["convp"] = self.dram_in("convp", [L, NFM, 128, 4])
        I["ident"] = self.dram_in("ident", [128, 128])
        I["blockones"] = self.dram_in("blockones", [128, 128])
        self.declare_mixer_inputs()
        self.out = nc.dram_tensor("out", [T, OW], F32, kind="ExternalOutput").ap()
        Sc = self.Sc = {}
        for n, rows in [("hyuT", 3 * CW), ("hygT", CW), ("mbBT", 128 * MBG), ("mbCT", 128 * MBG), ("mlqT", CW),
                        ("mlkT", CW), ("naqT", CW), ("nakT", CW)]:
            Sc[n] = self.dram_scr(n, [rows, T], dbg=True)
        for n, cols in [("mbx", CW), ("mbB", 128 * MBG), ("mlk", CW), ("mbz", CW), ("mlv", CW), ("mlo", CW),
                        ("mlz", CW), ("nav", CW), ("nag", CW)]:
            Sc[n] = self.dram_scr(n, [T, cols], dbg=True)
        Sc["smallT"] = self.dram_scr("smallT", [NSP, T], F32, dbg=True)
        mbxB = self.dram_scr("mbxB", [T, CW + 128 * MBG])
        Sc["mbx"], Sc["mbB"], Sc["mbxB"] = mbxB[:, 0:CW], mbxB[:, CW:CW + 128 * MBG], mbxB
        mbBCT = self.dram_scr("mbBCT", [2 * 128 * MBG, T])
        Sc["mbBT"], Sc["mbCT"], Sc["mbBCT"] = mbBCT[0:128 * MBG, :], mbBCT[128 * MBG:2 * 128 * MBG, :], mbBCT
        mlkqT = self.dram_scr("mlkqT", [2 * CW, T])
        Sc["mlkT"], Sc["mlqT"], Sc["mlkqT"] = mlkqT[0:CW, :], mlkqT[CW:2 * CW, :], mlkqT
        self.Tn = {}
        for n, shp, dt in [("ytm", [T, YW], BF16), ("yhyT", [HC, T], BF16), ("xres", [T, OW], F32),
                           ("ytm_g", [SP * T, YW], BF16), ("yhy_g", [SP * HC, T], BF16), ("xg", [SP * T, OW], F32)]:
            if self.debug and n in ("ytm", "yhyT"):
                self.Tn[n] = nc.dram_tensor(n, shp, dt, kind="ExternalOutput")
            elif n in getattr(self, "feed", ()):
                self.Tn[n] = nc.dram_tensor(n, shp, dt, kind="ExternalInput")
            else:
                self.Tn[n] = nc.dram_tensor(n, shp, dt)
            Sc[n] = self.Tn[n].ap()
        self.declare_mixer_scratch()

        with ExitStack() as st:
            self.S = Sched(nc, st)
            S = self.S
            self.ident_f = st.enter_context(_alloc(nc, "sbuf", "ident_f", [128, 128], F32))
            self.ident_b = st.enter_context(_alloc(nc, "sbuf", "ident_b", [128, 128], BF16))
            self.bones_b = st.enter_context(_alloc(nc, "sbuf", "bones_b", [128, 128], BF16))
            tmpc = st.enter_context(_alloc(nc, "sbuf", "tmpc", [128, 128], F32))
            S.dma("sp", self.ident_f[:], I["ident"], writes=["ident_f"])
            S.dma("sp", tmpc[:], I["blockones"], writes=["tmpc"])
            S.op("dve", lambda e: e.tensor_copy(out=self.ident_b[:], in_=self.ident_f[:]), reads=["ident_f"], writes=["ident_b"])
            S.op("dve", lambda e: e.tensor_copy(out=self.bones_b[:], in_=tmpc[:]), reads=["tmpc"], writes=["bones_b"])
            S.barrier()
            for l in range(self.nlayers):
                x_dst = self.out if l == self.nlayers - 1 else Sc["xres"]
                x_res = I["xh"] if l == 0 else Sc["xres"]
                self.marks = getattr(self, "marks", [])
                mark = lambda nm: self.marks.append((nm, l, dict(S.cnt)))
                mark("start")
                if "A" in self.phases:
                    self.phaseA(l)
                    S.barrier()
                    mark("A")
                if "HY" in self.phases:
                    self.phaseHY(l)
                    S.barrier()
                    mark("HY")
                if "NA" in self.phases:
                    self.phaseNA(l)
                    S.barrier()
                    mark("NA")
                if "DLA" in self.phases:
                    self.phaseDLA(l)
                    S.barrier()
                    mark("DLA")
                if "Z" in self.phases:
                    if SP > 1 and "G" in self.phases:
                        S.collective("ytm", "ytm_g", self.Tn, st, 1024)
                        S.collective("yhyT", "yhy_g", self.Tn, st, 64)
                        S.barrier()
                    self.phaseZ(l, x_res, x_dst)
                    S.barrier()
                    if SP > 1 and l < self.nlayers - 1 and "G" in self.phases:
                        S.collective("xres", "xg", self.Tn, st, 1024)
                        S.barrier()
                    mark("Z")
            S.barrier()
        return nc

    def declare_mixer_inputs(self):
        _na_declare_inputs(self)

    def declare_mixer_scratch(self):
        _dla_declare(self)
        _hy_declare(self)

    def phaseA(self, l):
        nc, S, I, Sc = self.nc, self.S, self.I, self.Sc
        with ExitStack() as st:
            sb = lambda n, s, d=F32: st.enter_context(_alloc(nc, "sbuf", n, list(s), d))
            ps = lambda n, s, d=F32: st.enter_context(_alloc(nc, "psum", n, list(s), d))
            hT = sb("hT", [128, 8, T], BF16)
            with ExitStack() as st0:
                sb0 = lambda n, s, d=F32: st0.enter_context(_alloc(nc, "sbuf", n, list(s), d))
                nwb = sb0("nwb", [128, D])
                S.dma("sp", nwb[:], I["norm_w"][l].broadcast_to([128, D]), writes=["nwb"])
                xt = [sb0("xt%d" % i, [128, D]) for i in range(2)]
                junk = sb0("junk", [128, D], BF16)
                ss = [sb0("ss%d" % i, [128, 1]) for i in range(2)]
                xn = [sb0("xn%d" % i, [128, D], BF16) for i in range(2)]
                pt = [st0.enter_context(_alloc(nc, "psum", "pt%d" % i, [128, 512], BF16)) for i in range(2)]
                for i in range(NT):
                    b = i % 2
                    if l == 0 or SP == 1:
                        S.dma("sp", xt[b][:], I["x"][128 * i:128 * i + 128, :], writes=[("xt", b)])
                    else:
                        S.dma("sp", xt[b][:].rearrange("p (r c) -> p r c", r=SP),
                              Sc["xg"].rearrange("(q r t) c -> q t r c", q=8, r=SP)[i // 8][128 * (i % 8):128 * (i % 8) + 128, :, :], writes=[("xt", b)])
                    S.op("act", lambda e: e.activation(out=junk[:], in_=xt[b][:], func=AF.Square, accum_out=ss[b][:]),
                         reads=[("xt", b)], writes=["junk", ("ss", b)])
                    S.op("dve", lambda e: e.tensor_scalar(out=ss[b][:], in0=ss[b][:], scalar1=1.0 / D, scalar2=EPS,
                                                          op0=ALU.mult, op1=ALU.add), reads=[("ss", b)], writes=[("ss", b)])
                    S.op("act", lambda e: e.activation(out=ss[b][:], in_=ss[b][:], func=AF.Sqrt), reads=[("ss", b)], writes=[("ss", b)])
                    S.op("dve", lambda e: e.reciprocal(out=ss[b][:], in_=ss[b][:]), reads=[("ss", b)], writes=[("ss", b)])
                    S.op("dve", lambda e: e.scalar_tensor_tensor(out=xn[b][:], in0=xt[b][:], scalar=ss[b][:], in1=nwb[:],
                                                                 op0=ALU.mult, op1=ALU.mult),
                         reads=[("xt", b), ("ss", b), "nwb"], writes=[("xn", b)])
                    for h in range(2):
                        for k in range(4):
                            kc = 4 * h + k
                            S.op("pe", lambda e: e.transpose(pt[h][:, 128 * k:128 * k + 128], xn[b][:, 128 * kc:128 * kc + 128], self.ident_b[:]),
                                 reads=[("xn", b)], writes=[("pt", h)])
                        dst = hT[:, 4 * h:4 * h + 4, 128 * i:128 * i + 128]
                        src = pt[h][:].rearrange("p (k t) -> p k t", t=128)
                        if h == 0:
                            S.op("act", lambda e: e.activation(out=dst, in_=src, func=AF.Copy), reads=[("pt", h)], writes=[("hT", i)])
                        else:
                            S.op("dve", lambda e: e.tensor_copy(out=dst, in_=src), reads=[("pt", h)], writes=[("hT", i)])
            S.barrier()
            hkeys = [("hT", i) for i in range(NT)]
            with ExitStack() as st1:
                sb1 = lambda n, s, d=F32: st1.enter_context(_alloc(nc, "sbuf", n, list(s), d))
                wst = [sb1("wst%d" % i, [128, 1024]) for i in range(2)]
                wb = [sb1("wb%d" % i, [128, 8, 128], BF16) for i in range(2)]
                row = [sb1("row%d" % i, [128, T + 2], BF16) for i in range(2)]
                acc = [sb1("acc%d" % i, [128, 1024]) for i in range(2)]
                ob = [sb1("ob%d" % i, [128, 1024], BF16) for i in range(2)]
                tms = [sb1("tms%d" % i, [128, 8, 128], BF16) for i in range(2)]
                sq = [sb1("sq%d" % i, [128, 512], BF16) for i in range(2)]
                rt = [sb1("rt%d" % i, [128, 512]) for i in range(2)]
                cp = [sb1("cp%d" % i, [128, 4]) for i in range(2)]
                pm = [st1.enter_context(_alloc(nc, "psum", "pm%d" % i, [128, 512], F32)) for i in range(4)]
                ptr = [st1.enter_context(_alloc(nc, "psum", "ptr%d" % i, [128, 512], BF16)) for i in range(2)]
                pss = [st1.enter_context(_alloc(nc, "psum", "pss%d" % i, [128, 512], F32)) for i in range(2)]
                for b in range(2):
                    S.op("pool", lambda e: e.memset(row[b][:, 0:1], 0.0), writes=[("rowh", b)])
                    S.op("pool", lambda e: e.memset(row[b][:, T + 1:T + 2], 0.0), writes=[("rowh", b)])
                cnt = {"ev": 0, "tr": 0, "ob": 0, "ac": 0, "q": 0}

                def mm_block(wtile, wkey, M, dst_fn, dst_keys_fn):
                    for i in range(16):
                        p = pm[i % 4]
                        for kc in range(8):
                            S.op("pe", lambda e: e.matmul(p[0:M, :], lhsT=wtile[:, kc, 0:M], rhs=hT[:, kc, 512 * i:512 * i + 512],
                                                          start=(kc == 0), stop=(kc == 7)),
                                 reads=[wkey] + hkeys[4 * i:4 * i + 4], writes=[("pm", i % 4)])
                        dst = dst_fn(i)
                        if cnt["ev"] % 2 == 0:
                            S.op("act", lambda e: e.activation(out=dst, in_=p[0:M, :], func=AF.Copy), reads=[("pm", i % 4)], writes=dst_keys_fn(i))
                        else:
                            S.op("dve", lambda e: e.tensor_copy(out=dst, in_=p[0:M, :]), reads=[("pm", i % 4)], writes=dst_keys_fn(i))
                        cnt["ev"] += 1

                for bi, (gname, gi, col0, ptype) in enumerate(FM_BLOCKS):
                    if gname in _SKIP or "fm" in _SKIP:
                        continue
                    b = bi % 2
                    S.dma("sp", wst[b][:], I["wfm"][l, bi], writes=[("wst", b)])
                    S.op("pool", lambda e: e.tensor_copy(out=wb[b][:].rearrange("p k c -> p (k c)"), in_=wst[b][:]),
                         reads=[("wst", b)], writes=[("wtile", b)])
                    if ptype in ("conv", "convsilu"):
                        S.dma("sp", cp[b][:], I["convp"][l, bi], writes=[("cp", b)])
                    elif ptype == "qknorm":
                        S.dma("sp", cp[b][:], I["convp"][l, bi], writes=[("cp", b)])
                    rw = row[b]
                    mm_block(wb[b], ("wtile", b), 128, lambda i: rw[:, 1 + 512 * i:1 + 512 * i + 512], lambda i: [("row", b, i)])
                    rkeys = [("row", b, i) for i in range(16)] + [("rowh", b)]
                    r0 = 128 * gi
                    if ptype in ("conv", "convsilu", "silu"):
                        for j in range(8):
                            a = acc[cnt["ac"] % 2]; ak = ("acc", cnt["ac"] % 2); cnt["ac"] += 1
                            o = ob[cnt["ob"] % 2]; okey = ("ob", cnt["ob"] % 2); cnt["ob"] += 1
                            rk = [("row", b, i) for i in range(max(0, 2 * j - 1), min(16, 2 * j + 3))] + [("rowh", b)]
                            c0 = 1024 * j
                            if ptype == "silu":
                                S.op("act", lambda e: e.activation(out=o[:], in_=rw[:, 1 + c0:1 + c0 + 1024], func=AF.Silu), reads=rk, writes=[okey])
                            else:
                                S.op("act", lambda e: e.activation(out=a[:], in_=rw[:, 1 + c0:1 + c0 + 1024], func=AF.Identity,
                                                                   scale=cp[b][:, 1:2], bias=cp[b][:, 3:4]), reads=rk + [("cp", b)], writes=[ak])
                                S.op("dve", lambda e: e.scalar_tensor_tensor(out=a[:], in0=rw[:, c0:c0 + 1024], scalar=cp[b][:, 0:1], in1=a[:],
                                                                             op0=ALU.mult, op1=ALU.add), reads=rk + [("cp", b), ak], writes=[ak])
                                S.op("dve", lambda e: e.scalar_tensor_tensor(out=a[:], in0=rw[:, 2 + c0:2 + c0 + 1024], scalar=cp[b][:, 2:3], in1=a[:],
                                                                             op0=ALU.mult, op1=ALU.add), reads=rk + [("cp", b), ak], writes=[ak])
                                if ptype == "convsilu":
                                    S.op("act", lambda e: e.activation(out=o[:], in_=a[:], func=AF.Silu), reads=[ak], writes=[okey])
                                else:
                                    S.op("pool", lambda e: e.tensor_copy(out=o[:], in_=a[:]), reads=[ak], writes=[okey])
                            fm_dst = {"hyu": "hyuT", "hyg": "hygT", "mbB": "mbBT", "mbC": "mbCT", "mlq": "mlqT", "mlk": "mlkT"}.get(gname)
                            if fm_dst is not None:
                                S.dma("pool", Sc[fm_dst][r0:r0 + 128, c0:c0 + 1024], o[:], reads=[okey], writes=[(fm_dst, gi, j)])
                            tm_dst = {"mbx": "mbx", "mbB": "mbB", "mlk": "mlk"}.get(gname)
                            if tm_dst is not None:
                                tmb = cnt["tr"] % 2; cnt["tr"] += 1
                                for h in range(2):
                                    for k in range(4):
                                        s = 4 * h + k
                                        S.op("pe", lambda e: e.transpose(ptr[h][:, 128 * k:128 * k + 128], o[:, 128 * s:128 * s + 128], self.ident_b[:]),
                                             reads=[okey], writes=[("ptr", h)])
                                    src = ptr[h][:].rearrange("p (k c) -> p k c", c=128)
                                    if h == 0:
                                        S.op("act", lambda e: e.activation(out=tms[tmb][:, 0:4, :], in_=src, func=AF.Copy), reads=[("ptr", h)], writes=[("tms", tmb, 0)])
                                    else:
                                        S.op("dve", lambda e: e.tensor_copy(out=tms[tmb][:, 4:8, :], in_=src), reads=[("ptr", h)], writes=[("tms", tmb, 1)])
                                d = Sc[tm_dst][c0:c0 + 1024, r0:r0 + 128].rearrange("(s p) c -> p s c", p=128)
                                S.dma("pool", d, tms[tmb][:], reads=[("tms", tmb, 0), ("tms", tmb, 1)], writes=[(tm_dst, gi, j)])
                    elif ptype == "qknorm":
                        fm_dst = {"naq": "naqT", "nak": "nakT"}[gname]
                        for j in range(8):
                            o = ob[cnt["ob"] % 2]; okey = ("ob", cnt["ob"] % 2); cnt["ob"] += 1
                            for hh in range(2):
                                q = cnt["q"] % 2; cnt["q"] += 1
                                c0 = 1024 * j + 512 * hh
                                rk = [("row", b, 2 * j + hh)]
                                S.op("act", lambda e: e.activation(out=sq[q][:], in_=rw[:, 1 + c0:1 + c0 + 512], func=AF.Square), reads=rk, writes=[("sq", q)])
                                S.op("pe", lambda e: e.matmul(pss[q][:], lhsT=self.bones_b[:], rhs=sq[q][:], start=True, stop=True),
                                     reads=[("sq", q)], writes=[("pss", q)])
                                S.op("dve", lambda e: e.tensor_scalar(out=rt[q][:], in0=pss[q][:], scalar1=1.0 / 64, scalar2=EPS, op0=ALU.mult, op1=ALU.add),
                                     reads=[("pss", q)], writes=[("rt", q)])
                                S.op("act", lambda e: e.activation(out=rt[q][:], in_=rt[q][:], func=AF.Sqrt), reads=[("rt", q)], writes=[("rt", q)])
                                S.op("dve", lambda e: e.reciprocal(out=rt[q][:], in_=rt[q][:]), reads=[("rt", q)], writes=[("rt", q)])
                                S.op("dve", lambda e: e.scalar_tensor_tensor(out=o[:, 512 * hh:512 * hh + 512], in0=rw[:, 1 + c0:1 + c0 + 512], scalar=cp[b][:, 0:1],
                                                                             in1=rt[q][:], op0=ALU.mult, op1=ALU.mult),
                                     reads=rk + [("rt", q), ("cp", b)], writes=[okey])
                            S.dma("pool", Sc[fm_dst][r0:r0 + 128, 1024 * j:1024 * j + 1024], o[:], reads=[okey], writes=[(fm_dst, gi, j)])
            S.barrier()
            with ExitStack() as st3:
                sb3 = lambda n, s, d=F32: st3.enter_context(_alloc(nc, "sbuf", n, list(s), d))
                smallsb = sb3("smallsb", [NSP, T])
                wsst = sb3("wsst", [128, 8 * NSP])
                wsb = sb3("wsb", [128, 8, NSP], BF16)
                pm3 = [st3.enter_context(_alloc(nc, "psum", "pm3%d" % i, [128, 512], F32)) for i in range(4)]
                S.dma("sp", wsst[:], I["wsm"][l], writes=["wsst"])
                S.op("pool", lambda e: e.tensor_copy(out=wsb[:].rearrange("p k c -> p (k c)"), in_=wsst[:]), reads=["wsst"], writes=["wsb"])
                for i in range(16 if "small" not in _SKIP else 0):
                    p = pm3[i % 4]
                    for kc in range(8):
                        S.op("pe", lambda e: e.matmul(p[0:NSP, :], lhsT=wsb[:, kc, :], rhs=hT[:, kc, 512 * i:512 * i + 512], start=(kc == 0), stop=(kc == 7)),
                             reads=["wsb"] + hkeys[4 * i:4 * i + 4], writes=[("pm3", i % 4)])
                    S.op("act", lambda e: e.activation(out=smallsb[:, 512 * i:512 * i + 512], in_=p[0:NSP, :], func=AF.Copy), reads=[("pm3", i % 4)], writes=[("smallsb", i)])
                S.dma("pool", Sc["smallT"], smallsb[:], reads=[("smallsb", i) for i in range(16)], writes=["smallT"])
            S.barrier()
            with ExitStack() as st2:
                sb2 = lambda n, s, d=F32: st2.enter_context(_alloc(nc, "sbuf", n, list(s), d))
                wst2 = sb2("wst2", [128, 4096])
                wtb = [sb2("wtb%d" % i, [128, 8, 512], BF16) for i in range(2)]
                ot = [sb2("ot%d" % i, [128, 512], BF16) for i in range(4)]
                pm2 = [st2.enter_context(_alloc(nc, "psum", "pm2%d" % i, [128, 512], F32)) for i in range(4)]
                ppg = 512 // CW
                for g in range(NTM if "tm" not in _SKIP else 0):
                    b = g % 2
                    parts = TM_PARTS[ppg * g:ppg * g + ppg]
                    S.dma("sp", wst2[:], I["wtm"][l, g], writes=["wst2"])
                    S.op("pool", lambda e: e.tensor_copy(out=wtb[b][:].rearrange("p k c -> p (k c)"), in_=wst2[:]), reads=["wst2"], writes=[("wtb", b)])
                    for i in range(NT):
                        p = pm2[i % 4]
                        for kc in range(8):
                            S.op("pe", lambda e: e.matmul(p[:], lhsT=hT[:, kc, 128 * i:128 * i + 128], rhs=wtb[b][:, kc, :], start=(kc == 0), stop=(kc == 7)),
                                 reads=[("wtb", b), ("hT", i)], writes=[("pm2", i % 4)])
                        o = ot[i % 4]
                        for pi, (gname, col0, act) in enumerate(parts):
                            func = {"silu": AF.Silu, "copy": AF.Copy, "sigmoid": AF.Sigmoid}[act]
                            sl = slice(CW * pi, CW * pi + CW)
                            S.op("act", lambda e: e.activation(out=o[:, sl], in_=p[:, sl], func=func), reads=[("pm2", i % 4)], writes=[("ot", i % 4, pi)])
                            S.dma("pool" if (i + pi) % 2 else "sp", Sc[gname][128 * i:128 * i + 128, :], o[:, sl], reads=[("ot", i % 4, pi)], writes=[(gname, i)])

    def phaseZ(self, l, x_res, x_dst):
        nc, S, I, Sc = self.nc, self.S, self.I, self.Sc
        gathered = SP > 1 and "G" in self.phases
        with ExitStack() as st:
            sb = lambda n, s, d=F32: st.enter_context(_alloc(nc, "sbuf", n, list(s), d))
            wo = sb("wo", [128, 16, OW], BF16)
            wos = [sb("wos%d" % i, [128, 2048]) for i in range(2)]
            nck = 16 * OW // 2048
            kpc = 2048 // OW
            for c in range(nck):
                S.dma("sp", wos[c % 2][:], I["wout"][l, :, 2048 * c:2048 * c + 2048], writes=[("wos", c % 2)])
                S.op("pool", lambda e: e.tensor_copy(out=wo[:, kpc * c:kpc * c + kpc, :].rearrange("p k c -> p (k c)"), in_=wos[c % 2][:]),
                     reads=[("wos", c % 2)], writes=["wo"])
            yt = [sb("yt%d" % i, [128, SP, YW], BF16) for i in range(2)]
            yT = [sb("yT%d" % i, [128, 16, 128], BF16) for i in range(2)]
            xt = [sb("xz%d" % i, [128, OW]) for i in range(2)]
            oz = [sb("oz%d" % i, [128, OW]) for i in range(2)]
            ptz = [st.enter_context(_alloc(nc, "psum", "ptz%d" % i, [128, 512], BF16)) for i in range(3)]
            pz = [st.enter_context(_alloc(nc, "psum", "pz%d" % i, [128, 512], F32)) for i in range(4)]
            ysrc = Sc["ytm_g"] if gathered else Sc["ytm"]
            hsrc = Sc["yhy_g"] if gathered else Sc["yhyT"]
            nr = SP if gathered else 1
            cpr = YW // 128
            for i in range(NT):
                b = i % 2
                S.dma("sp", yt[b][:, 0:nr, :], ysrc.rearrange("(q r t) c -> q t r c", q=8, r=nr)[i // 8][128 * (i % 8):128 * (i % 8) + 128, :, :], writes=[("yt", b)])
                S.dma("sp", xt[b][:], x_res[128 * i:128 * i + 128, :], writes=[("xz", b)])
                S.dma("pool", yT[b][:, 0:4 * nr // SP, :], hsrc[:, 128 * i:128 * i + 128].rearrange("(k p) t -> p k t", p=128), writes=[("yTh", b)])
                for h in range(3):
                    for k in range(4):
                        q = 4 * h + k
                        r, cc = q // cpr, q % cpr
                        S.op("pe", lambda e: e.transpose(ptz[h][:, 128 * k:128 * k + 128], yt[b][:, r, 128 * cc:128 * cc + 128], self.ident_b[:]),
                             reads=[("yt", b)], writes=[("ptz", h)])
                    src = ptz[h][:].rearrange("p (k c) -> p k c", c=128)
                    dst = yT[b][:, 4 + 4 * h:8 + 4 * h, :]
                    if h == 1:
                        S.op("dve", lambda e: e.tensor_copy(out=dst, in_=src), reads=[("ptz", h)], writes=[("yTt", b, h)])
                    else:
                        S.op("act", lambda e: e.activation(out=dst, in_=src, func=AF.Copy), reads=[("ptz", h)], writes=[("yTt", b, h)])
                for half in range(OW // 512):
                    p = pz[(2 * i + half) % 4]
                    pk = ("pz", (2 * i + half) % 4)
                    for kc in range(16):
                        S.op("pe", lambda e: e.matmul(p[:], lhsT=yT[b][:, kc, :], rhs=wo[:, kc, 512 * half:512 * half + 512], start=(kc == 0), stop=(kc == 15)),
                             reads=["wo", ("yTh", b)] + [("yTt", b, h) for h in range(3)], writes=[pk])
                    S.op("dve", lambda e: e.tensor_tensor(out=oz[b][:, 512 * half:512 * half + 512], in0=p[:], in1=xt[b][:, 512 * half:512 * half + 512], op=ALU.add),
                         reads=[pk, ("xz", b)], writes=[("oz", b, half)])
                S.dma("pool", x_dst[128 * i:128 * i + 128, :], oz[b][:], reads=[("oz", b, h2) for h2 in range(OW // 512)], writes=[("xdst", i)])


def _fm_col0(gname, gi, j):
    if gname == "hyu":
        part, sub = gi // NB, gi % NB
        return O_HYU + 512 * part + CW * j + 128 * sub
    base = {"hyg": O_HYG, "mbx": O_MBX, "mlq": O_MLQ, "mlk": O_MLK, "naq": O_NAQ, "nak": O_NAK}.get(gname)
    if base is not None:
        return base + CW * j + 128 * gi
    return {"mbB": O_MBB, "mbC": O_MBC}[gname] + 128 * (MBG * j + gi)


def _small_cols(j):
    cols = []
    for d in range(2):
        cols += [O_MBDT + 8 * d + MBU * j + u for u in range(MBU)]
    for d in range(2):
        for g in range(2):
            cols += [O_MLG + 8 * d + 4 * g + MLU * j + u for u in range(MLU)]
    return cols


def _prep_weights(inp, j):
    L = DEPTH
    w_in = np.asarray(inp["w_in"], np.float32)
    w_out = np.asarray(inp["w_out"], np.float32)
    W = {}
    wfm = np.empty((L, NFM, 128, 1024), np.float32)
    convp = np.zeros((L, NFM, 128, 4), np.float32)
    for bi, (gname, gi, _c, ptype) in enumerate(FM_BLOCKS):
        col0 = _fm_col0(gname, gi, j)
        blk = w_in[:, :, col0:col0 + 128]
        wfm[:, bi] = blk.reshape(L, 8, 128, 128).transpose(0, 2, 1, 3).reshape(L, 128, 1024)
        if ptype in ("conv", "convsilu"):
            if gname == "hyu":
                cw, cb, c0 = inp["hy_conv_w"], inp["hy_conv_b"], col0 - O_HYU
            elif gname in ("mbx", "mbB", "mbC"):
                cw, cb, c0 = inp["mb_conv_w"], inp["mb_conv_b"], col0 - O_MBX
            else:
                cw, cb, c0 = inp["ml_conv_w"], inp["ml_conv_b"], col0 - O_MLQ
            convp[:, bi, :, 0:3] = np.asarray(cw)[:, :, c0:c0 + 128].transpose(0, 2, 1)
            convp[:, bi, :, 3] = np.asarray(cb)[:, c0:c0 + 128]
        elif ptype == "qknorm":
            nw = np.asarray(inp["na_qnorm_w"] if gname == "naq" else inp["na_knorm_w"])
            convp[:, bi, :, 0] = np.tile(nw, (1, 2))
    W["wfm"] = wfm
    W["convp"] = convp
    scols = _small_cols(j)
    wsm = np.zeros((L, 8, 128, NSP), np.float32)
    wsm[:, :, :, :NSM] = w_in[:, :, scols].reshape(L, 8, 128, NSM)
    W["wsm"] = np.ascontiguousarray(wsm.transpose(0, 2, 1, 3).reshape(L, 128, 8 * NSP))
    wtm = np.empty((L, NTM, 128, 4096), np.float32)
    ppg = 512 // CW
    for g in range(NTM):
        cols = []
        for (gname, col0, act) in TM_PARTS[ppg * g:ppg * g + ppg]:
            cols += list(range(col0 + CW * j, col0 + CW * j + CW))
        wtm[:, g] = w_in[:, :, cols].reshape(L, 8, 128, 512).transpose(0, 2, 1, 3).reshape(L, 128, 4096)
    W["wtm"] = wtm
    rows = []
    for q in range(HC // 64):
        for r in range(SP):
            rows += list(range(HC * r + 64 * q, HC * r + 64 * q + 64))
    for r in range(SP):
        for base in (512, 1024, 1536):
            rows += list(range(base + CW * r, base + CW * r + CW))
    wo = w_out[:, rows, OW * j:OW * j + OW]
    W["wout"] = np.ascontiguousarray(wo.reshape(L, 16, 128, OW).transpose(0, 2, 1, 3).reshape(L, 128, 16 * OW))
    W["norm_w"] = np.asarray(inp["norm_w"], np.float32).reshape(L, 1, D)
    W["ident"] = np.eye(128, dtype=np.float32)
    bo = np.zeros((128, 128), np.float32)
    bo[:64, :64] = 1.0
    bo[64:, 64:] = 1.0
    W["blockones"] = bo
    return W


def _prep_na(inp, j):
    jc = j
    L = DEPTH
    rpb = np.asarray(inp["na_rpb"], np.float32)
    kk = np.arange(128)
    il, kc = kk // 64, kk % 64
    w = np.arange(64)
    cs = np.clip(w - 8, 0, 48)
    valid = (kc[:, None] >= cs[None, :]) & (kc[:, None] < cs[None, :] + 16)
    coff = np.clip(kc[:, None] - w[None, :] + 15, 0, 30)
    bias = np.zeros((L, 8, 128, 8, 4, 64), np.float32)
    for v in range(8):
        for j in range(4):
            i = 2 * j + il
            roff = v + i
            bias[:, :, :, v, j, :] = rpb[:, :, roff[:, None], coff]
    mask = np.broadcast_to(valid[:, None, :], (128, 4, 64)).astype(np.float32).reshape(128, 256)
    return {"na_bias": np.ascontiguousarray(bias.reshape(L, 8, 128, 2048)[:, NAH * jc:NAH * jc + NAH]), "na_mask": np.ascontiguousarray(mask)}


def _na_declare_inputs(self):
    self.I["na_bias"] = self.dram_in("na_bias", [DEPTH, NAH, 128, 2048])
    self.I["na_mask"] = self.dram_in("na_mask", [128, 256])


def _phaseNA(self, l):
    nc, S, I, Sc = self.nc, self.S, self.I, self.Sc
    NH = NAH
    with ExitStack() as st:
        sb = lambda n, s, d=F32: st.enter_context(_alloc(nc, "sbuf", n, list(s), d))
        KT = [sb("naKT%d" % i, [64, T], BF16) for i in range(2)]
        QT = [sb("naQT%d" % i, [64, T], BF16) for i in range(2)]
        Ve = [sb("naVe%d" % i, [128, 64, 65], BF16) for i in range(2)]
        Vo = [sb("naVo%d" % i, [128, 63, 65], BF16) for i in range(2)]
        EBr = sb("naEBr", [128, 2048])
        EBM = [sb("naEBM%d" % i, [128, 8, 256]) for i in range(2)]
        msk = sb("namask", [128, 256])
        G = [sb("naG%d" % i, [64, 128, 64], BF16) for i in range(2)]
        O = sb("naO", [64, 128, 64])
        Ob = sb("naOb", [64, 128, 64], BF16)
        E = [sb("naE%d" % i, [128, 256]) for i in range(2)]
        Pb = [sb("naP%d" % i, [128, 256], BF16) for i in range(2)]
        rec = [sb("narec%d" % i, [64, 1]) for i in range(2)]
        pS = [st.enter_context(_alloc(nc, "psum", "napS%d" % i, [128, 512], F32)) for i in range(2)]
        pO = [st.enter_context(_alloc(nc, "psum", "napO%d" % i, [128, 512], F32)) for i in range(2)]
        S.dma("sp", msk[:], I["na_mask"], writes=["namask"])
        for b in range(2):
            S.op("pool", lambda e: e.memset(Ve[b][:, :, 64:65], 1.0), writes=[("Veo", b)])
            S.op("pool", lambda e: e.memset(Vo[b][:, :, 64:65], 1.0), writes=[("Voo", b)])
        for h in range(NH):
            b = h % 2
            S.dma("sp", KT[b][:], Sc["nakT"][64 * h:64 * h + 64, :], writes=[("KT", b)])
            S.dma("sp", QT[b][:], Sc["naqT"][64 * h:64 * h + 64, :], writes=[("QT", b)])
            S.dma("pool", Ve[b][:, :, 0:64], Sc["nav"][:, 64 * h:64 * h + 64].rearrange("(i p) d -> p i d", p=128), writes=[("Ve", b)])
            S.dma("pool", Vo[b][:, :, 0:64], Sc["nav"][64:T - 64, 64 * h:64 * h + 64].rearrange("(i p) d -> p i d", p=128), writes=[("Vo", b)])
            S.dma("sp", G[b][:], Sc["nag"][:, 64 * h:64 * h + 64].rearrange("(r w) d -> w r d", w=64), writes=[("G", b)])
            S.dma("sp", EBr[:], I["na_bias"][l, h], writes=["EBr"])
            S.op("act", lambda e: e.activation(out=EBr[:], in_=EBr[:], func=AF.Exp), reads=["EBr"], writes=["EBr"])
            S.op("dve", lambda e: e.tensor_tensor(out=EBM[b][:], in0=EBr[:].rearrange("p (v c) -> p v c", c=256),
                                                  in1=msk[:].unsqueeze(1).broadcast_to([128, 8, 256]), op=ALU.mult),
                 reads=["EBr", "namask"], writes=[("EBM", b)])
            for r in range(128):
                rs = min(max(r - 4, 0), 120)
                v = rs - r + 7
                rb = r % 2
                for j in range(4):
                    S.op("pe", lambda e: e.matmul(pS[rb][:, 64 * j:64 * j + 64], lhsT=KT[b][:, 64 * rs + 128 * j:64 * rs + 128 * j + 128],
                                                  rhs=QT[b][:, 64 * r:64 * r + 64], start=True, stop=True),
                         reads=[("KT", b), ("QT", b)], writes=[("pS", rb)])
                S.op("act", lambda e: e.activation(out=E[rb][:], in_=pS[rb][:, 0:256], func=AF.Exp, scale=0.125), reads=[("pS", rb)], writes=[("E", rb)])
                S.op("dve", lambda e: e.tensor_tensor(out=Pb[rb][:], in0=E[rb][:], in1=EBM[b][:, v, :], op=ALU.mult),
                     reads=[("E", rb), ("EBM", b)], writes=[("P", rb)])
                for j in range(4):
                    if rs % 2 == 0:
                        vt = Ve[b][:, rs // 2 + j, :]
                    else:
                        vt = Vo[b][:, (rs - 1) // 2 + j, :]
                    S.op("pe", lambda e: e.matmul(pO[rb][0:64, 0:65], lhsT=Pb[rb][:, 64 * j:64 * j + 64], rhs=vt, start=(j == 0), stop=(j == 3)),
                         reads=[("P", rb), ("Ve", b), ("Vo", b), ("Veo", b), ("Voo", b)], writes=[("pO", rb)])
                S.op("dve", lambda e: e.reciprocal(out=rec[rb][:], in_=pO[rb][0:64, 64:65]), reads=[("pO", rb)], writes=[("rec", rb)])
                S.op("act", lambda e: e.activation(out=O[:, r, :], in_=pO[rb][0:64, 0:64], func=AF.Copy, scale=rec[rb][:]),
                     reads=[("pO", rb), ("rec", rb)], writes=["O"])
            S.op("dve", lambda e: e.tensor_tensor(out=Ob[:].rearrange("p r d -> p (r d)"), in0=O[:].rearrange("p r d -> p (r d)"),
                                                  in1=G[b][:].rearrange("p r d -> p (r d)"), op=ALU.mult), reads=["O", ("G", b)], writes=["Ob"])
            S.dma("pool", Sc["ytm"][:, 2 * CW + 64 * h:2 * CW + 64 * h + 64].rearrange("(r w) d -> w r d", w=64), Ob[:], reads=["Ob"], writes=[("ytm_na", h)])


Prog.phaseNA = _phaseNA


def _prep_mixer(inp, j):
    W = {}
    W.update(_prep_na(inp, j))
    W.update(_prep_dla(inp, j))
    W.update(_prep_hy(inp, j))
    return W


def _prep_dla(inp, j):
    L = DEPTH
    gpar = np.zeros((L, 4, 64, 2), np.float32)
    dtb = np.asarray(inp["mb_dt_bias"], np.float32)[:, :, MBU * j:MBU * j + MBU]
    alog = np.asarray(inp["mb_a_log"], np.float32)[:, :, MBU * j:MBU * j + MBU]
    gb = np.asarray(inp["ml_gate_b"], np.float32)[:, :, :, MLU * j:MLU * j + MLU]
    for d in range(2):
        gpar[:, d, :8 * MBU, 0] = np.repeat(dtb[:, d, :], 8, axis=1)
        gpar[:, d, :8 * MBU, 1] = np.repeat(alog[:, d, :], 8, axis=1)
        gpar[:, 2 + d, :8 * MLU, 0] = np.repeat(gb[:, d, 0, :], 8, axis=1)
        gpar[:, 2 + d, :8 * MLU, 1] = np.repeat(gb[:, d, 1, :], 8, axis=1)
    rmask = np.ones((64, 1024), np.float32)
    rmask[:, ::128] = 0.0
    s = np.arange(128)[:, None]
    ll = np.arange(128)[None, :]
    negmask = np.stack([np.where(s <= ll, 0.0, -30000.0), np.where(s >= ll, 0.0, -30000.0)]).astype(np.float32)
    return {"gpar": gpar, "rmask": rmask, "negmask": negmask,
            "dsk": np.ascontiguousarray(np.asarray(inp["mb_d"], np.float32)[:, MBU * j:MBU * j + MBU]).reshape(L, 1, MBU),
            "mbnw": np.ascontiguousarray(np.asarray(inp["mb_norm_w"], np.float32)[:, CW * j:CW * j + CW]).reshape(L, 1, CW),
            "mlnw": np.ascontiguousarray(np.asarray(inp["ml_norm_w"], np.float32)[:, CW * j:CW * j + CW]).reshape(L, 1, CW)}


def _dla_declare(self):
    I, Sc = self.I, self.Sc
    I["gpar"] = self.dram_in("gpar", [DEPTH, 4, 64, 2])
    I["rmask"] = self.dram_in("rmask", [64, 1024])
    I["negmask"] = self.dram_in("negmask", [2, 128, 128])
    I["dsk"] = self.dram_in("dsk", [DEPTH, 1, MBU])
    I["mbnw"] = self.dram_in("mbnw", [DEPTH, 1, CW])
    I["mlnw"] = self.dram_in("mlnw", [DEPTH, 1, CW])
    Sc["gq"] = self.dram_scr("gq", [4, 4, 8, T], F32, dbg=True)
    Sc["gcs"] = self.dram_scr("gcs", [4, 8, T], F32, dbg=True)
    Sc["gtot"] = self.dram_scr("gtot", [4, 8, 64], F32, dbg=True)
    Sc["yf_mb"] = self.dram_scr("yf_mb", [T, CW], F32)
    Sc["hf_ml"] = self.dram_scr("hf_ml", [T, CW], F32)
    Sc["yb_mb"] = self.dram_scr("yb_mb", [T, CW], F32)
    Sc["hb_ml"] = self.dram_scr("hb_ml", [T, CW], F32)


def _dla_streams(self, l, mixer):
    nc, S, I, Sc = self.nc, self.S, self.I, self.Sc
    U = MBU if mixer == "mb" else MLU
    P = 8 * U
    with ExitStack() as st:
        sb = lambda n, s, d=F32: st.enter_context(_alloc(nc, "sbuf", n, list(s), d))
        rm = sb("rm", [64, 1024])
        S.dma("sp", rm[:], I["rmask"], writes=["rm"])
        for d in range(2):
            md = (0 if mixer == "mb" else 2) + d
            k = lambda n: (n, d)
            gp = sb("gp%d" % d, [64, 2])
            S.dma("sp", gp[:], I["gpar"][l, md], writes=[k("gp")])
            sc = sb("sc%d" % d, [64, 1024]); a = sb("a%d" % d, [64, 1024]); cs = sb("cs%d" % d, [64, 1024])
            t1 = sb("t1%d" % d, [64, 1024]); t2 = sb("t2%d" % d, [64, 1024]); pp = sb("pp%d" % d, [64, 2])
            if mixer == "mb":
                S.dma("sp", t1[0:P, :], Sc["smallT"][MBU * d:MBU * d + MBU, :].rearrange("u (s n) -> (u s) n", n=1024), writes=[k("t1")])
                S.op("act", lambda e: e.activation(out=t1[0:P, :], in_=t1[0:P, :], func=AF.Exp, bias=gp[0:P, 0:1]), reads=[k("t1"), k("gp")], writes=[k("t1")])
                S.op("act", lambda e: e.activation(out=sc[0:P, :], in_=t1[0:P, :], func=AF.Ln, bias=1.0), reads=[k("t1")], writes=[k("sc")])
                S.op("act", lambda e: e.activation(out=pp[0:P, 0:1], in_=gp[0:P, 1:2], func=AF.Exp), reads=[k("gp")], writes=[k("pp")])
                S.op("dve", lambda e: e.tensor_scalar(out=pp[0:P, 0:1], in0=pp[0:P, 0:1], scalar1=-1.0, scalar2=None, op0=ALU.mult), reads=[k("pp")], writes=[k("pp")])
                S.op("dve", lambda e: e.tensor_scalar(out=a[0:P, :], in0=sc[0:P, :], scalar1=pp[0:P, 0:1], scalar2=None, op0=ALU.mult),
                     reads=[k("sc"), k("pp")], writes=[k("a")])
            else:
                r0 = 2 * MBU + 2 * MLU * d
                S.dma("sp", t1[0:P, :], Sc["smallT"][r0:r0 + MLU, :].rearrange("u (s n) -> (u s) n", n=1024), writes=[k("t1")])
                S.dma("sp", t2[0:P, :], Sc["smallT"][r0 + MLU:r0 + 2 * MLU, :].rearrange("u (s n) -> (u s) n", n=1024), writes=[k("t2")])
                S.op("act", lambda e: e.activation(out=sc[0:P, :], in_=t1[0:P, :], func=AF.Exp, bias=gp[0:P, 0:1]), reads=[k("t1"), k("gp")], writes=[k("sc")])
                S.op("dve", lambda e: e.tensor_scalar(out=sc[0:P, :], in0=sc[0:P, :], scalar1=float(128.0 ** -0.5), scalar2=None, op0=ALU.mult), reads=[k("sc")], writes=[k("sc")])
                S.op("dve", lambda e: e.tensor_scalar(out=pp[0:P, 0:1], in0=gp[0:P, 1:2], scalar1=-1.0, scalar2=None, op0=ALU.mult), reads=[k("gp")], writes=[k("pp")])
                S.op("act", lambda e: e.activation(out=t2[0:P, :], in_=t2[0:P, :], func=AF.Exp, scale=-1.0, bias=pp[0:P, 0:1]), reads=[k("t2"), k("pp")], writes=[k("t2")])
                S.op("act", lambda e: e.activation(out=t2[0:P, :], in_=t2[0:P, :], func=AF.Ln, bias=1.0), reads=[k("t2")], writes=[k("t2")])
                S.op("dve", lambda e: e.tensor_scalar(out=a[0:P, :], in0=t2[0:P, :], scalar1=-1.0, scalar2=None, op0=ALU.mult), reads=[k("t2")], writes=[k("a")])
            S.op("dve", lambda e: e.tensor_tensor_scan(out=cs[0:P, :], data0=rm[0:P, :], data1=a[0:P, :], initial=0.0, op0=ALU.mult, op1=ALU.add),
                 reads=["rm", k("a")], writes=[k("cs")])
            cs3 = cs[0:P, :].rearrange("p (c n) -> p c n", n=128)
            totb = cs3[:, :, 127:128].broadcast_to([P, 8, 128])
            S.dma("sp", Sc["gtot"][md, 0:U, :].rearrange("u (s c) -> (u s) c", c=8), cs3[:, :, 127], reads=[k("cs")], writes=[("gtot", md)], allow_slow_non_contiguous=True)
            t13 = t1[0:P, :].rearrange("p (c n) -> p c n", n=128)
            S.op("dve", lambda e: e.tensor_tensor(out=t13, in0=totb, in1=cs3, op=ALU.subtract), reads=[k("cs")], writes=[k("t1")])
            if d == 1:
                S.op("dve", lambda e: e.tensor_tensor(out=t2[0:P, :], in0=cs[0:P, :], in1=a[0:P, :], op=ALU.subtract), reads=[k("cs"), k("a")], writes=[k("t2")])
                S.op("dve", lambda e: e.tensor_tensor(out=cs[0:P, :], in0=t1[0:P, :], in1=a[0:P, :], op=ALU.add), reads=[k("t1"), k("a")], writes=[k("cs")])
                wexp = t2
                wk = k("t2")
            else:
                wexp = t1
                wk = k("t1")
            unf = lambda ap: ap.rearrange("u (s n) -> (u s) n", n=1024)
            S.dma("sp", unf(Sc["gcs"][md, 0:U, :]), cs[0:P, :], reads=[k("cs")], writes=[("gcs", md)])
            S.op("act", lambda e: e.activation(out=wexp[0:P, :], in_=wexp[0:P, :], func=AF.Exp), reads=[wk], writes=[wk])
            S.op("dve", lambda e: e.tensor_tensor(out=wexp[0:P, :], in0=wexp[0:P, :], in1=sc[0:P, :], op=ALU.mult), reads=[wk, k("sc")], writes=[wk])
            S.dma("sp", unf(Sc["gq"][md, 2, 0:U, :]), wexp[0:P, :], reads=[wk], writes=[("gq", md, 2)])
            S.dma("sp", unf(Sc["gq"][md, 3, 0:U, :]), sc[0:P, :], reads=[k("sc")], writes=[("gq", md, 3)])
            S.op("act", lambda e: e.activation(out=a[0:P, :], in_=cs[0:P, :], func=AF.Exp), reads=[k("cs")], writes=[k("a")])
            S.dma("sp", unf(Sc["gq"][md, 1, 0:U, :]), a[0:P, :], reads=[k("a")], writes=[("gq", md, 1)])
            S.op("dve", lambda e: e.tensor_scalar(out=cs[0:P, :], in0=cs[0:P, :], scalar1=-1.0, scalar2=None, op0=ALU.mult), reads=[k("cs")], writes=[k("cs")])
            S.dma("sp", unf(Sc["gq"][md, 0, 0:U, :]), cs[0:P, :], reads=[k("cs")], writes=[("gq", md, 0)])


def _dla_run(self, l, mixer, d):
    nc, I, Sc = self.nc, self.I, self.Sc
    S = _KeyNS(self.S, (mixer, d))
    mb = mixer == "mb"
    U = MBU if mb else MLU
    PW = 64 if mb else 129
    PS = 64 if mb else 256
    md = (0 if mb else 2) + d
    with ExitStack() as st:
        sb = lambda n, s, dt=F32: st.enter_context(_alloc(nc, "sbuf", n, list(s), dt))
        pst = lambda n, dt=F32: st.enter_context(_alloc(nc, "psum", n, [128, 512], dt))
        Q4 = self.Q4s[d]
        q4t = sb("q4t", [128, 64, 32])
        etot = sb("etot", [128, U, 64])
        negm = sb("negm", [128, 128])
        H = sb("H", [128, U, PW]); Hb = sb("Hb", [128, U, PW], BF16)
        csb = [sb("csb%d" % i, [128, U, 128]) for i in range(4)]
        LT = [sb("LT%d" % i, [128, U, 128]) for i in range(2)]
        MT = [sb("MT%d" % i, [128, U, 128], BF16) for i in range(2)]
        if mb:
            XK = [sb("XK%d" % i, [128, U * 64 + 128 * MBG], BF16) for i in range(4)]
            Xv = [t[:, 0:U * 64].rearrange("p (u w) -> p u w", w=64) for t in XK]
        else:
            Xv = [sb("Xv%d" % i, [128, U, PW], BF16)[:] for i in range(4)]
        Xw = [sb("Xw%d" % i, [128, U, PW], BF16) for i in range(2)]
        if mb:
            Kt = [t[:, U * 64:U * 64 + 128 * MBG] for t in XK]
        else:
            Kt = [sb("Kt%d" % i, [128, CW], BF16)[:] for i in range(4)]
        NG = MBG if mb else MLU
        KQ = [sb("KQ%d" % i, [128, 2, NG, 128], BF16) for i in range(4)]
        KTf = [t[:, 0, :, :] for t in KQ]
        QTf = [t[:, 1, :, :] for t in KQ]
        y2s = [sb("y2s%d" % i, [128, U, PW]) for i in range(2)]
        yo = [sb("yo%d" % i, [128, U, PW]) for i in range(2)]
        fin = [sb("fin%d" % i, [128, CW]) for i in range(2)]
        pG = pst("pG")
        NPT = 1 if mb else (MLU + 1) // 2
        py1 = [pst("py1%d" % i) for i in range(NPT)]
        py2 = [pst("py2%d" % i) for i in range(NPT)]
        pS_ = [pst("pS%d" % i) for i in range(NPT)]
        pQ = py1[0]

        def pview(tiles, u, w):
            if mb:
                return tiles[0][:, 64 * u:64 * u + w]
            return tiles[u // 2][:, 256 * (u % 2):256 * (u % 2) + w]

        def pall(tiles, h):
            if mb:
                return tiles[0][:, 0:64 * U].rearrange("p (u w) -> p u w", w=64)
            return tiles[h][:].rearrange("p (u w) -> p u w", w=256)[:, :, 0:129]

        S.op("pool", lambda e: e.memset(Q4[:], 0.0), writes=["Q4"])
        S.dma("sp", Q4[:], Sc["gq"][md].rearrange("q u t -> (q u) t"), writes=["Q4"])
        for g4 in range(4):
            for k in range(16):
                c = 16 * g4 + k
                S.op("pe", lambda e: e.transpose(pQ[:, 32 * k:32 * k + 32], Q4[:, 128 * c:128 * c + 128], self.ident_f[0:32, 0:32]), reads=["Q4"], writes=["py1"])
            S.op("dve", lambda e: e.tensor_copy(out=q4t[:, 16 * g4:16 * g4 + 16, :], in_=pQ[:].rearrange("p (k q) -> p k q", q=32)), reads=["py1"], writes=["q4t"])
        S.dma("sp", etot[:].rearrange("p u c -> p (u c)"), Sc["gtot"][md:md + 1, 0:U, :].rearrange("o u c -> o (u c)").broadcast_to([128, U * 64]), writes=["etot"])
        S.op("act", lambda e: e.activation(out=etot[:], in_=etot[:], func=AF.Exp), reads=["etot"], writes=["etot"])
        S.dma("sp", negm[:], I["negmask"][d], writes=["negm"])
        S.op("pool", lambda e: e.memset(H[:], 0.0), writes=["H"])
        S.op("pool", lambda e: e.memset(Hb[:], 0.0), writes=["Hb"])
        if not mb:
            for i in range(4):
                S.op("pool", lambda e: e.memset(Xv[i][:, :, 128:129], 1.0), writes=[("Xvo", i)])
        rden = [sb("rden%d" % i, [128, 4]) for i in range(2)]

        order = list(range(64)) if d == 0 else list(range(63, -1, -1))
        srcKQ = Sc["mbBCT"] if mb else Sc["mlkqT"]

        def issue_loads(step_):
            c_, b4 = order[step_], step_ % 4
            q0 = 128 * c_
            S.dma("sp", csb[b4][:], Sc["gcs"][md, 0:U, q0:q0 + 128].partition_broadcast(128), writes=[("csb", b4)])
            if mb:
                S.dma("act", XK[b4][:], Sc["mbxB"][q0:q0 + 128, :], writes=[("Xv", b4), ("Kt", b4)])
            else:
                S.dma("act", Xv[b4][:, :, 0:128], Sc["mlv"][q0:q0 + 128, :].rearrange("p (u w) -> p u w", w=128), writes=[("Xv", b4)])
                S.dma("act", Kt[b4][:], Sc["mlk"][q0:q0 + 128, :], writes=[("Kt", b4)])
            S.dma("act", KQ[b4][:], srcKQ[:, q0:q0 + 128].rearrange("(a g n) s -> n a g s", a=2, n=128), writes=[("KTf", b4), ("QTf", b4)])

        for s0 in range(3):
            issue_loads(s0)
        yield "setup"
        for step, c in enumerate(order):
            k = step % 2
            k4 = step % 4
            r0 = 128 * c
            xvk = [("Xv", k4)] + ([] if mb else [("Xvo", k4)])
            yield "s"
            for g in range(NG):
                S.op("pe", lambda e: e.matmul(pG[:, 128 * g:128 * g + 128], lhsT=KTf[k4][:, g, :], rhs=QTf[k4][:, g, :], start=True, stop=True),
                     reads=[("KTf", k4), ("QTf", k4)], writes=[("pG", g)])
            S.op("dve", lambda e: e.tensor_tensor(out=csb[k4][:], in0=csb[k4][:], in1=negm[:].unsqueeze(1).broadcast_to([128, U, 128]), op=ALU.add),
                 reads=[("csb", k4), "negm"], writes=[("csb", k4)])
            yield "s"
            for u in range(U):
                S.op("act", lambda e: e.activation(out=LT[k][:, u, :], in_=csb[k4][:, u, :], func=AF.Exp, bias=q4t[:, c, u:u + 1]),
                     reads=[("csb", k4), "q4t"], writes=[("LT", k, u)])
            if step + 3 < 64:
                issue_loads(step + 3)
            yield "s"
            for u in range(U):
                g = (u // 4) if mb else u
                S.op("dve", lambda e: e.scalar_tensor_tensor(out=MT[k][:, u, :], in0=pG[:, 128 * g:128 * g + 128], scalar=q4t[:, c, 24 + u:25 + u],
                                                             in1=LT[k][:, u, :], op0=ALU.mult, op1=ALU.mult),
                     reads=[("pG", g), ("LT", k, u), "q4t"], writes=[("MT", k, u)])
            yield "s"
            for u in range(U):
                S.op("pe", lambda e: e.matmul(pview(py1, u, PW), lhsT=MT[k][:, u, :], rhs=Xv[k4][:, u, :], start=True, stop=True),
                     reads=[("MT", k, u)] + xvk, writes=["py1"])
            if mb:
                for g in range(MBG):
                    S.op("pe", lambda e: e.matmul(py2[0][:, 256 * g:256 * g + 256], lhsT=QTf[k4][:, g, :], rhs=Hb[:, 4 * g:4 * g + 4, :],
                                                  start=True, stop=True), reads=[("QTf", k4), "Hb"], writes=["py2"])
            else:
                for u in range(U):
                    S.op("pe", lambda e: e.matmul(pview(py2, u, PW), lhsT=QTf[k4][:, u, :], rhs=Hb[:, u, :], start=True, stop=True),
                         reads=[("QTf", k4), "Hb"], writes=["py2"])
            yield "s"
            ecs_b = lambda u0, n: q4t[:, c, 8 + u0:8 + u0 + n].unsqueeze(2).broadcast_to([128, n, PW])
            w_b = q4t[:, c, 16:16 + U].unsqueeze(2).broadcast_to([128, U, PW])
            if mb:
                S.op("dve", lambda e: e.tensor_tensor(out=y2s[k][:], in0=pall(py2, 0), in1=ecs_b(0, U), op=ALU.mult), reads=["py2", "q4t"], writes=[("y2s", k)])
                S.op("dve", lambda e: e.tensor_tensor(out=yo[k][:], in0=pall(py1, 0), in1=y2s[k][:], op=ALU.add), reads=["py1", ("y2s", k)], writes=[("yo", k)])
            else:
                for h in range(NPT):
                    S.op("dve", lambda e: e.tensor_tensor(out=y2s[k][:, 2 * h:2 * h + 2, :], in0=pall(py2, h), in1=ecs_b(2 * h, 2), op=ALU.mult),
                         reads=["py2", "q4t"], writes=[("y2s", k, h)])
                    S.op("dve", lambda e: e.tensor_tensor(out=yo[k][:, 2 * h:2 * h + 2, :], in0=pall(py1, h), in1=y2s[k][:, 2 * h:2 * h + 2, :], op=ALU.add),
                         reads=["py1", ("y2s", k, h)], writes=[("yo", k, h)])
            yok = [("yo", k)] if mb else [("yo", k, h) for h in range(NPT)]
            yield "s"
            S.op("pool", lambda e: e.tensor_tensor(out=Xw[k][:], in0=Xv[k4], in1=w_b, op=ALU.mult), reads=xvk + ["q4t"], writes=[("Xw", k)])
            if mb:
                for g in range(MBG):
                    S.op("pe", lambda e: e.matmul(pS_[0][:, 256 * g:256 * g + 256], lhsT=Kt[k4][:, 128 * g:128 * g + 128], rhs=Xw[k][:, 4 * g:4 * g + 4, :],
                                                  start=True, stop=True), reads=[("Kt", k4), ("Xw", k)], writes=["pS"])
            else:
                for u in range(U):
                    S.op("pe", lambda e: e.matmul(pview(pS_, u, PW), lhsT=Kt[k4][:, 128 * u:128 * u + 128], rhs=Xw[k][:, u, :], start=True, stop=True),
                         reads=[("Kt", k4), ("Xw", k)], writes=["pS"])
            yield "s"
            S.op("pool", lambda e: e.tensor_tensor(out=H[:], in0=H[:], in1=etot[:, :, c:c + 1].broadcast_to([128, U, PW]), op=ALU.mult),
                 reads=["H", "etot"], writes=["H"])
            if mb:
                S.op("dve", lambda e: e.tensor_tensor(out=H[:], in0=pall(pS_, 0), in1=H[:], op=ALU.add), reads=["H", "pS"], writes=["H"])
            else:
                for h in range(NPT):
                    S.op("dve", lambda e: e.tensor_tensor(out=H[:, 2 * h:2 * h + 2, :], in0=pall(pS_, h), in1=H[:, 2 * h:2 * h + 2, :], op=ALU.add),
                         reads=["H", "pS"], writes=["H"])
            S.op("act", lambda e: e.activation(out=Hb[:], in_=H[:], func=AF.Copy), reads=["H"], writes=["Hb"])
            yield "s"
            f = fin[k]
            if mb:
                ysrc = yo[k][:].rearrange("p u w -> p (u w)")
                fk = yok
            else:
                S.op("act", lambda e: e.activation(out=rden[k][:, 0:MLU], in_=yo[k][:, :, 128], func=AF.Abs), reads=yok, writes=[("rden", k)])
                S.op("dve", lambda e: e.tensor_scalar(out=rden[k][:], in0=rden[k][:], scalar1=1.0, scalar2=None, op0=ALU.max), reads=[("rden", k)], writes=[("rden", k)])
                S.op("dve", lambda e: e.reciprocal(out=rden[k][:], in_=rden[k][:]), reads=[("rden", k)], writes=[("rden", k)])
                S.op("pool", lambda e: e.tensor_tensor(out=f[:].rearrange("p (u w) -> p u w", w=128), in0=yo[k][:, :, 0:128],
                                                       in1=rden[k][:, 0:MLU].unsqueeze(2).broadcast_to([128, MLU, 128]), op=ALU.mult),
                     reads=yok + [("rden", k)], writes=[("fin", k)])
                ysrc = f[:]
                fk = [("fin", k)]
            dst_f = (Sc["yf_mb"] if mb else Sc["hf_ml"]) if d == 0 else (Sc["yb_mb"] if mb else Sc["hb_ml"])
            S.dma("pool", dst_f[r0:r0 + 128, :], ysrc, reads=fk, writes=[("ydir", c)])
            yield "chunk"
        yield "done"


def _dla_final(self, l, mixer):
    nc, S, I, Sc = self.nc, self.S, self.I, self.Sc
    mb = mixer == "mb"
    with ExitStack() as st:
        sb = lambda n, s, dt=F32: st.enter_context(_alloc(nc, "sbuf", n, list(s), dt))
        nwb = sb("nwb2", [128, CW])
        S.dma("sp", nwb[:], I["mbnw" if mb else "mlnw"][l].broadcast_to([128, CW]), writes=["nwb2"])
        if mb:
            dskb = sb("dskb", [128, MBU])
            S.dma("sp", dskb[:], I["dsk"][l].broadcast_to([128, MBU]), writes=["dskb"])
        NB_ = 3
        prev = [sb("prev%d" % i, [128, CW]) for i in range(NB_)]
        cur = [sb("cur%d" % i, [128, CW]) for i in range(NB_)]
        Zt = [sb("Zt%d" % i, [128, CW], BF16) for i in range(NB_)]
        Ot = [sb("Ot%d" % i, [128, CW], BF16) for i in range(NB_)]
        ssq = [sb("ssq%d" % i, [128, 4]) for i in range(NB_)]
        junk = sb("junkd", [128, CW], BF16)
        outb = [sb("outb%d" % i, [128, CW], BF16) for i in range(NB_)]
        for c in range(64):
            k = c % NB_
            r0 = 128 * c
            S.dma("sp", prev[k][:], (Sc["yf_mb"] if mb else Sc["hf_ml"])[r0:r0 + 128, :], writes=[("prev", k)])
            S.dma("pool", cur[k][:], (Sc["yb_mb"] if mb else Sc["hb_ml"])[r0:r0 + 128, :], writes=[("cur", k)])
            S.dma("sp", Zt[k][:], (Sc["mbz"] if mb else Sc["mlz"])[r0:r0 + 128, :], writes=[("Zt", k)])
            S.dma("pool", Ot[k][:], (Sc["mbx"] if mb else Sc["mlo"])[r0:r0 + 128, :], writes=[("Ot", k)])
            S.op("pool", lambda e: e.tensor_tensor(out=prev[k][:], in0=prev[k][:], in1=cur[k][:], op=ALU.add), reads=[("prev", k), ("cur", k)], writes=[("prev", k)])
            if mb:
                S.op("pool", lambda e: e.tensor_tensor(out=cur[k][:].rearrange("p (u w) -> p u w", w=64), in0=Ot[k][:].rearrange("p (u w) -> p u w", w=64),
                                                       in1=dskb[:].unsqueeze(2).broadcast_to([128, MBU, 64]), op=ALU.mult),
                     reads=[("Ot", k), ("cur", k), "dskb"], writes=[("cur", k)])
                S.op("dve", lambda e: e.tensor_tensor(out=prev[k][:], in0=prev[k][:], in1=cur[k][:], op=ALU.add), reads=[("prev", k), ("cur", k)], writes=[("prev", k)])
                S.op("dve", lambda e: e.tensor_tensor(out=prev[k][:], in0=prev[k][:], in1=Zt[k][:], op=ALU.mult), reads=[("prev", k), ("Zt", k)], writes=[("prev", k)])
                ngr, gw = MBG, 256
            else:
                S.op("dve", lambda e: e.tensor_tensor(out=prev[k][:], in0=prev[k][:], in1=Ot[k][:], op=ALU.mult), reads=[("prev", k), ("Ot", k)], writes=[("prev", k)])
                ngr, gw = MLU, 128
            for g in range(ngr):
                S.op("act", lambda e: e.activation(out=junk[:, 0:gw], in_=prev[k][:, gw * g:gw * g + gw], func=AF.Square, accum_out=ssq[k][:, g:g + 1]),
                     reads=[("prev", k)], writes=["junkd", ("ssq", k)])
            S.op("dve", lambda e: e.tensor_scalar(out=ssq[k][:, 0:ngr], in0=ssq[k][:, 0:ngr], scalar1=1.0 / gw, scalar2=EPS, op0=ALU.mult, op1=ALU.add),
                 reads=[("ssq", k)], writes=[("ssq", k)])
            S.op("act", lambda e: e.activation(out=ssq[k][:, 0:ngr], in_=ssq[k][:, 0:ngr], func=AF.Sqrt), reads=[("ssq", k)], writes=[("ssq", k)])
            S.op("dve", lambda e: e.reciprocal(out=ssq[k][:, 0:ngr], in_=ssq[k][:, 0:ngr]), reads=[("ssq", k)], writes=[("ssq", k)])
            for g in range(ngr):
                S.op("dve", lambda e: e.scalar_tensor_tensor(out=(outb[k] if mb else prev[k])[:, gw * g:gw * g + gw], in0=prev[k][:, gw * g:gw * g + gw],
                                                             scalar=ssq[k][:, g:g + 1], in1=nwb[:, gw * g:gw * g + gw], op0=ALU.mult, op1=ALU.mult),
                     reads=[("prev", k), ("ssq", k), "nwb2"], writes=[("outb", k) if mb else ("prev", k)])
            if not mb:
                S.op("pool", lambda e: e.tensor_tensor(out=outb[k][:], in0=prev[k][:], in1=Zt[k][:], op=ALU.mult), reads=[("prev", k), ("Zt", k)], writes=[("outb", k)])
            col0 = 0 if mb else CW
            S.dma("pool", Sc["ytm"][r0:r0 + 128, col0:col0 + CW], outb[k][:], reads=[("outb", k)], writes=[("ytm_dla", mixer, c)])


def _phaseDLA(self, l):
    S, nc = self.S, self.nc
    for mixer in ("mb", "ml"):
        _dla_streams(self, l, mixer)
        S.barrier()
        with ExitStack() as st:
            self.Q4s = [st.enter_context(_alloc(nc, "sbuf", "Q4_%d" % d, [32, T], F32)) for d in range(2)]
            gens = [_dla_run(self, l, mixer, d) for d in (0, 1)]
            for g in gens:
                next(g)
            while True:
                rs = [next(g) for g in gens]
                if all(r == "done" for r in rs):
                    break
            S.barrier()
            for g in reversed(gens):
                try:
                    next(g)
                except StopIteration:
                    pass
        _dla_final(self, l, mixer)
        S.barrier()


Prog.phaseDLA = _phaseDLA


N2L = 2 * T
CG = 32


def _prep_hy(inp, j):
    L = DEPTH
    n = np.arange(128)
    ang = 2.0 * np.pi * np.outer(n, n) / 128.0
    Fre, Fim = np.cos(ang), -np.sin(ang)
    dft = np.stack([Fre, Fim, Fre, -Fim], axis=1).astype(np.float32)
    angt = 2.0 * np.pi * np.outer(n, n) / float(N2L)
    twd = np.stack([np.cos(angt), -np.sin(angt)], axis=1).astype(np.float32)
    t = np.arange(T, dtype=np.float32)
    t_norm = t / np.float32(T)
    bands = np.arange(1, 9, dtype=np.float32)
    a = (np.float32(2.0 * math.pi / T)) * t[:, None] * bands[None, :]
    pos = np.concatenate([t_norm[:, None], np.cos(a), np.sin(a)], axis=-1).astype(np.float32)
    hyp = np.zeros((L, 64, 4), np.float32)
    hyp[:, :, 0] = np.asarray(inp["hy_b1"]); hyp[:, :, 1] = np.asarray(inp["hy_freq"]); hyp[:, :, 2] = np.asarray(inp["hy_b2"])
    dec = np.asarray(inp["hy_decay"], np.float32).reshape(L, 4, 512)[:, :, HC * j:HC * j + HC].reshape(L, 4 * NB, 128).transpose(0, 2, 1)
    w3 = np.asarray(inp["hy_w3"], np.float32).reshape(L, 64, 4, 512)[:, :, :, HC * j:HC * j + HC].reshape(L, 64, 4 * HC)
    skip = np.asarray(inp["hy_skip"], np.float32)[:, :, HC * j:HC * j + HC].reshape(L, 2, 1, HC)
    return {"dft": np.ascontiguousarray(dft.reshape(128, 512)), "twd": np.ascontiguousarray(twd.reshape(128, 256)),
            "posT": np.ascontiguousarray(pos.T), "tneg": (-t_norm).reshape(1, T).astype(np.float32),
            "hyp": hyp, "hydec": np.ascontiguousarray(dec),
            "hyw1": np.asarray(inp["hy_w1"], np.float32), "hyw2": np.asarray(inp["hy_w2"], np.float32),
            "hyw3": np.ascontiguousarray(w3), "hyskip": np.ascontiguousarray(skip)}


def _hy_declare(self):
    I, Sc = self.I, self.Sc
    I["dft"] = self.dram_in("dft", [128, 512]); I["twd"] = self.dram_in("twd", [128, 256])
    I["posT"] = self.dram_in("posT", [17, T]); I["tneg"] = self.dram_in("tneg", [1, T])
    I["hyp"] = self.dram_in("hyp", [DEPTH, 64, 4]); I["hydec"] = self.dram_in("hydec", [DEPTH, 128, 4 * NB])
    I["hyw1"] = self.dram_in("hyw1", [DEPTH, 17, 64]); I["hyw2"] = self.dram_in("hyw2", [DEPTH, 64, 64])
    I["hyw3"] = self.dram_in("hyw3", [DEPTH, 64, 4 * HC]); I["hyskip"] = self.dram_in("hyskip", [DEPTH, 2, 1, HC])
    Sc["gflt"] = self.dram_scr("gflt", [2, HC, N2L], BF16, dbg=True)


def _phaseHY(self, l):
    nc, S, I, Sc = self.nc, self.S, self.I, self.Sc
    PI = float(np.pi)
    with ExitStack() as st:
        sb = lambda n, s, d=F32: st.enter_context(_alloc(nc, "sbuf", n, list(s), d))
        pst = lambda n, d=F32: st.enter_context(_alloc(nc, "psum", n, [128, 512], d))
        w1 = sb("hw1", [17, 64]); w2 = sb("hw2", [64, 64]); w3f = sb("hw3f", [64, 4 * HC]); w3b = sb("hw3b", [64, 4 * HC], BF16)
        hp = sb("hhp", [64, 4]); fb = sb("hfb", [64, 2])
        hid = sb("hhid", [64, T], BF16)
        tn = sb("htn", [128, T]); dec = sb("hdec", [128, 4 * NB])
        S.dma("sp", w1[:], I["hyw1"][l], writes=["w1"]); S.dma("sp", w2[:], I["hyw2"][l], writes=["w2"])
        S.dma("sp", w3f[:], I["hyw3"][l], writes=["w3f"]); S.dma("sp", hp[:], I["hyp"][l], writes=["hp"])
        S.dma("sp", tn[:], I["tneg"].broadcast_to([128, T]), writes=["tn"]); S.dma("sp", dec[:], I["hydec"][l], writes=["dec"])
        S.op("pool", lambda e: e.tensor_copy(out=w3b[:], in_=w3f[:]), reads=["w3f"], writes=["w3b"])
        S.op("dve", lambda e: e.tensor_tensor(out=fb[:, 0:1], in0=hp[:, 0:1], in1=hp[:, 1:2], op=ALU.mult), reads=["hp"], writes=["fb"])
        S.op("dve", lambda e: e.tensor_tensor(out=fb[:, 1:2], in0=hp[:, 2:3], in1=hp[:, 1:2], op=ALU.mult), reads=["hp", "fb"], writes=["fb"])
        pt_ = [sb("hpt%d" % i, [17, 512]) for i in range(2)]
        arg = [sb("harg%d" % i, [64, 512]) for i in range(2)]
        ta = [sb("hta%d" % i, [64, 512]) for i in range(2)]
        tb = [sb("htb%d" % i, [64, 512]) for i in range(2)]
        h1 = [sb("hh1%d" % i, [64, 512]) for i in range(2)]
        pz = [pst("hpz%d" % i) for i in range(2)]

        def sin_layer(src_ps, pk, col, out_ap, okey, k):
            a = arg[k]; ak = ("arg", k)
            S.op("dve", lambda e: e.tensor_scalar(out=a[:], in0=src_ps[0:64, :], scalar1=hp[:, 1:2], scalar2=fb[:, col:col + 1], op0=ALU.mult, op1=ALU.add),
                 reads=[pk, "hp", "fb"], writes=[ak])
            S.op("dve", lambda e: e.tensor_scalar(out=ta[k][:], in0=a[:], scalar1=PI, scalar2=-2 * PI, op0=ALU.is_gt, op1=ALU.mult), reads=[ak], writes=[("ta", k)])
            S.op("dve", lambda e: e.tensor_scalar(out=tb[k][:], in0=a[:], scalar1=-PI, scalar2=2 * PI, op0=ALU.is_lt, op1=ALU.mult), reads=[ak], writes=[("tb", k)])
            S.op("pool", lambda e: e.tensor_tensor(out=ta[k][:], in0=ta[k][:], in1=tb[k][:], op=ALU.add), reads=[("ta", k), ("tb", k)], writes=[("ta", k)])
            S.op("pool", lambda e: e.tensor_tensor(out=a[:], in0=a[:], in1=ta[k][:], op=ALU.add), reads=[ak, ("ta", k)], writes=[ak])
            S.op("act", lambda e: e.activation(out=out_ap, in_=a[:], func=AF.Sin), reads=[ak], writes=[okey])

        for c in range(16):
            k = c % 2
            S.dma("sp", pt_[k][:], I["posT"][:, 512 * c:512 * c + 512], writes=[("pt", k)])
            S.op("pe", lambda e: e.matmul(pz[0][0:64, :], lhsT=w1[:], rhs=pt_[k][:], start=True, stop=True), reads=["w1", ("pt", k)], writes=["pz0"])
            sin_layer(pz[0], "pz0", 0, h1[k][:], ("h1", k), k)
            S.op("pe", lambda e: e.matmul(pz[1][0:64, :], lhsT=w2[:], rhs=h1[k][:], start=True, stop=True), reads=["w2", ("h1", k)], writes=["pz1"])
            sin_layer(pz[1], "pz1", 1, hid[:, 512 * c:512 * c + 512], ("hid", c), k)
        hidk = [("hid", c) for c in range(16)]
        gt = [sb("hgt%d" % i, [128, N2L], BF16) for i in range(2)]
        win = [sb("hwin%d" % i, [128, 512]) for i in range(2)]
        pf = [pst("hpf%d" % i) for i in range(2)]
        for i in range(2):
            S.op("pool", lambda e: e.memset(gt[i][:, T:T + 1], 0.0), writes=[("gtz", i)])
        it = 0
        for o in range(2):
            for cb in range(NB):
                g = gt[(o * NB + cb) % 2]; gk = ("gt", (o * NB + cb) % 2)
                gparts = []
                for dr in range(2):
                    col0 = (o * 2 + dr) * HC + 128 * cb
                    di = (o * 2 + dr) * NB + cb
                    for c in range(16):
                        k = it % 2; it += 1
                        S.op("pe", lambda e: e.matmul(pf[k][:], lhsT=w3b[:, col0:col0 + 128], rhs=hid[:, 512 * c:512 * c + 512], start=True, stop=True),
                             reads=["w3b", ("hid", c)], writes=[("pf", k)])
                        S.op("act", lambda e: e.activation(out=win[k][:], in_=tn[:, 512 * c:512 * c + 512], func=AF.Exp, scale=dec[:, di:di + 1]),
                             reads=["tn", "dec"], writes=[("win", k)])
                        pk = (gk, dr, c)
                        gparts.append(pk)
                        if dr == 0:
                            S.op("dve", lambda e: e.tensor_tensor(out=g[:, 512 * c:512 * c + 512], in0=pf[k][:], in1=win[k][:], op=ALU.mult),
                                 reads=[("pf", k), ("win", k)], writes=[pk])
                        else:
                            j0 = 1 if c == 0 else 0
                            lo = N2L - 512 * c - 511
                            hi = N2L - 512 * c - j0 + 1
                            S.op("dve", lambda e: e.tensor_tensor(out=g[:, lo:hi][:, ::-1], in0=pf[k][:, j0:512], in1=win[k][:, j0:512], op=ALU.mult),
                                 reads=[("pf", k), ("win", k)], writes=[pk])
                S.dma("pool", Sc["gflt"][o, 128 * cb:128 * cb + 128, :], g[:], reads=gparts + [("gtz", (o * NB + cb) % 2)], writes=[("gflt", o, cb)])
    S.barrier()
    with ExitStack() as st:
        sb = lambda n, s, d=F32: st.enter_context(_alloc(nc, "sbuf", n, list(s), d))
        dftf = sb("dftf", [128, 512]); dft = sb("dftb", [128, 4, 128], BF16); twd = sb("twd", [128, 2, 128])
        S.dma("sp", dftf[:], I["dft"], writes=["dftf"]); S.dma("sp", twd[:].rearrange("p a k -> p (a k)"), I["twd"], writes=["twd"])
        S.op("dve", lambda e: e.tensor_copy(out=dft[:].rearrange("p a k -> p (a k)"), in_=dftf[:]), reads=["dftf"], writes=["dft"])
        Fre, Fim, nFim = dft[:, 0, :], dft[:, 1, :], dft[:, 3, :]
        Fcat = dft[:, 0:2, :].rearrange("p a k -> p (a k)")
        Fci2 = dft[:, 1:3, :].rearrange("p a k -> p (a k)")
        Fci1 = dft[:, 2:4, :].rearrange("p a k -> p (a k)")
        G = sb("hyG", [128, 2, CG, 2, 128], BF16)
        gblk = [sb("gblk%d" % i, [128, CG, 128], BF16) for i in range(2)]
        sig = {n: sb("sig_" + n, [64, CG, 128], BF16) for n in ("v", "x1", "x2", "g")}
        zblk = sb("zblk", [64, CG, 128], BF16); oblk = sb("oblk", [64, CG, 128], BF16)
        skb = sb("skb", [64, 2, CG])
        NQ = CG // 4
        Ap = [[sb("Ap%d_%d" % (c_, i), [128, 4, 2, 128], BF16) for i in range(2)] for c_ in range(2)]
        Yp = [[sb("Yp%d_%d" % (c_, i), [128, 4, 2, 128], BF16) for i in range(2)] for c_ in range(2)]
        Bp = [[sb("Bp%d_%d" % (c_, i), [128, 4, 2, 128], BF16) for i in range(2)] for c_ in range(2)]
        tt = [[[sb("tt%d_%d_%d" % (c_, i, j), [128, 4, 128]) for j in range(4)] for i in range(2)] for c_ in range(2)]
        ep = [[[sb("ep%d_%d_%d" % (c_, i, j), [64, 4, 128]) for j in range(2)] for i in range(2)] for c_ in range(2)]
        pAall = st.enter_context(_alloc(nc, "psum", "hpA", [128, 2048], F32))
        pBall = st.enter_context(_alloc(nc, "psum", "hpB", [128, 2048], F32))
        Tre = twd[:, 0, :].unsqueeze(1).broadcast_to([128, 4, 128])
        Tim = twd[:, 1, :].unsqueeze(1).broadcast_to([128, 4, 128])

        def chain(cg, sx):
            pA = pAall[:, 1024 * sx:1024 * sx + 1024]
            pB = pBall[:, 1024 * sx:1024 * sx + 1024]
            pA3 = pA.rearrange("p (c k) -> p c k", k=256)
            pBr = pB.rearrange("p (r c k) -> p r c k", r=2, c=4)
            kA, kB = ("pA", sx), ("pB", sx)
            cn = {"c": 0, "n": 0}

            def cmul(out_t, okey, are, aim, akeys, bre, bim, bkeys, conj):
                i = cn["c"] % 2; cn["c"] += 1
                t1, t2, t3, t4 = tt[sx][i]
                ks = [("tt", sx, i, j) for j in range(4)]
                S.op("dve", lambda e: e.tensor_tensor(out=t1[:], in0=are, in1=bre, op=ALU.mult), reads=akeys + bkeys, writes=[ks[0]])
                S.op("dve", lambda e: e.tensor_tensor(out=t2[:], in0=aim, in1=bim, op=ALU.mult), reads=akeys + bkeys, writes=[ks[1]])
                S.op("dve", lambda e: e.tensor_tensor(out=t3[:], in0=are, in1=bim, op=ALU.mult), reads=akeys + bkeys, writes=[ks[2]])
                S.op("dve", lambda e: e.tensor_tensor(out=t4[:], in0=aim, in1=bre, op=ALU.mult), reads=akeys + bkeys, writes=[ks[3]])
                if not conj:
                    S.op("pool", lambda e: e.tensor_tensor(out=out_t[:, :, 0, :], in0=t1[:], in1=t2[:], op=ALU.subtract), reads=ks[0:2], writes=[okey + ("re",)])
                    S.op("pool", lambda e: e.tensor_tensor(out=out_t[:, :, 1, :], in0=t3[:], in1=t4[:], op=ALU.add), reads=ks[2:4], writes=[okey + ("im",)])
                else:
                    S.op("pool", lambda e: e.tensor_tensor(out=out_t[:, :, 0, :], in0=t1[:], in1=t2[:], op=ALU.add), reads=ks[0:2], writes=[okey + ("re",)])
                    S.op("pool", lambda e: e.tensor_tensor(out=out_t[:, :, 1, :], in0=t4[:], in1=t3[:], op=ALU.subtract), reads=ks[2:4], writes=[okey + ("im",)])

            def fwd_quad(src_fn, K, skeys, i):
                for ch in range(4):
                    S.op("pe", lambda e: e.matmul(pA3[:, ch, :], lhsT=src_fn(ch), rhs=Fcat[0:K, :], start=True, stop=True),
                         reads=skeys + ["dft"], writes=[kA])
                yield "s"
                cmul(Ap[sx][i], ("Ap", sx, i), pA3[:, :, 0:128], pA3[:, :, 128:256], [kA], Tre, Tim, ["twd"], False)
                yield "s"
                rre = Ap[sx][i][:, :, 0, :]
                rim = Ap[sx][i][:, :, 1, :]
                kk = [("Ap", sx, i, "re"), ("Ap", sx, i, "im"), "dft"]
                S.op("pe", lambda e: e.matmul(pB[:, 0:512], lhsT=Fre, rhs=rre, start=True, stop=False), reads=kk, writes=[kB])
                S.op("pe", lambda e: e.matmul(pB[:, 0:512], lhsT=nFim, rhs=rim, start=False, stop=True), reads=kk, writes=[kB])
                S.op("pe", lambda e: e.matmul(pB[:, 512:1024], lhsT=Fim, rhs=rre, start=True, stop=False), reads=kk, writes=[kB])
                S.op("pe", lambda e: e.matmul(pB[:, 512:1024], lhsT=Fre, rhs=rim, start=False, stop=True), reads=kk, writes=[kB])
                yield "s"

            for o in range(2):
                for oc8 in range(CG // 8):
                    i = cn["n"] % 2; cn["n"] += 1
                    ch0 = 8 * oc8 + 4 * sx
                    yield from fwd_quad(lambda ch: gblk[o][:, ch0 + ch, :], 128, [("gblk", o)], i)
                    gv = G[:, o, ch0:ch0 + 4, :, :]
                    S.op("act", lambda e: e.activation(out=gv[:, :, 0, :], in_=pBr[:, 0, :, :], func=AF.Copy), reads=[kB], writes=[("G", o, oc8, sx, 0)])
                    S.op("act", lambda e: e.activation(out=gv[:, :, 1, :], in_=pBr[:, 1, :, :], func=AF.Copy), reads=[kB], writes=[("G", o, oc8, sx, 1)])
                    yield "s"
            for o in range(2):
                src_t = sig["v"] if o == 0 else zblk
                for oc8 in range(CG // 8):
                    i = cn["n"] % 2; cn["n"] += 1
                    ch0 = 8 * oc8 + 4 * sx
                    skeys = [("sig", "v")] if o == 0 else [("zblk", oc8, sx)]
                    yield from fwd_quad(lambda ch: src_t[:, ch0 + ch, :], 64, skeys, i)
                    gv = G[:, o, ch0:ch0 + 4, :, :]
                    gk = [("G", o, oc8, sx, 0), ("G", o, oc8, sx, 1)]
                    cmul(Yp[sx][i], ("Yp", sx, i), pBr[:, 0, :, :], pBr[:, 1, :, :], [kB], gv[:, :, 0, :], gv[:, :, 1, :], gk, False)
                    yield "s"
                    for ch in range(4):
                        S.op("pe", lambda e: e.matmul(pA3[:, ch, :], lhsT=Yp[sx][i][:, ch, 0, :], rhs=Fci1, start=True, stop=False),
                             reads=[("Yp", sx, i, "re"), ("Yp", sx, i, "im"), "dft"], writes=[kA])
                        S.op("pe", lambda e: e.matmul(pA3[:, ch, :], lhsT=Yp[sx][i][:, ch, 1, :], rhs=Fci2, start=False, stop=True),
                             reads=[("Yp", sx, i, "re"), ("Yp", sx, i, "im"), "dft"], writes=[kA])
                    yield "s"
                    cmul(Bp[sx][i], ("Bp", sx, i), pA3[:, :, 0:128], pA3[:, :, 128:256], [kA], Tre, Tim, ["twd"], True)
                    yield "s"
                    kk = [("Bp", sx, i, "re"), ("Bp", sx, i, "im"), "dft"]
                    S.op("pe", lambda e: e.matmul(pB[0:64, 0:512], lhsT=Fre[:, 0:64], rhs=Bp[sx][i][:, :, 0, :], start=True, stop=False), reads=kk, writes=[kB])
                    S.op("pe", lambda e: e.matmul(pB[0:64, 0:512], lhsT=Fim[:, 0:64], rhs=Bp[sx][i][:, :, 1, :], start=False, stop=True), reads=kk, writes=[kB])
                    yield "s"
                    e1, e2 = ep[sx][i]
                    yv = pB[0:64, 0:512].rearrange("p (c k) -> p c k", k=128)
                    skv = skb[:, o, ch0:ch0 + 4].unsqueeze(2).broadcast_to([64, 4, 128])
                    uu = src_t[:, ch0:ch0 + 4, :]
                    S.op("pool", lambda e: e.tensor_tensor(out=e1[:], in0=uu, in1=skv, op=ALU.mult), reads=skeys + ["skb"], writes=[("e1", sx, i)])
                    S.op("dve", lambda e: e.scalar_tensor_tensor(out=e2[:], in0=yv, scalar=1.0 / N2L, in1=e1[:], op0=ALU.mult, op1=ALU.add),
                         reads=[kB, ("e1", sx, i)], writes=[("e2", sx, i)])
                    if o == 0:
                        S.op("pool", lambda e: e.tensor_tensor(out=zblk[:, ch0:ch0 + 4, :], in0=e2[:], in1=sig["x1"][:, ch0:ch0 + 4, :], op=ALU.mult),
                             reads=[("e2", sx, i), ("sig", "x1")], writes=[("zblk", oc8, sx)])
                    else:
                        S.op("pool", lambda e: e.tensor_tensor(out=e1[:], in0=e2[:], in1=sig["x2"][:, ch0:ch0 + 4, :], op=ALU.mult),
                             reads=[("e2", sx, i), ("sig", "x2")], writes=[("e1", sx, i)])
                        S.op("pool", lambda e: e.tensor_tensor(out=oblk[:, ch0:ch0 + 4, :], in0=e1[:], in1=sig["g"][:, ch0:ch0 + 4, :], op=ALU.mult),
                             reads=[("e1", sx, i), ("sig", "g")], writes=[("oblk", oc8, sx)])
                    yield "s"

        for cg in range(HC // CG):
            c0 = CG * cg
            for o in range(2):
                S.dma("sp", gblk[o][:], Sc["gflt"][o, c0:c0 + CG, :].rearrange("c (a b) -> a c b", b=128), writes=[("gblk", o)])
            for n_, src, r0 in (("v", "hyuT", 0), ("x1", "hyuT", HC), ("x2", "hyuT", 2 * HC), ("g", "hygT", 0)):
                S.dma("sp" if n_ in ("v", "x1") else "act", sig[n_][:], Sc[src][r0 + c0:r0 + c0 + CG, :].rearrange("c (a b) -> a c b", b=128), writes=[("sig", n_)])
            S.dma("sp", skb[:], I["hyskip"][l, :, :, c0:c0 + CG].rearrange("o x c -> x o c").broadcast_to([64, 2, CG]), writes=["skb"])
            gens = [chain(cg, sx) for sx in range(2)]
            live = list(gens)
            while live:
                for g_ in list(live):
                    try:
                        next(g_)
                    except StopIteration:
                        live.remove(g_)
            S.dma("pool", Sc["yhyT"][c0:c0 + CG, :].rearrange("c (a b) -> a c b", b=128), oblk[:],
                  reads=[("oblk", q, sx) for q in range(CG // 8) for sx in range(2)], writes=[("yhyT", cg)])


Prog.phaseHY = _phaseHY


NCORES = 8


def kernel(**inputs):
    P = Prog(nlayers=DEPTH, debug=False)
    nc = P.build()
    names = set(P.I.keys())
    x = np.asarray(inputs["x"], np.float32)
    Wj = []
    for j in range(SP):
        W = _prep_weights(inputs, j)
        W.update(_prep_mixer(inputs, j))
        Wj.append({k: v for k, v in W.items() if k in names})
    in_maps = []
    for c in range(NCORES):
        b, j = c // SP, c % SP
        m = dict(Wj[j])
        m["x"] = np.ascontiguousarray(x[b])
        m["xh"] = np.ascontiguousarray(x[b][:, OW * j:OW * j + OW])
        in_maps.append(m)
    res = run_bass_kernel_spmd(nc, in_maps, core_ids=list(range(NCORES)))
    out = np.empty((NCORES // SP, T, D), np.float32)
    for c in range(NCORES):
        b, j = c // SP, c % SP
        out[b][:, OW * j:OW * j + OW] = np.asarray(res.results[c]["out"], np.float32)
    return out
```

```python
import math
from contextlib import ExitStack

import numpy as np
import concourse.bass as bass
import concourse.mybir as mybir
from concourse.bass_utils import run_bass_kernel_spmd

F32 = mybir.dt.float32
BF16 = mybir.dt.bfloat16
AF = mybir.ActivationFunctionType
ALU = mybir.AluOpType

T = 8192
D = 1024
NT = 64
DEPTH = 2
EPS = 1e-6
SAME_ENGINE_SYNC = True
SAME_WAW = False
SAME_WAR = False

O_HYU, O_HYG, O_MBX, O_MBB, O_MBC, O_MBZ, O_MBDT = 0, 1536, 2048, 2560, 2816, 3072, 3584
O_MLQ, O_MLK, O_MLV, O_MLO, O_MLZ, O_MLG = 3600, 4112, 4624, 5136, 5648, 6160
O_NAQ, O_NAK, O_NAV, O_NAG = 6176, 6688, 7200, 7712

SP = 2
CW = 512 // SP
HC = CW
MBU = 8 // SP
MBG = 2 // SP
MLU = 4 // SP
NAH = 8 // SP
OW = 1024 // SP
YW = 3 * CW
NSM = 2 * MBU + 4 * MLU
NSP = 32
NB = CW // 128
FM_GROUPS = [
    ("hyu", O_HYU, 3 * NB, "conv"),
    ("hyg", O_HYG, NB, "silu"),
    ("mbx", O_MBX, NB, "convsilu"),
    ("mbB", O_MBB, MBG, "convsilu"),
    ("mbC", O_MBC, MBG, "convsilu"),
    ("mlq", O_MLQ, NB, "convsilu"),
    ("mlk", O_MLK, NB, "convsilu"),
    ("naq", O_NAQ, NB, "qknorm"),
    ("nak", O_NAK, NB, "qknorm"),
]
FM_BLOCKS = []
for _n, _o, _k, _t in FM_GROUPS:
    for _i in range(_k):
        FM_BLOCKS.append((_n, _i, _o + 128 * _i, _t))
NFM = len(FM_BLOCKS)
TM_PARTS = [("mbz", O_MBZ, "silu"), ("mlv", O_MLV, "copy"), ("mlo", O_MLO, "sigmoid"),
            ("mlz", O_MLZ, "silu"), ("nav", O_NAV, "copy"), ("nag", O_NAG, "silu")]
NTM = len(TM_PARTS) * CW // 512


import os
_SKIP = set(os.environ.get("KSKIP", "").split(","))
_UID = [0]


def _alloc(nc, kind, name, shape, dt):
    _UID[0] += 1
    nm = "%s_%d" % (name, _UID[0])
    if kind == "sbuf":
        return nc.sbuf_tensor(nm, shape, dt)
    return nc.psum_tensor(nm, shape, dt)


class Sched:
    def __init__(self, nc, stack, n_dma_sems=14):
        self.nc = nc
        self.eng = {"pe": nc.tensor, "dve": nc.vector, "act": nc.scalar, "pool": nc.gpsimd, "sp": nc.sync}
        self.sem = {e: stack.enter_context(nc.semaphore("c_" + e)) for e in self.eng}
        self.cnt = {e: 0 for e in self.eng}
        self.seen = {e: {} for e in self.eng}
        self.dq = {}
        for q in ("sp", "pool", "act"):
            self.dq[q] = [[stack.enter_context(nc.semaphore("d_%s%d" % (q, i))), 0] for i in range(n_dma_sems)]
        self.dqi = {q: 0 for q in self.dq}
        self.last_w = {}
        self.readers = {}
        self.n_wait = 0
        self.n_ins = 0
        self.ccs = []
        self.cc_toks = []

    def _wait(self, e, tok, raw=True):
        key, sem, val, prod = tok
        if prod == e and (e == "pe" or not SAME_ENGINE_SYNC or not raw):
            return
        if self.seen[e].get(key, 0) >= val:
            return
        self.eng[e].wait_ge(sem, val)
        self.seen[e][key] = val
        self.n_wait += 1

    def _deps(self, e, reads, writes):
        for k in reads:
            t = self.last_w.get(k)
            if t is not None:
                self._wait(e, t, True)
        for k in writes:
            t = self.last_w.get(k)
            if t is not None:
                self._wait(e, t, SAME_WAW)
            for t in self.readers.get(k, ()):
                self._wait(e, t, SAME_WAR)

    def _commit(self, tok, reads, writes):
        for k in writes:
            self.last_w[k] = tok
            self.readers[k] = []
        for k in reads:
            if k in writes:
                continue
            lst = self.readers.setdefault(k, [])
            lst.append(tok)
            if len(lst) > 48:
                best = {}
                for t in lst:
                    if t[0] not in best or best[t[0]][2] < t[2]:
                        best[t[0]] = t
                self.readers[k] = list(best.values())

    def op(self, e, fn, reads=(), writes=()):
        self._deps(e, reads, writes)
        ins = fn(self.eng[e])
        self.cnt[e] += 1
        ins.then_inc(self.sem[e], 1)
        tok = ("c_" + e, self.sem[e], self.cnt[e], e)
        self._commit(tok, reads, writes)
        self.n_ins += 1
        return ins

    def dma(self, q, out, in_, reads=(), writes=(), **kw):
        self._deps(q, reads, writes)
        idx = self.dqi[q]
        slot = self.dq[q][idx]
        self.dqi[q] = (idx + 1) % len(self.dq[q])
        sem, val = slot
        key = "d_%s%d" % (q, idx)
        if val > 0 and self.seen[q].get(key, 0) < val:
            self.eng[q].wait_ge(sem, val)
            self.seen[q][key] = val
        ins = self.eng[q].dma_start(out=out, in_=in_, **kw)
        ins.then_inc(sem, 16)
        slot[1] = val + 16
        tok = (key, sem, val + 16, None)
        self._commit(tok, reads, writes)
        self.n_ins += 1
        return ins

    def collective(self, src, dst, tn, stack, rpc):
        self._deps("pool", [src], [dst])
        groups = [[SP * g + r for r in range(SP)] for g in range(8 // SP)]
        rows = tn[src].shape[0]
        toks = []
        for q in range(rows // rpc):
            sem = stack.enter_context(self.nc.semaphore("cc%d" % len(self.ccs)))
            self.ccs.append(sem)
            ins = self.nc.gpsimd.collective_compute(
                "AllGather", ALU.bypass, replica_groups=groups,
                ins=[tn[src].ap()[rpc * q:rpc * q + rpc, :].opt()],
                outs=[tn[dst].ap()[SP * rpc * q:SP * rpc * q + SP * rpc, :].opt()])
            ins.then_inc(sem)
            tok = ("cc%d" % (len(self.ccs) - 1), sem, 1, None)
            self.eng["pool"].wait_ge(sem, 1)
            self.seen["pool"][tok[0]] = 1
            toks.append(tok)
            self.cc_toks.append(tok)
        self._commit(toks[-1], [src], [dst])

    def barrier(self):
        for e in self.eng:
            for t in self.cc_toks:
                self._wait(e, t)
        for e in self.eng:
            for p in self.eng:
                if p != e and self.cnt[p] > 0:
                    self._wait(e, ("c_" + p, self.sem[p], self.cnt[p], p))
                elif p == e and self.cnt[p] > 0 and e != "pe":
                    self._wait(e, ("c_" + p, self.sem[p], self.cnt[p], None))
            for q in self.dq:
                for i, (sem, val) in enumerate(self.dq[q]):
                    if val > 0:
                        self._wait(e, ("d_%s%d" % (q, i), sem, val, None))
        self.last_w = {}
        self.readers = {}


class _KeyNS:
    def __init__(self, S, prefix):
        self.S = S
        self.p = prefix

    def _k(self, keys):
        return [(self.p, k) for k in keys]

    def op(self, e, fn, reads=(), writes=()):
        return self.S.op(e, fn, self._k(reads), self._k(writes))

    def dma(self, q, out, in_, reads=(), writes=(), **kw):
        return self.S.dma(q, out, in_, self._k(reads), self._k(writes), **kw)


class Prog:
    def __init__(self, nlayers=DEPTH, debug=False, phases=("A", "HY", "NA", "DLA", "Z", "G")):
        self.nlayers = nlayers
        self.debug = debug
        self.phases = phases

    def dram_in(self, name, shape, dt=F32):
        return self.nc.dram_tensor(name, list(shape), dt, kind="ExternalInput").ap()

    def dram_scr(self, name, shape, dt=BF16, dbg=False):
        kind = "ExternalOutput" if (dbg and self.debug) else "Internal"
        if name in getattr(self, "feed", ()):
            kind = "ExternalInput"
        return self.nc.dram_tensor(name, list(shape), dt, kind=kind).ap()

    def build(self):
        nc = bass.Bass("TRN2", target_bir_lowering=False)
        self.nc = nc
        L = DEPTH
        I = self.I = {}
        I["x"] = self.dram_in("x", [T, D])
        I["norm_w"] = self.dram_in("norm_w", [L, 1, D])
        I["wfm"] = self.dram_in("wfm", [L, NFM, 128, 1024])
        I["wsm"] = self.dram_in("wsm", [L, 128, 8 * NSP])
        I["xh"] = self.dram_in("xh", [T, OW])
        I["wtm"] = self.dram_in("wtm", [L, NTM, 128, 4096])
        I["wout"] = self.dram_in("wout", [L, 128, 16 * OW])
        I["convp"] = self.dram_in("convp", [L, NFM, 128, 4])
        I["ident"] = self.dram_in("ident", [128, 128])
        I["blockones"] = self.dram_in("blockones", [128, 128])
        self.declare_mixer_inputs()
        self.out = nc.dram_tensor("out", [T, OW], F32, kind="ExternalOutput").ap()
        Sc = self.Sc = {}
        for n, rows in [("hyuT", 3 * CW), ("hygT", CW), ("mbBT", 128 * MBG), ("mbCT", 128 * MBG), ("mlqT", CW),
                        ("mlkT", CW), ("naqT", CW), ("nakT", CW)]:
            Sc[n] = self.dram_scr(n, [rows, T], dbg=True)
        for n, cols in [("mbx", CW), ("mbB", 128 * MBG), ("mlk", CW), ("mbz", CW), ("mlv", CW), ("mlo", CW),
                        ("mlz", CW), ("nav", CW), ("nag", CW)]:
            Sc[n] = self.dram_scr(n, [T, cols], dbg=True)
        Sc["smallT"] = self.dram_scr("smallT", [NSP, T], F32, dbg=True)
        mbxB = self.dram_scr("mbxB", [T, CW + 128 * MBG])
        Sc["mbx"], Sc["mbB"], Sc["mbxB"] = mbxB[:, 0:CW], mbxB[:, CW:CW + 128 * MBG], mbxB
        mbBCT = self.dram_scr("mbBCT", [2 * 128 * MBG, T])
        Sc["mbBT"], Sc["mbCT"], Sc["mbBCT"] = mbBCT[0:128 * MBG, :], mbBCT[128 * MBG:2 * 128 * MBG, :], mbBCT
        mlkqT = self.dram_scr("mlkqT", [2 * CW, T])
        Sc["mlkT"], Sc["mlqT"], Sc["mlkqT"] = mlkqT[0:CW, :], mlkqT[CW:2 * CW, :], mlkqT
        self.Tn = {}
        for n, shp, dt in [("ytm", [T, YW], BF16), ("yhyT", [HC, T], BF16), ("xres", [T, OW], F32),
                           ("ytm_g", [SP * T, YW], BF16), ("yhy_g", [SP * HC, T], BF16), ("xg", [SP * T, OW], F32)]:
            if self.debug and n in ("ytm", "yhyT"):
                self.Tn[n] = nc.dram_tensor(n, shp, dt, kind="ExternalOutput")
            elif n in getattr(self, "feed", ()):
                self.Tn[n] = nc.dram_tensor(n, shp, dt, kind="ExternalInput")
            else:
                self.Tn[n] = nc.dram_tensor(n, shp, dt)
            Sc[n] = self.Tn[n].ap()
        self.declare_mixer_scratch()

        with ExitStack() as st:
            self.S = Sched(nc, st)
            S = self.S
            self.ident_f = st.enter_context(_alloc(nc, "sbuf", "ident_f", [128, 128], F32))
            self.ident_b = st.enter_context(_alloc(nc, "sbuf", "ident_b", [128, 128], BF16))
            self.bones_b = st.enter_context(_alloc(nc, "sbuf", "bones_b", [128, 128], BF16))
            tmpc = st.enter_context(_alloc(nc, "sbuf", "tmpc", [128, 128], F32))
            S.dma("sp", self.ident_f[:], I["ident"], writes=["ident_f"])
            S.dma("sp", tmpc[:], I["blockones"], writes=["tmpc"])
            S.op("dve", lambda e: e.tensor_copy(out=self.ident_b[:], in_=self.ident_f[:]), reads=["ident_f"], writes=["ident_b"])
            S.op("dve", lambda e: e.tensor_copy(out=self.bones_b[:], in_=tmpc[:]), reads=["tmpc"], writes=["bones_b"])
            S.barrier()
            for l in range(self.nlayers):
                x_dst = self.out if l == self.nlayers - 1 else Sc["xres"]
                x_res = I["xh"] if l == 0 else Sc["xres"]
                self.marks = getattr(self, "marks", [])
                mark = lambda nm: self.marks.append((nm, l, dict(S.cnt)))
                mark("start")
                if "A" in self.phases:
                    self.phaseA(l)
                    S.barrier()
                    mark("A")
                if "HY" in self.phases:
                    self.phaseHY(l)
                    S.barrier()
                    mark("HY")
                if "NA" in self.phases:
                    self.phaseNA(l)
                    S.barrier()
                    mark("NA")
                if "DLA" in self.phases:
                    self.phaseDLA(l)
                    S.barrier()
                    mark("DLA")
                if "Z" in self.phases:
                    if SP > 1 and "G" in self.phases:
                        S.collective("ytm", "ytm_g", self.Tn, st, 1024)
                        S.collective("yhyT", "yhy_g", self.Tn, st, 64)
                        S.barrier()
                    self.phaseZ(l, x_res, x_dst)
                    S.barrier()
                    if SP > 1 and l < self.nlayers - 1 and "G" in self.phases:
                        S.collective("xres", "xg", self.Tn, st, 1024)
                        S.barrier()
                    mark("Z")
            S.barrier()
        return nc

    def declare_mixer_inputs(self):
        _na_declare_inputs(self)

    def declare_mixer_scratch(self):
        _dla_declare(self)
        _hy_declare(self)

    def phaseA(self, l):
        nc, S, I, Sc = self.nc, self.S, self.I, self.Sc
        with ExitStack() as st:
            sb = lambda n, s, d=F32: st.enter_context(_alloc(nc, "sbuf", n, list(s), d))
            ps = lambda n, s, d=F32: st.enter_context(_alloc(nc, "psum", n, list(s), d))
            hT = sb("hT", [128, 8, T], BF16)
            with ExitStack() as st0:
                sb0 = lambda n, s, d=F32: st0.enter_context(_alloc(nc, "sbuf", n, list(s), d))
                nwb = sb0("nwb", [128, D])
                S.dma("sp", nwb[:], I["norm_w"][l].broadcast_to([128, D]), writes=["nwb"])
                xt = [sb0("xt%d" % i, [128, D]) for i in range(2)]
                junk = sb0("junk", [128, D], BF16)
                ss = [sb0("ss%d" % i, [128, 1]) for i in range(2)]
                xn = [sb0("xn%d" % i, [128, D], BF16) for i in range(2)]
                pt = [st0.enter_context(_alloc(nc, "psum", "pt%d" % i, [128, 512], BF16)) for i in range(2)]
                for i in range(NT):
                    b = i % 2
                    if l == 0 or SP == 1:
                        S.dma("sp", xt[b][:], I["x"][128 * i:128 * i + 128, :], writes=[("xt", b)])
                    else:
                        S.dma("sp", xt[b][:].rearrange("p (r c) -> p r c", r=SP),
                              Sc["xg"].rearrange("(q r t) c -> q t r c", q=8, r=SP)[i // 8][128 * (i % 8):128 * (i % 8) + 128, :, :], writes=[("xt", b)])
                    S.op("act", lambda e: e.activation(out=junk[:], in_=xt[b][:], func=AF.Square, accum_out=ss[b][:]),
                         reads=[("xt", b)], writes=["junk", ("ss", b)])
                    S.op("dve", lambda e: e.tensor_scalar(out=ss[b][:], in0=ss[b][:], scalar1=1.0 / D, scalar2=EPS,
                                                          op0=ALU.mult, op1=ALU.add), reads=[("ss", b)], writes=[("ss", b)])
                    S.op("act", lambda e: e.activation(out=ss[b][:], in_=ss[b][:], func=AF.Sqrt), reads=[("ss", b)], writes=[("ss", b)])
                    S.op("dve", lambda e: e.reciprocal(out=ss[b][:], in_=ss[b][:]), reads=[("ss", b)], writes=[("ss", b)])
                    S.op("dve", lambda e: e.scalar_tensor_tensor(out=xn[b][:], in0=xt[b][:], scalar=ss[b][:], in1=nwb[:],
                                                                 op0=ALU.mult, op1=ALU.mult),
                         reads=[("xt", b), ("ss", b), "nwb"], writes=[("xn", b)])
                    for h in range(2):
                        for k in range(4):
                            kc = 4 * h + k
                            S.op("pe", lambda e: e.transpose(pt[h][:, 128 * k:128 * k + 128], xn[b][:, 128 * kc:128 * kc + 128], self.ident_b[:]),
                                 reads=[("xn", b)], writes=[("pt", h)])
                        dst = hT[:, 4 * h:4 * h + 4, 128 * i:128 * i + 128]
                        src = pt[h][:].rearrange("p (k t) -> p k t", t=128)
                        if h == 0:
                            S.op("act", lambda e: e.activation(out=dst, in_=src, func=AF.Copy), reads=[("pt", h)], writes=[("hT", i)])
                        else:
                            S.op("dve", lambda e: e.tensor_copy(out=dst, in_=src), reads=[("pt", h)], writes=[("hT", i)])
            S.barrier()
            hkeys = [("hT", i) for i in range(NT)]
            with ExitStack() as st1:
                sb1 = lambda n, s, d=F32: st1.enter_context(_alloc(nc, "sbuf", n, list(s), d))
                wst = [sb1("wst%d" % i, [128, 1024]) for i in range(2)]
                wb = [sb1("wb%d" % i, [128, 8, 128], BF16) for i in range(2)]
                row = [sb1("row%d" % i, [128, T + 2], BF16) for i in range(2)]
                acc = [sb1("acc%d" % i, [128, 1024]) for i in range(2)]
                ob = [sb1("ob%d" % i, [128, 1024], BF16) for i in range(2)]
                tms = [sb1("tms%d" % i, [128, 8, 128], BF16) for i in range(2)]
                sq = [sb1("sq%d" % i, [128, 512], BF16) for i in range(2)]
                rt = [sb1("rt%d" % i, [128, 512]) for i in range(2)]
                cp = [sb1("cp%d" % i, [128, 4]) for i in range(2)]
                pm = [st1.enter_context(_alloc(nc, "psum", "pm%d" % i, [128, 512], F32)) for i in range(4)]
                ptr = [st1.enter_context(_alloc(nc, "psum", "ptr%d" % i, [128, 512], BF16)) for i in range(2)]
                pss = [st1.enter_context(_alloc(nc, "psum", "pss%d" % i, [128, 512], F32)) for i in range(2)]
                for b in range(2):
                    S.op("pool", lambda e: e.memset(row[b][:, 0:1], 0.0), writes=[("rowh", b)])
                    S.op("pool", lambda e: e.memset(row[b][:, T + 1:T + 2], 0.0), writes=[("rowh", b)])
                cnt = {"ev": 0, "tr": 0, "ob": 0, "ac": 0, "q": 0}

                def mm_block(wtile, wkey, M, dst_fn, dst_keys_fn):
                    for i in range(16):
                        p = pm[i % 4]
                        for kc in range(8):
                            S.op("pe", lambda e: e.matmul(p[0:M, :], lhsT=wtile[:, kc, 0:M], rhs=hT[:, kc, 512 * i:512 * i + 512],
                                                          start=(kc == 0), stop=(kc == 7)),
                                 reads=[wkey] + hkeys[4 * i:4 * i + 4], writes=[("pm", i % 4)])
                        dst = dst_fn(i)
                        if cnt["ev"] % 2 == 0:
                            S.op("act", lambda e: e.activation(out=dst, in_=p[0:M, :], func=AF.Copy), reads=[("pm", i % 4)], writes=dst_keys_fn(i))
                        else:
                            S.op("dve", lambda e: e.tensor_copy(out=dst, in_=p[0:M, :]), reads=[("pm", i % 4)], writes=dst_keys_fn(i))
                        cnt["ev"] += 1

                for bi, (gname, gi, col0, ptype) in enumerate(FM_BLOCKS):
                    if gname in _SKIP or "fm" in _SKIP:
                        continue
                    b = bi % 2
                    S.dma("sp", wst[b][:], I["wfm"][l, bi], writes=[("wst", b)])
                    S.op("pool", lambda e: e.tensor_copy(out=wb[b][:].rearrange("p k c -> p (k c)"), in_=wst[b][:]),
                         reads=[("wst", b)], writes=[("wtile", b)])
                    if ptype in ("conv", "convsilu"):
                        S.dma("sp", cp[b][:], I["convp"][l, bi], writes=[("cp", b)])
                    elif ptype == "qknorm":
                        S.dma("sp", cp[b][:], I["convp"][l, bi], writes=[("cp", b)])
                    rw = row[b]
                    mm_block(wb[b], ("wtile", b), 128, lambda i: rw[:, 1 + 512 * i:1 + 512 * i + 512], lambda i: [("row", b, i)])
                    rkeys = [("row", b, i) for i in range(16)] + [("rowh", b)]
                    r0 = 128 * gi
                    if ptype in ("conv", "convsilu", "silu"):
                        for j in range(8):
                            a = acc[cnt["ac"] % 2]; ak = ("acc", cnt["ac"] % 2); cnt["ac"] += 1
                            o = ob[cnt["ob"] % 2]; okey = ("ob", cnt["ob"] % 2); cnt["ob"] += 1
                            rk = [("row", b, i) for i in range(max(0, 2 * j - 1), min(16, 2 * j + 3))] + [("rowh", b)]
                            c0 = 1024 * j
                            if ptype == "silu":
                                S.op("act", lambda e: e.activation(out=o[:], in_=rw[:, 1 + c0:1 + c0 + 1024], func=AF.Silu), reads=rk, writes=[okey])
                            else:
                                S.op("act", lambda e: e.activation(out=a[:], in_=rw[:, 1 + c0:1 + c0 + 1024], func=AF.Identity,
                                                                   scale=cp[b][:, 1:2], bias=cp[b][:, 3:4]), reads=rk + [("cp", b)], writes=[ak])
                                S.op("dve", lambda e: e.scalar_tensor_tensor(out=a[:], in0=rw[:, c0:c0 + 1024], scalar=cp[b][:, 0:1], in1=a[:],
                                                                             op0=ALU.mult, op1=ALU.add), reads=rk + [("cp", b), ak], writes=[ak])
                                S.op("dve", lambda e: e.scalar_tensor_tensor(out=a[:], in0=rw[:, 2 + c0:2 + c0 + 1024], scalar=cp[b][:, 2:3], in1=a[:],
                                                                             op0=ALU.mult, op1=ALU.add), reads=rk + [("cp", b), ak], writes=[ak])
                                if ptype == "convsilu":
                                    S.op("act", lambda e: e.activation(out=o[:], in_=a[:], func=AF.Silu), reads=[ak], writes=[okey])
                                else:
                                    S.op("pool", lambda e: e.tensor_copy(out=o[:], in_=a[:]), reads=[ak], writes=[okey])
                            fm_dst = {"hyu": "hyuT", "hyg": "hygT", "mbB": "mbBT", "mbC": "mbCT", "mlq": "mlqT", "mlk": "mlkT"}.get(gname)
                            if fm_dst is not None:
                                S.dma("pool", Sc[fm_dst][r0:r0 + 128, c0:c0 + 1024], o[:], reads=[okey], writes=[(fm_dst, gi, j)])
                            tm_dst = {"mbx": "mbx", "mbB": "mbB", "mlk": "mlk"}.get(gname)
                            if tm_dst is not None:
                                tmb = cnt["tr"] % 2; cnt["tr"] += 1
                                for h in range(2):
                                    for k in range(4):
                                        s = 4 * h + k
                                        S.op("pe", lambda e: e.transpose(ptr[h][:, 128 * k:128 * k + 128], o[:, 128 * s:128 * s + 128], self.ident_b[:]),
                                             reads=[okey], writes=[("ptr", h)])
                                    src = ptr[h][:].rearrange("p (k c) -> p k c", c=128)
                                    if h == 0:
                                        S.op("act", lambda e: e.activation(out=tms[tmb][:, 0:4, :], in_=src, func=AF.Copy), reads=[("ptr", h)], writes=[("tms", tmb, 0)])
                                    else:
                                        S.op("dve", lambda e: e.tensor_copy(out=tms[tmb][:, 4:8, :], in_=src), reads=[("ptr", h)], writes=[("tms", tmb, 1)])
                                d = Sc[tm_dst][c0:c0 + 1024, r0:r0 + 128].rearrange("(s p) c -> p s c", p=128)
                                S.dma("pool", d, tms[tmb][:], reads=[("tms", tmb, 0), ("tms", tmb, 1)], writes=[(tm_dst, gi, j)])
                    elif ptype == "qknorm":
                        fm_dst = {"naq": "naqT", "nak": "nakT"}[gname]
                        for j in range(8):
                            o = ob[cnt["ob"] % 2]; okey = ("ob", cnt["ob"] % 2); cnt["ob"] += 1
                            for hh in range(2):
                                q = cnt["q"] % 2; cnt["q"] += 1
                                c0 = 1024 * j + 512 * hh
                                rk = [("row", b, 2 * j + hh)]
                                S.op("act", lambda e: e.activation(out=sq[q][:], in_=rw[:, 1 + c0:1 + c0 + 512], func=AF.Square), reads=rk, writes=[("sq", q)])
                                S.op("pe", lambda e: e.matmul(pss[q][:], lhsT=self.bones_b[:], rhs=sq[q][:], start=True, stop=True),
                                     reads=[("sq", q)], writes=[("pss", q)])
                                S.op("dve", lambda e: e.tensor_scalar(out=rt[q][:], in0=pss[q][:], scalar1=1.0 / 64, scalar2=EPS, op0=ALU.mult, op1=ALU.add),
                                     reads=[("pss", q)], writes=[("rt", q)])
                                S.op("act", lambda e: e.activation(out=rt[q][:], in_=rt[q][:], func=AF.Sqrt), reads=[("rt", q)], writes=[("rt", q)])
                                S.op("dve", lambda e: e.reciprocal(out=rt[q][:], in_=rt[q][:]), reads=[("rt", q)], writes=[("rt", q)])
                                S.op("dve", lambda e: e.scalar_tensor_tensor(out=o[:, 512 * hh:512 * hh + 512], in0=rw[:, 1 + c0:1 + c0 + 512], scalar=cp[b][:, 0:1],
                                                                             in1=rt[q][:], op0=ALU.mult, op1=ALU.mult),
                                     reads=rk + [("rt", q), ("cp", b)], writes=[okey])
                            S.dma("pool", Sc[fm_dst][r0:r0 + 128, 1024 * j:1024 * j + 1024], o[:], reads=[okey], writes=[(fm_dst, gi, j)])
            S.barrier()
            with ExitStack() as st3:
                sb3 = lambda n, s, d=F32: st3.enter_context(_alloc(nc, "sbuf", n, list(s), d))
                smallsb = sb3("smallsb", [NSP, T])
                wsst = sb3("wsst", [128, 8 * NSP])
                wsb = sb3("wsb", [128, 8, NSP], BF16)
                pm3 = [st3.enter_context(_alloc(nc, "psum", "pm3%d" % i, [128, 512], F32)) for i in range(4)]
                S.dma("sp", wsst[:], I["wsm"][l], writes=["wsst"])
                S.op("pool", lambda e: e.tensor_copy(out=wsb[:].rearrange("p k c -> p (k c)"), in_=wsst[:]), reads=["wsst"], writes=["wsb"])
                for i in range(16 if "small" not in _SKIP else 0):
                    p = pm3[i % 4]
                    for kc in range(8):
                        S.op("pe", lambda e: e.matmul(p[0:NSP, :], lhsT=wsb[:, kc, :], rhs=hT[:, kc, 512 * i:512 * i + 512], start=(kc == 0), stop=(kc == 7)),
                             reads=["wsb"] + hkeys[4 * i:4 * i + 4], writes=[("pm3", i % 4)])
                    S.op("act", lambda e: e.activation(out=smallsb[:, 512 * i:512 * i + 512], in_=p[0:NSP, :], func=AF.Copy), reads=[("pm3", i % 4)], writes=[("smallsb", i)])
                S.dma("pool", Sc["smallT"], smallsb[:], reads=[("smallsb", i) for i in range(16)], writes=["smallT"])
            S.barrier()
            with ExitStack() as st2:
                sb2 = lambda n, s, d=F32: st2.enter_context(_alloc(nc, "sbuf", n, list(s), d))
                wst2 = sb2("wst2", [128, 4096])
                wtb = [sb2("wtb%d" % i, [128, 8, 512], BF16) for i in range(2)]
                ot = [sb2("ot%d" % i, [128, 512], BF16) for i in range(4)]
                pm2 = [st2.enter_context(_alloc(nc, "psum", "pm2%d" % i, [128, 512], F32)) for i in range(4)]
                ppg = 512 // CW
                for g in range(NTM if "tm" not in _SKIP else 0):
                    b = g % 2
                    parts = TM_PARTS[ppg * g:ppg * g + ppg]
                    S.dma("sp", wst2[:], I["wtm"][l, g], writes=["wst2"])
                    S.op("pool", lambda e: e.tensor_copy(out=wtb[b][:].rearrange("p k c -> p (k c)"), in_=wst2[:]), reads=["wst2"], writes=[("wtb", b)])
                    for i in range(NT):
                        p = pm2[i % 4]
                        for kc in range(8):
                            S.op("pe", lambda e: e.matmul(p[:], lhsT=hT[:, kc, 128 * i:128 * i + 128], rhs=wtb[b][:, kc, :], start=(kc == 0), stop=(kc == 7)),
                                 reads=[("wtb", b), ("hT", i)], writes=[("pm2", i % 4)])
                        o = ot[i % 4]
                        for pi, (gname, col0, act) in enumerate(parts):
                            func = {"silu": AF.Silu, "copy": AF.Copy, "sigmoid": AF.Sigmoid}[act]
                            sl = slice(CW * pi, CW * pi + CW)
                            S.op("act", lambda e: e.activation(out=o[:, sl], in_=p[:, sl], func=func), reads=[("pm2", i % 4)], writes=[("ot", i % 4, pi)])
                            S.dma("pool" if (i + pi) % 2 else "sp", Sc[gname][128 * i:128 * i + 128, :], o[:, sl], reads=[("ot", i % 4, pi)], writes=[(gname, i)])

    def phaseZ(self, l, x_res, x_dst):
        nc, S, I, Sc = self.nc, self.S, self.I, self.Sc
        gathered = SP > 1 and "G" in self.phases
        with ExitStack() as st:
            sb = lambda n, s, d=F32: st.enter_context(_alloc(nc, "sbuf", n, list(s), d))
            wo = sb("wo", [128, 16, OW], BF16)
            wos = [sb("wos%d" % i, [128, 2048]) for i in range(2)]
            nck = 16 * OW // 2048
            kpc = 2048 // OW
            for c in range(nck):
                S.dma("sp", wos[c % 2][:], I["wout"][l, :, 2048 * c:2048 * c + 2048], writes=[("wos", c % 2)])
                S.op("pool", lambda e: e.tensor_copy(out=wo[:, kpc * c:kpc * c + kpc, :].rearrange("p k c -> p (k c)"), in_=wos[c % 2][:]),
                     reads=[("wos", c % 2)], writes=["wo"])
            yt = [sb("yt%d" % i, [128, SP, YW], BF16) for i in range(2)]
            yT = [sb("yT%d" % i, [128, 16, 128], BF16) for i in range(2)]
            xt = [sb("xz%d" % i, [128, OW]) for i in range(2)]
            oz = [sb("oz%d" % i, [128, OW]) for i in range(2)]
            ptz = [st.enter_context(_alloc(nc, "psum", "ptz%d" % i, [128, 512], BF16)) for i in range(3)]
            pz = [st.enter_context(_alloc(nc, "psum", "pz%d" % i, [128, 512], F32)) for i in range(4)]
            ysrc = Sc["ytm_g"] if gathered else Sc["ytm"]
            hsrc = Sc["yhy_g"] if gathered else Sc["yhyT"]
            nr = SP if gathered else 1
            cpr = YW // 128
            for i in range(NT):
                b = i % 2
                S.dma("sp", yt[b][:, 0:nr, :], ysrc.rearrange("(q r t) c -> q t r c", q=8, r=nr)[i // 8][128 * (i % 8):128 * (i % 8) + 128, :, :], writes=[("yt", b)])
                S.dma("sp", xt[b][:], x_res[128 * i:128 * i + 128, :], writes=[("xz", b)])
                S.dma("pool", yT[b][:, 0:4 * nr // SP, :], hsrc[:, 128 * i:128 * i + 128].rearrange("(k p) t -> p k t", p=128), writes=[("yTh", b)])
                for h in range(3):
                    for k in range(4):
                        q = 4 * h + k
                        r, cc = q // cpr, q % cpr
                        S.op("pe", lambda e: e.transpose(ptz[h][:, 128 * k:128 * k + 128], yt[b][:, r, 128 * cc:128 * cc + 128], self.ident_b[:]),
                             reads=[("yt", b)], writes=[("ptz", h)])
                    src = ptz[h][:].rearrange("p (k c) -> p k c", c=128)
                    dst = yT[b][:, 4 + 4 * h:8 + 4 * h, :]
                    if h == 1:
                        S.op("dve", lambda e: e.tensor_copy(out=dst, in_=src), reads=[("ptz", h)], writes=[("yTt", b, h)])
                    else:
                        S.op("act", lambda e: e.activation(out=dst, in_=src, func=AF.Copy), reads=[("ptz", h)], writes=[("yTt", b, h)])
                for half in range(OW // 512):
                    p = pz[(2 * i + half) % 4]
                    pk = ("pz", (2 * i + half) % 4)
                    for kc in range(16):
                        S.op("pe", lambda e: e.matmul(p[:], lhsT=yT[b][:, kc, :], rhs=wo[:, kc, 512 * half:512 * half + 512], start=(kc == 0), stop=(kc == 15)),
                             reads=["wo", ("yTh", b)] + [("yTt", b, h) for h in range(3)], writes=[pk])
                    S.op("dve", lambda e: e.tensor_tensor(out=oz[b][:, 512 * half:512 * half + 512], in0=p[:], in1=xt[b][:, 512 * half:512 * half + 512], op=ALU.add),
                         reads=[pk, ("xz", b)], writes=[("oz", b, half)])
                S.dma("pool", x_dst[128 * i:128 * i + 128, :], oz[b][:], reads=[("oz", b, h2) for h2 in range(OW // 512)], writes=[("xdst", i)])


def _fm_col0(gname, gi, j):
    if gname == "hyu":
        part, sub = gi // NB, gi % NB
        return O_HYU + 512 * part + CW * j + 128 * sub
    base = {"hyg": O_HYG, "mbx": O_MBX, "mlq": O_MLQ, "mlk": O_MLK, "naq": O_NAQ, "nak": O_NAK}.get(gname)
    if base is not None:
        return base + CW * j + 128 * gi
    return {"mbB": O_MBB, "mbC": O_MBC}[gname] + 128 * (MBG * j + gi)


def _small_cols(j):
    cols = []
    for d in range(2):
        cols += [O_MBDT + 8 * d + MBU * j + u for u in range(MBU)]
    for d in range(2):
        for g in range(2):
            cols += [O_MLG + 8 * d + 4 * g + MLU * j + u for u in range(MLU)]
    return cols


def _prep_weights(inp, j):
    L = DEPTH
    w_in = np.asarray(inp["w_in"], np.float32)
    w_out = np.asarray(inp["w_out"], np.float32)
    W = {}
    wfm = np.empty((L, NFM, 128, 1024), np.float32)
    convp = np.zeros((L, NFM, 128, 4), np.float32)
    for bi, (gname, gi, _c, ptype) in enumerate(FM_BLOCKS):
        col0 = _fm_col0(gname, gi, j)
        blk = w_in[:, :, col0:col0 + 128]
        wfm[:, bi] = blk.reshape(L, 8, 128, 128).transpose(0, 2, 1, 3).reshape(L, 128, 1024)
        if ptype in ("conv", "convsilu"):
            if gname == "hyu":
                cw, cb, c0 = inp["hy_conv_w"], inp["hy_conv_b"], col0 - O_HYU
            elif gname in ("mbx", "mbB", "mbC"):
                cw, cb, c0 = inp["mb_conv_w"], inp["mb_conv_b"], col0 - O_MBX
            else:
                cw, cb, c0 = inp["ml_conv_w"], inp["ml_conv_b"], col0 - O_MLQ
            convp[:, bi, :, 0:3] = np.asarray(cw)[:, :, c0:c0 + 128].transpose(0, 2, 1)
            convp[:, bi, :, 3] = np.asarray(cb)[:, c0:c0 + 128]
        elif ptype == "qknorm":
            nw = np.asarray(inp["na_qnorm_w"] if gname == "naq" else inp["na_knorm_w"])
            convp[:, bi, :, 0] = np.tile(nw, (1, 2))
    W["wfm"] = wfm
    W["convp"] = convp
    scols = _small_cols(j)
    wsm = np.zeros((L, 8, 128, NSP), np.float32)
    wsm[:, :, :, :NSM] = w_in[:, :, scols].reshape(L, 8, 128, NSM)
    W["wsm"] = np.ascontiguousarray(wsm.transpose(0, 2, 1, 3).reshape(L, 128, 8 * NSP))
    wtm = np.empty((L, NTM, 128, 4096), np.float32)
    ppg = 512 // CW
    for g in range(NTM):
        cols = []
        for (gname, col0, act) in TM_PARTS[ppg * g:ppg * g + ppg]:
            cols += list(range(col0 + CW * j, col0 + CW * j + CW))
        wtm[:, g] = w_in[:, :, cols].reshape(L, 8, 128, 512).transpose(0, 2, 1, 3).reshape(L, 128, 4096)
    W["wtm"] = wtm
    rows = []
    for q in range(HC // 64):
        for r in range(SP):
            rows += list(range(HC * r + 64 * q, HC * r + 64 * q + 64))
    for r in range(SP):
        for base in (512, 1024, 1536):
            rows += list(range(base + CW * r, base + CW * r + CW))
    wo = w_out[:, rows, OW * j:OW * j + OW]
    W["wout"] = np.ascontiguousarray(wo.reshape(L, 16, 128, OW).transpose(0, 2, 1, 3).reshape(L, 128, 16 * OW))
    W["norm_w"] = np.asarray(inp["norm_w"], np.float32).reshape(L, 1, D)
    W["ident"] = np.eye(128, dtype=np.float32)
    bo = np.zeros((128, 128), np.float32)
    bo[:64, :64] = 1.0
    bo[64:, 64:] = 1.0
    W["blockones"] = bo
    return W


def _prep_na(inp, j):
    jc = j
    L = DEPTH
    rpb = np.asarray(inp["na_rpb"], np.float32)
    kk = np.arange(128)
    il, kc = kk // 64, kk % 64
    w = np.arange(64)
    cs = np.clip(w - 8, 0, 48)
    valid = (kc[:, None] >= cs[None, :]) & (kc[:, None] < cs[None, :] + 16)
    coff = np.clip(kc[:, None] - w[None, :] + 15, 0, 30)
    bias = np.zeros((L, 8, 128, 8, 4, 64), np.float32)
    for v in range(8):
        for j in range(4):
            i = 2 * j + il
            roff = v + i
            bias[:, :, :, v, j, :] = rpb[:, :, roff[:, None], coff]
    mask = np.broadcast_to(valid[:, None, :], (128, 4, 64)).astype(np.float32).reshape(128, 256)
    return {"na_bias": np.ascontiguousarray(bias.reshape(L, 8, 128, 2048)[:, NAH * jc:NAH * jc + NAH]), "na_mask": np.ascontiguousarray(mask)}


def _na_declare_inputs(self):
    self.I["na_bias"] = self.dram_in("na_bias", [DEPTH, NAH, 128, 2048])
    self.I["na_mask"] = self.dram_in("na_mask", [128, 256])


def _phaseNA(self, l):
    nc, S, I, Sc = self.nc, self.S, self.I, self.Sc
    NH = NAH
    with ExitStack() as st:
        sb = lambda n, s, d=F32: st.enter_context(_alloc(nc, "sbuf", n, list(s), d))
        KT = [sb("naKT%d" % i, [64, T], BF16) for i in range(2)]
        QT = [sb("naQT%d" % i, [64, T], BF16) for i in range(2)]
        Ve = [sb("naVe%d" % i, [128, 64, 65], BF16) for i in range(2)]
        Vo = [sb("naVo%d" % i, [128, 63, 65], BF16) for i in range(2)]
        EBr = sb("naEBr", [128, 2048])
        EBM = [sb("naEBM%d" % i, [128, 8, 256]) for i in range(2)]
        msk = sb("namask", [128, 256])
        G = [sb("naG%d" % i, [64, 128, 64], BF16) for i in range(2)]
        O = sb("naO", [64, 128, 64], BF16)
        Ob = sb("naOb", [64, 128, 64], BF16)
        E = [sb("naE%d" % i, [128, 256]) for i in range(4)]
        Pb = [sb("naP%d" % i, [128, 256], BF16) for i in range(4)]
        rec = [sb("narec%d" % i, [64, 1]) for i in range(4)]
        pS = [st.enter_context(_alloc(nc, "psum", "napS%d" % i, [128, 512], F32)) for i in range(4)]
        pO = [st.enter_context(_alloc(nc, "psum", "napO%d" % i, [128, 512], F32)) for i in range(4)]
        S.dma("sp", msk[:], I["na_mask"], writes=["namask"])
        for b in range(2):
            S.op("pool", lambda e: e.memset(Ve[b][:, :, 64:65], 1.0), writes=[("Veo", b)])
            S.op("pool", lambda e: e.memset(Vo[b][:, :, 64:65], 1.0), writes=[("Voo", b)])
        for h in range(NH):
            b = h % 2
            S.dma("sp", KT[b][:], Sc["nakT"][64 * h:64 * h + 64, :], writes=[("KT", b)])
            S.dma("sp", QT[b][:], Sc["naqT"][64 * h:64 * h + 64, :], writes=[("QT", b)])
            S.dma("pool", Ve[b][:, :, 0:64], Sc["nav"][:, 64 * h:64 * h + 64].rearrange("(i p) d -> p i d", p=128), writes=[("Ve", b)])
            S.dma("pool", Vo[b][:, :, 0:64], Sc["nav"][64:T - 64, 64 * h:64 * h + 64].rearrange("(i p) d -> p i d", p=128), writes=[("Vo", b)])
            S.dma("sp", G[b][:], Sc["nag"][:, 64 * h:64 * h + 64].rearrange("(r w) d -> w r d", w=64), writes=[("G", b)])
            S.dma("sp", EBr[:], I["na_bias"][l, h], writes=["EBr"])
            S.op("act", lambda e: e.activation(out=EBr[:], in_=EBr[:], func=AF.Exp), reads=["EBr"], writes=["EBr"])
            S.op("dve", lambda e: e.tensor_tensor(out=EBM[b][:], in0=EBr[:].rearrange("p (v c) -> p v c", c=256),
                                                  in1=msk[:].unsqueeze(1).broadcast_to([128, 8, 256]), op=ALU.mult),
                 reads=["EBr", "namask"], writes=[("EBM", b)])
            NR = 4
            for r0_ in range(0, 128, NR):
                rows = list(range(r0_, r0_ + NR))
                rsv = {r: min(max(r - 4, 0), 120) for r in rows}
                for r in rows:
                    rs, rb = rsv[r], r % NR
                    for j in range(4):
                        S.op("pe", lambda e: e.matmul(pS[rb][:, 64 * j:64 * j + 64], lhsT=KT[b][:, 64 * rs + 128 * j:64 * rs + 128 * j + 128],
                                                      rhs=QT[b][:, 64 * r:64 * r + 64], start=True, stop=True),
                             reads=[("KT", b), ("QT", b)], writes=[("pS", rb)])
                for r in rows:
                    rb = r % NR
                    S.op("act", lambda e: e.activation(out=E[rb][:], in_=pS[rb][:, 0:256], func=AF.Exp, scale=0.125), reads=[("pS", rb)], writes=[("E", rb)])
                for r in rows:
                    rb = r % NR
                    v = rsv[r] - r + 7
                    S.op("dve", lambda e: e.tensor_tensor(out=Pb[rb][:], in0=E[rb][:], in1=EBM[b][:, v, :], op=ALU.mult),
                         reads=[("E", rb), ("EBM", b)], writes=[("P", rb)])
                for r in rows:
                    rs, rb = rsv[r], r % NR
                    for j in range(4):
                        if rs % 2 == 0:
                            vt = Ve[b][:, rs // 2 + j, :]
                        else:
                            vt = Vo[b][:, (rs - 1) // 2 + j, :]
                        S.op("pe", lambda e: e.matmul(pO[rb][0:64, 0:65], lhsT=Pb[rb][:, 64 * j:64 * j + 64], rhs=vt, start=(j == 0), stop=(j == 3)),
                             reads=[("P", rb), ("Ve", b), ("Vo", b), ("Veo", b), ("Voo", b)], writes=[("pO", rb)])
                for r in rows:
                    rb = r % NR
                    S.op("dve", lambda e: e.reciprocal(out=rec[rb][:], in_=pO[rb][0:64, 64:65]), reads=[("pO", rb)], writes=[("rec", rb)])
                for r in rows:
                    rb = r % NR
                    S.op("act", lambda e: e.activation(out=O[:, r, :], in_=pO[rb][0:64, 0:64], func=AF.Copy, scale=rec[rb][:]),
                         reads=[("pO", rb), ("rec", rb)], writes=["O"])
            S.op("dve", lambda e: e.tensor_tensor(out=Ob[:].rearrange("p r d -> p (r d)"), in0=O[:].rearrange("p r d -> p (r d)"),
                                                  in1=G[b][:].rearrange("p r d -> p (r d)"), op=ALU.mult), reads=["O", ("G", b)], writes=["Ob"])
            S.dma("pool", Sc["ytm"][:, 2 * CW + 64 * h:2 * CW + 64 * h + 64].rearrange("(r w) d -> w r d", w=64), Ob[:], reads=["Ob"], writes=[("ytm_na", h)])


Prog.phaseNA = _phaseNA


def _prep_mixer(inp, j):
    W = {}
    W.update(_prep_na(inp, j))
    W.update(_prep_dla(inp, j))
    W.update(_prep_hy(inp, j))
    return W


def _prep_dla(inp, j):
    L = DEPTH
    gpar = np.zeros((L, 4, 64, 2), np.float32)
    dtb = np.asarray(inp["mb_dt_bias"], np.float32)[:, :, MBU * j:MBU * j + MBU]
    alog = np.asarray(inp["mb_a_log"], np.float32)[:, :, MBU * j:MBU * j + MBU]
    gb = np.asarray(inp["ml_gate_b"], np.float32)[:, :, :, MLU * j:MLU * j + MLU]
    for d in range(2):
        gpar[:, d, :8 * MBU, 0] = np.repeat(dtb[:, d, :], 8, axis=1)
        gpar[:, d, :8 * MBU, 1] = np.repeat(alog[:, d, :], 8, axis=1)
        gpar[:, 2 + d, :8 * MLU, 0] = np.repeat(gb[:, d, 0, :], 8, axis=1)
        gpar[:, 2 + d, :8 * MLU, 1] = np.repeat(gb[:, d, 1, :], 8, axis=1)
    rmask = np.ones((64, 1024), np.float32)
    rmask[:, ::128] = 0.0
    s = np.arange(128)[:, None]
    ll = np.arange(128)[None, :]
    negmask = np.stack([np.where(s <= ll, 0.0, -30000.0), np.where(s >= ll, 0.0, -30000.0)]).astype(np.float32)
    return {"gpar": gpar, "rmask": rmask, "negmask": negmask,
            "dsk": np.ascontiguousarray(np.asarray(inp["mb_d"], np.float32)[:, MBU * j:MBU * j + MBU]).reshape(L, 1, MBU),
            "mbnw": np.ascontiguousarray(np.asarray(inp["mb_norm_w"], np.float32)[:, CW * j:CW * j + CW]).reshape(L, 1, CW),
            "mlnw": np.ascontiguousarray(np.asarray(inp["ml_norm_w"], np.float32)[:, CW * j:CW * j + CW]).reshape(L, 1, CW)}


def _dla_declare(self):
    I, Sc = self.I, self.Sc
    I["gpar"] = self.dram_in("gpar", [DEPTH, 4, 64, 2])
    I["rmask"] = self.dram_in("rmask", [64, 1024])
    I["negmask"] = self.dram_in("negmask", [2, 128, 128])
    I["dsk"] = self.dram_in("dsk", [DEPTH, 1, MBU])
    I["mbnw"] = self.dram_in("mbnw", [DEPTH, 1, CW])
    I["mlnw"] = self.dram_in("mlnw", [DEPTH, 1, CW])
    Sc["gq"] = self.dram_scr("gq", [4, 4, 8, T], F32, dbg=True)
    Sc["gcs"] = self.dram_scr("gcs", [4, 8, T], F32, dbg=True)
    Sc["gtot"] = self.dram_scr("gtot", [4, 8, 64], F32, dbg=True)
    Sc["yf_mb"] = self.dram_scr("yf_mb", [T, CW], F32)
    Sc["hf_ml"] = self.dram_scr("hf_ml", [T, CW], F32)
    Sc["yb_mb"] = self.dram_scr("yb_mb", [T, CW], F32)
    Sc["hb_ml"] = self.dram_scr("hb_ml", [T, CW], F32)


def _dla_streams(self, l, mixer):
    nc, S, I, Sc = self.nc, self.S, self.I, self.Sc
    U = MBU if mixer == "mb" else MLU
    P = 8 * U
    with ExitStack() as st:
        sb = lambda n, s, d=F32: st.enter_context(_alloc(nc, "sbuf", n, list(s), d))
        rm = sb("rm", [64, 1024])
        S.dma("sp", rm[:], I["rmask"], writes=["rm"])
        for d in range(2):
            md = (0 if mixer == "mb" else 2) + d
            k = lambda n: (n, d)
            gp = sb("gp%d" % d, [64, 2])
            S.dma("sp", gp[:], I["gpar"][l, md], writes=[k("gp")])
            sc = sb("sc%d" % d, [64, 1024]); a = sb("a%d" % d, [64, 1024]); cs = sb("cs%d" % d, [64, 1024])
            t1 = sb("t1%d" % d, [64, 1024]); t2 = sb("t2%d" % d, [64, 1024]); pp = sb("pp%d" % d, [64, 2])
            if mixer == "mb":
                S.dma("sp", t1[0:P, :], Sc["smallT"][MBU * d:MBU * d + MBU, :].rearrange("u (s n) -> (u s) n", n=1024), writes=[k("t1")])
                S.op("act", lambda e: e.activation(out=t1[0:P, :], in_=t1[0:P, :], func=AF.Exp, bias=gp[0:P, 0:1]), reads=[k("t1"), k("gp")], writes=[k("t1")])
                S.op("act", lambda e: e.activation(out=sc[0:P, :], in_=t1[0:P, :], func=AF.Ln, bias=1.0), reads=[k("t1")], writes=[k("sc")])
                S.op("act", lambda e: e.activation(out=pp[0:P, 0:1], in_=gp[0:P, 1:2], func=AF.Exp), reads=[k("gp")], writes=[k("pp")])
                S.op("dve", lambda e: e.tensor_scalar(out=pp[0:P, 0:1], in0=pp[0:P, 0:1], scalar1=-1.0, scalar2=None, op0=ALU.mult), reads=[k("pp")], writes=[k("pp")])
                S.op("dve", lambda e: e.tensor_scalar(out=a[0:P, :], in0=sc[0:P, :], scalar1=pp[0:P, 0:1], scalar2=None, op0=ALU.mult),
                     reads=[k("sc"), k("pp")], writes=[k("a")])
            else:
                r0 = 2 * MBU + 2 * MLU * d
                S.dma("sp", t1[0:P, :], Sc["smallT"][r0:r0 + MLU, :].rearrange("u (s n) -> (u s) n", n=1024), writes=[k("t1")])
                S.dma("sp", t2[0:P, :], Sc["smallT"][r0 + MLU:r0 + 2 * MLU, :].rearrange("u (s n) -> (u s) n", n=1024), writes=[k("t2")])
                S.op("act", lambda e: e.activation(out=sc[0:P, :], in_=t1[0:P, :], func=AF.Exp, bias=gp[0:P, 0:1]), reads=[k("t1"), k("gp")], writes=[k("sc")])
                S.op("dve", lambda e: e.tensor_scalar(out=sc[0:P, :], in0=sc[0:P, :], scalar1=float(128.0 ** -0.5), scalar2=None, op0=ALU.mult), reads=[k("sc")], writes=[k("sc")])
                S.op("dve", lambda e: e.tensor_scalar(out=pp[0:P, 0:1], in0=gp[0:P, 1:2], scalar1=-1.0, scalar2=None, op0=ALU.mult), reads=[k("gp")], writes=[k("pp")])
                S.op("act", lambda e: e.activation(out=t2[0:P, :], in_=t2[0:P, :], func=AF.Exp, scale=-1.0, bias=pp[0:P, 0:1]), reads=[k("t2"), k("pp")], writes=[k("t2")])
                S.op("act", lambda e: e.activation(out=t2[0:P, :], in_=t2[0:P, :], func=AF.Ln, bias=1.0), reads=[k("t2")], writes=[k("t2")])
                S.op("dve", lambda e: e.tensor_scalar(out=a[0:P, :], in0=t2[0:P, :], scalar1=-1.0, scalar2=None, op0=ALU.mult), reads=[k("t2")], writes=[k("a")])
            S.op("dve", lambda e: e.tensor_tensor_scan(out=cs[0:P, :], data0=rm[0:P, :], data1=a[0:P, :], initial=0.0, op0=ALU.mult, op1=ALU.add),
                 reads=["rm", k("a")], writes=[k("cs")])
            cs3 = cs[0:P, :].rearrange("p (c n) -> p c n", n=128)
            totb = cs3[:, :, 127:128].broadcast_to([P, 8, 128])
            S.dma("sp", Sc["gtot"][md, 0:U, :].rearrange("u (s c) -> (u s) c", c=8), cs3[:, :, 127], reads=[k("cs")], writes=[("gtot", md)], allow_slow_non_contiguous=True)
            t13 = t1[0:P, :].rearrange("p (c n) -> p c n", n=128)
            S.op("dve", lambda e: e.tensor_tensor(out=t13, in0=totb, in1=cs3, op=ALU.subtract), reads=[k("cs")], writes=[k("t1")])
            if d == 1:
                S.op("dve", lambda e: e.tensor_tensor(out=t2[0:P, :], in0=cs[0:P, :], in1=a[0:P, :], op=ALU.subtract), reads=[k("cs"), k("a")], writes=[k("t2")])
                S.op("dve", lambda e: e.tensor_tensor(out=cs[0:P, :], in0=t1[0:P, :], in1=a[0:P, :], op=ALU.add), reads=[k("t1"), k("a")], writes=[k("cs")])
                wexp = t2
                wk = k("t2")
            else:
                wexp = t1
                wk = k("t1")
            unf = lambda ap: ap.rearrange("u (s n) -> (u s) n", n=1024)
            S.dma("sp", unf(Sc["gcs"][md, 0:U, :]), cs[0:P, :], reads=[k("cs")], writes=[("gcs", md)])
            S.op("act", lambda e: e.activation(out=wexp[0:P, :], in_=wexp[0:P, :], func=AF.Exp), reads=[wk], writes=[wk])
            S.op("dve", lambda e: e.tensor_tensor(out=wexp[0:P, :], in0=wexp[0:P, :], in1=sc[0:P, :], op=ALU.mult), reads=[wk, k("sc")], writes=[wk])
            S.dma("sp", unf(Sc["gq"][md, 2, 0:U, :]), wexp[0:P, :], reads=[wk], writes=[("gq", md, 2)])
            S.dma("sp", unf(Sc["gq"][md, 3, 0:U, :]), sc[0:P, :], reads=[k("sc")], writes=[("gq", md, 3)])
            S.op("act", lambda e: e.activation(out=a[0:P, :], in_=cs[0:P, :], func=AF.Exp), reads=[k("cs")], writes=[k("a")])
            S.dma("sp", unf(Sc["gq"][md, 1, 0:U, :]), a[0:P, :], reads=[k("a")], writes=[("gq", md, 1)])
            S.op("dve", lambda e: e.tensor_scalar(out=cs[0:P, :], in0=cs[0:P, :], scalar1=-1.0, scalar2=None, op0=ALU.mult), reads=[k("cs")], writes=[k("cs")])
            S.dma("sp", unf(Sc["gq"][md, 0, 0:U, :]), cs[0:P, :], reads=[k("cs")], writes=[("gq", md, 0)])


def _dla_run(self, l, mixer, d):
    nc, I, Sc = self.nc, self.I, self.Sc
    S = _KeyNS(self.S, (mixer, d))
    mb = mixer == "mb"
    U = MBU if mb else MLU
    PW = 64 if mb else 129
    PS = 64 if mb else 256
    md = (0 if mb else 2) + d
    with ExitStack() as st:
        sb = lambda n, s, dt=F32: st.enter_context(_alloc(nc, "sbuf", n, list(s), dt))
        pst = lambda n, dt=F32: st.enter_context(_alloc(nc, "psum", n, [128, 512], dt))
        Q4 = self.Q4s[d]
        q4t = sb("q4t", [128, 64, 32])
        etot = sb("etot", [128, U, 64])
        negm = sb("negm", [128, 128])
        H = sb("H", [128, U, PW]); Hb = sb("Hb", [128, U, PW], BF16)
        csb = [sb("csb%d" % i, [128, U, 128]) for i in range(4)]
        LT = [sb("LT%d" % i, [128, U, 128]) for i in range(2)]
        MT = [sb("MT%d" % i, [128, U, 128], BF16) for i in range(2)]
        if mb:
            XK = [sb("XK%d" % i, [128, U * 64 + 128 * MBG], BF16) for i in range(4)]
            Xv = [t[:, 0:U * 64].rearrange("p (u w) -> p u w", w=64) for t in XK]
        else:
            Xv = [sb("Xv%d" % i, [128, U, PW], BF16)[:] for i in range(4)]
        Xw = [sb("Xw%d" % i, [128, U, PW], BF16) for i in range(2)]
        if mb:
            Kt = [t[:, U * 64:U * 64 + 128 * MBG] for t in XK]
        else:
            Kt = [sb("Kt%d" % i, [128, CW], BF16)[:] for i in range(4)]
        NG = MBG if mb else MLU
        KQ = [sb("KQ%d" % i, [128, 2, NG, 128], BF16) for i in range(4)]
        KTf = [t[:, 0, :, :] for t in KQ]
        QTf = [t[:, 1, :, :] for t in KQ]
        y2s = [sb("y2s%d" % i, [128, U, PW]) for i in range(2)]
        yo = [sb("yo%d" % i, [128, U, PW]) for i in range(2)]
        fin = [sb("fin%d" % i, [128, CW]) for i in range(2)]
        pG = pst("pG")
        NPT = 1 if mb else (MLU + 1) // 2
        py1 = [pst("py1%d" % i) for i in range(NPT)]
        py2 = [pst("py2%d" % i) for i in range(NPT)]
        pS_ = [pst("pS%d" % i) for i in range(NPT)]
        pQ = py1[0]

        def pview(tiles, u, w):
            if mb:
                return tiles[0][:, 64 * u:64 * u + w]
            return tiles[u // 2][:, 256 * (u % 2):256 * (u % 2) + w]

        def pall(tiles, h):
            if mb:
                return tiles[0][:, 0:64 * U].rearrange("p (u w) -> p u w", w=64)
            return tiles[h][:].rearrange("p (u w) -> p u w", w=256)[:, :, 0:129]

        S.op("pool", lambda e: e.memset(Q4[:], 0.0), writes=["Q4"])
        S.dma("sp", Q4[:], Sc["gq"][md].rearrange("q u t -> (q u) t"), writes=["Q4"])
        for g4 in range(4):
            for k in range(16):
                c = 16 * g4 + k
                S.op("pe", lambda e: e.transpose(pQ[:, 32 * k:32 * k + 32], Q4[:, 128 * c:128 * c + 128], self.ident_f[0:32, 0:32]), reads=["Q4"], writes=["py1"])
            S.op("dve", lambda e: e.tensor_copy(out=q4t[:, 16 * g4:16 * g4 + 16, :], in_=pQ[:].rearrange("p (k q) -> p k q", q=32)), reads=["py1"], writes=["q4t"])
        S.dma("sp", etot[:].rearrange("p u c -> p (u c)"), Sc["gtot"][md:md + 1, 0:U, :].rearrange("o u c -> o (u c)").broadcast_to([128, U * 64]), writes=["etot"])
        S.op("act", lambda e: e.activation(out=etot[:], in_=etot[:], func=AF.Exp), reads=["etot"], writes=["etot"])
        S.dma("sp", negm[:], I["negmask"][d], writes=["negm"])
        S.op("pool", lambda e: e.memset(H[:], 0.0), writes=["H"])
        S.op("pool", lambda e: e.memset(Hb[:], 0.0), writes=["Hb"])
        if not mb:
            for i in range(4):
                S.op("pool", lambda e: e.memset(Xv[i][:, :, 128:129], 1.0), writes=[("Xvo", i)])
        rden = [sb("rden%d" % i, [128, 4]) for i in range(2)]

        order = list(range(64)) if d == 0 else list(range(63, -1, -1))
        srcKQ = Sc["mbBCT"] if mb else Sc["mlkqT"]

        def issue_loads(step_):
            c_, b4 = order[step_], step_ % 4
            q0 = 128 * c_
            S.dma("sp", csb[b4][:], Sc["gcs"][md, 0:U, q0:q0 + 128].partition_broadcast(128), writes=[("csb", b4)])
            if mb:
                S.dma("act", XK[b4][:], Sc["mbxB"][q0:q0 + 128, :], writes=[("Xv", b4), ("Kt", b4)])
            else:
                S.dma("act", Xv[b4][:, :, 0:128], Sc["mlv"][q0:q0 + 128, :].rearrange("p (u w) -> p u w", w=128), writes=[("Xv", b4)])
                S.dma("act", Kt[b4][:], Sc["mlk"][q0:q0 + 128, :], writes=[("Kt", b4)])
            S.dma("act", KQ[b4][:], srcKQ[:, q0:q0 + 128].rearrange("(a g n) s -> n a g s", a=2, n=128), writes=[("KTf", b4), ("QTf", b4)])

        for s0 in range(3):
            issue_loads(s0)
        yield "setup"
        for step, c in enumerate(order):
            k = step % 2
            k4 = step % 4
            r0 = 128 * c
            xvk = [("Xv", k4)] + ([] if mb else [("Xvo", k4)])
            yield "s"
            for g in range(NG):
                S.op("pe", lambda e: e.matmul(pG[:, 128 * g:128 * g + 128], lhsT=KTf[k4][:, g, :], rhs=QTf[k4][:, g, :], start=True, stop=True),
                     reads=[("KTf", k4), ("QTf", k4)], writes=[("pG", g)])
            S.op("dve", lambda e: e.tensor_tensor(out=csb[k4][:], in0=csb[k4][:], in1=negm[:].unsqueeze(1).broadcast_to([128, U, 128]), op=ALU.add),
                 reads=[("csb", k4), "negm"], writes=[("csb", k4)])
            yield "s"
            for u in range(U):
                S.op("act", lambda e: e.activation(out=LT[k][:, u, :], in_=csb[k4][:, u, :], func=AF.Exp, bias=q4t[:, c, u:u + 1]),
                     reads=[("csb", k4), "q4t"], writes=[("LT", k, u)])
            if step + 3 < 64:
                issue_loads(step + 3)
            yield "s"
            for u in range(U):
                g = (u // 4) if mb else u
                S.op("dve", lambda e: e.scalar_tensor_tensor(out=MT[k][:, u, :], in0=pG[:, 128 * g:128 * g + 128], scalar=q4t[:, c, 24 + u:25 + u],
                                                             in1=LT[k][:, u, :], op0=ALU.mult, op1=ALU.mult),
                     reads=[("pG", g), ("LT", k, u), "q4t"], writes=[("MT", k, u)])
            yield "s"
            for u in range(U):
                S.op("pe", lambda e: e.matmul(pview(py1, u, PW), lhsT=MT[k][:, u, :], rhs=Xv[k4][:, u, :], start=True, stop=True),
                     reads=[("MT", k, u)] + xvk, writes=["py1"])
            if mb:
                for g in range(MBG):
                    S.op("pe", lambda e: e.matmul(py2[0][:, 256 * g:256 * g + 256], lhsT=QTf[k4][:, g, :], rhs=Hb[:, 4 * g:4 * g + 4, :],
                                                  start=True, stop=True), reads=[("QTf", k4), "Hb"], writes=["py2"])
            else:
                for u in range(U):
                    S.op("pe", lambda e: e.matmul(pview(py2, u, PW), lhsT=QTf[k4][:, u, :], rhs=Hb[:, u, :], start=True, stop=True),
                         reads=[("QTf", k4), "Hb"], writes=["py2"])
            yield "s"
            ecs_b = lambda u0, n: q4t[:, c, 8 + u0:8 + u0 + n].unsqueeze(2).broadcast_to([128, n, PW])
            w_b = q4t[:, c, 16:16 + U].unsqueeze(2).broadcast_to([128, U, PW])
            if mb:
                S.op("dve", lambda e: e.tensor_tensor(out=y2s[k][:], in0=pall(py2, 0), in1=ecs_b(0, U), op=ALU.mult), reads=["py2", "q4t"], writes=[("y2s", k)])
                S.op("dve", lambda e: e.tensor_tensor(out=yo[k][:], in0=pall(py1, 0), in1=y2s[k][:], op=ALU.add), reads=["py1", ("y2s", k)], writes=[("yo", k)])
            else:
                for h in range(NPT):
                    S.op("dve", lambda e: e.tensor_tensor(out=y2s[k][:, 2 * h:2 * h + 2, :], in0=pall(py2, h), in1=ecs_b(2 * h, 2), op=ALU.mult),
                         reads=["py2", "q4t"], writes=[("y2s", k, h)])
                    S.op("dve", lambda e: e.tensor_tensor(out=yo[k][:, 2 * h:2 * h + 2, :], in0=pall(py1, h), in1=y2s[k][:, 2 * h:2 * h + 2, :], op=ALU.add),
                         reads=["py1", ("y2s", k, h)], writes=[("yo", k, h)])
            yok = [("yo", k)] if mb else [("yo", k, h) for h in range(NPT)]
            yield "s"
            S.op("pool", lambda e: e.tensor_tensor(out=Xw[k][:], in0=Xv[k4], in1=w_b, op=ALU.mult), reads=xvk + ["q4t"], writes=[("Xw", k)])
            if mb:
                for g in range(MBG):
                    S.op("pe", lambda e: e.matmul(pS_[0][:, 256 * g:256 * g + 256], lhsT=Kt[k4][:, 128 * g:128 * g + 128], rhs=Xw[k][:, 4 * g:4 * g + 4, :],
                                                  start=True, stop=True), reads=[("Kt", k4), ("Xw", k)], writes=["pS"])
            else:
                for u in range(U):
                    S.op("pe", lambda e: e.matmul(pview(pS_, u, PW), lhsT=Kt[k4][:, 128 * u:128 * u + 128], rhs=Xw[k][:, u, :], start=True, stop=True),
                         reads=[("Kt", k4), ("Xw", k)], writes=["pS"])
            yield "s"
            S.op("pool", lambda e: e.tensor_tensor(out=H[:], in0=H[:], in1=etot[:, :, c:c + 1].broadcast_to([128, U, PW]), op=ALU.mult),
                 reads=["H", "etot"], writes=["H"])
            if mb:
                S.op("dve", lambda e: e.tensor_tensor(out=H[:], in0=pall(pS_, 0), in1=H[:], op=ALU.add), reads=["H", "pS"], writes=["H"])
            else:
                for h in range(NPT):
                    S.op("dve", lambda e: e.tensor_tensor(out=H[:, 2 * h:2 * h + 2, :], in0=pall(pS_, h), in1=H[:, 2 * h:2 * h + 2, :], op=ALU.add),
                         reads=["H", "pS"], writes=["H"])
            S.op("act", lambda e: e.activation(out=Hb[:], in_=H[:], func=AF.Copy), reads=["H"], writes=["Hb"])
            yield "s"
            f = fin[k]
            if mb:
                ysrc = yo[k][:].rearrange("p u w -> p (u w)")
                fk = yok
            else:
                S.op("act", lambda e: e.activation(out=rden[k][:, 0:MLU], in_=yo[k][:, :, 128], func=AF.Abs), reads=yok, writes=[("rden", k)])
                S.op("dve", lambda e: e.tensor_scalar(out=rden[k][:], in0=rden[k][:], scalar1=1.0, scalar2=None, op0=ALU.max), reads=[("rden", k)], writes=[("rden", k)])
                S.op("dve", lambda e: e.reciprocal(out=rden[k][:], in_=rden[k][:]), reads=[("rden", k)], writes=[("rden", k)])
                S.op("pool", lambda e: e.tensor_tensor(out=f[:].rearrange("p (u w) -> p u w", w=128), in0=yo[k][:, :, 0:128],
                                                       in1=rden[k][:, 0:MLU].unsqueeze(2).broadcast_to([128, MLU, 128]), op=ALU.mult),
                     reads=yok + [("rden", k)], writes=[("fin", k)])
                ysrc = f[:]
                fk = [("fin", k)]
            dst_f = (Sc["yf_mb"] if mb else Sc["hf_ml"]) if d == 0 else (Sc["yb_mb"] if mb else Sc["hb_ml"])
            S.dma("pool", dst_f[r0:r0 + 128, :], ysrc, reads=fk, writes=[("ydir", c)])
            yield "chunk"
        yield "done"


def _dla_final(self, l, mixer):
    nc, S, I, Sc = self.nc, self.S, self.I, self.Sc
    mb = mixer == "mb"
    with ExitStack() as st:
        sb = lambda n, s, dt=F32: st.enter_context(_alloc(nc, "sbuf", n, list(s), dt))
        nwb = sb("nwb2", [128, CW])
        S.dma("sp", nwb[:], I["mbnw" if mb else "mlnw"][l].broadcast_to([128, CW]), writes=["nwb2"])
        if mb:
            dskb = sb("dskb", [128, MBU])
            S.dma("sp", dskb[:], I["dsk"][l].broadcast_to([128, MBU]), writes=["dskb"])
        NB_ = 3
        prev = [sb("prev%d" % i, [128, CW]) for i in range(NB_)]
        cur = [sb("cur%d" % i, [128, CW]) for i in range(NB_)]
        Zt = [sb("Zt%d" % i, [128, CW], BF16) for i in range(NB_)]
        Ot = [sb("Ot%d" % i, [128, CW], BF16) for i in range(NB_)]
        ssq = [sb("ssq%d" % i, [128, 4]) for i in range(NB_)]
        junk = sb("junkd", [128, CW], BF16)
        outb = [sb("outb%d" % i, [128, CW], BF16) for i in range(NB_)]
        for c in range(64):
            k = c % NB_
            r0 = 128 * c
            S.dma("sp", prev[k][:], (Sc["yf_mb"] if mb else Sc["hf_ml"])[r0:r0 + 128, :], writes=[("prev", k)])
            S.dma("pool", cur[k][:], (Sc["yb_mb"] if mb else Sc["hb_ml"])[r0:r0 + 128, :], writes=[("cur", k)])
            S.dma("sp", Zt[k][:], (Sc["mbz"] if mb else Sc["mlz"])[r0:r0 + 128, :], writes=[("Zt", k)])
            S.dma("pool", Ot[k][:], (Sc["mbx"] if mb else Sc["mlo"])[r0:r0 + 128, :], writes=[("Ot", k)])
            S.op("pool", lambda e: e.tensor_tensor(out=prev[k][:], in0=prev[k][:], in1=cur[k][:], op=ALU.add), reads=[("prev", k), ("cur", k)], writes=[("prev", k)])
            if mb:
                S.op("pool", lambda e: e.tensor_tensor(out=cur[k][:].rearrange("p (u w) -> p u w", w=64), in0=Ot[k][:].rearrange("p (u w) -> p u w", w=64),
                                                       in1=dskb[:].unsqueeze(2).broadcast_to([128, MBU, 64]), op=ALU.mult),
                     reads=[("Ot", k), ("cur", k), "dskb"], writes=[("cur", k)])
                S.op("dve", lambda e: e.tensor_tensor(out=prev[k][:], in0=prev[k][:], in1=cur[k][:], op=ALU.add), reads=[("prev", k), ("cur", k)], writes=[("prev", k)])
                S.op("dve", lambda e: e.tensor_tensor(out=prev[k][:], in0=prev[k][:], in1=Zt[k][:], op=ALU.mult), reads=[("prev", k), ("Zt", k)], writes=[("prev", k)])
                ngr, gw = MBG, 256
            else:
                S.op("dve", lambda e: e.tensor_tensor(out=prev[k][:], in0=prev[k][:], in1=Ot[k][:], op=ALU.mult), reads=[("prev", k), ("Ot", k)], writes=[("prev", k)])
                ngr, gw = MLU, 128
            for g in range(ngr):
                S.op("act", lambda e: e.activation(out=junk[:, 0:gw], in_=prev[k][:, gw * g:gw * g + gw], func=AF.Square, accum_out=ssq[k][:, g:g + 1]),
                     reads=[("prev", k)], writes=["junkd", ("ssq", k)])
            S.op("dve", lambda e: e.tensor_scalar(out=ssq[k][:, 0:ngr], in0=ssq[k][:, 0:ngr], scalar1=1.0 / gw, scalar2=EPS, op0=ALU.mult, op1=ALU.add),
                 reads=[("ssq", k)], writes=[("ssq", k)])
            S.op("act", lambda e: e.activation(out=ssq[k][:, 0:ngr], in_=ssq[k][:, 0:ngr], func=AF.Sqrt), reads=[("ssq", k)], writes=[("ssq", k)])
            S.op("dve", lambda e: e.reciprocal(out=ssq[k][:, 0:ngr], in_=ssq[k][:, 0:ngr]), reads=[("ssq", k)], writes=[("ssq", k)])
            for g in range(ngr):
                S.op("dve", lambda e: e.scalar_tensor_tensor(out=(outb[k] if mb else prev[k])[:, gw * g:gw * g + gw], in0=prev[k][:, gw * g:gw * g + gw],
                                                             scalar=ssq[k][:, g:g + 1], in1=nwb[:, gw * g:gw * g + gw], op0=ALU.mult, op1=ALU.mult),
                     reads=[("prev", k), ("ssq", k), "nwb2"], writes=[("outb", k) if mb else ("prev", k)])
            if not mb:
                S.op("pool", lambda e: e.tensor_tensor(out=outb[k][:], in0=prev[k][:], in1=Zt[k][:], op=ALU.mult), reads=[("prev", k), ("Zt", k)], writes=[("outb", k)])
            col0 = 0 if mb else CW
            S.dma("pool", Sc["ytm"][r0:r0 + 128, col0:col0 + CW], outb[k][:], reads=[("outb", k)], writes=[("ytm_dla", mixer, c)])


def _phaseDLA(self, l):
    S, nc = self.S, self.nc
    for mixer in ("mb", "ml"):
        _dla_streams(self, l, mixer)
        S.barrier()
        with ExitStack() as st:
            self.Q4s = [st.enter_context(_alloc(nc, "sbuf", "Q4_%d" % d, [32, T], F32)) for d in range(2)]
            gens = [_dla_run(self, l, mixer, d) for d in (0, 1)]
            for g in gens:
                next(g)
            while True:
                rs = [next(g) for g in gens]
                if all(r == "done" for r in rs):
                    break
            S.barrier()
            for g in reversed(gens):
                try:
                    next(g)
                except StopIteration:
                    pass
        _dla_final(self, l, mixer)
        S.barrier()


Prog.phaseDLA = _phaseDLA


N2L = 2 * T
CG = 32


def _prep_hy(inp, j):
    L = DEPTH
    n = np.arange(128)
    ang = 2.0 * np.pi * np.outer(n, n) / 128.0
    Fre, Fim = np.cos(ang), -np.sin(ang)
    dft = np.stack([Fre, Fim, Fre, -Fim], axis=1).astype(np.float32)
    angt = 2.0 * np.pi * np.outer(n, n) / float(N2L)
    twd = np.stack([np.cos(angt), -np.sin(angt)], axis=1).astype(np.float32)
    t = np.arange(T, dtype=np.float32)
    t_norm = t / np.float32(T)
    bands = np.arange(1, 9, dtype=np.float32)
    a = (np.float32(2.0 * math.pi / T)) * t[:, None] * bands[None, :]
    pos = np.concatenate([t_norm[:, None], np.cos(a), np.sin(a)], axis=-1).astype(np.float32)
    hyp = np.zeros((L, 64, 4), np.float32)
    hyp[:, :, 0] = np.asarray(inp["hy_b1"]); hyp[:, :, 1] = np.asarray(inp["hy_freq"]); hyp[:, :, 2] = np.asarray(inp["hy_b2"])
    dec = np.asarray(inp["hy_decay"], np.float32).reshape(L, 4, 512)[:, :, HC * j:HC * j + HC].reshape(L, 4 * NB, 128).transpose(0, 2, 1)
    w3 = np.asarray(inp["hy_w3"], np.float32).reshape(L, 64, 4, 512)[:, :, :, HC * j:HC * j + HC].reshape(L, 64, 4 * HC)
    skip = np.asarray(inp["hy_skip"], np.float32)[:, :, HC * j:HC * j + HC].reshape(L, 2, 1, HC)
    return {"dft": np.ascontiguousarray(dft.reshape(128, 512)), "twd": np.ascontiguousarray(twd.reshape(128, 256)),
            "posT": np.ascontiguousarray(pos.T), "tneg": (-t_norm).reshape(1, T).astype(np.float32),
            "hyp": hyp, "hydec": np.ascontiguousarray(dec),
            "hyw1": np.asarray(inp["hy_w1"], np.float32), "hyw2": np.asarray(inp["hy_w2"], np.float32),
            "hyw3": np.ascontiguousarray(w3), "hyskip": np.ascontiguousarray(skip)}


def _hy_declare(self):
    I, Sc = self.I, self.Sc
    I["dft"] = self.dram_in("dft", [128, 512]); I["twd"] = self.dram_in("twd", [128, 256])
    I["posT"] = self.dram_in("posT", [17, T]); I["tneg"] = self.dram_in("tneg", [1, T])
    I["hyp"] = self.dram_in("hyp", [DEPTH, 64, 4]); I["hydec"] = self.dram_in("hydec", [DEPTH, 128, 4 * NB])
    I["hyw1"] = self.dram_in("hyw1", [DEPTH, 17, 64]); I["hyw2"] = self.dram_in("hyw2", [DEPTH, 64, 64])
    I["hyw3"] = self.dram_in("hyw3", [DEPTH, 64, 4 * HC]); I["hyskip"] = self.dram_in("hyskip", [DEPTH, 2, 1, HC])
    Sc["gflt"] = self.dram_scr("gflt", [2, HC, N2L], BF16, dbg=True)


def _phaseHY(self, l):
    nc, S, I, Sc = self.nc, self.S, self.I, self.Sc
    PI = float(np.pi)
    with ExitStack() as st:
        sb = lambda n, s, d=F32: st.enter_context(_alloc(nc, "sbuf", n, list(s), d))
        pst = lambda n, d=F32: st.enter_context(_alloc(nc, "psum", n, [128, 512], d))
        w1 = sb("hw1", [17, 64]); w2 = sb("hw2", [64, 64]); w3f = sb("hw3f", [64, 4 * HC]); w3b = sb("hw3b", [64, 4 * HC], BF16)
        hp = sb("hhp", [64, 4]); fb = sb("hfb", [64, 2])
        hid = sb("hhid", [64, T], BF16)
        tn = sb("htn", [128, T]); dec = sb("hdec", [128, 4 * NB])
        S.dma("sp", w1[:], I["hyw1"][l], writes=["w1"]); S.dma("sp", w2[:], I["hyw2"][l], writes=["w2"])
        S.dma("sp", w3f[:], I["hyw3"][l], writes=["w3f"]); S.dma("sp", hp[:], I["hyp"][l], writes=["hp"])
        S.dma("sp", tn[:], I["tneg"].broadcast_to([128, T]), writes=["tn"]); S.dma("sp", dec[:], I["hydec"][l], writes=["dec"])
        S.op("pool", lambda e: e.tensor_copy(out=w3b[:], in_=w3f[:]), reads=["w3f"], writes=["w3b"])
        S.op("dve", lambda e: e.tensor_tensor(out=fb[:, 0:1], in0=hp[:, 0:1], in1=hp[:, 1:2], op=ALU.mult), reads=["hp"], writes=["fb"])
        S.op("dve", lambda e: e.tensor_tensor(out=fb[:, 1:2], in0=hp[:, 2:3], in1=hp[:, 1:2], op=ALU.mult), reads=["hp", "fb"], writes=["fb"])
        pt_ = [sb("hpt%d" % i, [17, 512]) for i in range(2)]
        arg = [sb("harg%d" % i, [64, 512]) for i in range(2)]
        ta = [sb("hta%d" % i, [64, 512]) for i in range(2)]
        tb = [sb("htb%d" % i, [64, 512]) for i in range(2)]
        h1 = [sb("hh1%d" % i, [64, 512]) for i in range(2)]
        pz = [pst("hpz%d" % i) for i in range(2)]

        def sin_layer(src_ps, pk, col, out_ap, okey, k):
            a = arg[k]; ak = ("arg", k)
            S.op("dve", lambda e: e.tensor_scalar(out=a[:], in0=src_ps[0:64, :], scalar1=hp[:, 1:2], scalar2=fb[:, col:col + 1], op0=ALU.mult, op1=ALU.add),
                 reads=[pk, "hp", "fb"], writes=[ak])
            S.op("dve", lambda e: e.tensor_scalar(out=ta[k][:], in0=a[:], scalar1=PI, scalar2=-2 * PI, op0=ALU.is_gt, op1=ALU.mult), reads=[ak], writes=[("ta", k)])
            S.op("dve", lambda e: e.tensor_scalar(out=tb[k][:], in0=a[:], scalar1=-PI, scalar2=2 * PI, op0=ALU.is_lt, op1=ALU.mult), reads=[ak], writes=[("tb", k)])
            S.op("pool", lambda e: e.tensor_tensor(out=ta[k][:], in0=ta[k][:], in1=tb[k][:], op=ALU.add), reads=[("ta", k), ("tb", k)], writes=[("ta", k)])
            S.op("pool", lambda e: e.tensor_tensor(out=a[:], in0=a[:], in1=ta[k][:], op=ALU.add), reads=[ak, ("ta", k)], writes=[ak])
            S.op("act", lambda e: e.activation(out=out_ap, in_=a[:], func=AF.Sin), reads=[ak], writes=[okey])

        for c in range(16):
            k = c % 2
            S.dma("sp", pt_[k][:], I["posT"][:, 512 * c:512 * c + 512], writes=[("pt", k)])
            S.op("pe", lambda e: e.matmul(pz[0][0:64, :], lhsT=w1[:], rhs=pt_[k][:], start=True, stop=True), reads=["w1", ("pt", k)], writes=["pz0"])
            sin_layer(pz[0], "pz0", 0, h1[k][:], ("h1", k), k)
            S.op("pe", lambda e: e.matmul(pz[1][0:64, :], lhsT=w2[:], rhs=h1[k][:], start=True, stop=True), reads=["w2", ("h1", k)], writes=["pz1"])
            sin_layer(pz[1], "pz1", 1, hid[:, 512 * c:512 * c + 512], ("hid", c), k)
        hidk = [("hid", c) for c in range(16)]
        gt = [sb("hgt%d" % i, [128, N2L], BF16) for i in range(2)]
        win = [sb("hwin%d" % i, [128, 512]) for i in range(2)]
        pf = [pst("hpf%d" % i) for i in range(2)]
        for i in range(2):
            S.op("pool", lambda e: e.memset(gt[i][:, T:T + 1], 0.0), writes=[("gtz", i)])
        it = 0
        for o in range(2):
            for cb in range(NB):
                g = gt[(o * NB + cb) % 2]; gk = ("gt", (o * NB + cb) % 2)
                gparts = []
                for dr in range(2):
                    col0 = (o * 2 + dr) * HC + 128 * cb
                    di = (o * 2 + dr) * NB + cb
                    for c in range(16):
                        k = it % 2; it += 1
                        S.op("pe", lambda e: e.matmul(pf[k][:], lhsT=w3b[:, col0:col0 + 128], rhs=hid[:, 512 * c:512 * c + 512], start=True, stop=True),
                             reads=["w3b", ("hid", c)], writes=[("pf", k)])
                        S.op("act", lambda e: e.activation(out=win[k][:], in_=tn[:, 512 * c:512 * c + 512], func=AF.Exp, scale=dec[:, di:di + 1]),
                             reads=["tn", "dec"], writes=[("win", k)])
                        pk = (gk, dr, c)
                        gparts.append(pk)
                        if dr == 0:
                            S.op("dve", lambda e: e.tensor_tensor(out=g[:, 512 * c:512 * c + 512], in0=pf[k][:], in1=win[k][:], op=ALU.mult),
                                 reads=[("pf", k), ("win", k)], writes=[pk])
                        else:
                            j0 = 1 if c == 0 else 0
                            lo = N2L - 512 * c - 511
                            hi = N2L - 512 * c - j0 + 1
                            S.op("dve", lambda e: e.tensor_tensor(out=g[:, lo:hi][:, ::-1], in0=pf[k][:, j0:512], in1=win[k][:, j0:512], op=ALU.mult),
                                 reads=[("pf", k), ("win", k)], writes=[pk])
                S.dma("pool", Sc["gflt"][o, 128 * cb:128 * cb + 128, :], g[:], reads=gparts + [("gtz", (o * NB + cb) % 2)], writes=[("gflt", o, cb)])
    S.barrier()
    with ExitStack() as st:
        sb = lambda n, s, d=F32: st.enter_context(_alloc(nc, "sbuf", n, list(s), d))
        dftf = sb("dftf", [128, 512]); dft = sb("dftb", [128, 4, 128], BF16); twd = sb("twd", [128, 2, 128])
        S.dma("sp", dftf[:], I["dft"], writes=["dftf"]); S.dma("sp", twd[:].rearrange("p a k -> p (a k)"), I["twd"], writes=["twd"])
        S.op("dve", lambda e: e.tensor_copy(out=dft[:].rearrange("p a k -> p (a k)"), in_=dftf[:]), reads=["dftf"], writes=["dft"])
        Fre, Fim, nFim = dft[:, 0, :], dft[:, 1, :], dft[:, 3, :]
        Fcat = dft[:, 0:2, :].rearrange("p a k -> p (a k)")
        Fci2 = dft[:, 1:3, :].rearrange("p a k -> p (a k)")
        Fci1 = dft[:, 2:4, :].rearrange("p a k -> p (a k)")
        G = sb("hyG", [128, 2, CG, 2, 128], BF16)
        gblk = [sb("gblk%d" % i, [128, CG, 128], BF16) for i in range(2)]
        sig = {n: sb("sig_" + n, [64, CG, 128], BF16) for n in ("v", "x1", "x2", "g")}
        zblk = sb("zblk", [64, CG, 128], BF16); oblk = sb("oblk", [64, CG, 128], BF16)
        skb = sb("skb", [64, 2, CG])
        NQ = CG // 4
        Ap = [[sb("Ap%d_%d" % (c_, i), [128, 4, 2, 128], BF16) for i in range(2)] for c_ in range(2)]
        Yp = [[sb("Yp%d_%d" % (c_, i), [128, 4, 2, 128], BF16) for i in range(2)] for c_ in range(2)]
        Bp = [[sb("Bp%d_%d" % (c_, i), [128, 4, 2, 128], BF16) for i in range(2)] for c_ in range(2)]
        tt = [[[sb("tt%d_%d_%d" % (c_, i, j), [128, 4, 128]) for j in range(4)] for i in range(2)] for c_ in range(2)]
        ep = [[[sb("ep%d_%d_%d" % (c_, i, j), [64, 4, 128]) for j in range(2)] for i in range(2)] for c_ in range(2)]
        pAall = st.enter_context(_alloc(nc, "psum", "hpA", [128, 2048], F32))
        pBall = st.enter_context(_alloc(nc, "psum", "hpB", [128, 2048], F32))
        Tre = twd[:, 0, :].unsqueeze(1).broadcast_to([128, 4, 128])
        Tim = twd[:, 1, :].unsqueeze(1).broadcast_to([128, 4, 128])

        def chain(cg, sx):
            pA = pAall[:, 1024 * sx:1024 * sx + 1024]
            pB = pBall[:, 1024 * sx:1024 * sx + 1024]
            pA3 = pA.rearrange("p (c k) -> p c k", k=256)
            pBr = pB.rearrange("p (r c k) -> p r c k", r=2, c=4)
            kA, kB = ("pA", sx), ("pB", sx)
            cn = {"c": 0, "n": 0}

            def cmul(out_t, okey, are, aim, akeys, bre, bim, bkeys, conj):
                i = cn["c"] % 2; cn["c"] += 1
                t1, t2, t3, t4 = tt[sx][i]
                ks = [("tt", sx, i, j) for j in range(4)]
                S.op("dve", lambda e: e.tensor_tensor(out=t1[:], in0=are, in1=bre, op=ALU.mult), reads=akeys + bkeys, writes=[ks[0]])
                S.op("dve", lambda e: e.tensor_tensor(out=t2[:], in0=aim, in1=bim, op=ALU.mult), reads=akeys + bkeys, writes=[ks[1]])
                S.op("dve", lambda e: e.tensor_tensor(out=t3[:], in0=are, in1=bim, op=ALU.mult), reads=akeys + bkeys, writes=[ks[2]])
                S.op("dve", lambda e: e.tensor_tensor(out=t4[:], in0=aim, in1=bre, op=ALU.mult), reads=akeys + bkeys, writes=[ks[3]])
                if not conj:
                    S.op("pool", lambda e: e.tensor_tensor(out=out_t[:, :, 0, :], in0=t1[:], in1=t2[:], op=ALU.subtract), reads=ks[0:2], writes=[okey + ("re",)])
                    S.op("pool", lambda e: e.tensor_tensor(out=out_t[:, :, 1, :], in0=t3[:], in1=t4[:], op=ALU.add), reads=ks[2:4], writes=[okey + ("im",)])
                else:
                    S.op("pool", lambda e: e.tensor_tensor(out=out_t[:, :, 0, :], in0=t1[:], in1=t2[:], op=ALU.add), reads=ks[0:2], writes=[okey + ("re",)])
                    S.op("pool", lambda e: e.tensor_tensor(out=out_t[:, :, 1, :], in0=t4[:], in1=t3[:], op=ALU.subtract), reads=ks[2:4], writes=[okey + ("im",)])

            def fwd_quad(src_fn, K, skeys, i):
                for ch in range(4):
                    S.op("pe", lambda e: e.matmul(pA3[:, ch, :], lhsT=src_fn(ch), rhs=Fcat[0:K, :], start=True, stop=True),
                         reads=skeys + ["dft"], writes=[kA])
                yield "s"
                cmul(Ap[sx][i], ("Ap", sx, i), pA3[:, :, 0:128], pA3[:, :, 128:256], [kA], Tre, Tim, ["twd"], False)
                yield "s"
                rre = Ap[sx][i][:, :, 0, :]
                rim = Ap[sx][i][:, :, 1, :]
                kk = [("Ap", sx, i, "re"), ("Ap", sx, i, "im"), "dft"]
                S.op("pe", lambda e: e.matmul(pB[:, 0:512], lhsT=Fre, rhs=rre, start=True, stop=False), reads=kk, writes=[kB])
                S.op("pe", lambda e: e.matmul(pB[:, 0:512], lhsT=nFim, rhs=rim, start=False, stop=True), reads=kk, writes=[kB])
                S.op("pe", lambda e: e.matmul(pB[:, 512:1024], lhsT=Fim, rhs=rre, start=True, stop=False), reads=kk, writes=[kB])
                S.op("pe", lambda e: e.matmul(pB[:, 512:1024], lhsT=Fre, rhs=rim, start=False, stop=True), reads=kk, writes=[kB])
                yield "s"

            for o in range(2):
                for oc8 in range(CG // 8):
                    i = cn["n"] % 2; cn["n"] += 1
                    ch0 = 8 * oc8 + 4 * sx
                    yield from fwd_quad(lambda ch: gblk[o][:, ch0 + ch, :], 128, [("gblk", o)], i)
                    gv = G[:, o, ch0:ch0 + 4, :, :]
                    S.op("act", lambda e: e.activation(out=gv[:, :, 0, :], in_=pBr[:, 0, :, :], func=AF.Copy), reads=[kB], writes=[("G", o, oc8, sx, 0)])
                    S.op("act", lambda e: e.activation(out=gv[:, :, 1, :], in_=pBr[:, 1, :, :], func=AF.Copy), reads=[kB], writes=[("G", o, oc8, sx, 1)])
                    yield "s"
            for o in range(2):
                src_t = sig["v"] if o == 0 else zblk
                for oc8 in range(CG // 8):
                    i = cn["n"] % 2; cn["n"] += 1
                    ch0 = 8 * oc8 + 4 * sx
                    skeys = [("sig", "v")] if o == 0 else [("zblk", oc8, sx)]
                    yield from fwd_quad(lambda ch: src_t[:, ch0 + ch, :], 64, skeys, i)
                    gv = G[:, o, ch0:ch0 + 4, :, :]
                    gk = [("G", o, oc8, sx, 0), ("G", o, oc8, sx, 1)]
                    cmul(Yp[sx][i], ("Yp", sx, i), pBr[:, 0, :, :], pBr[:, 1, :, :], [kB], gv[:, :, 0, :], gv[:, :, 1, :], gk, False)
                    yield "s"
                    for ch in range(4):
                        S.op("pe", lambda e: e.matmul(pA3[:, ch, :], lhsT=Yp[sx][i][:, ch, 0, :], rhs=Fci1, start=True, stop=False),
                             reads=[("Yp", sx, i, "re"), ("Yp", sx, i, "im"), "dft"], writes=[kA])
                        S.op("pe", lambda e: e.matmul(pA3[:, ch, :], lhsT=Yp[sx][i][:, ch, 1, :], rhs=Fci2, start=False, stop=True),
                             reads=[("Yp", sx, i, "re"), ("Yp", sx, i, "im"), "dft"], writes=[kA])
                    yield "s"
                    cmul(Bp[sx][i], ("Bp", sx, i), pA3[:, :, 0:128], pA3[:, :, 128:256], [kA], Tre, Tim, ["twd"], True)
                    yield "s"
                    kk = [("Bp", sx, i, "re"), ("Bp", sx, i, "im"), "dft"]
                    S.op("pe", lambda e: e.matmul(pB[0:64, 0:512], lhsT=Fre[:, 0:64], rhs=Bp[sx][i][:, :, 0, :], start=True, stop=False), reads=kk, writes=[kB])
                    S.op("pe", lambda e: e.matmul(pB[0:64, 0:512], lhsT=Fim[:, 0:64], rhs=Bp[sx][i][:, :, 1, :], start=False, stop=True), reads=kk, writes=[kB])
                    yield "s"
                    e1, e2 = ep[sx][i]
                    yv = pB[0:64, 0:512].rearrange("p (c k) -> p c k", k=128)
                    skv = skb[:, o, ch0:ch0 + 4].unsqueeze(2).broadcast_to([64, 4, 128])
                    uu = src_t[:, ch0:ch0 + 4, :]
                    S.op("pool", lambda e: e.tensor_tensor(out=e1[:], in0=uu, in1=skv, op=ALU.mult), reads=skeys + ["skb"], writes=[("e1", sx, i)])
                    S.op("dve", lambda e: e.scalar_tensor_tensor(out=e2[:], in0=yv, scalar=1.0 / N2L, in1=e1[:], op0=ALU.mult, op1=ALU.add),
                         reads=[kB, ("e1", sx, i)], writes=[("e2", sx, i)])
                    if o == 0:
                        S.op("pool", lambda e: e.tensor_tensor(out=zblk[:, ch0:ch0 + 4, :], in0=e2[:], in1=sig["x1"][:, ch0:ch0 + 4, :], op=ALU.mult),
                             reads=[("e2", sx, i), ("sig", "x1")], writes=[("zblk", oc8, sx)])
                    else:
                        S.op("pool", lambda e: e.tensor_tensor(out=e1[:], in0=e2[:], in1=sig["x2"][:, ch0:ch0 + 4, :], op=ALU.mult),
                             reads=[("e2", sx, i), ("sig", "x2")], writes=[("e1", sx, i)])
                        S.op("pool", lambda e: e.tensor_tensor(out=oblk[:, ch0:ch0 + 4, :], in0=e1[:], in1=sig["g"][:, ch0:ch0 + 4, :], op=ALU.mult),
                             reads=[("e1", sx, i), ("sig", "g")], writes=[("oblk", oc8, sx)])
                    yield "s"

        for cg in range(HC // CG):
            c0 = CG * cg
            for o in range(2):
                S.dma("sp", gblk[o][:], Sc["gflt"][o, c0:c0 + CG, :].rearrange("c (a b) -> a c b", b=128), writes=[("gblk", o)])
            for n_, src, r0 in (("v", "hyuT", 0), ("x1", "hyuT", HC), ("x2", "hyuT", 2 * HC), ("g", "hygT", 0)):
                S.dma("sp" if n_ in ("v", "x1") else "act", sig[n_][:], Sc[src][r0 + c0:r0 + c0 + CG, :].rearrange("c (a b) -> a c b", b=128), writes=[("sig", n_)])
            S.dma("sp", skb[:], I["hyskip"][l, :, :, c0:c0 + CG].rearrange("o x c -> x o c").broadcast_to([64, 2, CG]), writes=["skb"])
            gens = [chain(cg, sx) for sx in range(2)]
            live = list(gens)
            while live:
                for g_ in list(live):
                    try:
                        next(g_)
                    except StopIteration:
                        live.remove(g_)
            S.dma("pool", Sc["yhyT"][c0:c0 + CG, :].rearrange("c (a b) -> a c b", b=128), oblk[:],
                  reads=[("oblk", q, sx) for q in range(CG // 8) for sx in range(2)], writes=[("yhyT", cg)])


Prog.phaseHY = _phaseHY


NCORES = 8


def kernel(**inputs):
    P = Prog(nlayers=DEPTH, debug=False)
    nc = P.build()
    names = set(P.I.keys())
    x = np.asarray(inputs["x"], np.float32)
    Wj = []
    for j in range(SP):
        W = _prep_weights(inputs, j)
        W.update(_prep_mixer(inputs, j))
        Wj.append({k: v for k, v in W.items() if k in names})
    in_maps = []
    for c in range(NCORES):
        b, j = c // SP, c % SP
        m = dict(Wj[j])
        m["x"] = np.ascontiguousarray(x[b])
        m["xh"] = np.ascontiguousarray(x[b][:, OW * j:OW * j + OW])
        in_maps.append(m)
    res = run_bass_kernel_spmd(nc, in_maps, core_ids=list(range(NCORES)))
    out = np.empty((NCORES // SP, T, D), np.float32)
    for c in range(NCORES):
        b, j = c // SP, c % SP
        out[b][:, OW * j:OW * j + OW] = np.asarray(res.results[c]["out"], np.float32)
    return out
```

```python
import math
from contextlib import ExitStack

import numpy as np
import concourse.bass as bass
import concourse.mybir as mybir
from concourse.bass_utils import run_bass_kernel_spmd

F32 = mybir.dt.float32
BF16 = mybir.dt.bfloat16
AF = mybir.ActivationFunctionType
ALU = mybir.AluOpType

T = 8192
D = 1024
NT = 64
DEPTH = 2
EPS = 1e-6
SAME_ENGINE_SYNC = True
SAME_WAW = False
SAME_WAR = False

O_HYU, O_HYG, O_MBX, O_MBB, O_MBC, O_MBZ, O_MBDT = 0, 1536, 2048, 2560, 2816, 3072, 3584
O_MLQ, O_MLK, O_MLV, O_MLO, O_MLZ, O_MLG = 3600, 4112, 4624, 5136, 5648, 6160
O_NAQ, O_NAK, O_NAV, O_NAG = 6176, 6688, 7200, 7712

SP = 2
CW = 512 // SP
HC = CW
MBU = 8 // SP
MBG = 2 // SP
MLU = 4 // SP
NAH = 8 // SP
OW = 1024 // SP
YW = 3 * CW
NSM = 2 * MBU + 4 * MLU
NSP = 32
NB = CW // 128
FM_GROUPS = [
    ("hyu", O_HYU, 3 * NB, "conv"),
    ("hyg", O_HYG, NB, "silu"),
    ("mbx", O_MBX, NB, "convsilu"),
    ("mbB", O_MBB, MBG, "convsilu"),
    ("mbC", O_MBC, MBG, "convsilu"),
    ("mlq", O_MLQ, NB, "convsilu"),
    ("mlk", O_MLK, NB, "convsilu"),
    ("naq", O_NAQ, NB, "qknorm"),
    ("nak", O_NAK, NB, "qknorm"),
]
FM_BLOCKS = []
for _n, _o, _k, _t in FM_GROUPS:
    for _i in range(_k):
        FM_BLOCKS.append((_n, _i, _o + 128 * _i, _t))
NFM = len(FM_BLOCKS)
TM_PARTS = [("mbz", O_MBZ, "silu"), ("mlv", O_MLV, "copy"), ("mlo", O_MLO, "sigmoid"),
            ("mlz", O_MLZ, "silu"), ("nav", O_NAV, "copy"), ("nag", O_NAG, "silu")]
NTM = len(TM_PARTS) * CW // 512


import os
_SKIP = set(os.environ.get("KSKIP", "").split(","))
_UID = [0]


def _alloc(nc, kind, name, shape, dt):
    _UID[0] += 1
    nm = "%s_%d" % (name, _UID[0])
    if kind == "sbuf":
        return nc.sbuf_tensor(nm, shape, dt)
    return nc.psum_tensor(nm, shape, dt)


class Sched:
    def __init__(self, nc, stack, n_dma_sems=14):
        self.nc = nc
        self.eng = {"pe": nc.tensor, "dve": nc.vector, "act": nc.scalar, "pool": nc.gpsimd, "sp": nc.sync}
        self.sem = {e: stack.enter_context(nc.semaphore("c_" + e)) for e in self.eng}
        self.cnt = {e: 0 for e in self.eng}
        self.seen = {e: {} for e in self.eng}
        self.dq = {}
        for q in ("sp", "pool", "act"):
            self.dq[q] = [[stack.enter_context(nc.semaphore("d_%s%d" % (q, i))), 0] for i in range(n_dma_sems)]
        self.dqi = {q: 0 for q in self.dq}
        self.last_w = {}
        self.readers = {}
        self.n_wait = 0
        self.n_ins = 0
        self.ccs = []
        self.cc_toks = []

    def _wait(self, e, tok, raw=True):
        key, sem, val, prod = tok
        if prod == e and (e == "pe" or not SAME_ENGINE_SYNC or not raw):
            return
        if self.seen[e].get(key, 0) >= val:
            return
        self.eng[e].wait_ge(sem, val)
        self.seen[e][key] = val
        self.n_wait += 1

    def _deps(self, e, reads, writes):
        for k in reads:
            t = self.last_w.get(k)
            if t is not None:
                self._wait(e, t, True)
        for k in writes:
            t = self.last_w.get(k)
            if t is not None:
                self._wait(e, t, SAME_WAW)
            for t in self.readers.get(k, ()):
                self._wait(e, t, SAME_WAR)

    def _commit(self, tok, reads, writes):
        for k in writes:
            self.last_w[k] = tok
            self.readers[k] = []
        for k in reads:
            if k in writes:
                continue
            lst = self.readers.setdefault(k, [])
            lst.append(tok)
            if len(lst) > 48:
                best = {}
                for t in lst:
                    if t[0] not in best or best[t[0]][2] < t[2]:
                        best[t[0]] = t
                self.readers[k] = list(best.values())

    def op(self, e, fn, reads=(), writes=()):
        self._deps(e, reads, writes)
        ins = fn(self.eng[e])
        self.cnt[e] += 1
        ins.then_inc(self.sem[e], 1)
        tok = ("c_" + e, self.sem[e], self.cnt[e], e)
        self._commit(tok, reads, writes)
        self.n_ins += 1
        return ins

    def dma(self, q, out, in_, reads=(), writes=(), **kw):
        self._deps(q, reads, writes)
        idx = self.dqi[q]
        slot = self.dq[q][idx]
        self.dqi[q] = (idx + 1) % len(self.dq[q])
        sem, val = slot
        key = "d_%s%d" % (q, idx)
        if val > 0 and self.seen[q].get(key, 0) < val:
            self.eng[q].wait_ge(sem, val)
            self.seen[q][key] = val
        ins = self.eng[q].dma_start(out=out, in_=in_, **kw)
        ins.then_inc(sem, 16)
        slot[1] = val + 16
        tok = (key, sem, val + 16, None)
        self._commit(tok, reads, writes)
        self.n_ins += 1
        return ins

    def collective(self, src, dst, tn, stack, rpc):
        self._deps("pool", [src], [dst])
        groups = [[SP * g + r for r in range(SP)] for g in range(8 // SP)]
        rows = tn[src].shape[0]
        toks = []
        for q in range(rows // rpc):
            sem = stack.enter_context(self.nc.semaphore("cc%d" % len(self.ccs)))
            self.ccs.append(sem)
            ins = self.nc.gpsimd.collective_compute(
                "AllGather", ALU.bypass, replica_groups=groups,
                ins=[tn[src].ap()[rpc * q:rpc * q + rpc, :].opt()],
                outs=[tn[dst].ap()[SP * rpc * q:SP * rpc * q + SP * rpc, :].opt()])
            ins.then_inc(sem)
            tok = ("cc%d" % (len(self.ccs) - 1), sem, 1, None)
            self.eng["pool"].wait_ge(sem, 1)
            self.seen["pool"][tok[0]] = 1
            toks.append(tok)
            self.cc_toks.append(tok)
        self._commit(toks[-1], [src], [dst])

    def barrier(self):
        for e in self.eng:
            for t in self.cc_toks:
                self._wait(e, t)
        for e in self.eng:
            for p in self.eng:
                if p != e and self.cnt[p] > 0:
                    self._wait(e, ("c_" + p, self.sem[p], self.cnt[p], p))
                elif p == e and self.cnt[p] > 0 and e != "pe":
                    self._wait(e, ("c_" + p, self.sem[p], self.cnt[p], None))
            for q in self.dq:
                for i, (sem, val) in enumerate(self.dq[q]):
                    if val > 0:
                        self._wait(e, ("d_%s%d" % (q, i), sem, val, None))
        self.last_w = {}
        self.readers = {}


class _KeyNS:
    def __init__(self, S, prefix):
        self.S = S
        self.p = prefix

    def _k(self, keys):
        return [(self.p, k) for k in keys]

    def op(self, e, fn, reads=(), writes=()):
        return self.S.op(e, fn, self._k(reads), self._k(writes))

    def dma(self, q, out, in_, reads=(), writes=(), **kw):
        return self.S.dma(q, out, in_, self._k(reads), self._k(writes), **kw)


class Prog:
    def __init__(self, nlayers=DEPTH, debug=False, phases=("A", "HY", "NA", "DLA", "Z", "G")):
        self.nlayers = nlayers
        self.debug = debug
        self.phases = phases

    def dram_in(self, name, shape, dt=F32):
        return self.nc.dram_tensor(name, list(shape), dt, kind="ExternalInput").ap()

    def dram_scr(self, name, shape, dt=BF16, dbg=False):
        kind = "ExternalOutput" if (dbg and self.debug) else "Internal"
        if name in getattr(self, "feed", ()):
            kind = "ExternalInput"
        return self.nc.dram_tensor(name, list(shape), dt, kind=kind).ap()

    def build(self):
        nc = bass.Bass("TRN2", target_bir_lowering=False)
        self.nc = nc
        L = DEPTH
        I = self.I = {}
        I["x"] = self.dram_in("x", [T, D])
        I["norm_w"] = self.dram_in("norm_w", [L, 1, D])
        I["wfm"] = self.dram_in("wfm", [L, NFM, 128, 1024])
        I["wsm"] = self.dram_in("wsm", [L, 128, 8 * NSP])
        I["xh"] = self.dram_in("xh", [T, OW])
        I["wtm"] = self.dram_in("wtm", [L, NTM, 128, 4096])
        I["wout"] = self.dram_in("wout", [L, 128, 16 * OW])
        I["convp"] = self.dram_in("convp", [L, NFM, 128, 4])
        I["ident"] = self.dram_in("ident", [128, 128])
        I["blockones"] = self.dram_in("blockones", [128, 128])
        self.declare_mixer_inputs()
        self.out = nc.dram_tensor("out", [T, OW], F32, kind="ExternalOutput").ap()
        Sc = self.Sc = {}
        for n, rows in [("hyuT", 3 * CW), ("hygT", CW), ("mbBT", 128 * MBG), ("mbCT", 128 * MBG), ("mlqT", CW),
                        ("mlkT", CW), ("naqT", CW), ("nakT", CW)]:
            Sc[n] = self.dram_scr(n, [rows, T], dbg=True)
        for n, cols in [("mbx", CW), ("mbB", 128 * MBG), ("mlk", CW), ("mbz", CW), ("mlv", CW), ("mlo", CW),
                        ("mlz", CW), ("nav", CW), ("nag", CW)]:
            Sc[n] = self.dram_scr(n, [T, cols], dbg=True)
        Sc["smallT"] = self.dram_scr("smallT", [NSP, T], F32, dbg=True)
        mbxB = self.dram_scr("mbxB", [T, CW + 128 * MBG])
        Sc["mbx"], Sc["mbB"], Sc["mbxB"] = mbxB[:, 0:CW], mbxB[:, CW:CW + 128 * MBG], mbxB
        mbBCT = self.dram_scr("mbBCT", [2 * 128 * MBG, T])
        Sc["mbBT"], Sc["mbCT"], Sc["mbBCT"] = mbBCT[0:128 * MBG, :], mbBCT[128 * MBG:2 * 128 * MBG, :], mbBCT
        mlkqT = self.dram_scr("mlkqT", [2 * CW, T])
        Sc["mlkT"], Sc["mlqT"], Sc["mlkqT"] = mlkqT[0:CW, :], mlkqT[CW:2 * CW, :], mlkqT
        self.Tn = {}
        for n, shp, dt in [("ytm", [T, YW], BF16), ("yhyT", [HC, T], BF16), ("xres", [T, OW], F32),
                           ("ytm_g", [SP * T, YW], BF16), ("yhy_g", [SP * HC, T], BF16), ("xg", [SP * T, OW], F32)]:
            if self.debug and n in ("ytm", "yhyT"):
                self.Tn[n] = nc.dram_tensor(n, shp, dt, kind="ExternalOutput")
            elif n in getattr(self, "feed", ()):
                self.Tn[n] = nc.dram_tensor(n, shp, dt, kind="ExternalInput")
            else:
                self.Tn[n] = nc.dram_tensor(n, shp, dt)
            Sc[n] = self.Tn[n].ap()
        self.declare_mixer_scratch()

        with ExitStack() as st:
            self.S = Sched(nc, st)
            S = self.S
            self.ident_f = st.enter_context(_alloc(nc, "sbuf", "ident_f", [128, 128], F32))
            self.ident_b = st.enter_context(_alloc(nc, "sbuf", "ident_b", [128, 128], BF16))
            self.bones_b = st.enter_context(_alloc(nc, "sbuf", "bones_b", [128, 128], BF16))
            tmpc = st.enter_context(_alloc(nc, "sbuf", "tmpc", [128, 128], F32))
            S.dma("sp", self.ident_f[:], I["ident"], writes=["ident_f"])
            S.dma("sp", tmpc[:], I["blockones"], writes=["tmpc"])
            S.op("dve", lambda e: e.tensor_copy(out=self.ident_b[:], in_=self.ident_f[:]), reads=["ident_f"], writes=["ident_b"])
            S.op("dve", lambda e: e.tensor_copy(out=self.bones_b[:], in_=tmpc[:]), reads=["tmpc"], writes=["bones_b"])
            S.barrier()
            for l in range(self.nlayers):
                x_dst = self.out if l == self.nlayers - 1 else Sc["xres"]
                x_res = I["xh"] if l == 0 else Sc["xres"]
                self.marks = getattr(self, "marks", [])
                mark = lambda nm: self.marks.append((nm, l, dict(S.cnt)))
                mark("start")
                if "A" in self.phases:
                    self.phaseA(l)
                    S.barrier()
                    mark("A")
                if "HY" in self.phases:
                    self.phaseHY(l)
                    S.barrier()
                    mark("HY")
                if "NA" in self.phases:
                    self.phaseNA(l)
                    S.barrier()
                    mark("NA")
                if "DLA" in self.phases:
                    self.phaseDLA(l)
                    S.barrier()
                    mark("DLA")
                if "Z" in self.phases:
                    if SP > 1 and "G" in self.phases:
                        S.collective("ytm", "ytm_g", self.Tn, st, 1024)
                        S.collective("yhyT", "yhy_g", self.Tn, st, 64)
                        S.barrier()
                    self.phaseZ(l, x_res, x_dst)
                    S.barrier()
                    if SP > 1 and l < self.nlayers - 1 and "G" in self.phases:
                        S.collective("xres", "xg", self.Tn, st, 1024)
                        S.barrier()
                    mark("Z")
            S.barrier()
        return nc

    def declare_mixer_inputs(self):
        _na_declare_inputs(self)

    def declare_mixer_scratch(self):
        _dla_declare(self)
        _hy_declare(self)

    def phaseA(self, l):
        nc, S, I, Sc = self.nc, self.S, self.I, self.Sc
        with ExitStack() as st:
            sb = lambda n, s, d=F32: st.enter_context(_alloc(nc, "sbuf", n, list(s), d))
            ps = lambda n, s, d=F32: st.enter_context(_alloc(nc, "psum", n, list(s), d))
            hT = sb("hT", [128, 8, T], BF16)
            with ExitStack() as st0:
                sb0 = lambda n, s, d=F32: st0.enter_context(_alloc(nc, "sbuf", n, list(s), d))
                nwb = sb0("nwb", [128, D])
                S.dma("sp", nwb[:], I["norm_w"][l].broadcast_to([128, D]), writes=["nwb"])
                xt = [sb0("xt%d" % i, [128, D]) for i in range(2)]
                junk = sb0("junk", [128, D], BF16)
                ss = [sb0("ss%d" % i, [128, 1]) for i in range(2)]
                xn = [sb0("xn%d" % i, [128, D], BF16) for i in range(2)]
                pt = [st0.enter_context(_alloc(nc, "psum", "pt%d" % i, [128, 512], BF16)) for i in range(2)]
                for i in range(NT):
                    b = i % 2
                    if l == 0 or SP == 1:
                        S.dma("sp", xt[b][:], I["x"][128 * i:128 * i + 128, :], writes=[("xt", b)])
                    else:
                        S.dma("sp", xt[b][:].rearrange("p (r c) -> p r c", r=SP),
                              Sc["xg"].rearrange("(q r t) c -> q t r c", q=8, r=SP)[i // 8][128 * (i % 8):128 * (i % 8) + 128, :, :], writes=[("xt", b)])
                    S.op("act", lambda e: e.activation(out=junk[:], in_=xt[b][:], func=AF.Square, accum_out=ss[b][:]),
                         reads=[("xt", b)], writes=["junk", ("ss", b)])
                    S.op("dve", lambda e: e.tensor_scalar(out=ss[b][:], in0=ss[b][:], scalar1=1.0 / D, scalar2=EPS,
                                                          op0=ALU.mult, op1=ALU.add), reads=[("ss", b)], writes=[("ss", b)])
                    S.op("act", lambda e: e.activation(out=ss[b][:], in_=ss[b][:], func=AF.Sqrt), reads=[("ss", b)], writes=[("ss", b)])
                    S.op("dve", lambda e: e.reciprocal(out=ss[b][:], in_=ss[b][:]), reads=[("ss", b)], writes=[("ss", b)])
                    S.op("dve", lambda e: e.scalar_tensor_tensor(out=xn[b][:], in0=xt[b][:], scalar=ss[b][:], in1=nwb[:],
                                                                 op0=ALU.mult, op1=ALU.mult),
                         reads=[("xt", b), ("ss", b), "nwb"], writes=[("xn", b)])
                    for h in range(2):
                        for k in range(4):
                            kc = 4 * h + k
                            S.op("pe", lambda e: e.transpose(pt[h][:, 128 * k:128 * k + 128], xn[b][:, 128 * kc:128 * kc + 128], self.ident_b[:]),
                                 reads=[("xn", b)], writes=[("pt", h)])
                        dst = hT[:, 4 * h:4 * h + 4, 128 * i:128 * i + 128]
                        src = pt[h][:].rearrange("p (k t) -> p k t", t=128)
                        if h == 0:
                            S.op("act", lambda e: e.activation(out=dst, in_=src, func=AF.Copy), reads=[("pt", h)], writes=[("hT", i)])
                        else:
                            S.op("dve", lambda e: e.tensor_copy(out=dst, in_=src), reads=[("pt", h)], writes=[("hT", i)])
            S.barrier()
            hkeys = [("hT", i) for i in range(NT)]
            with ExitStack() as st1:
                sb1 = lambda n, s, d=F32: st1.enter_context(_alloc(nc, "sbuf", n, list(s), d))
                wst = [sb1("wst%d" % i, [128, 1024]) for i in range(2)]
                wb = [sb1("wb%d" % i, [128, 8, 128], BF16) for i in range(2)]
                row = [sb1("row%d" % i, [128, T + 2], BF16) for i in range(2)]
                acc = [sb1("acc%d" % i, [128, 1024]) for i in range(2)]
                ob = [sb1("ob%d" % i, [128, 1024], BF16) for i in range(2)]
                tms = [sb1("tms%d" % i, [128, 8, 128], BF16) for i in range(2)]
                sq = [sb1("sq%d" % i, [128, 512], BF16) for i in range(2)]
                rt = [sb1("rt%d" % i, [128, 512]) for i in range(2)]
                cp = [sb1("cp%d" % i, [128, 4]) for i in range(2)]
                pm = [st1.enter_context(_alloc(nc, "psum", "pm%d" % i, [128, 512], F32)) for i in range(3)]
                ptr = [st1.enter_context(_alloc(nc, "psum", "ptr%d" % i, [128, 512], BF16)) for i in range(4)]
                pss1 = st1.enter_context(_alloc(nc, "psum", "pss", [128, 512], F32))
                pss = [pss1, pss1]
                for b in range(2):
                    S.op("pool", lambda e: e.memset(row[b][:, 0:1], 0.0), writes=[("rowh", b)])
                    S.op("pool", lambda e: e.memset(row[b][:, T + 1:T + 2], 0.0), writes=[("rowh", b)])
                cnt = {"ev": 0}

                def gen_mm(bi):
                    gname, gi, col0, ptype = FM_BLOCKS[bi]
                    b = bi % 2
                    S.dma("sp", wst[b][:], I["wfm"][l, bi], writes=[("wst", b)])
                    S.op("pool", lambda e: e.tensor_copy(out=wb[b][:].rearrange("p k c -> p (k c)"), in_=wst[b][:]),
                         reads=[("wst", b)], writes=[("wtile", b)])
                    if ptype in ("conv", "convsilu", "qknorm"):
                        S.dma("sp", cp[b][:], I["convp"][l, bi], writes=[("cp", b)])
                    rw = row[b]
                    for i in range(16):
                        p = pm[i % 3]
                        for kc in range(8):
                            S.op("pe", lambda e: e.matmul(p[:, :], lhsT=wb[b][:, kc, :], rhs=hT[:, kc, 512 * i:512 * i + 512],
                                                          start=(kc == 0), stop=(kc == 7)),
                                 reads=[("wtile", b)] + hkeys[4 * i:4 * i + 4], writes=[("pm", i % 3)])
                        dst = rw[:, 1 + 512 * i:1 + 512 * i + 512]
                        if cnt["ev"] % 2 == 0:
                            S.op("act", lambda e: e.activation(out=dst, in_=p[:, :], func=AF.Copy), reads=[("pm", i % 3)], writes=[("row", b, i)])
                        else:
                            S.op("dve", lambda e: e.tensor_copy(out=dst, in_=p[:, :]), reads=[("pm", i % 3)], writes=[("row", b, i)])
                        cnt["ev"] += 1
                        yield "mm"

                def gen_post(bi, chn):
                    gname, gi, col0, ptype = FM_BLOCKS[bi]
                    b = bi % 2
                    rw = row[b]
                    r0 = 128 * gi
                    a, ak = acc[chn], ("acc", chn)
                    o, okey = ob[chn], ("ob", chn)
                    for j in range(chn, 8, 2):
                        c0 = 1024 * j
                        if ptype in ("conv", "convsilu", "silu"):
                            rk = [("row", b, i) for i in range(max(0, 2 * j - 1), min(16, 2 * j + 3))] + [("rowh", b)]
                            if ptype == "silu":
                                S.op("act", lambda e: e.activation(out=o[:], in_=rw[:, 1 + c0:1 + c0 + 1024], func=AF.Silu), reads=rk, writes=[okey])
                                yield "p"
                            else:
                                S.op("act", lambda e: e.activation(out=a[:], in_=rw[:, 1 + c0:1 + c0 + 1024], func=AF.Identity,
                                                                   scale=cp[b][:, 1:2], bias=cp[b][:, 3:4]), reads=rk + [("cp", b)], writes=[ak])
                                yield "p"
                                S.op("dve", lambda e: e.scalar_tensor_tensor(out=a[:], in0=rw[:, c0:c0 + 1024], scalar=cp[b][:, 0:1], in1=a[:],
                                                                             op0=ALU.mult, op1=ALU.add), reads=rk + [("cp", b), ak], writes=[ak])
                                S.op("dve", lambda e: e.scalar_tensor_tensor(out=a[:], in0=rw[:, 2 + c0:2 + c0 + 1024], scalar=cp[b][:, 2:3], in1=a[:],
                                                                             op0=ALU.mult, op1=ALU.add), reads=rk + [("cp", b), ak], writes=[ak])
                                yield "p"
                                if ptype == "convsilu":
                                    S.op("act", lambda e: e.activation(out=o[:], in_=a[:], func=AF.Silu), reads=[ak], writes=[okey])
                                else:
                                    S.op("pool", lambda e: e.tensor_copy(out=o[:], in_=a[:]), reads=[ak], writes=[okey])
                                yield "p"
                            fm_dst = {"hyu": "hyuT", "hyg": "hygT", "mbB": "mbBT", "mbC": "mbCT", "mlq": "mlqT", "mlk": "mlkT"}.get(gname)
                            if fm_dst is not None:
                                S.dma("sp", Sc[fm_dst][r0:r0 + 128, c0:c0 + 1024], o[:], reads=[okey], writes=[(fm_dst, gi, j)])
                            tm_dst = {"mbx": "mbx", "mbB": "mbB", "mlk": "mlk"}.get(gname)
                            if tm_dst is not None:
                                for h in range(2):
                                    pt_ = ptr[2 * chn + h]
                                    pk_ = ("ptr", chn, h)
                                    for k in range(4):
                                        s_ = 4 * h + k
                                        S.op("pe", lambda e: e.transpose(pt_[:, 128 * k:128 * k + 128], o[:, 128 * s_:128 * s_ + 128], self.ident_b[:]),
                                             reads=[okey], writes=[pk_])
                                    yield "p"
                                    src = pt_[:].rearrange("p (k c) -> p k c", c=128)
                                    if h == 0:
                                        S.op("act", lambda e: e.activation(out=tms[chn][:, 0:4, :], in_=src, func=AF.Copy), reads=[pk_], writes=[("tms", chn, 0)])
                                    else:
                                        S.op("dve", lambda e: e.tensor_copy(out=tms[chn][:, 4:8, :], in_=src), reads=[pk_], writes=[("tms", chn, 1)])
                                d = Sc[tm_dst][c0:c0 + 1024, r0:r0 + 128].rearrange("(s p) c -> p s c", p=128)
                                S.dma("sp", d, tms[chn][:], reads=[("tms", chn, 0), ("tms", chn, 1)], writes=[(tm_dst, gi, j)])
                            yield "p"
                        else:
                            fm_dst = {"naq": "naqT", "nak": "nakT"}[gname]
                            q = chn
                            for hh in range(2):
                                cq = c0 + 512 * hh
                                rk = [("row", b, 2 * j + hh)]
                                S.op("act", lambda e: e.activation(out=sq[q][:], in_=rw[:, 1 + cq:1 + cq + 512], func=AF.Square), reads=rk, writes=[("sq", q)])
                                yield "p"
                                S.op("pe", lambda e: e.matmul(pss[q][:], lhsT=self.bones_b[:], rhs=sq[q][:], start=True, stop=True),
                                     reads=[("sq", q)], writes=["pss"])
                                S.op("dve", lambda e: e.tensor_scalar(out=rt[q][:], in0=pss[q][:], scalar1=1.0 / 64, scalar2=EPS, op0=ALU.mult, op1=ALU.add),
                                     reads=["pss"], writes=[("rt", q)])
                                yield "p"
                                S.op("act", lambda e: e.activation(out=rt[q][:], in_=rt[q][:], func=AF.Sqrt), reads=[("rt", q)], writes=[("rt", q)])
                                yield "p"
                                S.op("dve", lambda e: e.reciprocal(out=rt[q][:], in_=rt[q][:]), reads=[("rt", q)], writes=[("rt", q)])
                                S.op("dve", lambda e: e.scalar_tensor_tensor(out=o[:, 512 * hh:512 * hh + 512], in0=rw[:, 1 + cq:1 + cq + 512], scalar=cp[b][:, 0:1],
                                                                             in1=rt[q][:], op0=ALU.mult, op1=ALU.mult),
                                     reads=rk + [("rt", q), ("cp", b)], writes=[okey])
                                yield "p"
                            S.dma("sp", Sc[fm_dst][r0:r0 + 128, c0:c0 + 1024], o[:], reads=[okey], writes=[(fm_dst, gi, j)])
                            yield "p"

                nblk = 0 if "fm" in _SKIP else NFM
                for bi in range(nblk + 1):
                    live = []
                    if bi < nblk:
                        live.append(gen_mm(bi))
                    if bi >= 1:
                        live += [gen_post(bi - 1, 0), gen_post(bi - 1, 1)]
                    while live:
                        for g_ in list(live):
                            try:
                                next(g_)
                            except StopIteration:
                                live.remove(g_)
            S.barrier()
            with ExitStack() as st3:
                sb3 = lambda n, s, d=F32: st3.enter_context(_alloc(nc, "sbuf", n, list(s), d))
                smallsb = sb3("smallsb", [NSP, T])
                wsst = sb3("wsst", [128, 8 * NSP])
                wsb = sb3("wsb", [128, 8, NSP], BF16)
                pm3 = [st3.enter_context(_alloc(nc, "psum", "pm3%d" % i, [128, 512], F32)) for i in range(4)]
                S.dma("sp", wsst[:], I["wsm"][l], writes=["wsst"])
                S.op("pool", lambda e: e.tensor_copy(out=wsb[:].rearrange("p k c -> p (k c)"), in_=wsst[:]), reads=["wsst"], writes=["wsb"])
                for i in range(16 if "small" not in _SKIP else 0):
                    p = pm3[i % 4]
                    for kc in range(8):
                        S.op("pe", lambda e: e.matmul(p[0:NSP, :], lhsT=wsb[:, kc, :], rhs=hT[:, kc, 512 * i:512 * i + 512], start=(kc == 0), stop=(kc == 7)),
                             reads=["wsb"] + hkeys[4 * i:4 * i + 4], writes=[("pm3", i % 4)])
                    S.op("act", lambda e: e.activation(out=smallsb[:, 512 * i:512 * i + 512], in_=p[0:NSP, :], func=AF.Copy), reads=[("pm3", i % 4)], writes=[("smallsb", i)])
                S.dma("pool", Sc["smallT"], smallsb[:], reads=[("smallsb", i) for i in range(16)], writes=["smallT"])
            S.barrier()
            with ExitStack() as st2:
                sb2 = lambda n, s, d=F32: st2.enter_context(_alloc(nc, "sbuf", n, list(s), d))
                wst2 = sb2("wst2", [128, 4096])
                wtb = [sb2("wtb%d" % i, [128, 8, 512], BF16) for i in range(2)]
                ot = [sb2("ot%d" % i, [128, 512], BF16) for i in range(4)]
                pm2 = [st2.enter_context(_alloc(nc, "psum", "pm2%d" % i, [128, 512], F32)) for i in range(4)]
                ppg = 512 // CW
                for g in range(NTM if "tm" not in _SKIP else 0):
                    b = g % 2
                    parts = TM_PARTS[ppg * g:ppg * g + ppg]
                    S.dma("sp", wst2[:], I["wtm"][l, g], writes=["wst2"])
                    S.op("pool", lambda e: e.tensor_copy(out=wtb[b][:].rearrange("p k c -> p (k c)"), in_=wst2[:]), reads=["wst2"], writes=[("wtb", b)])
                    for i in range(NT):
                        p = pm2[i % 4]
                        for kc in range(8):
                            S.op("pe", lambda e: e.matmul(p[:], lhsT=hT[:, kc, 128 * i:128 * i + 128], rhs=wtb[b][:, kc, :], start=(kc == 0), stop=(kc == 7)),
                                 reads=[("wtb", b), ("hT", i)], writes=[("pm2", i % 4)])
                        o = ot[i % 4]
                        for pi, (gname, col0, act) in enumerate(parts):
                            func = {"silu": AF.Silu, "copy": AF.Copy, "sigmoid": AF.Sigmoid}[act]
                            sl = slice(CW * pi, CW * pi + CW)
                            S.op("act", lambda e: e.activation(out=o[:, sl], in_=p[:, sl], func=func), reads=[("pm2", i % 4)], writes=[("ot", i % 4, pi)])
                            S.dma("pool" if (i + pi) % 2 else "sp", Sc[gname][128 * i:128 * i + 128, :], o[:, sl], reads=[("ot", i % 4, pi)], writes=[(gname, i)])

    def phaseZ(self, l, x_res, x_dst):
        nc, S, I, Sc = self.nc, self.S, self.I, self.Sc
        gathered = SP > 1 and "G" in self.phases
        with ExitStack() as st:
            sb = lambda n, s, d=F32: st.enter_context(_alloc(nc, "sbuf", n, list(s), d))
            wo = sb("wo", [128, 16, OW], BF16)
            wos = [sb("wos%d" % i, [128, 2048]) for i in range(2)]
            nck = 16 * OW // 2048
            kpc = 2048 // OW
            for c in range(nck):
                S.dma("sp", wos[c % 2][:], I["wout"][l, :, 2048 * c:2048 * c + 2048], writes=[("wos", c % 2)])
                S.op("pool", lambda e: e.tensor_copy(out=wo[:, kpc * c:kpc * c + kpc, :].rearrange("p k c -> p (k c)"), in_=wos[c % 2][:]),
                     reads=[("wos", c % 2)], writes=["wo"])
            yt = [sb("yt%d" % i, [128, SP, YW], BF16) for i in range(2)]
            yT = [sb("yT%d" % i, [128, 16, 128], BF16) for i in range(2)]
            xt = [sb("xz%d" % i, [128, OW]) for i in range(2)]
            oz = [sb("oz%d" % i, [128, OW]) for i in range(2)]
            ptz = [st.enter_context(_alloc(nc, "psum", "ptz%d" % i, [128, 512], BF16)) for i in range(3)]
            pz = [st.enter_context(_alloc(nc, "psum", "pz%d" % i, [128, 512], F32)) for i in range(4)]
            ysrc = Sc["ytm_g"] if gathered else Sc["ytm"]
            hsrc = Sc["yhy_g"] if gathered else Sc["yhyT"]
            nr = SP if gathered else 1
            cpr = YW // 128
            for i in range(NT):
                b = i % 2
                S.dma("sp", yt[b][:, 0:nr, :], ysrc.rearrange("(q r t) c -> q t r c", q=8, r=nr)[i // 8][128 * (i % 8):128 * (i % 8) + 128, :, :], writes=[("yt", b)])
                S.dma("sp", xt[b][:], x_res[128 * i:128 * i + 128, :], writes=[("xz", b)])
                S.dma("pool", yT[b][:, 0:4 * nr // SP, :], hsrc[:, 128 * i:128 * i + 128].rearrange("(k p) t -> p k t", p=128), writes=[("yTh", b)])
                for h in range(3):
                    for k in range(4):
                        q = 4 * h + k
                        r, cc = q // cpr, q % cpr
                        S.op("pe", lambda e: e.transpose(ptz[h][:, 128 * k:128 * k + 128], yt[b][:, r, 128 * cc:128 * cc + 128], self.ident_b[:]),
                             reads=[("yt", b)], writes=[("ptz", h)])
                    src = ptz[h][:].rearrange("p (k c) -> p k c", c=128)
                    dst = yT[b][:, 4 + 4 * h:8 + 4 * h, :]
                    if h == 1:
                        S.op("dve", lambda e: e.tensor_copy(out=dst, in_=src), reads=[("ptz", h)], writes=[("yTt", b, h)])
                    else:
                        S.op("act", lambda e: e.activation(out=dst, in_=src, func=AF.Copy), reads=[("ptz", h)], writes=[("yTt", b, h)])
                for half in range(OW // 512):
                    p = pz[(2 * i + half) % 4]
                    pk = ("pz", (2 * i + half) % 4)
                    for kc in range(16):
                        S.op("pe", lambda e: e.matmul(p[:], lhsT=yT[b][:, kc, :], rhs=wo[:, kc, 512 * half:512 * half + 512], start=(kc == 0), stop=(kc == 15)),
                             reads=["wo", ("yTh", b)] + [("yTt", b, h) for h in range(3)], writes=[pk])
                    S.op("dve", lambda e: e.tensor_tensor(out=oz[b][:, 512 * half:512 * half + 512], in0=p[:], in1=xt[b][:, 512 * half:512 * half + 512], op=ALU.add),
                         reads=[pk, ("xz", b)], writes=[("oz", b, half)])
                S.dma("pool", x_dst[128 * i:128 * i + 128, :], oz[b][:], reads=[("oz", b, h2) for h2 in range(OW // 512)], writes=[("xdst", i)])


def _fm_col0(gname, gi, j):
    if gname == "hyu":
        part, sub = gi // NB, gi % NB
        return O_HYU + 512 * part + CW * j + 128 * sub
    base = {"hyg": O_HYG, "mbx": O_MBX, "mlq": O_MLQ, "mlk": O_MLK, "naq": O_NAQ, "nak": O_NAK}.get(gname)
    if base is not None:
        return base + CW * j + 128 * gi
    return {"mbB": O_MBB, "mbC": O_MBC}[gname] + 128 * (MBG * j + gi)


def _small_cols(j):
    cols = []
    for d in range(2):
        cols += [O_MBDT + 8 * d + MBU * j + u for u in range(MBU)]
    for d in range(2):
        for g in range(2):
            cols += [O_MLG + 8 * d + 4 * g + MLU * j + u for u in range(MLU)]
    return cols


def _prep_weights(inp, j):
    L = DEPTH
    w_in = np.asarray(inp["w_in"], np.float32)
    w_out = np.asarray(inp["w_out"], np.float32)
    W = {}
    wfm = np.empty((L, NFM, 128, 1024), np.float32)
    convp = np.zeros((L, NFM, 128, 4), np.float32)
    for bi, (gname, gi, _c, ptype) in enumerate(FM_BLOCKS):
        col0 = _fm_col0(gname, gi, j)
        blk = w_in[:, :, col0:col0 + 128]
        wfm[:, bi] = blk.reshape(L, 8, 128, 128).transpose(0, 2, 1, 3).reshape(L, 128, 1024)
        if ptype in ("conv", "convsilu"):
            if gname == "hyu":
                cw, cb, c0 = inp["hy_conv_w"], inp["hy_conv_b"], col0 - O_HYU
            elif gname in ("mbx", "mbB", "mbC"):
                cw, cb, c0 = inp["mb_conv_w"], inp["mb_conv_b"], col0 - O_MBX
            else:
                cw, cb, c0 = inp["ml_conv_w"], inp["ml_conv_b"], col0 - O_MLQ
            convp[:, bi, :, 0:3] = np.asarray(cw)[:, :, c0:c0 + 128].transpose(0, 2, 1)
            convp[:, bi, :, 3] = np.asarray(cb)[:, c0:c0 + 128]
        elif ptype == "qknorm":
            nw = np.asarray(inp["na_qnorm_w"] if gname == "naq" else inp["na_knorm_w"])
            convp[:, bi, :, 0] = np.tile(nw, (1, 2))
    W["wfm"] = wfm
    W["convp"] = convp
    scols = _small_cols(j)
    wsm = np.zeros((L, 8, 128, NSP), np.float32)
    wsm[:, :, :, :NSM] = w_in[:, :, scols].reshape(L, 8, 128, NSM)
    W["wsm"] = np.ascontiguousarray(wsm.transpose(0, 2, 1, 3).reshape(L, 128, 8 * NSP))
    wtm = np.empty((L, NTM, 128, 4096), np.float32)
    ppg = 512 // CW
    for g in range(NTM):
        cols = []
        for (gname, col0, act) in TM_PARTS[ppg * g:ppg * g + ppg]:
            cols += list(range(col0 + CW * j, col0 + CW * j + CW))
        wtm[:, g] = w_in[:, :, cols].reshape(L, 8, 128, 512).transpose(0, 2, 1, 3).reshape(L, 128, 4096)
    W["wtm"] = wtm
    rows = []
    for q in range(HC // 64):
        for r in range(SP):
            rows += list(range(HC * r + 64 * q, HC * r + 64 * q + 64))
    for r in range(SP):
        for base in (512, 1024, 1536):
            rows += list(range(base + CW * r, base + CW * r + CW))
    wo = w_out[:, rows, OW * j:OW * j + OW]
    W["wout"] = np.ascontiguousarray(wo.reshape(L, 16, 128, OW).transpose(0, 2, 1, 3).reshape(L, 128, 16 * OW))
    W["norm_w"] = np.asarray(inp["norm_w"], np.float32).reshape(L, 1, D)
    W["ident"] = np.eye(128, dtype=np.float32)
    bo = np.zeros((128, 128), np.float32)
    bo[:64, :64] = 1.0
    bo[64:, 64:] = 1.0
    W["blockones"] = bo
    return W


def _prep_na(inp, j):
    jc = j
    L = DEPTH
    rpb = np.asarray(inp["na_rpb"], np.float32)
    kk = np.arange(128)
    il, kc = kk // 64, kk % 64
    w = np.arange(64)
    cs = np.clip(w - 8, 0, 48)
    valid = (kc[:, None] >= cs[None, :]) & (kc[:, None] < cs[None, :] + 16)
    coff = np.clip(kc[:, None] - w[None, :] + 15, 0, 30)
    bias = np.zeros((L, 8, 128, 8, 4, 64), np.float32)
    for v in range(8):
        for j in range(4):
            i = 2 * j + il
            roff = v + i
            bias[:, :, :, v, j, :] = rpb[:, :, roff[:, None], coff]
    mask = np.broadcast_to(valid[:, None, :], (128, 4, 64)).astype(np.float32).reshape(128, 256)
    return {"na_bias": np.ascontiguousarray(bias.reshape(L, 8, 128, 2048)[:, NAH * jc:NAH * jc + NAH]), "na_mask": np.ascontiguousarray(mask)}


def _na_declare_inputs(self):
    self.I["na_bias"] = self.dram_in("na_bias", [DEPTH, NAH, 128, 2048])
    self.I["na_mask"] = self.dram_in("na_mask", [128, 256])


def _phaseNA(self, l):
    nc, S, I, Sc = self.nc, self.S, self.I, self.Sc
    NH = NAH
    with ExitStack() as st:
        sb = lambda n, s, d=F32: st.enter_context(_alloc(nc, "sbuf", n, list(s), d))
        KT = [sb("naKT%d" % i, [64, T], BF16) for i in range(2)]
        QT = [sb("naQT%d" % i, [64, T], BF16) for i in range(2)]
        Ve = [sb("naVe%d" % i, [128, 64, 65], BF16) for i in range(2)]
        Vo = [sb("naVo%d" % i, [128, 63, 65], BF16) for i in range(2)]
        EBr = sb("naEBr", [128, 2048])
        EBM = [sb("naEBM%d" % i, [128, 8, 256]) for i in range(2)]
        msk = sb("namask", [128, 256])
        G = [sb("naG%d" % i, [64, 128, 64], BF16) for i in range(2)]
        O = sb("naO", [64, 128, 64], BF16)
        Ob = sb("naOb", [64, 128, 64], BF16)
        E = [sb("naE%d" % i, [128, 256]) for i in range(4)]
        Pb = [sb("naP%d" % i, [128, 256], BF16) for i in range(4)]
        rec = [sb("narec%d" % i, [64, 1]) for i in range(4)]
        pS = [st.enter_context(_alloc(nc, "psum", "napS%d" % i, [128, 512], F32)) for i in range(4)]
        pO = [st.enter_context(_alloc(nc, "psum", "napO%d" % i, [128, 512], F32)) for i in range(4)]
        S.dma("sp", msk[:], I["na_mask"], writes=["namask"])
        for b in range(2):
            S.op("pool", lambda e: e.memset(Ve[b][:, :, 64:65], 1.0), writes=[("Veo", b)])
            S.op("pool", lambda e: e.memset(Vo[b][:, :, 64:65], 1.0), writes=[("Voo", b)])
        for h in range(NH):
            b = h % 2
            S.dma("sp", KT[b][:], Sc["nakT"][64 * h:64 * h + 64, :], writes=[("KT", b)])
            S.dma("sp", QT[b][:], Sc["naqT"][64 * h:64 * h + 64, :], writes=[("QT", b)])
            S.dma("pool", Ve[b][:, :, 0:64], Sc["nav"][:, 64 * h:64 * h + 64].rearrange("(i p) d -> p i d", p=128), writes=[("Ve", b)])
            S.dma("pool", Vo[b][:, :, 0:64], Sc["nav"][64:T - 64, 64 * h:64 * h + 64].rearrange("(i p) d -> p i d", p=128), writes=[("Vo", b)])
            S.dma("sp", G[b][:], Sc["nag"][:, 64 * h:64 * h + 64].rearrange("(r w) d -> w r d", w=64), writes=[("G", b)])
            S.dma("sp", EBr[:], I["na_bias"][l, h], writes=["EBr"])
            S.op("act", lambda e: e.activation(out=EBr[:], in_=EBr[:], func=AF.Exp), reads=["EBr"], writes=["EBr"])
            S.op("dve", lambda e: e.tensor_tensor(out=EBM[b][:], in0=EBr[:].rearrange("p (v c) -> p v c", c=256),
                                                  in1=msk[:].unsqueeze(1).broadcast_to([128, 8, 256]), op=ALU.mult),
                 reads=["EBr", "namask"], writes=[("EBM", b)])
            NR = 4
            for r0_ in range(0, 128, NR):
                rows = list(range(r0_, r0_ + NR))
                rsv = {r: min(max(r - 4, 0), 120) for r in rows}
                for r in rows:
                    rs, rb = rsv[r], r % NR
                    for j in range(4):
                        S.op("pe", lambda e: e.matmul(pS[rb][:, 64 * j:64 * j + 64], lhsT=KT[b][:, 64 * rs + 128 * j:64 * rs + 128 * j + 128],
                                                      rhs=QT[b][:, 64 * r:64 * r + 64], start=True, stop=True),
                             reads=[("KT", b), ("QT", b)], writes=[("pS", rb)])
                for r in rows:
                    rb = r % NR
                    S.op("act", lambda e: e.activation(out=E[rb][:], in_=pS[rb][:, 0:256], func=AF.Exp, scale=0.125), reads=[("pS", rb)], writes=[("E", rb)])
                for r in rows:
                    rb = r % NR
                    v = rsv[r] - r + 7
                    S.op("dve", lambda e: e.tensor_tensor(out=Pb[rb][:], in0=E[rb][:], in1=EBM[b][:, v, :], op=ALU.mult),
                         reads=[("E", rb), ("EBM", b)], writes=[("P", rb)])
                for r in rows:
                    rs, rb = rsv[r], r % NR
                    for j in range(4):
                        if rs % 2 == 0:
                            vt = Ve[b][:, rs // 2 + j, :]
                        else:
                            vt = Vo[b][:, (rs - 1) // 2 + j, :]
                        S.op("pe", lambda e: e.matmul(pO[rb][0:64, 0:65], lhsT=Pb[rb][:, 64 * j:64 * j + 64], rhs=vt, start=(j == 0), stop=(j == 3)),
                             reads=[("P", rb), ("Ve", b), ("Vo", b), ("Veo", b), ("Voo", b)], writes=[("pO", rb)])
                for r in rows:
                    rb = r % NR
                    S.op("dve", lambda e: e.reciprocal(out=rec[rb][:], in_=pO[rb][0:64, 64:65]), reads=[("pO", rb)], writes=[("rec", rb)])
                for r in rows:
                    rb = r % NR
                    S.op("act", lambda e: e.activation(out=O[:, r, :], in_=pO[rb][0:64, 0:64], func=AF.Copy, scale=rec[rb][:]),
                         reads=[("pO", rb), ("rec", rb)], writes=["O"])
            S.op("dve", lambda e: e.tensor_tensor(out=Ob[:].rearrange("p r d -> p (r d)"), in0=O[:].rearrange("p r d -> p (r d)"),
                                                  in1=G[b][:].rearrange("p r d -> p (r d)"), op=ALU.mult), reads=["O", ("G", b)], writes=["Ob"])
            S.dma("pool", Sc["ytm"][:, 2 * CW + 64 * h:2 * CW + 64 * h + 64].rearrange("(r w) d -> w r d", w=64), Ob[:], reads=["Ob"], writes=[("ytm_na", h)])


Prog.phaseNA = _phaseNA


def _prep_mixer(inp, j):
    W = {}
    W.update(_prep_na(inp, j))
    W.update(_prep_dla(inp, j))
    W.update(_prep_hy(inp, j))
    return W


def _prep_dla(inp, j):
    L = DEPTH
    gpar = np.zeros((L, 4, 64, 2), np.float32)
    dtb = np.asarray(inp["mb_dt_bias"], np.float32)[:, :, MBU * j:MBU * j + MBU]
    alog = np.asarray(inp["mb_a_log"], np.float32)[:, :, MBU * j:MBU * j + MBU]
    gb = np.asarray(inp["ml_gate_b"], np.float32)[:, :, :, MLU * j:MLU * j + MLU]
    for d in range(2):
        gpar[:, d, :8 * MBU, 0] = np.repeat(dtb[:, d, :], 8, axis=1)
        gpar[:, d, :8 * MBU, 1] = np.repeat(alog[:, d, :], 8, axis=1)
        gpar[:, 2 + d, :8 * MLU, 0] = np.repeat(gb[:, d, 0, :], 8, axis=1)
        gpar[:, 2 + d, :8 * MLU, 1] = np.repeat(gb[:, d, 1, :], 8, axis=1)
    rmask = np.ones((64, 1024), np.float32)
    rmask[:, ::128] = 0.0
    s = np.arange(128)[:, None]
    ll = np.arange(128)[None, :]
    negmask = np.stack([np.where(s <= ll, 0.0, -30000.0), np.where(s >= ll, 0.0, -30000.0)]).astype(np.float32)
    return {"gpar": gpar, "rmask": rmask, "negmask": negmask,
            "dsk": np.ascontiguousarray(np.asarray(inp["mb_d"], np.float32)[:, MBU * j:MBU * j + MBU]).reshape(L, 1, MBU),
            "mbnw": np.ascontiguousarray(np.asarray(inp["mb_norm_w"], np.float32)[:, CW * j:CW * j + CW]).reshape(L, 1, CW),
            "mlnw": np.ascontiguousarray(np.asarray(inp["ml_norm_w"], np.float32)[:, CW * j:CW * j + CW]).reshape(L, 1, CW)}


def _dla_declare(self):
    I, Sc = self.I, self.Sc
    I["gpar"] = self.dram_in("gpar", [DEPTH, 4, 64, 2])
    I["rmask"] = self.dram_in("rmask", [64, 1024])
    I["negmask"] = self.dram_in("negmask", [2, 128, 128])
    I["dsk"] = self.dram_in("dsk", [DEPTH, 1, MBU])
    I["mbnw"] = self.dram_in("mbnw", [DEPTH, 1, CW])
    I["mlnw"] = self.dram_in("mlnw", [DEPTH, 1, CW])
    Sc["gq"] = self.dram_scr("gq", [4, 4, 8, T], F32, dbg=True)
    Sc["gcs"] = self.dram_scr("gcs", [4, 8, T], F32, dbg=True)
    Sc["gtot"] = self.dram_scr("gtot", [4, 8, 64], F32, dbg=True)
    Sc["yf_mb"] = self.dram_scr("yf_mb", [T, CW], F32)
    Sc["hf_ml"] = self.dram_scr("hf_ml", [T, CW], F32)
    Sc["yb_mb"] = self.dram_scr("yb_mb", [T, CW], F32)
    Sc["hb_ml"] = self.dram_scr("hb_ml", [T, CW], F32)


def _dla_streams(self, l, mixer):
    nc, S, I, Sc = self.nc, self.S, self.I, self.Sc
    U = MBU if mixer == "mb" else MLU
    P = 8 * U
    with ExitStack() as st:
        sb = lambda n, s, d=F32: st.enter_context(_alloc(nc, "sbuf", n, list(s), d))
        rm = sb("rm", [64, 1024])
        S.dma("sp", rm[:], I["rmask"], writes=["rm"])
        for d in range(2):
            md = (0 if mixer == "mb" else 2) + d
            k = lambda n: (n, d)
            gp = sb("gp%d" % d, [64, 2])
            S.dma("sp", gp[:], I["gpar"][l, md], writes=[k("gp")])
            sc = sb("sc%d" % d, [64, 1024]); a = sb("a%d" % d, [64, 1024]); cs = sb("cs%d" % d, [64, 1024])
            t1 = sb("t1%d" % d, [64, 1024]); t2 = sb("t2%d" % d, [64, 1024]); pp = sb("pp%d" % d, [64, 2])
            if mixer == "mb":
                S.dma("sp", t1[0:P, :], Sc["smallT"][MBU * d:MBU * d + MBU, :].rearrange("u (s n) -> (u s) n", n=1024), writes=[k("t1")])
                S.op("act", lambda e: e.activation(out=t1[0:P, :], in_=t1[0:P, :], func=AF.Exp, bias=gp[0:P, 0:1]), reads=[k("t1"), k("gp")], writes=[k("t1")])
                S.op("act", lambda e: e.activation(out=sc[0:P, :], in_=t1[0:P, :], func=AF.Ln, bias=1.0), reads=[k("t1")], writes=[k("sc")])
                S.op("act", lambda e: e.activation(out=pp[0:P, 0:1], in_=gp[0:P, 1:2], func=AF.Exp), reads=[k("gp")], writes=[k("pp")])
                S.op("dve", lambda e: e.tensor_scalar(out=pp[0:P, 0:1], in0=pp[0:P, 0:1], scalar1=-1.0, scalar2=None, op0=ALU.mult), reads=[k("pp")], writes=[k("pp")])
                S.op("dve", lambda e: e.tensor_scalar(out=a[0:P, :], in0=sc[0:P, :], scalar1=pp[0:P, 0:1], scalar2=None, op0=ALU.mult),
                     reads=[k("sc"), k("pp")], writes=[k("a")])
            else:
                r0 = 2 * MBU + 2 * MLU * d
                S.dma("sp", t1[0:P, :], Sc["smallT"][r0:r0 + MLU, :].rearrange("u (s n) -> (u s) n", n=1024), writes=[k("t1")])
                S.dma("sp", t2[0:P, :], Sc["smallT"][r0 + MLU:r0 + 2 * MLU, :].rearrange("u (s n) -> (u s) n", n=1024), writes=[k("t2")])
                S.op("act", lambda e: e.activation(out=sc[0:P, :], in_=t1[0:P, :], func=AF.Exp, bias=gp[0:P, 0:1]), reads=[k("t1"), k("gp")], writes=[k("sc")])
                S.op("dve", lambda e: e.tensor_scalar(out=sc[0:P, :], in0=sc[0:P, :], scalar1=float(128.0 ** -0.5), scalar2=None, op0=ALU.mult), reads=[k("sc")], writes=[k("sc")])
                S.op("dve", lambda e: e.tensor_scalar(out=pp[0:P, 0:1], in0=gp[0:P, 1:2], scalar1=-1.0, scalar2=None, op0=ALU.mult), reads=[k("gp")], writes=[k("pp")])
                S.op("act", lambda e: e.activation(out=t2[0:P, :], in_=t2[0:P, :], func=AF.Exp, scale=-1.0, bias=pp[0:P, 0:1]), reads=[k("t2"), k("pp")], writes=[k("t2")])
                S.op("act", lambda e: e.activation(out=t2[0:P, :], in_=t2[0:P, :], func=AF.Ln, bias=1.0), reads=[k("t2")], writes=[k("t2")])
                S.op("dve", lambda e: e.tensor_scalar(out=a[0:P, :], in0=t2[0:P, :], scalar1=-1.0, scalar2=None, op0=ALU.mult), reads=[k("t2")], writes=[k("a")])
            S.op("dve", lambda e: e.tensor_tensor_scan(out=cs[0:P, :], data0=rm[0:P, :], data1=a[0:P, :], initial=0.0, op0=ALU.mult, op1=ALU.add),
                 reads=["rm", k("a")], writes=[k("cs")])
            cs3 = cs[0:P, :].rearrange("p (c n) -> p c n", n=128)
            totb = cs3[:, :, 127:128].broadcast_to([P, 8, 128])
            S.dma("sp", Sc["gtot"][md, 0:U, :].rearrange("u (s c) -> (u s) c", c=8), cs3[:, :, 127], reads=[k("cs")], writes=[("gtot", md)], allow_slow_non_contiguous=True)
            t13 = t1[0:P, :].rearrange("p (c n) -> p c n", n=128)
            S.op("dve", lambda e: e.tensor_tensor(out=t13, in0=totb, in1=cs3, op=ALU.subtract), reads=[k("cs")], writes=[k("t1")])
            if d == 1:
                S.op("dve", lambda e: e.tensor_tensor(out=t2[0:P, :], in0=cs[0:P, :], in1=a[0:P, :], op=ALU.subtract), reads=[k("cs"), k("a")], writes=[k("t2")])
                S.op("dve", lambda e: e.tensor_tensor(out=cs[0:P, :], in0=t1[0:P, :], in1=a[0:P, :], op=ALU.add), reads=[k("t1"), k("a")], writes=[k("cs")])
                wexp = t2
                wk = k("t2")
            else:
                wexp = t1
                wk = k("t1")
            unf = lambda ap: ap.rearrange("u (s n) -> (u s) n", n=1024)
            S.dma("sp", unf(Sc["gcs"][md, 0:U, :]), cs[0:P, :], reads=[k("cs")], writes=[("gcs", md)])
            S.op("act", lambda e: e.activation(out=wexp[0:P, :], in_=wexp[0:P, :], func=AF.Exp), reads=[wk], writes=[wk])
            S.op("dve", lambda e: e.tensor_tensor(out=wexp[0:P, :], in0=wexp[0:P, :], in1=sc[0:P, :], op=ALU.mult), reads=[wk, k("sc")], writes=[wk])
            S.dma("sp", unf(Sc["gq"][md, 2, 0:U, :]), wexp[0:P, :], reads=[wk], writes=[("gq", md, 2)])
            S.dma("sp", unf(Sc["gq"][md, 3, 0:U, :]), sc[0:P, :], reads=[k("sc")], writes=[("gq", md, 3)])
            S.op("act", lambda e: e.activation(out=a[0:P, :], in_=cs[0:P, :], func=AF.Exp), reads=[k("cs")], writes=[k("a")])
            S.dma("sp", unf(Sc["gq"][md, 1, 0:U, :]), a[0:P, :], reads=[k("a")], writes=[("gq", md, 1)])
            S.op("dve", lambda e: e.tensor_scalar(out=cs[0:P, :], in0=cs[0:P, :], scalar1=-1.0, scalar2=None, op0=ALU.mult), reads=[k("cs")], writes=[k("cs")])
            S.dma("sp", unf(Sc["gq"][md, 0, 0:U, :]), cs[0:P, :], reads=[k("cs")], writes=[("gq", md, 0)])


def _dla_run(self, l, mixer, d):
    nc, I, Sc = self.nc, self.I, self.Sc
    S = _KeyNS(self.S, (mixer, d))
    mb = mixer == "mb"
    U = MBU if mb else MLU
    PW = 64 if mb else 129
    PS = 64 if mb else 256
    md = (0 if mb else 2) + d
    with ExitStack() as st:
        sb = lambda n, s, dt=F32: st.enter_context(_alloc(nc, "sbuf", n, list(s), dt))
        pst = lambda n, dt=F32: st.enter_context(_alloc(nc, "psum", n, [128, 512], dt))
        Q4 = self.Q4s[d]
        q4t = sb("q4t", [128, 64, 32])
        etot = sb("etot", [128, U, 64])
        negm = sb("negm", [128, 128])
        H = sb("H", [128, U, PW]); Hb = sb("Hb", [128, U, PW], BF16)
        csb = [sb("csb%d" % i, [128, U, 128]) for i in range(4)]
        LT = [sb("LT%d" % i, [128, U, 128]) for i in range(2)]
        MT = [sb("MT%d" % i, [128, U, 128], BF16) for i in range(2)]
        if mb:
            XK = [sb("XK%d" % i, [128, U * 64 + 128 * MBG], BF16) for i in range(4)]
            Xv = [t[:, 0:U * 64].rearrange("p (u w) -> p u w", w=64) for t in XK]
        else:
            Xv = [sb("Xv%d" % i, [128, U, PW], BF16)[:] for i in range(4)]
        Xw = [sb("Xw%d" % i, [128, U, PW], BF16) for i in range(2)]
        if mb:
            Kt = [t[:, U * 64:U * 64 + 128 * MBG] for t in XK]
        else:
            Kt = [sb("Kt%d" % i, [128, CW], BF16)[:] for i in range(4)]
        NG = MBG if mb else MLU
        KQ = [sb("KQ%d" % i, [128, 2, NG, 128], BF16) for i in range(4)]
        KTf = [t[:, 0, :, :] for t in KQ]
        QTf = [t[:, 1, :, :] for t in KQ]
        y2s = [sb("y2s%d" % i, [128, U, PW]) for i in range(2)]
        yo = [sb("yo%d" % i, [128, U, PW]) for i in range(2)]
        fin = [sb("fin%d" % i, [128, CW]) for i in range(2)]
        pG = pst("pG")
        NPT = 1 if mb else (MLU + 1) // 2
        py1 = [pst("py1%d" % i) for i in range(NPT)]
        py2 = [pst("py2%d" % i) for i in range(NPT)]
        pS_ = [pst("pS%d" % i) for i in range(NPT)]
        pQ = py1[0]

        def pview(tiles, u, w):
            if mb:
                return tiles[0][:, 64 * u:64 * u + w]
            return tiles[u // 2][:, 256 * (u % 2):256 * (u % 2) + w]

        def pall(tiles, h):
            if mb:
                return tiles[0][:, 0:64 * U].rearrange("p (u w) -> p u w", w=64)
            return tiles[h][:].rearrange("p (u w) -> p u w", w=256)[:, :, 0:129]

        S.op("pool", lambda e: e.memset(Q4[:], 0.0), writes=["Q4"])
        S.dma("sp", Q4[:], Sc["gq"][md].rearrange("q u t -> (q u) t"), writes=["Q4"])
        for g4 in range(4):
            for k in range(16):
                c = 16 * g4 + k
                S.op("pe", lambda e: e.transpose(pQ[:, 32 * k:32 * k + 32], Q4[:, 128 * c:128 * c + 128], self.ident_f[0:32, 0:32]), reads=["Q4"], writes=["py1"])
            S.op("dve", lambda e: e.tensor_copy(out=q4t[:, 16 * g4:16 * g4 + 16, :], in_=pQ[:].rearrange("p (k q) -> p k q", q=32)), reads=["py1"], writes=["q4t"])
        S.dma("sp", etot[:].rearrange("p u c -> p (u c)"), Sc["gtot"][md:md + 1, 0:U, :].rearrange("o u c -> o (u c)").broadcast_to([128, U * 64]), writes=["etot"])
        S.op("act", lambda e: e.activation(out=etot[:], in_=etot[:], func=AF.Exp), reads=["etot"], writes=["etot"])
        S.dma("sp", negm[:], I["negmask"][d], writes=["negm"])
        S.op("pool", lambda e: e.memset(H[:], 0.0), writes=["H"])
        S.op("pool", lambda e: e.memset(Hb[:], 0.0), writes=["Hb"])
        if not mb:
            for i in range(4):
                S.op("pool", lambda e: e.memset(Xv[i][:, :, 128:129], 1.0), writes=[("Xvo", i)])
        rden = [sb("rden%d" % i, [128, 4]) for i in range(2)]

        order = list(range(64)) if d == 0 else list(range(63, -1, -1))
        srcKQ = Sc["mbBCT"] if mb else Sc["mlkqT"]

        def issue_loads(step_):
            c_, b4 = order[step_], step_ % 4
            q0 = 128 * c_
            S.dma("sp", csb[b4][:], Sc["gcs"][md, 0:U, q0:q0 + 128].partition_broadcast(128), writes=[("csb", b4)])
            if mb:
                S.dma("act", XK[b4][:], Sc["mbxB"][q0:q0 + 128, :], writes=[("Xv", b4), ("Kt", b4)])
            else:
                S.dma("act", Xv[b4][:, :, 0:128], Sc["mlv"][q0:q0 + 128, :].rearrange("p (u w) -> p u w", w=128), writes=[("Xv", b4)])
                S.dma("act", Kt[b4][:], Sc["mlk"][q0:q0 + 128, :], writes=[("Kt", b4)])
            S.dma("act", KQ[b4][:], srcKQ[:, q0:q0 + 128].rearrange("(a g n) s -> n a g s", a=2, n=128), writes=[("KTf", b4), ("QTf", b4)])

        for s0 in range(3):
            issue_loads(s0)
        yield "setup"
        for step, c in enumerate(order):
            k = step % 2
            k4 = step % 4
            r0 = 128 * c
            xvk = [("Xv", k4)] + ([] if mb else [("Xvo", k4)])
            yield "s"
            for g in range(NG):
                S.op("pe", lambda e: e.matmul(pG[:, 128 * g:128 * g + 128], lhsT=KTf[k4][:, g, :], rhs=QTf[k4][:, g, :], start=True, stop=True),
                     reads=[("KTf", k4), ("QTf", k4)], writes=[("pG", g)])
            S.op("dve", lambda e: e.tensor_tensor(out=csb[k4][:], in0=csb[k4][:], in1=negm[:].unsqueeze(1).broadcast_to([128, U, 128]), op=ALU.add),
                 reads=[("csb", k4), "negm"], writes=[("csb", k4)])
            yield "s"
            for u in range(U):
                S.op("act", lambda e: e.activation(out=LT[k][:, u, :], in_=csb[k4][:, u, :], func=AF.Exp, bias=q4t[:, c, u:u + 1]),
                     reads=[("csb", k4), "q4t"], writes=[("LT", k, u)])
            if step + 3 < 64:
                issue_loads(step + 3)
            yield "s"
            for u in range(U):
                g = (u // 4) if mb else u
                S.op("dve", lambda e: e.scalar_tensor_tensor(out=MT[k][:, u, :], in0=pG[:, 128 * g:128 * g + 128], scalar=q4t[:, c, 24 + u:25 + u],
                                                             in1=LT[k][:, u, :], op0=ALU.mult, op1=ALU.mult),
                     reads=[("pG", g), ("LT", k, u), "q4t"], writes=[("MT", k, u)])
            yield "s"
            for u in range(U):
                S.op("pe", lambda e: e.matmul(pview(py1, u, PW), lhsT=MT[k][:, u, :], rhs=Xv[k4][:, u, :], start=True, stop=True),
                     reads=[("MT", k, u)] + xvk, writes=["py1"])
            if mb:
                for g in range(MBG):
                    S.op("pe", lambda e: e.matmul(py2[0][:, 256 * g:256 * g + 256], lhsT=QTf[k4][:, g, :], rhs=Hb[:, 4 * g:4 * g + 4, :],
                                                  start=True, stop=True), reads=[("QTf", k4), "Hb"], writes=["py2"])
            else:
                for u in range(U):
                    S.op("pe", lambda e: e.matmul(pview(py2, u, PW), lhsT=QTf[k4][:, u, :], rhs=Hb[:, u, :], start=True, stop=True),
                         reads=[("QTf", k4), "Hb"], writes=["py2"])
            yield "s"
            ecs_b = lambda u0, n: q4t[:, c, 8 + u0:8 + u0 + n].unsqueeze(2).broadcast_to([128, n, PW])
            w_b = q4t[:, c, 16:16 + U].unsqueeze(2).broadcast_to([128, U, PW])
            if mb:
                S.op("dve", lambda e: e.tensor_tensor(out=y2s[k][:], in0=pall(py2, 0), in1=ecs_b(0, U), op=ALU.mult), reads=["py2", "q4t"], writes=[("y2s", k)])
                S.op("dve", lambda e: e.tensor_tensor(out=yo[k][:], in0=pall(py1, 0), in1=y2s[k][:], op=ALU.add), reads=["py1", ("y2s", k)], writes=[("yo", k)])
            else:
                for h in range(NPT):
                    S.op("dve", lambda e: e.tensor_tensor(out=y2s[k][:, 2 * h:2 * h + 2, :], in0=pall(py2, h), in1=ecs_b(2 * h, 2), op=ALU.mult),
                         reads=["py2", "q4t"], writes=[("y2s", k, h)])
                    S.op("dve", lambda e: e.tensor_tensor(out=yo[k][:, 2 * h:2 * h + 2, :], in0=pall(py1, h), in1=y2s[k][:, 2 * h:2 * h + 2, :], op=ALU.add),
                         reads=["py1", ("y2s", k, h)], writes=[("yo", k, h)])
            yok = [("yo", k)] if mb else [("yo", k, h) for h in range(NPT)]
            yield "s"
            S.op("pool", lambda e: e.tensor_tensor(out=Xw[k][:], in0=Xv[k4], in1=w_b, op=ALU.mult), reads=xvk + ["q4t"], writes=[("Xw", k)])
            if mb:
                for g in range(MBG):
                    S.op("pe", lambda e: e.matmul(pS_[0][:, 256 * g:256 * g + 256], lhsT=Kt[k4][:, 128 * g:128 * g + 128], rhs=Xw[k][:, 4 * g:4 * g + 4, :],
                                                  start=True, stop=True), reads=[("Kt", k4), ("Xw", k)], writes=["pS"])
            else:
                for u in range(U):
                    S.op("pe", lambda e: e.matmul(pview(pS_, u, PW), lhsT=Kt[k4][:, 128 * u:128 * u + 128], rhs=Xw[k][:, u, :], start=True, stop=True),
                         reads=[("Kt", k4), ("Xw", k)], writes=["pS"])
            yield "s"
            S.op("pool", lambda e: e.tensor_tensor(out=H[:], in0=H[:], in1=etot[:, :, c:c + 1].broadcast_to([128, U, PW]), op=ALU.mult),
                 reads=["H", "etot"], writes=["H"])
            if mb:
                S.op("dve", lambda e: e.tensor_tensor(out=H[:], in0=pall(pS_, 0), in1=H[:], op=ALU.add), reads=["H", "pS"], writes=["H"])
            else:
                for h in range(NPT):
                    S.op("dve", lambda e: e.tensor_tensor(out=H[:, 2 * h:2 * h + 2, :], in0=pall(pS_, h), in1=H[:, 2 * h:2 * h + 2, :], op=ALU.add),
                         reads=["H", "pS"], writes=["H"])
            S.op("act", lambda e: e.activation(out=Hb[:], in_=H[:], func=AF.Copy), reads=["H"], writes=["Hb"])
            yield "s"
            f = fin[k]
            if mb:
                ysrc = yo[k][:].rearrange("p u w -> p (u w)")
                fk = yok
            else:
                S.op("act", lambda e: e.activation(out=rden[k][:, 0:MLU], in_=yo[k][:, :, 128], func=AF.Abs), reads=yok, writes=[("rden", k)])
                S.op("dve", lambda e: e.tensor_scalar(out=rden[k][:], in0=rden[k][:], scalar1=1.0, scalar2=None, op0=ALU.max), reads=[("rden", k)], writes=[("rden", k)])
                S.op("dve", lambda e: e.reciprocal(out=rden[k][:], in_=rden[k][:]), reads=[("rden", k)], writes=[("rden", k)])
                S.op("pool", lambda e: e.tensor_tensor(out=f[:].rearrange("p (u w) -> p u w", w=128), in0=yo[k][:, :, 0:128],
                                                       in1=rden[k][:, 0:MLU].unsqueeze(2).broadcast_to([128, MLU, 128]), op=ALU.mult),
                     reads=yok + [("rden", k)], writes=[("fin", k)])
                ysrc = f[:]
                fk = [("fin", k)]
            dst_f = (Sc["yf_mb"] if mb else Sc["hf_ml"]) if d == 0 else (Sc["yb_mb"] if mb else Sc["hb_ml"])
            S.dma("pool", dst_f[r0:r0 + 128, :], ysrc, reads=fk, writes=[("ydir", c)])
            yield "chunk"
        yield "done"


def _dla_final(self, l, mixer):
    nc, S, I, Sc = self.nc, self.S, self.I, self.Sc
    mb = mixer == "mb"
    with ExitStack() as st:
        sb = lambda n, s, dt=F32: st.enter_context(_alloc(nc, "sbuf", n, list(s), dt))
        nwb = sb("nwb2", [128, CW])
        S.dma("sp", nwb[:], I["mbnw" if mb else "mlnw"][l].broadcast_to([128, CW]), writes=["nwb2"])
        if mb:
            dskb = sb("dskb", [128, MBU])
            S.dma("sp", dskb[:], I["dsk"][l].broadcast_to([128, MBU]), writes=["dskb"])
        NB_ = 3
        prev = [sb("prev%d" % i, [128, CW]) for i in range(NB_)]
        cur = [sb("cur%d" % i, [128, CW]) for i in range(NB_)]
        Zt = [sb("Zt%d" % i, [128, CW], BF16) for i in range(NB_)]
        Ot = [sb("Ot%d" % i, [128, CW], BF16) for i in range(NB_)]
        ssq = [sb("ssq%d" % i, [128, 4]) for i in range(NB_)]
        junk = sb("junkd", [128, CW], BF16)
        outb = [sb("outb%d" % i, [128, CW], BF16) for i in range(NB_)]
        for c in range(64):
            k = c % NB_
            r0 = 128 * c
            S.dma("sp", prev[k][:], (Sc["yf_mb"] if mb else Sc["hf_ml"])[r0:r0 + 128, :], writes=[("prev", k)])
            S.dma("pool", cur[k][:], (Sc["yb_mb"] if mb else Sc["hb_ml"])[r0:r0 + 128, :], writes=[("cur", k)])
            S.dma("sp", Zt[k][:], (Sc["mbz"] if mb else Sc["mlz"])[r0:r0 + 128, :], writes=[("Zt", k)])
            S.dma("pool", Ot[k][:], (Sc["mbx"] if mb else Sc["mlo"])[r0:r0 + 128, :], writes=[("Ot", k)])
            S.op("pool", lambda e: e.tensor_tensor(out=prev[k][:], in0=prev[k][:], in1=cur[k][:], op=ALU.add), reads=[("prev", k), ("cur", k)], writes=[("prev", k)])
            if mb:
                S.op("pool", lambda e: e.tensor_tensor(out=cur[k][:].rearrange("p (u w) -> p u w", w=64), in0=Ot[k][:].rearrange("p (u w) -> p u w", w=64),
                                                       in1=dskb[:].unsqueeze(2).broadcast_to([128, MBU, 64]), op=ALU.mult),
                     reads=[("Ot", k), ("cur", k), "dskb"], writes=[("cur", k)])
                S.op("dve", lambda e: e.tensor_tensor(out=prev[k][:], in0=prev[k][:], in1=cur[k][:], op=ALU.add), reads=[("prev", k), ("cur", k)], writes=[("prev", k)])
                S.op("dve", lambda e: e.tensor_tensor(out=prev[k][:], in0=prev[k][:], in1=Zt[k][:], op=ALU.mult), reads=[("prev", k), ("Zt", k)], writes=[("prev", k)])
                ngr, gw = MBG, 256
            else:
                S.op("dve", lambda e: e.tensor_tensor(out=prev[k][:], in0=prev[k][:], in1=Ot[k][:], op=ALU.mult), reads=[("prev", k), ("Ot", k)], writes=[("prev", k)])
                ngr, gw = MLU, 128
            for g in range(ngr):
                S.op("act", lambda e: e.activation(out=junk[:, 0:gw], in_=prev[k][:, gw * g:gw * g + gw], func=AF.Square, accum_out=ssq[k][:, g:g + 1]),
                     reads=[("prev", k)], writes=["junkd", ("ssq", k)])
            S.op("dve", lambda e: e.tensor_scalar(out=ssq[k][:, 0:ngr], in0=ssq[k][:, 0:ngr], scalar1=1.0 / gw, scalar2=EPS, op0=ALU.mult, op1=ALU.add),
                 reads=[("ssq", k)], writes=[("ssq", k)])
            S.op("act", lambda e: e.activation(out=ssq[k][:, 0:ngr], in_=ssq[k][:, 0:ngr], func=AF.Sqrt), reads=[("ssq", k)], writes=[("ssq", k)])
            S.op("dve", lambda e: e.reciprocal(out=ssq[k][:, 0:ngr], in_=ssq[k][:, 0:ngr]), reads=[("ssq", k)], writes=[("ssq", k)])
            for g in range(ngr):
                S.op("dve", lambda e: e.scalar_tensor_tensor(out=(outb[k] if mb else prev[k])[:, gw * g:gw * g + gw], in0=prev[k][:, gw * g:gw * g + gw],
                                                             scalar=ssq[k][:, g:g + 1], in1=nwb[:, gw * g:gw * g + gw], op0=ALU.mult, op1=ALU.mult),
                     reads=[("prev", k), ("ssq", k), "nwb2"], writes=[("outb", k) if mb else ("prev", k)])
            if not mb:
                S.op("pool", lambda e: e.tensor_tensor(out=outb[k][:], in0=prev[k][:], in1=Zt[k][:], op=ALU.mult), reads=[("prev", k), ("Zt", k)], writes=[("outb", k)])
            col0 = 0 if mb else CW
            S.dma("pool", Sc["ytm"][r0:r0 + 128, col0:col0 + CW], outb[k][:], reads=[("outb", k)], writes=[("ytm_dla", mixer, c)])


def _phaseDLA(self, l):
    S, nc = self.S, self.nc
    for mixer in ("mb", "ml"):
        _dla_streams(self, l, mixer)
        S.barrier()
        with ExitStack() as st:
            self.Q4s = [st.enter_context(_alloc(nc, "sbuf", "Q4_%d" % d, [32, T], F32)) for d in range(2)]
            gens = [_dla_run(self, l, mixer, d) for d in (0, 1)]
            for g in gens:
                next(g)
            while True:
                rs = [next(g) for g in gens]
                if all(r == "done" for r in rs):
                    break
            S.barrier()
            for g in reversed(gens):
                try:
                    next(g)
                except StopIteration:
                    pass
        _dla_final(self, l, mixer)
        S.barrier()


Prog.phaseDLA = _phaseDLA


N2L = 2 * T
CG = 32


def _prep_hy(inp, j):
    L = DEPTH
    n = np.arange(128)
    ang = 2.0 * np.pi * np.outer(n, n) / 128.0
    Fre, Fim = np.cos(ang), -np.sin(ang)
    dft = np.stack([Fre, Fim, Fre, -Fim], axis=1).astype(np.float32)
    angt = 2.0 * np.pi * np.outer(n, n) / float(N2L)
    twd = np.stack([np.cos(angt), -np.sin(angt)], axis=1).astype(np.float32)
    t = np.arange(T, dtype=np.float32)
    t_norm = t / np.float32(T)
    bands = np.arange(1, 9, dtype=np.float32)
    a = (np.float32(2.0 * math.pi / T)) * t[:, None] * bands[None, :]
    pos = np.concatenate([t_norm[:, None], np.cos(a), np.sin(a)], axis=-1).astype(np.float32)
    hyp = np.zeros((L, 64, 4), np.float32)
    hyp[:, :, 0] = np.asarray(inp["hy_b1"]); hyp[:, :, 1] = np.asarray(inp["hy_freq"]); hyp[:, :, 2] = np.asarray(inp["hy_b2"])
    dec = np.asarray(inp["hy_decay"], np.float32).reshape(L, 4, 512)[:, :, HC * j:HC * j + HC].reshape(L, 4 * NB, 128).transpose(0, 2, 1)
    w3 = np.asarray(inp["hy_w3"], np.float32).reshape(L, 64, 4, 512)[:, :, :, HC * j:HC * j + HC].reshape(L, 64, 4 * HC)
    skip = np.asarray(inp["hy_skip"], np.float32)[:, :, HC * j:HC * j + HC].reshape(L, 2, 1, HC)
    return {"dft": np.ascontiguousarray(dft.reshape(128, 512)), "twd": np.ascontiguousarray(twd.reshape(128, 256)),
            "posT": np.ascontiguousarray(pos.T), "tneg": (-t_norm).reshape(1, T).astype(np.float32),
            "hyp": hyp, "hydec": np.ascontiguousarray(dec),
            "hyw1": np.asarray(inp["hy_w1"], np.float32), "hyw2": np.asarray(inp["hy_w2"], np.float32),
            "hyw3": np.ascontiguousarray(w3), "hyskip": np.ascontiguousarray(skip)}


def _hy_declare(self):
    I, Sc = self.I, self.Sc
    I["dft"] = self.dram_in("dft", [128, 512]); I["twd"] = self.dram_in("twd", [128, 256])
    I["posT"] = self.dram_in("posT", [17, T]); I["tneg"] = self.dram_in("tneg", [1, T])
    I["hyp"] = self.dram_in("hyp", [DEPTH, 64, 4]); I["hydec"] = self.dram_in("hydec", [DEPTH, 128, 4 * NB])
    I["hyw1"] = self.dram_in("hyw1", [DEPTH, 17, 64]); I["hyw2"] = self.dram_in("hyw2", [DEPTH, 64, 64])
    I["hyw3"] = self.dram_in("hyw3", [DEPTH, 64, 4 * HC]); I["hyskip"] = self.dram_in("hyskip", [DEPTH, 2, 1, HC])
    Sc["gflt"] = self.dram_scr("gflt", [2, HC, N2L], BF16, dbg=True)


def _phaseHY(self, l):
    nc, S, I, Sc = self.nc, self.S, self.I, self.Sc
    PI = float(np.pi)
    with ExitStack() as st:
        sb = lambda n, s, d=F32: st.enter_context(_alloc(nc, "sbuf", n, list(s), d))
        pst = lambda n, d=F32: st.enter_context(_alloc(nc, "psum", n, [128, 512], d))
        w1 = sb("hw1", [17, 64]); w2 = sb("hw2", [64, 64]); w3f = sb("hw3f", [64, 4 * HC]); w3b = sb("hw3b", [64, 4 * HC], BF16)
        hp = sb("hhp", [64, 4]); fb = sb("hfb", [64, 2])
        hid = sb("hhid", [64, T], BF16)
        tn = sb("htn", [128, T]); dec = sb("hdec", [128, 4 * NB])
        S.dma("sp", w1[:], I["hyw1"][l], writes=["w1"]); S.dma("sp", w2[:], I["hyw2"][l], writes=["w2"])
        S.dma("sp", w3f[:], I["hyw3"][l], writes=["w3f"]); S.dma("sp", hp[:], I["hyp"][l], writes=["hp"])
        S.dma("sp", tn[:], I["tneg"].broadcast_to([128, T]), writes=["tn"]); S.dma("sp", dec[:], I["hydec"][l], writes=["dec"])
        S.op("pool", lambda e: e.tensor_copy(out=w3b[:], in_=w3f[:]), reads=["w3f"], writes=["w3b"])
        S.op("dve", lambda e: e.tensor_tensor(out=fb[:, 0:1], in0=hp[:, 0:1], in1=hp[:, 1:2], op=ALU.mult), reads=["hp"], writes=["fb"])
        S.op("dve", lambda e: e.tensor_tensor(out=fb[:, 1:2], in0=hp[:, 2:3], in1=hp[:, 1:2], op=ALU.mult), reads=["hp", "fb"], writes=["fb"])
        pt_ = [sb("hpt%d" % i, [17, 512]) for i in range(2)]
        arg = [sb("harg%d" % i, [64, 512]) for i in range(2)]
        ta = [sb("hta%d" % i, [64, 512]) for i in range(2)]
        tb = [sb("htb%d" % i, [64, 512]) for i in range(2)]
        h1 = [sb("hh1%d" % i, [64, 512]) for i in range(2)]
        pz = [pst("hpz%d" % i) for i in range(2)]

        def sin_layer(src_ps, pk, col, out_ap, okey, k):
            a = arg[k]; ak = ("arg", k)
            S.op("dve", lambda e: e.tensor_scalar(out=a[:], in0=src_ps[0:64, :], scalar1=hp[:, 1:2], scalar2=fb[:, col:col + 1], op0=ALU.mult, op1=ALU.add),
                 reads=[pk, "hp", "fb"], writes=[ak])
            S.op("dve", lambda e: e.tensor_scalar(out=ta[k][:], in0=a[:], scalar1=PI, scalar2=-2 * PI, op0=ALU.is_gt, op1=ALU.mult), reads=[ak], writes=[("ta", k)])
            S.op("dve", lambda e: e.tensor_scalar(out=tb[k][:], in0=a[:], scalar1=-PI, scalar2=2 * PI, op0=ALU.is_lt, op1=ALU.mult), reads=[ak], writes=[("tb", k)])
            S.op("pool", lambda e: e.tensor_tensor(out=ta[k][:], in0=ta[k][:], in1=tb[k][:], op=ALU.add), reads=[("ta", k), ("tb", k)], writes=[("ta", k)])
            S.op("pool", lambda e: e.tensor_tensor(out=a[:], in0=a[:], in1=ta[k][:], op=ALU.add), reads=[ak, ("ta", k)], writes=[ak])
            S.op("act", lambda e: e.activation(out=out_ap, in_=a[:], func=AF.Sin), reads=[ak], writes=[okey])

        for c in range(16):
            k = c % 2
            S.dma("sp", pt_[k][:], I["posT"][:, 512 * c:512 * c + 512], writes=[("pt", k)])
            S.op("pe", lambda e: e.matmul(pz[0][0:64, :], lhsT=w1[:], rhs=pt_[k][:], start=True, stop=True), reads=["w1", ("pt", k)], writes=["pz0"])
            sin_layer(pz[0], "pz0", 0, h1[k][:], ("h1", k), k)
            S.op("pe", lambda e: e.matmul(pz[1][0:64, :], lhsT=w2[:], rhs=h1[k][:], start=True, stop=True), reads=["w2", ("h1", k)], writes=["pz1"])
            sin_layer(pz[1], "pz1", 1, hid[:, 512 * c:512 * c + 512], ("hid", c), k)
        hidk = [("hid", c) for c in range(16)]
        gt = [sb("hgt%d" % i, [128, N2L], BF16) for i in range(2)]
        win = [sb("hwin%d" % i, [128, 512]) for i in range(2)]
        pf = [pst("hpf%d" % i) for i in range(2)]
        for i in range(2):
            S.op("pool", lambda e: e.memset(gt[i][:, T:T + 1], 0.0), writes=[("gtz", i)])
        it = 0
        for o in range(2):
            for cb in range(NB):
                g = gt[(o * NB + cb) % 2]; gk = ("gt", (o * NB + cb) % 2)
                gparts = []
                for dr in range(2):
                    col0 = (o * 2 + dr) * HC + 128 * cb
                    di = (o * 2 + dr) * NB + cb
                    for c in range(16):
                        k = it % 2; it += 1
                        S.op("pe", lambda e: e.matmul(pf[k][:], lhsT=w3b[:, col0:col0 + 128], rhs=hid[:, 512 * c:512 * c + 512], start=True, stop=True),
                             reads=["w3b", ("hid", c)], writes=[("pf", k)])
                        S.op("act", lambda e: e.activation(out=win[k][:], in_=tn[:, 512 * c:512 * c + 512], func=AF.Exp, scale=dec[:, di:di + 1]),
                             reads=["tn", "dec"], writes=[("win", k)])
                        pk = (gk, dr, c)
                        gparts.append(pk)
                        if dr == 0:
                            S.op("dve", lambda e: e.tensor_tensor(out=g[:, 512 * c:512 * c + 512], in0=pf[k][:], in1=win[k][:], op=ALU.mult),
                                 reads=[("pf", k), ("win", k)], writes=[pk])
                        else:
                            j0 = 1 if c == 0 else 0
                            lo = N2L - 512 * c - 511
                            hi = N2L - 512 * c - j0 + 1
                            S.op("dve", lambda e: e.tensor_tensor(out=g[:, lo:hi][:, ::-1], in0=pf[k][:, j0:512], in1=win[k][:, j0:512], op=ALU.mult),
                                 reads=[("pf", k), ("win", k)], writes=[pk])
                S.dma("pool", Sc["gflt"][o, 128 * cb:128 * cb + 128, :], g[:], reads=gparts + [("gtz", (o * NB + cb) % 2)], writes=[("gflt", o, cb)])
    S.barrier()
    with ExitStack() as st:
        sb = lambda n, s, d=F32: st.enter_context(_alloc(nc, "sbuf", n, list(s), d))
        dftf = sb("dftf", [128, 512]); dft = sb("dftb", [128, 4, 128], BF16); twd = sb("twd", [128, 2, 128])
        S.dma("sp", dftf[:], I["dft"], writes=["dftf"]); S.dma("sp", twd[:].rearrange("p a k -> p (a k)"), I["twd"], writes=["twd"])
        S.op("dve", lambda e: e.tensor_copy(out=dft[:].rearrange("p a k -> p (a k)"), in_=dftf[:]), reads=["dftf"], writes=["dft"])
        Fre, Fim, nFim = dft[:, 0, :], dft[:, 1, :], dft[:, 3, :]
        Fcat = dft[:, 0:2, :].rearrange("p a k -> p (a k)")
        Fci2 = dft[:, 1:3, :].rearrange("p a k -> p (a k)")
        Fci1 = dft[:, 2:4, :].rearrange("p a k -> p (a k)")
        G = sb("hyG", [128, 2, CG, 2, 128], BF16)
        gblk = [sb("gblk%d" % i, [128, CG, 128], BF16) for i in range(2)]
        sig = {n: sb("sig_" + n, [64, CG, 128], BF16) for n in ("v", "x1", "x2", "g")}
        zblk = sb("zblk", [64, CG, 128], BF16); oblk = sb("oblk", [64, CG, 128], BF16)
        skb = sb("skb", [64, 2, CG])
        NQ = CG // 4
        Ap = [[sb("Ap%d_%d" % (c_, i), [128, 4, 2, 128], BF16) for i in range(2)] for c_ in range(2)]
        Yp = [[sb("Yp%d_%d" % (c_, i), [128, 4, 2, 128], BF16) for i in range(2)] for c_ in range(2)]
        Bp = [[sb("Bp%d_%d" % (c_, i), [128, 4, 2, 128], BF16) for i in range(2)] for c_ in range(2)]
        tt = [[[sb("tt%d_%d_%d" % (c_, i, j), [128, 4, 128]) for j in range(4)] for i in range(2)] for c_ in range(2)]
        ep = [[[sb("ep%d_%d_%d" % (c_, i, j), [64, 4, 128]) for j in range(2)] for i in range(2)] for c_ in range(2)]
        pAall = st.enter_context(_alloc(nc, "psum", "hpA", [128, 2048], F32))
        pBall = st.enter_context(_alloc(nc, "psum", "hpB", [128, 2048], F32))
        Tre = twd[:, 0, :].unsqueeze(1).broadcast_to([128, 4, 128])
        Tim = twd[:, 1, :].unsqueeze(1).broadcast_to([128, 4, 128])

        def chain(cg, sx):
            pA = pAall[:, 1024 * sx:1024 * sx + 1024]
            pB = pBall[:, 1024 * sx:1024 * sx + 1024]
            pA3 = pA.rearrange("p (c k) -> p c k", k=256)
            pBr = pB.rearrange("p (r c k) -> p r c k", r=2, c=4)
            kA, kB = ("pA", sx), ("pB", sx)
            cn = {"c": 0, "n": 0}

            def cmul(out_t, okey, are, aim, akeys, bre, bim, bkeys, conj):
                i = cn["c"] % 2; cn["c"] += 1
                t1, t2, t3, t4 = tt[sx][i]
                ks = [("tt", sx, i, j) for j in range(4)]
                S.op("dve", lambda e: e.tensor_tensor(out=t1[:], in0=are, in1=bre, op=ALU.mult), reads=akeys + bkeys, writes=[ks[0]])
                S.op("dve", lambda e: e.tensor_tensor(out=t2[:], in0=aim, in1=bim, op=ALU.mult), reads=akeys + bkeys, writes=[ks[1]])
                S.op("dve", lambda e: e.tensor_tensor(out=t3[:], in0=are, in1=bim, op=ALU.mult), reads=akeys + bkeys, writes=[ks[2]])
                S.op("dve", lambda e: e.tensor_tensor(out=t4[:], in0=aim, in1=bre, op=ALU.mult), reads=akeys + bkeys, writes=[ks[3]])
                if not conj:
                    S.op("pool", lambda e: e.tensor_tensor(out=out_t[:, :, 0, :], in0=t1[:], in1=t2[:], op=ALU.subtract), reads=ks[0:2], writes=[okey + ("re",)])
                    S.op("pool", lambda e: e.tensor_tensor(out=out_t[:, :, 1, :], in0=t3[:], in1=t4[:], op=ALU.add), reads=ks[2:4], writes=[okey + ("im",)])
                else:
                    S.op("pool", lambda e: e.tensor_tensor(out=out_t[:, :, 0, :], in0=t1[:], in1=t2[:], op=ALU.add), reads=ks[0:2], writes=[okey + ("re",)])
                    S.op("pool", lambda e: e.tensor_tensor(out=out_t[:, :, 1, :], in0=t4[:], in1=t3[:], op=ALU.subtract), reads=ks[2:4], writes=[okey + ("im",)])

            def fwd_quad(src_fn, K, skeys, i):
                for ch in range(4):
                    S.op("pe", lambda e: e.matmul(pA3[:, ch, :], lhsT=src_fn(ch), rhs=Fcat[0:K, :], start=True, stop=True),
                         reads=skeys + ["dft"], writes=[kA])
                yield "s"
                cmul(Ap[sx][i], ("Ap", sx, i), pA3[:, :, 0:128], pA3[:, :, 128:256], [kA], Tre, Tim, ["twd"], False)
                yield "s"
                rre = Ap[sx][i][:, :, 0, :]
                rim = Ap[sx][i][:, :, 1, :]
                kk = [("Ap", sx, i, "re"), ("Ap", sx, i, "im"), "dft"]
                S.op("pe", lambda e: e.matmul(pB[:, 0:512], lhsT=Fre, rhs=rre, start=True, stop=False), reads=kk, writes=[kB])
                S.op("pe", lambda e: e.matmul(pB[:, 0:512], lhsT=nFim, rhs=rim, start=False, stop=True), reads=kk, writes=[kB])
                S.op("pe", lambda e: e.matmul(pB[:, 512:1024], lhsT=Fim, rhs=rre, start=True, stop=False), reads=kk, writes=[kB])
                S.op("pe", lambda e: e.matmul(pB[:, 512:1024], lhsT=Fre, rhs=rim, start=False, stop=True), reads=kk, writes=[kB])
                yield "s"

            for o in range(2):
                for oc8 in range(CG // 8):
                    i = cn["n"] % 2; cn["n"] += 1
                    ch0 = 8 * oc8 + 4 * sx
                    yield from fwd_quad(lambda ch: gblk[o][:, ch0 + ch, :], 128, [("gblk", o)], i)
                    gv = G[:, o, ch0:ch0 + 4, :, :]
                    S.op("act", lambda e: e.activation(out=gv[:, :, 0, :], in_=pBr[:, 0, :, :], func=AF.Copy), reads=[kB], writes=[("G", o, oc8, sx, 0)])
                    S.op("act", lambda e: e.activation(out=gv[:, :, 1, :], in_=pBr[:, 1, :, :], func=AF.Copy), reads=[kB], writes=[("G", o, oc8, sx, 1)])
                    yield "s"
            for o in range(2):
                src_t = sig["v"] if o == 0 else zblk
                for oc8 in range(CG // 8):
                    i = cn["n"] % 2; cn["n"] += 1
                    ch0 = 8 * oc8 + 4 * sx
                    skeys = [("sig", "v")] if o == 0 else [("zblk", oc8, sx)]
                    yield from fwd_quad(lambda ch: src_t[:, ch0 + ch, :], 64, skeys, i)
                    gv = G[:, o, ch0:ch0 + 4, :, :]
                    gk = [("G", o, oc8, sx, 0), ("G", o, oc8, sx, 1)]
                    cmul(Yp[sx][i], ("Yp", sx, i), pBr[:, 0, :, :], pBr[:, 1, :, :], [kB], gv[:, :, 0, :], gv[:, :, 1, :], gk, False)
                    yield "s"
                    for ch in range(4):
                        S.op("pe", lambda e: e.matmul(pA3[:, ch, :], lhsT=Yp[sx][i][:, ch, 0, :], rhs=Fci1, start=True, stop=False),
                             reads=[("Yp", sx, i, "re"), ("Yp", sx, i, "im"), "dft"], writes=[kA])
                        S.op("pe", lambda e: e.matmul(pA3[:, ch, :], lhsT=Yp[sx][i][:, ch, 1, :], rhs=Fci2, start=False, stop=True),
                             reads=[("Yp", sx, i, "re"), ("Yp", sx, i, "im"), "dft"], writes=[kA])
                    yield "s"
                    cmul(Bp[sx][i], ("Bp", sx, i), pA3[:, :, 0:128], pA3[:, :, 128:256], [kA], Tre, Tim, ["twd"], True)
                    yield "s"
                    kk = [("Bp", sx, i, "re"), ("Bp", sx, i, "im"), "dft"]
                    S.op("pe", lambda e: e.matmul(pB[0:64, 0:512], lhsT=Fre[:, 0:64], rhs=Bp[sx][i][:, :, 0, :], start=True, stop=False), reads=kk, writes=[kB])
                    S.op("pe", lambda e: e.matmul(pB[0:64, 0:512], lhsT=Fim[:, 0:64], rhs=Bp[sx][i][:, :, 1, :], start=False, stop=True), reads=kk, writes=[kB])
                    yield "s"
                    e1, e2 = ep[sx][i]
                    yv = pB[0:64, 0:512].rearrange("p (c k) -> p c k", k=128)
                    skv = skb[:, o, ch0:ch0 + 4].unsqueeze(2).broadcast_to([64, 4, 128])
                    uu = src_t[:, ch0:ch0 + 4, :]
                    S.op("pool", lambda e: e.tensor_tensor(out=e1[:], in0=uu, in1=skv, op=ALU.mult), reads=skeys + ["skb"], writes=[("e1", sx, i)])
                    S.op("dve", lambda e: e.scalar_tensor_tensor(out=e2[:], in0=yv, scalar=1.0 / N2L, in1=e1[:], op0=ALU.mult, op1=ALU.add),
                         reads=[kB, ("e1", sx, i)], writes=[("e2", sx, i)])
                    if o == 0:
                        S.op("pool", lambda e: e.tensor_tensor(out=zblk[:, ch0:ch0 + 4, :], in0=e2[:], in1=sig["x1"][:, ch0:ch0 + 4, :], op=ALU.mult),
                             reads=[("e2", sx, i), ("sig", "x1")], writes=[("zblk", oc8, sx)])
                    else:
                        S.op("pool", lambda e: e.tensor_tensor(out=e1[:], in0=e2[:], in1=sig["x2"][:, ch0:ch0 + 4, :], op=ALU.mult),
                             reads=[("e2", sx, i), ("sig", "x2")], writes=[("e1", sx, i)])
                        S.op("pool", lambda e: e.tensor_tensor(out=oblk[:, ch0:ch0 + 4, :], in0=e1[:], in1=sig["g"][:, ch0:ch0 + 4, :], op=ALU.mult),
                             reads=[("e1", sx, i), ("sig", "g")], writes=[("oblk", oc8, sx)])
                    yield "s"

        for cg in range(HC // CG):
            c0 = CG * cg
            for o in range(2):
                S.dma("sp", gblk[o][:], Sc["gflt"][o, c0:c0 + CG, :].rearrange("c (a b) -> a c b", b=128), writes=[("gblk", o)])
            for n_, src, r0 in (("v", "hyuT", 0), ("x1", "hyuT", HC), ("x2", "hyuT", 2 * HC), ("g", "hygT", 0)):
                S.dma("sp" if n_ in ("v", "x1") else "act", sig[n_][:], Sc[src][r0 + c0:r0 + c0 + CG, :].rearrange("c (a b) -> a c b", b=128), writes=[("sig", n_)])
            S.dma("sp", skb[:], I["hyskip"][l, :, :, c0:c0 + CG].rearrange("o x c -> x o c").broadcast_to([64, 2, CG]), writes=["skb"])
            gens = [chain(cg, sx) for sx in range(2)]
            live = list(gens)
            while live:
                for g_ in list(live):
                    try:
                        next(g_)
                    except StopIteration:
                        live.remove(g_)
            S.dma("pool", Sc["yhyT"][c0:c0 + CG, :].rearrange("c (a b) -> a c b", b=128), oblk[:],
                  reads=[("oblk", q, sx) for q in range(CG // 8) for sx in range(2)], writes=[("yhyT", cg)])


Prog.phaseHY = _phaseHY


NCORES = 8


def kernel(**inputs):
    P = Prog(nlayers=DEPTH, debug=False)
    nc = P.build()
    names = set(P.I.keys())
    x = np.asarray(inputs["x"], np.float32)
    Wj = []
    for j in range(SP):
        W = _prep_weights(inputs, j)
        W.update(_prep_mixer(inputs, j))
        Wj.append({k: v for k, v in W.items() if k in names})
    in_maps = []
    for c in range(NCORES):
        b, j = c // SP, c % SP
        m = dict(Wj[j])
        m["x"] = np.ascontiguousarray(x[b])
        m["xh"] = np.ascontiguousarray(x[b][:, OW * j:OW * j + OW])
        in_maps.append(m)
    res = run_bass_kernel_spmd(nc, in_maps, core_ids=list(range(NCORES)))
    out = np.empty((NCORES // SP, T, D), np.float32)
    for c in range(NCORES):
        b, j = c // SP, c % SP
        out[b][:, OW * j:OW * j + OW] = np.asarray(res.results[c]["out"], np.float32)
    return out
```
